# Optimizing a Trainium2 kernel written in Bass

```python
import math
import jax
import jax.numpy as jnp
from jax import lax
import numpy as np

D_MODEL = 2048
BATCH = 4
SEQ = 2048
DEPTH = 4
DEC_BATCH = 8
DEC_SEQ = 4
PAST_LEN = 16384
PAGE_SIZE = 128

N_EVEN = (DEPTH + 1) // 2
N_ODD = DEPTH // 2
A_WIDTH = D_MODEL // 2
A_GROUPS = 8
A_GROUP_DIM = A_WIDTH // A_GROUPS
A_CHUNK = 128
NSA_HEADS = 8
NSA_HEAD_DIM = 128
NSA_KV_HEADS = 2
NSA_GROUP = NSA_HEADS // NSA_KV_HEADS
B_WIDTH = NSA_HEADS * NSA_HEAD_DIM
KV_COLS = 2 * NSA_KV_HEADS * NSA_HEAD_DIM
CMP_BLOCK = 64
SEL_BLOCK = CMP_BLOCK
SEL_TOPN = 16
WINDOW = 512
SEL_QBLOCK = 64
WIN_QBLOCK = 128
ROPE_THETA = 10000.0
ATTN_SCALE = NSA_HEAD_DIM ** -0.5
SPLIT_EVEN = (A_WIDTH, 2 * A_WIDTH, 2 * A_WIDTH + B_WIDTH, 2 * A_WIDTH + B_WIDTH + KV_COLS,
              2 * A_WIDTH + B_WIDTH + 2 * KV_COLS, 2 * A_WIDTH + B_WIDTH + 3 * KV_COLS)
E_IN = 2 * A_WIDTH + B_WIDTH + 3 * KV_COLS + 3 * NSA_HEADS
MIX_OUT = A_WIDTH + B_WIDTH
SSM_INNER = 2 * D_MODEL
SSM_HEAD_DIM = 64
SSM_HEADS = SSM_INNER // SSM_HEAD_DIM
SSM_GROUPS = 8
SSM_STATE = 128
SSM_CONV = 4
SSM_CHUNK = 128
SSM_CONV_DIM = SSM_INNER + 2 * SSM_GROUPS * SSM_STATE
SPLIT_ODD = (SSM_INNER, SSM_INNER + SSM_CONV_DIM)
C_IN = SSM_INNER + SSM_CONV_DIM + SSM_HEADS
D_FF = 5632
FFN_CONV = 3

EPS = 1e-6
NEG = -1e30
FORCE = 1e4
TINY = 1e-30

kernel_name = 'hybrid_gmlp_nsa_ssd_convffn_step'


def rmsnorm(x, g):
    xf = x.astype(jnp.float32)
    y = xf * lax.rsqrt(jnp.mean(xf * xf, axis=-1, keepdims=True) + EPS)
    return (y * g.astype(jnp.float32)).astype(x.dtype)


def rope(x, pos):
    half = x.shape[-1] // 2
    inv = ROPE_THETA ** (-jnp.arange(half, dtype=jnp.float32) / half)
    ang = pos.astype(jnp.float32)[:, None] * inv[None, :]
    cos = jnp.cos(ang)[:, None, :]
    sin = jnp.sin(ang)[:, None, :]
    xf = x.astype(jnp.float32)
    x1, x2 = xf[..., :half], xf[..., half:]
    return jnp.concatenate([x1 * cos - x2 * sin, x2 * cos + x1 * sin], axis=-1).astype(x.dtype)


def causal_dwconv(x, buf, w, b):
    width = w.shape[0]
    L = x.shape[1]
    xp = jnp.concatenate([buf.astype(x.dtype), x], axis=1)
    y = xp[:, 0:L] * w[0]
    for k in range(1, width):
        y = y + xp[:, k:k + L] * w[k]
    return y + b, xp[:, xp.shape[1] - (width - 1):]


def gmlp_mix(u, v, ws, bs):
    bsz, L, _ = v.shape
    nc = L // A_CHUNK
    wm = jnp.where(jnp.tril(jnp.ones((A_CHUNK, A_CHUNK), bool)), ws, 0.0)
    vc = v.reshape(bsz, nc, A_CHUNK, A_GROUPS, A_GROUP_DIM)
    s = jnp.einsum('gij,bnjgc->bnigc', wm.astype(v.dtype), vc) + bs.T[:, :, None].astype(v.dtype)
    return u * s.reshape(bsz, L, A_WIDTH)


def kv_rows(t, gain, pos):
    bsz, L = t.shape[:2]
    t = t.reshape(bsz, L, 2, NSA_KV_HEADS, NSA_HEAD_DIM)
    k = rmsnorm(t[:, :, 0], gain)
    if pos is not None:
        k = rope(k, pos)
    return jnp.stack([k, t[:, :, 1]], axis=2)


def compress(full, pool):
    bsz, T = full.shape[:2]
    nb = T // CMP_BLOCK
    w = jax.nn.softmax(pool.astype(jnp.float32), axis=-1).astype(full.dtype)
    blocks = full.reshape(bsz, nb, CMP_BLOCK, 2, NSA_KV_HEADS, NSA_HEAD_DIM)
    s = jnp.einsum('bnlchd,hl->bnchd', blocks, w)
    return s[:, :, 0], s[:, :, 1]


def cmp_attend(q, kc, vc, pos):
    bsz, L = q.shape[:2]
    nb = kc.shape[1]
    qg = q.reshape(bsz, L, NSA_KV_HEADS, NSA_GROUP, NSA_HEAD_DIM)
    s = jnp.einsum('bqhgd,bnhd->bhgqn', qg, kc).astype(jnp.float32) * ATTN_SCALE
    blk_end = (jnp.arange(nb, dtype=jnp.int32) + 1) * CMP_BLOCK - 1
    valid = blk_end[None, :] <= pos[:, None]
    s = jnp.where(valid, s, NEG)
    e = jnp.where(valid, jnp.exp(s - jnp.max(s, axis=-1, keepdims=True)), 0.0)
    p = e / jnp.maximum(jnp.sum(e, axis=-1, keepdims=True), TINY)
    o = jnp.einsum('bhgqn,bnhd->bqhgd', p.astype(vc.dtype), vc)
    return o.reshape(bsz, L, NSA_HEADS, NSA_HEAD_DIM), p


def select_blocks(p, pos, nb):
    imp = jnp.sum(p, axis=2)
    blk = jnp.arange(nb, dtype=jnp.int32)[None, :]
    cur = (pos // SEL_BLOCK)[:, None]
    forced = (blk == 0) | (blk == cur)
    score = jnp.where(forced, FORCE, jnp.where(blk > cur, NEG, imp))
    return lax.top_k(score, min(SEL_TOPN, nb))[1]


def sel_attend(q, qpos, idx, kb, vb):
    bsz, qb = q.shape[:2]
    n = idx.shape[-1]
    flat = idx.reshape(bsz, NSA_KV_HEADS, qb * n)
    bi = jnp.arange(bsz)[:, None, None]
    hi = jnp.arange(NSA_KV_HEADS)[None, :, None]
    kg = kb[bi, hi, flat].reshape(bsz, NSA_KV_HEADS, qb, n * SEL_BLOCK, NSA_HEAD_DIM)
    vg = vb[bi, hi, flat].reshape(bsz, NSA_KV_HEADS, qb, n * SEL_BLOCK, NSA_HEAD_DIM)
    kpos = (idx[..., None] * SEL_BLOCK + jnp.arange(SEL_BLOCK, dtype=jnp.int32)).reshape(bsz, NSA_KV_HEADS, qb, n * SEL_BLOCK)
    qg = q.reshape(bsz, qb, NSA_KV_HEADS, NSA_GROUP, NSA_HEAD_DIM)
    s = jnp.einsum('bqhgd,bhqsd->bhgqs', qg, kg).astype(jnp.float32) * ATTN_SCALE
    valid = (kpos <= qpos[None, None, :, None])[:, :, None]
    p = jax.nn.softmax(jnp.where(valid, s, NEG), axis=-1).astype(vg.dtype)
    o = jnp.einsum('bhgqs,bhqsd->bqhgd', p, vg)
    return o.reshape(bsz, qb, NSA_HEADS, NSA_HEAD_DIM)


def sel_prompt(q, pos, idx, kb, vb):
    bsz, L = q.shape[:2]
    nqb = L // SEL_QBLOCK
    n = idx.shape[-1]
    qs = q.reshape(bsz, nqb, SEL_QBLOCK, NSA_HEADS, NSA_HEAD_DIM).swapaxes(0, 1)
    ps = pos.reshape(nqb, SEL_QBLOCK)
    ix = idx.reshape(bsz, NSA_KV_HEADS, nqb, SEL_QBLOCK, n).transpose(2, 0, 1, 3, 4)
    ob = lax.map(lambda a: sel_attend(a[0], a[1], a[2], kb, vb), (qs, ps, ix))
    return ob.swapaxes(0, 1).reshape(bsz, L, NSA_HEADS, NSA_HEAD_DIM)


def window_prompt(q, k, v, pos):
    bsz, L = q.shape[:2]
    nqb = L // WIN_QBLOCK
    span = WINDOW + WIN_QBLOCK
    pw = ((0, 0), (WINDOW, 0), (0, 0), (0, 0))
    kp, vp = jnp.pad(k, pw), jnp.pad(v, pw)
    kidx = (jnp.arange(nqb, dtype=jnp.int32) * WIN_QBLOCK)[:, None] + jnp.arange(span, dtype=jnp.int32)[None, :]
    kblk, vblk = kp[:, kidx], vp[:, kidx]
    kpos = (kidx - WINDOW)[:, None, :]
    qpos = pos.reshape(nqb, WIN_QBLOCK)[:, :, None]
    valid = (kpos <= qpos) & (kpos >= qpos - WINDOW) & (kpos >= 0)
    qg = q.reshape(bsz, nqb, WIN_QBLOCK, NSA_KV_HEADS, NSA_GROUP, NSA_HEAD_DIM)
    s = jnp.einsum('bnqhgd,bnshd->bnhgqs', qg, kblk).astype(jnp.float32) * ATTN_SCALE
    p = jax.nn.softmax(jnp.where(valid[None, :, None, None], s, NEG), axis=-1).astype(vblk.dtype)
    o = jnp.einsum('bnhgqs,bnshd->bnqhgd', p, vblk)
    return o.reshape(bsz, L, NSA_HEADS, NSA_HEAD_DIM)


def masked_attend(q, k, v, valid):
    bsz, sq = q.shape[:2]
    qg = q.reshape(bsz, sq, NSA_KV_HEADS, NSA_GROUP, NSA_HEAD_DIM)
    s = jnp.einsum('bqhgd,bshd->bhgqs', qg, k).astype(jnp.float32) * ATTN_SCALE
    p = jax.nn.softmax(jnp.where(valid, s, NEG), axis=-1).astype(v.dtype)
    return jnp.einsum('bhgqs,bshd->bqhgd', p, v).reshape(bsz, sq, NSA_HEADS, NSA_HEAD_DIM)


def gather_pages(pool, page_table):
    g = pool[page_table]
    return g.reshape(page_table.shape[0], page_table.shape[1] * PAGE_SIZE, *pool.shape[2:])


def even_mixer(xn, pos, w_in, w_out, v_gain, ws, bs, qn, kn, pool, past):
    bsz, L, _ = xn.shape
    u, v, q, kvc, kvs, kvw, gl = jnp.split(xn @ w_in, SPLIT_EVEN, axis=-1)
    u = jax.nn.gelu(u)
    v = rmsnorm(jax.nn.gelu(v).reshape(bsz, L, A_GROUPS, A_GROUP_DIM),
                v_gain.reshape(A_GROUPS, A_GROUP_DIM)).reshape(bsz, L, A_WIDTH)
    lpad = -(-L // A_CHUNK) * A_CHUNK - L
    pw = ((0, 0), (0, lpad), (0, 0))
    a_out = gmlp_mix(jnp.pad(u, pw), jnp.pad(v, pw), ws, bs)[:, :L]
    q = rmsnorm(q.reshape(bsz, L, NSA_HEADS, NSA_HEAD_DIM), qn)
    q_rot = rope(q, pos)
    rows_c = kv_rows(kvc, kn[0], None)
    rows_s = kv_rows(kvs, kn[1], pos)
    rows_w = kv_rows(kvw, kn[2], pos)
    if past is None:
        full_c, full_s = rows_c, rows_s
    else:
        past_c, past_s, buf_w, past_len = past
        full_c = jnp.concatenate([past_c, rows_c], axis=1)
        full_s = jnp.concatenate([past_s, rows_s], axis=1)
    t = full_c.shape[1]
    tpad = -(-t // CMP_BLOCK) * CMP_BLOCK - t
    pw5 = ((0, 0), (0, tpad), (0, 0), (0, 0), (0, 0))
    full_c, full_s = jnp.pad(full_c, pw5), jnp.pad(full_s, pw5)
    nb = full_c.shape[1] // CMP_BLOCK
    k_cmp, v_cmp = compress(full_c, pool)
    o_c, p_c = cmp_attend(q, k_cmp, v_cmp, pos)
    idx = select_blocks(p_c, pos, nb)
    kb = full_s[:, :, 0].reshape(bsz, nb, SEL_BLOCK, NSA_KV_HEADS, NSA_HEAD_DIM).transpose(0, 3, 1, 2, 4)
    vb = full_s[:, :, 1].reshape(bsz, nb, SEL_BLOCK, NSA_KV_HEADS, NSA_HEAD_DIM).transpose(0, 3, 1, 2, 4)
    if past is None:
        o_s = sel_prompt(q_rot, pos, idx, kb, vb)
        o_w = window_prompt(q_rot, rows_w[:, :, 0], rows_w[:, :, 1], pos)
        new_w = rows_w[:, L - min(WINDOW, L):]
    else:
        o_s = sel_attend(q_rot, pos, idx, kb, vb)
        wb = buf_w.shape[1]
        kw = jnp.concatenate([buf_w[:, :, 0], rows_w[:, :, 0]], axis=1)
        vw = jnp.concatenate([buf_w[:, :, 1], rows_w[:, :, 1]], axis=1)
        kpos = jnp.concatenate([past_len - wb + jnp.arange(wb, dtype=jnp.int32), pos])
        valid = (kpos[None, :] <= pos[:, None]) & (kpos[None, :] >= pos[:, None] - WINDOW)
        o_w = masked_attend(q_rot, kw, vw, valid)
        new_w = rows_w
    gate = jax.nn.sigmoid(gl.astype(jnp.float32)).reshape(bsz, L, NSA_HEADS, 3, 1)
    b_out = gate[:, :, :, 0] * o_c + gate[:, :, :, 1] * o_s + gate[:, :, :, 2] * o_w
    b_out = b_out.reshape(bsz, L, B_WIDTH).astype(xn.dtype)
    y = jnp.concatenate([a_out, b_out], axis=-1) @ w_out
    return y, rows_c, rows_s, new_w, v


def segsum(a):
    t = a.shape[-1]
    cs = jnp.cumsum(a, axis=-1)
    d = cs[..., :, None] - cs[..., None, :]
    return jnp.where(jnp.tril(jnp.ones((t, t), bool)), d, NEG)


def ssd(x, dt, a, bm, cm, h0):
    f32 = jnp.float32
    b, l, h, p = x.shape
    g, n = bm.shape[2], bm.shape[3]
    e = h // g
    q = SSM_CHUNK if l % SSM_CHUNK == 0 else l
    c = l // q
    xdt = (x.astype(f32) * dt[..., None]).reshape(b, c, q, g, e, p)
    da = (dt * a).reshape(b, c, q, g, e).transpose(0, 3, 4, 1, 2)
    acs = jnp.cumsum(da, axis=-1)
    bc = bm.astype(f32).reshape(b, c, q, g, n)
    cc = cm.astype(f32).reshape(b, c, q, g, n)
    cb = jnp.einsum('bcign,bcjgn->bgcij', cc, bc)
    mix = cb[:, :, None] * jnp.exp(segsum(da))
    y_diag = jnp.einsum('bgecij,bcjgep->bcigep', mix, xdt)
    decay = jnp.exp(acs[..., -1:] - acs).transpose(0, 3, 4, 1, 2)
    st = jnp.einsum('bcjgn,bcjgep->bcgepn', bc, xdt * decay[..., None])
    st = jnp.concatenate([h0.astype(f32).reshape(b, 1, g, e, p, n), st], axis=1)
    tot = jnp.pad(acs[..., -1], ((0, 0), (0, 0), (0, 0), (1, 0)))
    st = jnp.einsum('bgezc,bcgepn->bzgepn', jnp.exp(segsum(tot)), st)
    y_off = jnp.einsum('bcign,bcgepn->bcigep', cc, st[:, :-1]) * jnp.exp(acs).transpose(0, 3, 4, 1, 2)[..., None]
    return (y_diag + y_off).reshape(b, l, h, p), st[:, -1].reshape(b, h, p, n)


def odd_mixer(xn, w_in, conv_w, conv_b, dt_bias, a_log, d_skip, norm_g, w_out, conv_buf, h0):
    bsz, L, _ = xn.shape
    z, xbc, dt = jnp.split(xn @ w_in, SPLIT_ODD, axis=-1)
    xbc, new_buf = causal_dwconv(xbc, conv_buf, conv_w, conv_b)
    xbc = jax.nn.silu(xbc)
    xs, bm, cm = jnp.split(xbc, (SSM_INNER, SSM_INNER + SSM_GROUPS * SSM_STATE), axis=-1)
    dt = jax.nn.softplus(dt.astype(jnp.float32) + dt_bias.astype(jnp.float32))
    a = -jnp.exp(a_log.astype(jnp.float32))
    xh = xs.reshape(bsz, L, SSM_HEADS, SSM_HEAD_DIM)
    y, h_t = ssd(xh, dt, a, bm.reshape(bsz, L, SSM_GROUPS, SSM_STATE), cm.reshape(bsz, L, SSM_GROUPS, SSM_STATE), h0)
    y = y + xh.astype(jnp.float32) * d_skip.astype(jnp.float32)[:, None]
    y = y.reshape(bsz, L, SSM_INNER) * jax.nn.silu(z.astype(jnp.float32))
    y = rmsnorm(y.reshape(bsz, L, SSM_GROUPS, SSM_INNER // SSM_GROUPS), norm_g.reshape(SSM_GROUPS, SSM_INNER // SSM_GROUPS))
    y = y.reshape(bsz, L, SSM_INNER).astype(xn.dtype)
    return y @ w_out, new_buf, h_t.astype(h0.dtype)


def conv_ffn(xn, w_up, conv_w, conv_b, w_down, buf):
    hu = xn @ w_up
    hc, new_buf = causal_dwconv(hu, buf, conv_w, conv_b)
    g, u = jnp.split(hc, 2, axis=-1)
    return (jax.nn.silu(g) * u) @ w_down, new_buf


def setup_inputs(seed: int = 0) -> dict:
    key = jax.random.key(seed)
    ks = jax.random.split(key, 40)
    f32 = jnp.float32

    def nrm(i, shape, scale):
        return jax.random.normal(ks[i], shape, f32) * scale

    n_pages = PAST_LEN // PAGE_SIZE
    n_pool = (DEC_BATCH * n_pages * 5) // 4
    win_buf = min(WINDOW, PAST_LEN)
    page_table = jax.random.permutation(ks[0], n_pool)[: DEC_BATCH * n_pages].reshape(DEC_BATCH, n_pages).astype(jnp.int32)
    dt0 = jnp.exp(jax.random.uniform(ks[1], (N_ODD, SSM_HEADS), f32, math.log(1e-3), math.log(1e-1)))
    dt_bias = dt0 + jnp.log(-jnp.expm1(-dt0))
    a_log = jnp.log(jax.random.uniform(ks[2], (N_ODD, SSM_HEADS), f32, 1.0, 16.0))
    kvshape = (NSA_KV_HEADS, NSA_HEAD_DIM)
    return {
        'x_prompt': nrm(3, (BATCH, SEQ, D_MODEL), 1.0),
        'x_sample': nrm(4, (DEC_BATCH, DEC_SEQ, D_MODEL), 1.0),
        'cache_kv_cmp': nrm(5, (N_EVEN, n_pool, PAGE_SIZE, 2) + kvshape, 1.0),
        'cache_kv_sel': nrm(6, (N_EVEN, n_pool, PAGE_SIZE, 2) + kvshape, 1.0),
        'cache_kv_win': nrm(7, (N_EVEN, DEC_BATCH, win_buf, 2) + kvshape, 1.0),
        'state_ssm_conv': nrm(8, (N_ODD, DEC_BATCH, SSM_CONV - 1, SSM_CONV_DIM), 1.0),
        'state_ssm': nrm(9, (N_ODD, DEC_BATCH, SSM_HEADS, SSM_HEAD_DIM, SSM_STATE), 0.1),
        'state_ffn_conv': nrm(10, (DEPTH, DEC_BATCH, FFN_CONV - 1, 2 * D_FF), 1.0),
        'page_table': page_table,
        'norm_mix': 1.0 + nrm(11, (DEPTH, D_MODEL), 0.1),
        'norm_ffn': 1.0 + nrm(12, (DEPTH, D_MODEL), 0.1),
        'w_in_even': nrm(13, (N_EVEN, D_MODEL, E_IN), D_MODEL ** -0.5),
        'w_out_even': nrm(14, (N_EVEN, MIX_OUT, D_MODEL), MIX_OUT ** -0.5),
        'gmlp_v_norm': 1.0 + nrm(15, (N_EVEN, A_WIDTH), 0.1),
        'gmlp_ws': nrm(16, (N_EVEN, A_GROUPS, A_CHUNK, A_CHUNK), A_CHUNK ** -0.5),
        'gmlp_bs': 1.0 + nrm(17, (N_EVEN, A_GROUPS, A_CHUNK), 0.1),
        'q_norm': 1.0 + nrm(18, (N_EVEN, NSA_HEAD_DIM), 0.1),
        'k_norm': 1.0 + nrm(19, (N_EVEN, 3, NSA_HEAD_DIM), 0.1),
        'cmp_pool': nrm(20, (N_EVEN, NSA_KV_HEADS, CMP_BLOCK), 0.1),
        'w_in_odd': nrm(21, (N_ODD, D_MODEL, C_IN), D_MODEL ** -0.5),
        'ssm_conv_w': nrm(22, (N_ODD, SSM_CONV, SSM_CONV_DIM), SSM_CONV ** -0.5),
        'ssm_conv_b': nrm(23, (N_ODD, SSM_CONV_DIM), 0.01),
        'ssm_dt_bias': dt_bias,
        'ssm_a_log': a_log,
        'ssm_d': 1.0 + nrm(24, (N_ODD, SSM_HEADS), 0.1),
        'ssm_norm': 1.0 + nrm(25, (N_ODD, SSM_INNER), 0.1),
        'w_out_odd': nrm(26, (N_ODD, SSM_INNER, D_MODEL), SSM_INNER ** -0.5),
        'ffn_w_up': nrm(27, (DEPTH, D_MODEL, 2 * D_FF), D_MODEL ** -0.5),
        'ffn_conv_w': nrm(28, (DEPTH, FFN_CONV, 2 * D_FF), FFN_CONV ** -0.5),
        'ffn_conv_b': nrm(29, (DEPTH, 2 * D_FF), 0.01),
        'ffn_w_down': nrm(30, (DEPTH, D_FF, D_MODEL), D_FF ** -0.5),
    }


def reference(x_prompt, x_sample, cache_kv_cmp, cache_kv_sel, cache_kv_win, state_ssm_conv, state_ssm,
              state_ffn_conv, page_table, norm_mix, norm_ffn, w_in_even, w_out_even, gmlp_v_norm, gmlp_ws,
              gmlp_bs, q_norm, k_norm, cmp_pool, w_in_odd, ssm_conv_w, ssm_conv_b, ssm_dt_bias, ssm_a_log,
              ssm_d, ssm_norm, w_out_odd, ffn_w_up, ffn_conv_w, ffn_conv_b, ffn_w_down):
    bp, sp = x_prompt.shape[:2]
    ss = x_sample.shape[1]
    past_len = page_table.shape[1] * PAGE_SIZE
    pos_p = jnp.arange(sp, dtype=jnp.int32)
    pos_s = past_len + jnp.arange(ss, dtype=jnp.int32)
    hp, hs = x_prompt, x_sample
    p_cmp, p_sel, p_win, s_cmp, s_sel, s_win, s_v = [], [], [], [], [], [], []
    p_sconv, p_sst, s_sconv, s_sst, p_fconv, s_fconv = [], [], [], [], [], []
    for layer in range(DEPTH):
        i = layer // 2
        xp = rmsnorm(hp, norm_mix[layer])
        xs = rmsnorm(hs, norm_mix[layer])
        if layer % 2 == 0:
            wts = (w_in_even[i], w_out_even[i], gmlp_v_norm[i], gmlp_ws[i], gmlp_bs[i], q_norm[i], k_norm[i], cmp_pool[i])
            mp, rc, rs, rw, _ = even_mixer(xp, pos_p, *wts, None)
            past = (gather_pages(cache_kv_cmp[i], page_table), gather_pages(cache_kv_sel[i], page_table),
                    cache_kv_win[i], past_len)
            ms, qc, qs, qw, qv = even_mixer(xs, pos_s, *wts, past)
            p_cmp.append(rc)
            p_sel.append(rs)
            p_win.append(rw)
            s_cmp.append(qc)
            s_sel.append(qs)
            s_win.append(qw)
            s_v.append(qv)
        else:
            wts = (w_in_odd[i], ssm_conv_w[i], ssm_conv_b[i], ssm_dt_bias[i], ssm_a_log[i], ssm_d[i], ssm_norm[i], w_out_odd[i])
            mp, cbp, stp = odd_mixer(xp, *wts, jnp.zeros((bp, SSM_CONV - 1, SSM_CONV_DIM), xp.dtype),
                                     jnp.zeros((bp, SSM_HEADS, SSM_HEAD_DIM, SSM_STATE), xp.dtype))
            ms, cbs, sts = odd_mixer(xs, *wts, state_ssm_conv[i], state_ssm[i])
            p_sconv.append(cbp)
            p_sst.append(stp)
            s_sconv.append(cbs)
            s_sst.append(sts)
        hp = hp + mp
        hs = hs + ms
        fw = (ffn_w_up[layer], ffn_conv_w[layer], ffn_conv_b[layer], ffn_w_down[layer])
        fp, fbp = conv_ffn(rmsnorm(hp, norm_ffn[layer]), *fw, jnp.zeros((bp, FFN_CONV - 1, 2 * D_FF), hp.dtype))
        fs, fbs = conv_ffn(rmsnorm(hs, norm_ffn[layer]), *fw, state_ffn_conv[layer])
        p_fconv.append(fbp)
        s_fconv.append(fbs)
        hp = hp + fp
        hs = hs + fs
    return (hp, hs, jnp.stack(p_cmp), jnp.stack(p_sel), jnp.stack(p_win), jnp.stack(p_sconv), jnp.stack(p_sst),
            jnp.stack(p_fconv), jnp.stack(s_cmp), jnp.stack(s_sel), jnp.stack(s_win), jnp.stack(s_v),
            jnp.stack(s_sconv), jnp.stack(s_sst), jnp.stack(s_fconv))
```

```python
import numpy as np
import ml_dtypes
import concourse.bass as bass
import concourse.mybir as mybir
from concourse.bass_utils import run_bass_kernel_spmd
from contextlib import ExitStack

F32 = mybir.dt.float32
BF16 = mybir.dt.bfloat16
I32 = mybir.dt.int32
ALU = mybir.AluOpType
AF = mybir.ActivationFunctionType
AX = mybir.AxisListType

ENGS = ("pe", "act", "dve", "pool", "sp")
SAME_SYNC = {"pe": False, "act": True, "dve": True, "pool": True, "sp": False}
EPOCH = 12000
N_DMA_SEMS = 48

D = 2048
L = 2048
SS = 4
DFF = 5632
NCH = 44
NXC = 48
DEPTH = 4
EPS = 1e-6


class Tok:
    __slots__ = ("name", "lw", "rd")

    def __init__(self, name=""):
        self.name = name
        self.lw = None
        self.rd = []


class Op:
    __slots__ = ("eng", "fn", "waits", "sig", "idx", "dma", "sem", "val")


class Prog:
    def __init__(self, nc, es):
        self.nc = nc
        self.es = es
        self.ops = {e: [] for e in ENGS}
        self.nsig = {e: 0 for e in ENGS}
        self.esems = {e: [] for e in ENGS}
        self.dsems = [es.enter_context(nc.semaphore("dma%d" % i)) for i in range(N_DMA_SEMS)]
        self.dval = [0] * N_DMA_SEMS
        self.dnext = 0
        self.waited = {e: {} for e in ENGS}

    def _esem(self, eng, epoch):
        l = self.esems[eng]
        while len(l) <= epoch:
            l.append(self.es.enter_context(self.nc.semaphore("e_%s_%d" % (eng, len(l)))))
        return l[epoch]

    def op(self, eng, fn, R=(), W=(), sig=True, dma=False):
        o = Op()
        o.eng = eng
        o.fn = fn
        o.dma = dma
        o.sig = sig and not dma
        o.idx = len(self.ops[eng])
        deps = []
        for t in R:
            if t.lw is not None:
                deps.append(t.lw)
        for t in W:
            deps.extend(t.rd)
            if t.lw is not None:
                deps.append(t.lw)
        waits = []
        wd = self.waited[eng]
        for d in deps:
            if d.dma:
                s, v = d.sem, d.val
            else:
                if d.eng == eng and not SAME_SYNC[eng]:
                    continue
                if not d.sig and d.eng == eng:
                    continue
                if not d.sig:
                    lst = self.ops[d.eng]
                    j = d.idx
                    while j < len(lst) and not lst[j].sig:
                        j += 1
                    assert j < len(lst), "dependency on unsignaled op with no later signal"
                    d = lst[j]
                s, v = d.sem, d.val
            if wd.get(s, 0) >= v:
                continue
            wd[s] = v
            waits.append((s, v))
        if dma:
            k = self.dnext
            self.dnext = (self.dnext + 1) % N_DMA_SEMS
            prev = self.dval[k]
            s = self.dsems[k]
            if prev > 0 and wd.get(s, 0) < prev:
                wd[s] = prev
                waits.append((s, prev))
            self.dval[k] = prev + 16
            o.sem = s
            o.val = prev + 16
        elif o.sig:
            n = self.nsig[eng]
            self.nsig[eng] = n + 1
            o.sem = self._esem(eng, n // EPOCH)
            o.val = n % EPOCH + 1
        o.waits = waits
        self.ops[eng].append(o)
        for t in R:
            t.rd.append(o)
        for t in W:
            t.lw = o
            t.rd = []
        return o

    def inherit(self, new_toks, old_toks):
        ops = []
        for t in old_toks:
            ops.extend(t.rd)
            if t.lw is not None:
                ops.append(t.lw)
        for t in new_toks:
            t.rd = list(ops)
            t.lw = None

    def claim(self, region, new_toks):
        if not hasattr(self, "regions"):
            self.regions = {}
        old = self.regions.get(region, [])
        if old is not new_toks:
            self.inherit(new_toks, old)
        self.regions[region] = new_toks

    def emit(self):
        nc = self.nc
        finals = [(self.dsems[k], self.dval[k]) for k in range(N_DMA_SEMS) if self.dval[k] > 0]

        def run(e, name):
            for o in self.ops[name]:
                for (s, v) in o.waits:
                    e.wait_ge(s, v)
                ins = o.fn(e)
                if o.dma:
                    ins.then_inc(o.sem, 16)
                elif o.sig:
                    ins.then_inc(o.sem, 1)

        with nc.Block() as blk:
            @blk.tensor
            def _(e):
                run(e, "pe")

            @blk.scalar
            def _(e):
                run(e, "act")

            @blk.vector
            def _(e):
                run(e, "dve")

            @blk.gpsimd
            def _(e):
                run(e, "pool")

            @blk.sync
            def _(e):
                run(e, "sp")
                for (s, v) in finals:
                    e.wait_ge(s, v)


class K:
    pass


class Region:
    def __init__(self, P, name, t, words):
        self.P, self.name, self.t, self.words = P, name, t, words
        self.off = 0
        self.toks = []

    def reset(self):
        self.off = 0
        self.toks = []

    def alloc(self, shape, dtype, name=""):
        n = 1
        for d in shape:
            n *= d
        w = n if dtype == F32 or dtype == I32 else (n + 1) // 2
        self.off = (self.off + 7) // 8 * 8
        assert self.off + w <= self.words, "region %s overflow (%d + %d > %d)" % (self.name, self.off, w, self.words)
        ap = self.t[:, self.off:self.off + w]
        self.off += w
        if dtype == BF16:
            ap = ap.bitcast(BF16)[:, 0:n]
        elif dtype == I32:
            ap = ap.bitcast(I32)
        if len(shape) == 2:
            ap = ap.rearrange("p (a b) -> p a b", a=shape[0])
        elif len(shape) == 3:
            ap = ap.rearrange("p (a b c) -> p a b c", a=shape[0], b=shape[1])
        t = Tok(name)
        self.toks.append(t)
        return ap, t

    def commit(self):
        self.P.claim(self.name, self.toks)


def build(cfg):
    nc = bass.Bass("TRN2", target_bir_lowering=False)
    k = K()
    k.nc = nc
    k.cfg = cfg
    din = lambda n, s, d=F32: nc.dram_tensor(n, list(s), d, kind="ExternalInput").ap()
    dout = lambda n, s, d=F32: nc.dram_tensor(n, list(s), d, kind="ExternalOutput").ap()
    dscr = lambda n, s, d=F32: nc.dram_tensor(n, list(s), d, kind=("ExternalOutput" if cfg.get("debug") else "Internal")).ap()
    k.x_p = din("x_p", [L, D])
    k.x_s = din("x_s", [SS, D])
    k.norm_mix = din("norm_mix", [DEPTH, D])
    k.norm_ffn = din("norm_ffn", [DEPTH, D])
    k.ffn_w_up = din("ffn_w_up", [DEPTH, D, 2 * DFF])
    k.ffn_w_down = din("ffn_w_down", [DEPTH, DFF, D])
    k.ffn_cw = din("ffn_cw", [DEPTH, 128, 88, 3])
    k.ffn_cb = din("ffn_cb", [DEPTH, 128, 88])
    k.ffn_st = din("ffn_st", [DEPTH, 128, 88, 2])
    k.ident = din("ident", [128, 128], BF16)
    k.cst4 = din("cst4", [4, 128, 128], F32)
    k.w_in_odd = din("w_in_odd", [2, D, 10304])
    k.w_out_odd = din("w_out_odd", [2, 4096, D])
    k.ssm_cw = din("ssm_cw", [2, 128, NXC, 4])
    k.ssm_cb = din("ssm_cb", [2, 128, NXC])
    k.ssm_cst = din("ssm_cst", [2, 128, NXC, 3])
    k.ssm_st = din("ssm_st", [2, 4096, 128])
    k.ssm_dt_bias = din("ssm_dt_bias", [2, 64])
    k.ssm_a_log = din("ssm_a_log", [2, 64])
    k.ssm_d = din("ssm_d", [2, 64])
    k.ssm_norm = din("ssm_norm", [2, 4096])
    k.o_psc = dout("o_psc", [2, 128, NXC, 3])
    k.o_ssc = dout("o_ssc", [2, 128, NXC, 3])
    k.o_pst = dout("o_pst", [2, 4096, 128])
    k.o_sst = dout("o_sst", [2, 4096, 128])
    k.xs_scr = dscr("xs_scr", [2052, 4096], BF16)
    k.bt_scr = dscr("bt_scr", [2052, 1024], BF16)
    k.bT_scr = dscr("bT_scr", [8, 128, 2052], BF16)
    k.cT_scr = dscr("cT_scr", [8, 128, 2052], BF16)
    k.zs_scr = dscr("zs_scr", [2052, 4096], BF16)
    k.yn_scr = dscr("yn_scr", [16, 128, 32, 128], BF16)
    k.w_in_even = din("w_in_even", [2, D, 4632])
    k.w_out_even = din("w_out_even", [2, 2048, D])
    k.gmlp_v_norm = din("gmlp_v_norm", [2, 1024])
    k.gmlp_wsT = din("gmlp_wsT", [2, 128, 8, 128])
    k.gmlp_bs = din("gmlp_bs", [2, 8, 128])
    k.q_norm = din("q_norm", [2, 128])
    k.k_norm = din("k_norm", [2, 3, 128])
    k.cmp_pool = din("cmp_pool", [2, 2, 64])
    k.rope_cos = din("rope_cos", [128, 17, 64])
    k.rope_sin = din("rope_sin", [128, 17, 64])
    k.c_ecmp = din("c_ecmp", [128, 16, 32], BF16)
    k.c_ebig = din("c_ebig", [64, 2048], BF16)
    k.c_negb = din("c_negb", [2, 128, 128], BF16)
    k.c_valid = din("c_valid", [128, 16, 32])
    k.c_keep = din("c_keep", [128, 16, 32])
    k.c_add = din("c_add", [128, 16, 32])
    k.c_validT = din("c_validT", [32, 16, 128], BF16)
    k.cache_cmp = din("cache_cmp", [2, 1280 * 128, 512])
    k.cache_sel = din("cache_sel", [2, 1280 * 128, 512])
    k.cache_win = din("cache_win", [2, 512, 512])
    k.ptab = din("ptab", [1, 128], I32)
    k.c_iota = din("c_iota", [128, 1])
    k.c_sum16 = din("c_sum16", [16, 4])
    k.c_sum16T = din("c_sum16T", [4, 16])
    k.c_negnew = din("c_negnew", [16, 4])
    k.c_negwin = din("c_negwin", [16, 512])
    k.o_pkc = dout("o_pkc", [2, 2048, 512])
    k.o_pks = dout("o_pks", [2, 2048, 512])
    k.o_pkw = dout("o_pkw", [2, 512, 512])
    k.o_skc = dout("o_skc", [2, SS, 512])
    k.o_sks = dout("o_sks", [2, SS, 512])
    k.o_skw = dout("o_skw", [2, SS, 512])
    k.o_sv = dout("o_sv", [2, SS, 1024])
    k.uT_scr = dscr("uT_scr", [8, 128, 2052], BF16)
    k.qT_scr = dscr("qT_scr", [8, 128, 2052], BF16)
    k.qrT_scr = dscr("qrT_scr", [8, 128, 2052], BF16)
    k.ksT_scr = dscr("ksT_scr", [2, 128, 2052], BF16)
    k.kwT_scr = dscr("kwT_scr", [2, 128, 2052], BF16)
    k.vs_scr = dscr("vs_scr", [2052, 2, 130], BF16)
    k.vw_scr = dscr("vw_scr", [2052, 2, 130], BF16)
    k.mo_scr = dscr("mo_scr", [16, 128, 16, 128], BF16)
    k.y_p = dout("y_p", [L, D])
    k.y_s = dout("y_s", [SS, D])
    k.o_pfc = dout("o_pfc", [DEPTH, 128, 88, 2])
    k.o_sfc = dout("o_sfc", [DEPTH, 128, 88, 2])
    k.act_scr = dscr("act_scr", [16, 128, NCH, 128], BF16)

    es = ExitStack()
    with es:
        P = Prog(nc, es)
        k.P = P
        sb = lambda n, s, d: es.enter_context(nc.sbuf_tensor(n, list(s), d))
        k.RX = sb("RX", [128, 16 * 2052 // 2], F32)
        k.RY = sb("RY", [128, 22528], F32)
        k.hs = sb("hs", [SS, D], F32)
        k.identb = sb("identb", [128, 128], BF16)
        k.gbc = k.RY[:, 0:2048]
        k.hbuf = [k.RY[:, 2048:4096], k.RY[:, 4096:6144]]
        k.xnb = [k.RY[:, 6144:7168].bitcast(BF16), k.RY[:, 7168:8192].bitcast(BF16)]
        k.junk = k.RY[:, 8192:9216].bitcast(BF16)
        k.ss = [sb("ss%d" % i, [128, 1], F32) for i in range(2)]
        k.RZt = sb("RZ", [128, 10240], F32)
        k.rX = Region(P, "RX", k.RX, 16 * 2052 // 2)
        k.rY = Region(P, "RY", k.RY, 22528)
        k.rZ = Region(P, "RZ", k.RZt, 10240)
        k.identf = sb("identf", [128, 128], F32)
        k.utri = sb("utri", [128, 128], F32)
        k.negtriT = sb("negtriT", [128, 128], F32)
        k.onesf = sb("onesf", [128, 128], F32)
        k.ecmp = sb("ecmp", [128, 16, 32], BF16)
        k.t_ecmp = Tok("ecmp")
        k.pb = [es.enter_context(nc.psum_tensor("pb%d" % i, [128, 512], F32)) for i in range(8)]
        k.tpb = [Tok("pb%d" % i) for i in range(8)]
        k.t_hp = [Tok("hp%d" % i) for i in range(16)]
        k.t_hs = Tok("hs")
        k.t_xnT = Tok("xnT")
        k.t_ident = Tok("ident")
        k.xnT = k.RX[:, :].bitcast(BF16).rearrange("p (a b) -> p a b", a=16)

        P.op("sp", lambda e: e.dma_start(out=k.identb[:], in_=k.ident), W=[k.t_ident], dma=True)
        k.t_cst = Tok("cst4")
        P.op("sp", lambda e: e.dma_start(out=k.ecmp[:], in_=k.c_ecmp), W=[k.t_ecmp], dma=True)
        for ii, tns in enumerate([k.identf, k.utri, k.negtriT, k.onesf]):
            P.op("sp", lambda e, ii=ii, tns=tns: e.dma_start(out=tns[:], in_=k.cst4[ii]), W=[k.t_cst], dma=True)
        for tt in range(16):
            P.op("sp", lambda e, tt=tt: e.dma_start(out=k.y_p[tt * 128:(tt + 1) * 128, :], in_=k.x_p[tt * 128:(tt + 1) * 128, :]),
                 W=[k.t_hp[tt]], dma=True)
        P.op("sp", lambda e: e.dma_start(out=k.hs[:], in_=k.x_s), W=[k.t_hs], dma=True)

        for layer in range(cfg.get("layers", DEPTH)):
            if layer % 2 == 1 and cfg.get("odd", True):
                odd_phase(k, layer)
            if layer % 2 == 0 and cfg.get("even", True):
                even_phase(k, layer)
            if cfg.get("ffn", True):
                ffn_phase(k, layer)

        P.op("sp", lambda e: e.dma_start(out=k.y_s, in_=k.hs[:]), R=[k.t_hs], dma=True)
        P.emit()
    return nc


def norm_phase(k, gain_ap):
    P = k.P
    t_g = Tok("gbc")
    t_h = [Tok("hbuf0"), Tok("hbuf1")]
    t_xnb = [Tok("xnb0"), Tok("xnb1")]
    t_ss = [Tok("ss0"), Tok("ss1")]
    t_junk = Tok("junk")
    P.claim("RX", [k.t_xnT])
    P.claim("RY", [t_g, t_junk] + t_h + t_xnb)
    P.op("sp", lambda e: e.dma_start(out=k.gbc[:, :], in_=gain_ap.to_broadcast([128, D])), W=[t_g], dma=True)
    tp = k.tpb
    inv = float(D ** -0.5)
    for tt in range(17):
        b = tt % 2
        n = 128 if tt < 16 else SS
        if tt < 16:
            src, tsrc = k.hbuf[b], t_h[b]
            P.op("sp", lambda e, tt=tt, b=b: e.dma_start(out=k.hbuf[b][:], in_=k.y_p[tt * 128:(tt + 1) * 128, :]),
                 R=[k.t_hp[tt]], W=[t_h[b]], dma=True)
        else:
            src, tsrc = k.hs, k.t_hs
        ssb, xnb = k.ss[b], k.xnb[b]
        P.op("pool", lambda e, ssb=ssb: e.memset(ssb[:], 0.0), W=[t_ss[b]])
        P.op("act", lambda e, src=src, n=n, ssb=ssb: e.activation(k.junk[0:n, :], src[0:n, :], AF.Square, scale=inv, accum_out=ssb[0:n, :]),
             R=[tsrc], W=[t_junk, t_ss[b]])
        P.op("act", lambda e, n=n, ssb=ssb: e.activation(ssb[0:n, :], ssb[0:n, :], AF.Sqrt, bias=EPS), R=[t_ss[b]], W=[t_ss[b]])
        P.op("dve", lambda e, n=n, ssb=ssb: e.reciprocal(ssb[0:n, :], ssb[0:n, :]), R=[t_ss[b]], W=[t_ss[b]])
        P.op("dve", lambda e, n=n, src=src, ssb=ssb, xnb=xnb: e.scalar_tensor_tensor(xnb[0:n, :], src[0:n, :], ssb[0:n, 0:1], k.gbc[0:n, :], ALU.mult, ALU.mult),
             R=[tsrc, t_ss[b], t_g], W=[t_xnb[b]])
        if tt < 16:
            for q in range(4):
                bank = 6 + (q % 2)
                pt = k.pb[bank][:, 0:256].bitcast(BF16).rearrange("p (a b) -> p a b", a=4)
                for j in range(4):
                    kc = q * 4 + j
                    P.op("pe", lambda e, pt=pt, j=j, kc=kc, xnb=xnb: e.transpose(pt[:, j, :], xnb[:, kc * 128:(kc + 1) * 128], k.identb[:]),
                         R=[t_xnb[b], k.t_ident], W=[tp[bank]], sig=(j == 3))
                eng = "act" if q % 2 == 0 else "dve"
                if eng == "act":
                    P.op("act", lambda e, pt=pt, q=q, tt=tt: e.copy(k.xnT[:, q * 4:(q + 1) * 4, tt * 128:(tt + 1) * 128], pt),
                         R=[tp[bank]], W=[k.t_xnT])
                else:
                    P.op("dve", lambda e, pt=pt, q=q, tt=tt: e.tensor_copy(k.xnT[:, q * 4:(q + 1) * 4, tt * 128:(tt + 1) * 128], pt),
                         R=[tp[bank]], W=[k.t_xnT])
        else:
            bank = 6
            pt = k.pb[bank][:, 0:32].bitcast(BF16).rearrange("p (a b) -> p a b", a=16)
            for kc in range(16):
                P.op("pe", lambda e, pt=pt, kc=kc, xnb=xnb: e.transpose(pt[:, kc, :], xnb[0:SS, kc * 128:(kc + 1) * 128], k.identb[0:SS, 0:SS]),
                     R=[t_xnb[b], k.t_ident], W=[tp[bank]], sig=(kc == 15))
            P.op("act", lambda e, pt=pt: e.copy(k.xnT[:, :, 2048:2052], pt), R=[tp[bank]], W=[k.t_xnT])


def ffn_phase(k, layer):
    P = k.P
    nc = k.nc
    norm_phase(k, k.norm_ffn[layer:layer + 1, :])
    rZ = k.rZ
    rZ.reset()
    k.cw, t_cw = rZ.alloc([88, 3], F32, "cw")
    k.cbias, t_cb = rZ.alloc([88], F32, "cb")
    k.cst, t_cst = rZ.alloc([88, 2], F32, "cst")
    k.stgp, t_stgp = rZ.alloc([88, 2], F32, "stgp")
    k.stgs, t_stgs = rZ.alloc([88, 2], F32, "stgs")
    k.hub = [[None, None], [None, None]]
    t_hub = [[None, None], [None, None]]
    for a in range(2):
        for b in range(2):
            k.hub[a][b], t_hub[a][b] = rZ.alloc([516], F32, "hub")
    k.cv = [None, None]
    t_cv = [None, None]
    for a in range(2):
        k.cv[a], t_cv[a] = rZ.alloc([512], F32, "cv")
    k.sg, t_sg = rZ.alloc([512], F32, "sg")
    k.aT = [None, None]
    t_aT = [None, None]
    for a in range(2):
        k.aT[a], t_aT[a] = rZ.alloc([512], BF16, "aT")
    k.actS, t_actS = rZ.alloc([NCH, SS], BF16, "actS")
    rZ.commit()
    P.op("sp", lambda e: e.dma_start(out=k.cw[:], in_=k.ffn_cw[layer]), W=[t_cw], dma=True)
    P.op("sp", lambda e: e.dma_start(out=k.cbias[:], in_=k.ffn_cb[layer]), W=[t_cb], dma=True)
    P.op("sp", lambda e: e.dma_start(out=k.cst[:], in_=k.ffn_st[layer]), W=[t_cst], dma=True)
    wv = k.RY[:, :].bitcast(BF16).rearrange("p (s a b) -> p s a b", s=4, a=16)
    t_w = [Tok("wslot%d" % i) for i in range(4)]
    P.claim("RY", t_w)
    t_scr = [Tok("scr%d" % i) for i in range(16)]
    wup = k.ffn_w_up[layer].rearrange("(kc p) n -> p kc n", p=128)
    cnt = 0
    def load_w(cb):
        sg_, su_ = (2 * cb) % 4, (2 * cb + 1) % 4
        P.op("pool", lambda e, cb=cb, s=sg_: e.dma_start(out=wv[:, s, :, 0:512], in_=wup[:, :, cb * 512:(cb + 1) * 512]),
             W=[t_w[sg_]], dma=True)
        P.op("pool", lambda e, cb=cb, s=su_: e.dma_start(out=wv[:, s, :, 0:512], in_=wup[:, :, DFF + cb * 512:DFF + (cb + 1) * 512]),
             W=[t_w[su_]], dma=True)
    load_w(0)
    for cb in range(11):
        sg_, su_ = (2 * cb) % 4, (2 * cb + 1) % 4
        if cb + 1 < 11:
            load_w(cb + 1)
        for j in range(4):
            c = cb * 4 + j
            for tb in range(5):
                N = 512 if tb < 4 else SS
                c0 = tb * 512
                hb = cnt % 2
                cnt += 1
                for gu in range(2):
                    slot = sg_ if gu == 0 else su_
                    bank = (cnt % 2) * 2 + gu
                    ci = c + 44 * gu
                    for kc in range(16):
                        P.op("pe", lambda e, bank=bank, slot=slot, j=j, kc=kc, c0=c0, N=N: e.matmul(
                            k.pb[bank][:, 0:N], wv[:, slot, kc, j * 128:(j + 1) * 128], k.xnT[:, kc, c0:c0 + N],
                            start=(kc == 0), stop=(kc == 15)),
                            R=[t_w[slot], k.t_xnT], W=[k.tpb[bank]], sig=(kc == 15))
                    hub = k.hub[gu][hb]
                    th = t_hub[gu][hb]
                    if tb == 0:
                        P.op("pool", lambda e, hub=hub: e.memset(hub[:, 0:2], 0.0), W=[th])
                    elif tb < 4:
                        prev = k.hub[gu][1 - hb]
                        P.op("pool", lambda e, hub=hub, prev=prev: e.tensor_copy(hub[:, 0:2], prev[:, 512:514]),
                             R=[t_hub[gu][1 - hb]], W=[th])
                    else:
                        P.op("pool", lambda e, hub=hub, ci=ci: e.tensor_copy(hub[:, 0:2], k.cst[:, ci, :]), R=[t_cst], W=[th])
                    P.op("act", lambda e, hub=hub, bank=bank, N=N: e.copy(hub[:, 2:2 + N], k.pb[bank][:, 0:N]),
                         R=[k.tpb[bank]], W=[th])
                    cv = k.cv[gu]
                    P.op("dve", lambda e, cv=cv, hub=hub, ci=ci, N=N: e.tensor_scalar(
                        cv[:, 0:N], hub[:, 0:N], k.cw[:, ci, 0:1], k.cbias[:, ci:ci + 1], ALU.mult, ALU.add),
                        R=[th, t_cw, t_cb], W=[t_cv[gu]])
                    P.op("dve", lambda e, cv=cv, hub=hub, ci=ci, N=N: e.scalar_tensor_tensor(
                        cv[:, 0:N], hub[:, 1:1 + N], k.cw[:, ci, 1:2], cv[:, 0:N], ALU.mult, ALU.add),
                        R=[th, t_cw, t_cv[gu]], W=[t_cv[gu]])
                    P.op("dve", lambda e, cv=cv, hub=hub, ci=ci, N=N: e.scalar_tensor_tensor(
                        cv[:, 0:N], hub[:, 2:2 + N], k.cw[:, ci, 2:3], cv[:, 0:N], ALU.mult, ALU.add),
                        R=[th, t_cw, t_cv[gu]], W=[t_cv[gu]])
                    if tb == 3:
                        P.op("pool", lambda e, hub=hub, ci=ci: e.tensor_copy(k.stgp[:, ci, :], hub[:, 512:514]), R=[th], W=[t_stgp])
                    if tb == 4:
                        P.op("pool", lambda e, hub=hub, ci=ci: e.tensor_copy(k.stgs[:, ci, :], hub[:, 4:6]), R=[th], W=[t_stgs])
                P.op("act", lambda e, N=N: e.activation(k.sg[:, 0:N], k.cv[0][:, 0:N], AF.Silu), R=[t_cv[0]], W=[t_sg])
                if tb < 4:
                    ab = (c * 4 + tb) % 2
                    aT = k.aT[ab]
                    P.op("dve", lambda e, aT=aT: e.tensor_tensor(aT[:, :], k.sg[:, :], k.cv[1][:, :], ALU.mult),
                         R=[t_sg, t_cv[1]], W=[t_aT[ab]])
                    dst = k.act_scr[tb * 4:(tb + 1) * 4, :, c, :].rearrange("t p k -> p t k")
                    P.op("sp", lambda e, aT=aT, dst=dst: e.dma_start(out=dst, in_=aT[:, :].rearrange("p (t k) -> p t k", t=4)),
                         R=[t_aT[ab]], W=t_scr[tb * 4:(tb + 1) * 4], dma=True)
                else:
                    P.op("dve", lambda e, c=c: e.tensor_tensor(k.actS[:, c, :], k.sg[:, 0:SS], k.cv[1][:, 0:SS], ALU.mult),
                         R=[t_sg, t_cv[1]], W=[t_actS])
    P.op("sp", lambda e: e.dma_start(out=k.o_pfc[layer], in_=k.stgp[:]), R=[t_stgp], dma=True)
    P.op("sp", lambda e: e.dma_start(out=k.o_sfc[layer], in_=k.stgs[:]), R=[t_stgs], dma=True)
    out_proj(k, k.ffn_w_down[layer], NCH, k.act_scr, t_scr, k.actS, t_actS)


def out_proj(k, w_dram, nch, scr, t_scr, actS, t_actS, hcol=None, t_hc=None):
    P = k.P
    wd_v = k.RY[:, :].bitcast(BF16).rearrange("p (s a b) -> p s a b", s=2, a=NCH)
    t_wd = [Tok("wd0"), Tok("wd1")]
    P.claim("RY", t_wd)
    at_v = k.RX[:, 0:3 * NCH * 64].bitcast(BF16).rearrange("p (s a b) -> p s a b", s=3, a=NCH)
    t_at = [Tok("at%d" % i) for i in range(3)]
    hcol = [k.RX[:, 9216 + j * 512:9216 + (j + 1) * 512] for j in range(3)]
    t_hc = [Tok("hcol%d" % j) for j in range(3)]
    P.claim("RX", t_at + t_hc)
    wdn = w_dram.rearrange("(c p) n -> p c n", p=128)
    it = 0
    for db in range(4):
        ws = db % 2
        P.op("pool", lambda e, db=db, ws=ws: e.dma_start(out=wd_v[:, ws, 0:nch, :], in_=wdn[:, :, db * 512:(db + 1) * 512]),
             W=[t_wd[ws]], dma=True)
        for tt in range(17):
            bank = 4 + (it % 2)
            if tt < 16:
                sl = it % 3
                it += 1
                P.op("sp", lambda e, tt=tt, sl=sl: e.dma_start(out=at_v[:, sl, 0:nch, :], in_=scr[tt]),
                     R=[t_scr[tt]], W=[t_at[sl]], dma=True)
                hc = hcol[sl]
                P.op("sp", lambda e, tt=tt, db=db, hc=hc: e.dma_start(out=hc[:, :], in_=k.y_p[tt * 128:(tt + 1) * 128, db * 512:(db + 1) * 512]),
                     R=[k.t_hp[tt]], W=[t_hc[sl]], dma=True)
                for c in range(nch):
                    P.op("pe", lambda e, bank=bank, sl=sl, ws=ws, c=c: e.matmul(
                        k.pb[bank][:, :], at_v[:, sl, c, :], wd_v[:, ws, c, :], start=(c == 0), stop=(c == nch - 1)),
                        R=[t_at[sl], t_wd[ws]], W=[k.tpb[bank]], sig=(c == nch - 1))
                P.op("dve", lambda e, bank=bank, hc=hc: e.tensor_tensor(hc[:, :], k.pb[bank][:, :], hc[:, :], ALU.add),
                     R=[k.tpb[bank], t_hc[sl]], W=[t_hc[sl]])
                P.op("act", lambda e, tt=tt, db=db, hc=hc: e.dma_start(out=k.y_p[tt * 128:(tt + 1) * 128, db * 512:(db + 1) * 512], in_=hc[:, :]),
                     R=[t_hc[sl]], W=[k.t_hp[tt]], dma=True)
            else:
                it += 1
                for c in range(nch):
                    P.op("pe", lambda e, bank=bank, ws=ws, c=c: e.matmul(
                        k.pb[bank][0:SS, :], actS[:, c, :], wd_v[:, ws, c, :], start=(c == 0), stop=(c == nch - 1)),
                        R=[t_actS, t_wd[ws]], W=[k.tpb[bank]], sig=(c == nch - 1))
                P.op("dve", lambda e, bank=bank, db=db: e.tensor_tensor(
                    k.hs[:, db * 512:(db + 1) * 512], k.pb[bank][0:SS, :], k.hs[:, db * 512:(db + 1) * 512], ALU.add),
                    R=[k.tpb[bank], k.t_hs], W=[k.t_hs])


def odd_phase(k, layer):
    P = k.P
    i = layer // 2
    norm_phase(k, k.norm_mix[layer:layer + 1, :])
    rX, rY, rZ = k.rX, k.rY, k.rZ
    rY.reset()
    rZ.reset()
    wsl, t_w = [], []
    for s_ in range(4):
        a, t = rY.alloc([16, 512], BF16, "wsl")
        wsl.append(a)
        t_w.append(t)
    wdt, t_wdt = rY.alloc([16, 64], BF16, "wdt")
    zt, t_zt = [None, None], [None, None]
    trs, t_trs = [None, None], [None, None]
    sv, t_sv = [None, None], [None, None]
    for a_ in range(2):
        zt[a_], t_zt[a_] = rY.alloc([512], BF16, "zt")
        trs[a_], t_trs[a_] = rY.alloc([4, 128], BF16, "trs")
        sv[a_], t_sv[a_] = rY.alloc([512], BF16, "sv")
    rY.commit()
    cw, t_cw = rZ.alloc([NXC, 4], F32, "cw")
    cb, t_cb = rZ.alloc([NXC], F32, "cb")
    cst, t_cst = rZ.alloc([NXC, 3], F32, "cst")
    stgp, t_stgp = rZ.alloc([NXC, 3], F32, "stgp")
    stgs, t_stgs = rZ.alloc([NXC, 3], F32, "stgs")
    hub, t_hub = [None, None], [None, None]
    for a_ in range(2):
        hub[a_], t_hub[a_] = rZ.alloc([520], F32, "hub")
    cv, t_cv = rZ.alloc([512], F32, "cv")
    dt_all, t_dt = rZ.alloc([17, 64], F32, "dt_all")
    da_all, t_da = rZ.alloc([17, 64], F32, "da_all")
    dtb, t_dtb = rZ.alloc([64], F32, "dtb")
    abc, t_abc = rZ.alloc([64], F32, "abc")
    dbc, t_dbc = rZ.alloc([64], F32, "dbc")
    tmp64, t_tmp64 = rZ.alloc([64], F32, "tmp64")
    ynS, t_ynS = rZ.alloc([32, SS], BF16, "ynS")
    cs_sb, t_cs = rZ.alloc([64], F32, "cs")
    ncs, t_ncs = rZ.alloc([64], F32, "ncs")
    ecs, t_ecs = rZ.alloc([64], F32, "ecs")
    dec, t_dec = rZ.alloc([64], F32, "dec")
    etot, t_etot = rZ.alloc([64], F32, "etot")
    ssq, t_ssq = rZ.alloc([2], F32, "ssq")
    hst, t_hst = [None, None], [None, None]
    for a_ in range(2):
        hst[a_], t_hst[a_] = rZ.alloc([128], F32, "hst")
    rZ.commit()

    P.op("sp", lambda e: e.dma_start(out=cw, in_=k.ssm_cw[i]), W=[t_cw], dma=True)
    P.op("sp", lambda e: e.dma_start(out=cb, in_=k.ssm_cb[i]), W=[t_cb], dma=True)
    P.op("sp", lambda e: e.dma_start(out=cst, in_=k.ssm_cst[i]), W=[t_cst], dma=True)
    P.op("sp", lambda e: e.dma_start(out=dtb, in_=k.ssm_dt_bias[i:i + 1, :].to_broadcast([128, 64])), W=[t_dtb], dma=True)
    P.op("sp", lambda e: e.dma_start(out=abc, in_=k.ssm_a_log[i:i + 1, :].to_broadcast([128, 64])), W=[t_abc], dma=True)
    P.op("sp", lambda e: e.dma_start(out=dbc, in_=k.ssm_d[i:i + 1, :].to_broadcast([128, 64])), W=[t_dbc], dma=True)
    P.op("act", lambda e: e.activation(abc, abc, AF.Exp), R=[t_abc], W=[t_abc])
    P.op("act", lambda e: e.mul(abc, abc, -1.0), R=[t_abc], W=[t_abc])

    win = k.w_in_odd[i].rearrange("(kc p) n -> p kc n", p=128)
    blocks = [4096 + b * 512 for b in range(12)] + [b * 512 for b in range(8)]
    t_xs_scr = [Tok("xs_scr%d" % t) for t in range(17)]
    t_bt_scr = [Tok("bt_scr%d" % t) for t in range(17)]
    t_bT_scr = [Tok("bT_scr%d" % t) for t in range(17)]
    t_cT_scr = [Tok("cT_scr%d" % t) for t in range(17)]
    t_zs_scr = [Tok("zs_scr%d" % t) for t in range(17)]

    def load_w(bi):
        sl = bi % 4
        c0 = blocks[bi]
        P.op("pool", lambda e, sl=sl, c0=c0: e.dma_start(out=wsl[sl], in_=win[:, :, c0:c0 + 512]), W=[t_w[sl]], dma=True)

    load_w(0)
    load_w(1)
    P.op("pool", lambda e: e.dma_start(out=wdt, in_=win[:, :, 10240:10304]), W=[t_wdt], dma=True)
    cnt = 0
    for bi in range(20):
        if bi + 2 < 20:
            load_w(bi + 2)
        sl = bi % 4
        if bi < 12:
            for j in range(4):
                c = bi * 4 + j
                for tb in range(5):
                    N = 512 if tb < 4 else SS
                    c0 = tb * 512
                    hb = cnt % 2
                    bank = cnt % 2
                    cnt += 1
                    for kc in range(16):
                        P.op("pe", lambda e, bank=bank, sl=sl, j=j, kc=kc, c0=c0, N=N: e.matmul(
                            k.pb[bank][:, 0:N], wsl[sl][:, kc, j * 128:(j + 1) * 128], k.xnT[:, kc, c0:c0 + N],
                            start=(kc == 0), stop=(kc == 15)),
                            R=[t_w[sl], k.t_xnT], W=[k.tpb[bank]], sig=(kc == 15))
                    hu, th = hub[hb], t_hub[hb]
                    if tb == 0:
                        P.op("pool", lambda e, hu=hu: e.memset(hu[:, 0:3], 0.0), W=[th])
                    elif tb < 4:
                        prev = hub[1 - hb]
                        P.op("pool", lambda e, hu=hu, prev=prev: e.tensor_copy(hu[:, 0:3], prev[:, 512:515]), R=[t_hub[1 - hb]], W=[th])
                    else:
                        P.op("pool", lambda e, hu=hu, c=c: e.tensor_copy(hu[:, 0:3], cst[:, c, :]), R=[t_cst], W=[th])
                    P.op("act", lambda e, hu=hu, bank=bank, N=N: e.copy(hu[:, 3:3 + N], k.pb[bank][:, 0:N]), R=[k.tpb[bank]], W=[th])
                    P.op("dve", lambda e, hu=hu, c=c, N=N: e.tensor_scalar(cv[:, 0:N], hu[:, 0:N], cw[:, c, 0:1], cb[:, c:c + 1], ALU.mult, ALU.add),
                         R=[th, t_cw, t_cb], W=[t_cv])
                    for tap in range(1, 4):
                        P.op("dve", lambda e, hu=hu, c=c, N=N, tap=tap: e.scalar_tensor_tensor(
                            cv[:, 0:N], hu[:, tap:tap + N], cw[:, c, tap:tap + 1], cv[:, 0:N], ALU.mult, ALU.add),
                            R=[th, t_cw, t_cv], W=[t_cv])
                    if tb == 3:
                        P.op("pool", lambda e, hu=hu, c=c: e.tensor_copy(stgp[:, c, :], hu[:, 512:515]), R=[th], W=[t_stgp])
                    if tb == 4:
                        P.op("pool", lambda e, hu=hu, c=c: e.tensor_copy(stgs[:, c, :], hu[:, 4:7]), R=[th], W=[t_stgs])
                    sb_ = cnt % 2
                    svv, tsv = sv[sb_], t_sv[sb_]
                    P.op("act", lambda e, svv=svv, N=N: e.activation(svv[:, 0:N], cv[:, 0:N], AF.Silu), R=[t_cv], W=[tsv])
                    tts = list(range(tb * 4, tb * 4 + 4)) if tb < 4 else [16]
                    if c >= 32:
                        g = (c - 32) % 8
                        scr = k.bT_scr if c < 40 else k.cT_scr
                        tsc = t_bT_scr if c < 40 else t_cT_scr
                        P.op("sp", lambda e, scr=scr, g=g, svv=svv, c0=c0, N=N: e.dma_start(out=scr[g, :, c0:c0 + N], in_=svv[:, 0:N]),
                             R=[tsv], W=[tsc[t] for t in tts], dma=True)
                    if c < 40:
                        scr = k.xs_scr if c < 32 else k.bt_scr
                        tsc = t_xs_scr if c < 32 else t_bt_scr
                        col0 = c * 128 if c < 32 else (c - 32) * 128
                        trv, ttr = trs[sb_], t_trs[sb_]
                        bank2 = 6 + (cnt % 2)
                        pt = k.pb[bank2][:, 0:256].bitcast(BF16).rearrange("p (a b) -> p a b", a=4)
                        if tb < 4:
                            for q in range(4):
                                P.op("pe", lambda e, pt=pt, q=q, svv=svv: e.transpose(pt[:, q, :], svv[:, q * 128:(q + 1) * 128], k.identb[:, :]),
                                     R=[tsv, k.t_ident], W=[k.tpb[bank2]], sig=(q == 3))
                            P.op("act", lambda e, pt=pt, trv=trv: e.copy(trv, pt), R=[k.tpb[bank2]], W=[ttr])
                            dst = scr[c0:c0 + 512, col0:col0 + 128].rearrange("(t p) k -> p t k", p=128)
                            P.op("sp", lambda e, dst=dst, trv=trv: e.dma_start(out=dst, in_=trv), R=[ttr], W=[tsc[t] for t in tts], dma=True)
                        else:
                            P.op("pe", lambda e, pt=pt, svv=svv: e.transpose(pt[0:SS, 0, :], svv[:, 0:SS], k.identb[:, :]),
                                 R=[tsv, k.t_ident], W=[k.tpb[bank2]])
                            P.op("act", lambda e, pt=pt, trv=trv: e.copy(trv[0:SS, 0, :], pt[0:SS, 0, :]), R=[k.tpb[bank2]], W=[ttr])
                            P.op("sp", lambda e, scr=scr, col0=col0, trv=trv: e.dma_start(out=scr[2048:2048 + SS, col0:col0 + 128], in_=trv[0:SS, 0, :]),
                                 R=[ttr], W=[tsc[16]], dma=True)
        else:
            zb = bi - 12
            for tt in range(17):
                T = 128 if tt < 16 else SS
                tok0 = tt * 128
                bank = cnt % 2
                zb_ = cnt % 2
                cnt += 1
                for kc in range(16):
                    P.op("pe", lambda e, bank=bank, sl=sl, kc=kc, tok0=tok0, T=T: e.matmul(
                        k.pb[bank][0:T, :], k.xnT[:, kc, tok0:tok0 + T], wsl[sl][:, kc, :], start=(kc == 0), stop=(kc == 15)),
                        R=[t_w[sl], k.t_xnT], W=[k.tpb[bank]], sig=(kc == 15))
                P.op("act", lambda e, bank=bank, zb_=zb_, T=T: e.activation(zt[zb_][0:T, :], k.pb[bank][0:T, :], AF.Silu),
                     R=[k.tpb[bank]], W=[t_zt[zb_]])
                P.op("sp", lambda e, zb_=zb_, T=T, tok0=tok0, zb=zb: e.dma_start(out=k.zs_scr[tok0:tok0 + T, zb * 512:(zb + 1) * 512], in_=zt[zb_][0:T, :]),
                     R=[t_zt[zb_]], W=[t_zs_scr[tt]], dma=True)
    P.op("sp", lambda e: e.dma_start(out=k.o_psc[i], in_=stgp), R=[t_stgp], dma=True)
    P.op("sp", lambda e: e.dma_start(out=k.o_ssc[i], in_=stgs), R=[t_stgs], dma=True)
    for tt in range(17):
        T = 128 if tt < 16 else SS
        tok0 = tt * 128
        bank = cnt % 2
        cnt += 1
        for kc in range(16):
            P.op("pe", lambda e, bank=bank, kc=kc, tok0=tok0, T=T: e.matmul(
                k.pb[bank][0:T, 0:64], k.xnT[:, kc, tok0:tok0 + T], wdt[:, kc, :], start=(kc == 0), stop=(kc == 15)),
                R=[t_wdt, k.t_xnT], W=[k.tpb[bank]], sig=(kc == 15))
        P.op("dve", lambda e, bank=bank, T=T: e.tensor_tensor(tmp64[0:T, :], k.pb[bank][0:T, 0:64], dtb[0:T, :], ALU.add),
             R=[k.tpb[bank], t_dtb], W=[t_tmp64])
        P.op("act", lambda e, T=T: e.activation(tmp64[0:T, :], tmp64[0:T, :], AF.Exp), R=[t_tmp64], W=[t_tmp64])
        P.op("act", lambda e, T=T, tt=tt: e.activation(dt_all[0:T, tt, :], tmp64[0:T, :], AF.Ln, bias=1.0), R=[t_tmp64], W=[t_dt])
        P.op("dve", lambda e, T=T, tt=tt: e.tensor_tensor(da_all[0:T, tt, :], dt_all[0:T, tt, :], abc[0:T, :], ALU.mult),
             R=[t_dt, t_abc], W=[t_da])

    stop = k.cfg.get("odd_stop", 99)
    if stop <= 1:
        return
    rX.reset()
    rY.reset()
    H, t_H = rX.alloc([4096], F32, "H")
    Hb, t_Hb = rX.alloc([4096], BF16, "Hb")
    xs, t_xs = [None, None], [None, None]
    for a_ in range(2):
        xs[a_], t_xs[a_] = rX.alloc([4096], BF16, "xs")
    xdt, t_xdt = rX.alloc([4096], BF16, "xdt")
    xdd, t_xdd = rX.alloc([4096], BF16, "xdd")
    zs, t_zs = rX.alloc([4096], BF16, "zs")
    rX.commit()
    ngbc, t_ng = rY.alloc([4096], F32, "ngbc")
    Dm, t_Dm = rY.alloc([8, 128], F32, "Dm")
    Lf, t_Lf = rY.alloc([8, 128], F32, "Lf")
    Mb, t_Mb = [None, None], [None, None]
    BT, t_BT, CT, t_CT, Btm, t_Btm = [None, None], [None, None], [None, None], [None, None], [None, None], [None, None]
    for a_ in range(2):
        Mb[a_], t_Mb[a_] = rY.alloc([8, 128], BF16, "Mb")
        BT[a_], t_BT[a_] = rY.alloc([8, 128], BF16, "BT")
        CT[a_], t_CT[a_] = rY.alloc([8, 128], BF16, "CT")
        Btm[a_], t_Btm[a_] = rY.alloc([1024], BF16, "Btm")
    cbTs, t_cbTs = rY.alloc([128], F32, "cbTs")
    yv, t_yv = rY.alloc([512], F32, "yv")
    tmpv, t_tmpv = rY.alloc([512], F32, "tmpv")
    junk, t_junk = rY.alloc([512], BF16, "junk")
    ynb, t_ynb = rY.alloc([512], BF16, "ynb")
    ynT, t_ynT = [None, None], [None, None]
    for a_ in range(2):
        ynT[a_], t_ynT[a_] = rY.alloc([4, 128], BF16, "ynT")
    rY.commit()
    P.op("sp", lambda e: e.dma_start(out=ngbc, in_=k.ssm_norm[i:i + 1, :].to_broadcast([128, 4096])), W=[t_ng], dma=True)
    t_yn_scr = [Tok("yn_scr%d" % t) for t in range(16)]
    pb, tpb = k.pb, k.tpb
    PCv = [pb[2][:, :].rearrange("p (a b) -> p a b", a=4), pb[3][:, :].rearrange("p (a b) -> p a b", a=4)]

    def state_out(dst):
        for blk in range(32):
            hs_, th_ = hst[blk % 2], t_hst[blk % 2]
            P.op("pe", lambda e, blk=blk: e.matmul(pb[7][:, 0:128], H[:, blk * 128:(blk + 1) * 128], k.identf[:, :], start=True, stop=True),
                 R=[t_H, k.t_cst], W=[tpb[7]])
            P.op("act", lambda e, hs_=hs_: e.copy(hs_, pb[7][:, 0:128]), R=[tpb[7]], W=[th_])
            P.op("sp", lambda e, blk=blk, hs_=hs_: e.dma_start(out=dst[blk * 128:(blk + 1) * 128, :], in_=hs_), R=[th_], dma=True)

    def state_in(src):
        for blk in range(32):
            hs_, th_ = hst[blk % 2], t_hst[blk % 2]
            P.op("sp", lambda e, blk=blk, hs_=hs_: e.dma_start(out=hs_, in_=src[blk * 128:(blk + 1) * 128, :]), W=[th_], dma=True)
            P.op("pe", lambda e, hs_=hs_: e.matmul(pb[7][:, 0:128], hs_, k.identf[:, :], start=True, stop=True),
                 R=[th_, k.t_cst], W=[tpb[7]])
            P.op("act", lambda e, blk=blk: e.copy(H[:, blk * 128:(blk + 1) * 128], pb[7][:, 0:128]), R=[tpb[7]], W=[t_H])
        P.op("dve", lambda e: e.tensor_copy(Hb, H), R=[t_H], W=[t_Hb])

    def chunk(tt, T):
        tok0 = tt * 128
        b = tt % 2
        x_, tx_ = xs[b], t_xs[b]
        P.op("sp", lambda e: e.dma_start(out=x_[0:T, :], in_=k.xs_scr[tok0:tok0 + T, :]), R=[t_xs_scr[tt]], W=[tx_], dma=True)
        P.op("sp", lambda e: e.dma_start(out=zs[0:T, :], in_=k.zs_scr[tok0:tok0 + T, :]), R=[t_zs_scr[tt]], W=[t_zs], dma=True)
        P.op("sp", lambda e: e.dma_start(out=Btm[b][0:T, :], in_=k.bt_scr[tok0:tok0 + T, :]), R=[t_bt_scr[tt]], W=[t_Btm[b]], dma=True)
        P.op("sp", lambda e: e.dma_start(out=BT[b][:, :, 0:T], in_=k.bT_scr[:, :, tok0:tok0 + T].rearrange("g p t -> p g t")),
             R=[t_bT_scr[tt]], W=[t_BT[b]], dma=True)
        P.op("sp", lambda e: e.dma_start(out=CT[b][:, :, 0:T], in_=k.cT_scr[:, :, tok0:tok0 + T].rearrange("g p t -> p g t")),
             R=[t_cT_scr[tt]], W=[t_CT[b]], dma=True)
        da = da_all[0:T, tt, :]
        P.op("pe", lambda e: e.matmul(pb[0][0:T, 0:64], k.utri[0:T, 0:T], da, start=True, stop=True), R=[t_da, k.t_cst], W=[tpb[0]])
        P.op("act", lambda e: e.copy(cs_sb[0:T, :], pb[0][0:T, 0:64]), R=[tpb[0]], W=[t_cs])
        P.op("act", lambda e: e.mul(ncs[0:T, :], cs_sb[0:T, :], -1.0), R=[t_cs], W=[t_ncs])
        P.op("act", lambda e: e.activation(ecs[0:T, :], cs_sb[0:T, :], AF.Exp), R=[t_cs], W=[t_ecs])
        P.op("pe", lambda e: e.matmul(pb[0][:, 64:128], k.onesf[0:T, :], da, start=True, stop=True), R=[t_da, k.t_cst, t_cs, t_ecs, t_ncs], W=[tpb[0]])
        P.op("dve", lambda e: e.tensor_tensor(dec[0:T, :], pb[0][0:T, 64:128], cs_sb[0:T, :], ALU.subtract), R=[tpb[0], t_cs], W=[t_dec])
        P.op("act", lambda e: e.activation(dec[0:T, :], dec[0:T, :], AF.Exp), R=[t_dec], W=[t_dec])
        P.op("act", lambda e: e.activation(etot, pb[0][:, 64:128], AF.Exp), R=[tpb[0]], W=[t_etot])
        x3 = x_[0:T, :].rearrange("p (h d) -> p h d", h=64)
        P.op("pool", lambda e: e.tensor_tensor(xdt[0:T, :].rearrange("p (h d) -> p h d", h=64), x3,
                                               dt_all[0:T, tt, :].unsqueeze(2).to_broadcast([T, 64, 64]), ALU.mult),
             R=[tx_, t_dt], W=[t_xdt])
        P.op("pool", lambda e: e.tensor_tensor(xdd[0:T, :].rearrange("p (h d) -> p h d", h=64), xdt[0:T, :].rearrange("p (h d) -> p h d", h=64),
                                               dec[0:T, :].unsqueeze(2).to_broadcast([T, 64, 64]), ALU.mult),
             R=[t_xdt, t_dec], W=[t_xdd])
        def group(g):
            mb = g % 2
            P.op("pe", lambda e: e.matmul(pb[1][0:T, 0:T], BT[b][:, g, 0:T], CT[b][:, g, 0:T], start=True, stop=True),
                 R=[t_BT[b], t_CT[b]], W=[tpb[1]])
            P.op("act", lambda e: e.copy(cbTs[0:T, 0:T], pb[1][0:T, 0:T]), R=[tpb[1]], W=[t_cbTs])
            for e8 in range(8):
                h = g * 8 + e8
                P.op("pe", lambda e, e8=e8, h=h: e.matmul(PCv[e8 // 4][0:T, e8 % 4, 0:T], da_all[0:T, tt, h:h + 1].to_broadcast([T, T]),
                                                         k.utri[0:T, 0:T], start=True, stop=True),
                     R=[t_da, k.t_cst], W=[tpb[2 + e8 // 4]], sig=(e8 % 4 == 3))
            for half in range(2):
                P.op("dve", lambda e, half=half: e.tensor_tensor(Dm[0:T, half * 4:(half + 1) * 4, 0:T], PCv[half][0:T, :, 0:T],
                                                                 k.negtriT[0:T, 0:T].unsqueeze(1).to_broadcast([T, 4, T]), ALU.add),
                     R=[tpb[2 + half], k.t_cst], W=[t_Dm])
            for e8 in range(8):
                h = g * 8 + e8
                P.op("act", lambda e, e8=e8, h=h: e.activation(Lf[0:T, e8, 0:T], Dm[0:T, e8, 0:T], AF.Exp, bias=ncs[0:T, h:h + 1]),
                     R=[t_Dm, t_ncs], W=[t_Lf], sig=(e8 == 7))
            P.op("dve", lambda e, mb=mb: e.tensor_tensor(Mb[mb][0:T, :, 0:T], Lf[0:T, :, 0:T],
                                                         cbTs[0:T, 0:T].unsqueeze(1).to_broadcast([T, 8, T]), ALU.mult),
                 R=[t_Lf, t_cbTs], W=[t_Mb[mb]])
            for e8 in range(8):
                h = g * 8 + e8
                P.op("pe", lambda e, e8=e8, h=h, mb=mb: e.matmul(pb[4][0:T, e8 * 64:(e8 + 1) * 64], Mb[mb][0:T, e8, 0:T], xdt[0:T, h * 64:(h + 1) * 64],
                                                             start=True, stop=True),
                     R=[t_Mb[mb], t_xdt], W=[tpb[4]], sig=(e8 == 7))
            P.op("pe", lambda e: e.matmul(pb[5][0:T, :], CT[b][:, g, 0:T], Hb[:, g * 512:(g + 1) * 512], start=True, stop=True),
                 R=[t_CT[b], t_Hb], W=[tpb[5]])
            y3 = yv[0:T, :].rearrange("p (h d) -> p h d", h=8)
            t3 = tmpv[0:T, :].rearrange("p (h d) -> p h d", h=8)
            P.op("dve", lambda e: e.tensor_tensor(t3, pb[5][0:T, :].rearrange("p (h d) -> p h d", h=8),
                                                  ecs[0:T, g * 8:(g + 1) * 8].unsqueeze(2).to_broadcast([T, 8, 64]), ALU.mult),
                 R=[tpb[5], t_ecs], W=[t_tmpv])
            P.op("dve", lambda e: e.tensor_tensor(yv[0:T, :], pb[4][0:T, :], tmpv[0:T, :], ALU.add), R=[tpb[4], t_tmpv], W=[t_yv])
            P.op("pool", lambda e: e.tensor_tensor(t3, x_[0:T, g * 512:(g + 1) * 512].rearrange("p (h d) -> p h d", h=8),
                                                   dbc[0:T, g * 8:(g + 1) * 8].unsqueeze(2).to_broadcast([T, 8, 64]), ALU.mult),
                 R=[tx_, t_dbc, t_yv], W=[t_tmpv])
            P.op("dve", lambda e: e.tensor_tensor(yv[0:T, :], yv[0:T, :], tmpv[0:T, :], ALU.add), R=[t_yv, t_tmpv], W=[t_yv])
            P.op("dve", lambda e: e.tensor_tensor(yv[0:T, :], yv[0:T, :], zs[0:T, g * 512:(g + 1) * 512], ALU.mult), R=[t_yv, t_zs], W=[t_yv])
            P.op("pool", lambda e: e.memset(ssq[0:T, 0:1], 0.0), W=[t_ssq])
            P.op("act", lambda e: e.activation(junk[0:T, :], yv[0:T, :], AF.Square, scale=float(512 ** -0.5), accum_out=ssq[0:T, 0:1]),
                 R=[t_yv], W=[t_junk, t_ssq])
            P.op("act", lambda e: e.activation(ssq[0:T, 0:1], ssq[0:T, 0:1], AF.Sqrt, bias=EPS), R=[t_ssq], W=[t_ssq])
            P.op("dve", lambda e: e.reciprocal(ssq[0:T, 0:1], ssq[0:T, 0:1]), R=[t_ssq], W=[t_ssq])
            P.op("dve", lambda e: e.scalar_tensor_tensor(ynb[0:T, :], yv[0:T, :], ssq[0:T, 0:1], ngbc[0:T, g * 512:(g + 1) * 512], ALU.mult, ALU.mult),
                 R=[t_yv, t_ssq, t_ng], W=[t_ynb])
            ptv = pb[7][:, 0:256].bitcast(BF16).rearrange("p (a b) -> p a b", a=4)
            for q in range(4):
                P.op("pe", lambda e, q=q: e.transpose(ptv[:, q, 0:T], ynb[0:T, q * 128:(q + 1) * 128], k.identb[0:T, 0:T]),
                     R=[t_ynb, k.t_ident], W=[tpb[7]], sig=(q == 3))
            if T == 128:
                yb = g % 2
                P.op("act", lambda e, yb=yb: e.copy(ynT[yb], ptv), R=[tpb[7]], W=[t_ynT[yb]])
                P.op("sp", lambda e, yb=yb: e.dma_start(out=k.yn_scr[tt, :, g * 4:(g + 1) * 4, :], in_=ynT[yb]), R=[t_ynT[yb]], W=[t_yn_scr[tt]], dma=True)
            else:
                P.op("act", lambda e: e.copy(ynS[:, g * 4:(g + 1) * 4, :], ptv[:, :, 0:T]), R=[tpb[7]], W=[t_ynS])
            P.op("pe", lambda e: e.matmul(pb[6][:, :], Btm[b][0:T, g * 128:(g + 1) * 128], xdd[0:T, g * 512:(g + 1) * 512], start=True, stop=True),
                 R=[t_Btm[b], t_xdd], W=[tpb[6]])
            Hg = H[:, g * 512:(g + 1) * 512]
            P.op("pool", lambda e: e.tensor_tensor(Hg.rearrange("p (h d) -> p h d", h=8), Hg.rearrange("p (h d) -> p h d", h=8),
                                                   etot[:, g * 8:(g + 1) * 8].unsqueeze(2).to_broadcast([128, 8, 64]), ALU.mult),
                 R=[t_H, t_etot, tpb[5]], W=[t_H])
            P.op("dve", lambda e: e.tensor_tensor(Hg, pb[6][:, :], Hg, ALU.add), R=[tpb[6], t_H], W=[t_H])
            P.op("act", lambda e: e.copy(Hb[:, g * 512:(g + 1) * 512], Hg), R=[t_H, tpb[5]], W=[t_Hb])

        for g in range(8):
            group(g)

    P.op("pool", lambda e: e.memset(H, 0.0), W=[t_H])
    P.op("pool", lambda e: e.memset(Hb, 0.0), W=[t_Hb])
    for tt in range(16 if stop > 2 else 1):
        chunk(tt, 128)
    if stop <= 3:
        return
    state_out(k.o_pst[i])
    if stop <= 4:
        return
    state_in(k.ssm_st[i])
    chunk(16, SS)
    state_out(k.o_sst[i])
    if stop <= 5:
        return
    out_proj(k, k.w_out_odd[i], 32, k.yn_scr, t_yn_scr, ynS, t_ynS)


ATTN_SCALE = 128 ** -0.5
NEGM = -30000.0


def gelu_ops(P, dst, src_ps, tmp, R, W_dst, t_tmp, T, N):
    P.op("act", lambda e: e.activation(tmp, src_ps, AF.Square), R=R, W=[t_tmp])
    P.op("dve", lambda e: e.tensor_scalar(tmp, tmp, 0.044715, 1.0, ALU.mult, ALU.add), R=[t_tmp], W=[t_tmp])
    P.op("dve", lambda e: e.tensor_tensor(tmp, tmp, src_ps, ALU.mult), R=[t_tmp] + R, W=[t_tmp])
    P.op("act", lambda e: e.activation(tmp, tmp, AF.Sigmoid, scale=1.5957691216057308), R=[t_tmp], W=[t_tmp])
    P.op("dve", lambda e: e.tensor_tensor(dst, tmp, src_ps, ALU.mult), R=[t_tmp] + R, W=W_dst)


def even_phase(k, layer):
    P = k.P
    i = layer // 2
    pb, tpb = k.pb, k.tpb
    norm_phase(k, k.norm_mix[layer:layer + 1, :])
    rX, rY, rZ = k.rX, k.rY, k.rZ
    rY.reset()
    rZ.reset()
    wsl, t_w = [], []
    for s_ in range(4):
        a, t = rY.alloc([16, 512], BF16, "wsl")
        wsl.append(a)
        t_w.append(t)
    wgl, t_wgl = rY.alloc([16, 24], BF16, "wgl")
    vgain, t_vgain = rY.alloc([1024], F32, "vgain")
    bsbc, t_bsbc = rY.alloc([8, 128], F32, "bsbc")
    wmT, t_wmT = rY.alloc([8, 128], BF16, "wmT")
    wsT, t_wsT = rY.alloc([8, 128], F32, "wsT")
    gv, t_gv = rY.alloc([1024], F32, "gv")
    vn, t_vn = rY.alloc([1024], F32, "vn")
    rY.commit()
    vb16, t_vb16 = rZ.alloc([1024], BF16, "vb16")
    uTt, t_uTt = rZ.alloc([8, 128], BF16, "uTt")
    tS, t_tS = rZ.alloc([8, 128], F32, "tS")
    aT, t_aT = rZ.alloc([8, 128], BF16, "aT")
    raw, t_raw = [None, None], [None, None]
    for a_ in range(2):
        raw[a_], t_raw[a_] = rZ.alloc([512], F32, "raw")
    tmpf, t_tmpf = rZ.alloc([512], F32, "tmpf")
    rp, t_rp = [None] * 2, [None] * 2
    for a_ in range(2):
        rp[a_], t_rp[a_] = rZ.alloc([4, 64], F32, "rp")
    qn16, t_qn16 = rZ.alloc([512], BF16, "qn16")
    qr16, t_qr16 = rZ.alloc([512], BF16, "qr16")
    kb16, t_kb16 = rZ.alloc([2, 128], BF16, "kb16")
    va16, t_va16 = rZ.alloc([2, 130], BF16, "va16")
    trs, t_trs = [None, None], [None, None]
    for a_ in range(2):
        trs[a_], t_trs[a_] = rZ.alloc([4, 128], BF16, "trs")
    ub = [trs[a_][:, :, :].rearrange("p a b -> p (a b)") for a_ in range(2)]
    t_ub = t_trs
    cosT, t_cos = rZ.alloc([17, 64], F32, "cos")
    sinT, t_sin = rZ.alloc([17, 64], F32, "sin")
    gate, t_gate = rZ.alloc([17, 24], F32, "gate")
    st8, t_st8 = rZ.alloc([8], F32, "st8")
    qg, t_qg = rZ.alloc([128], F32, "qg")
    kg, t_kg = rZ.alloc([3, 128], F32, "kg")
    WF, t_WF = rZ.alloc([2, 16, 32], BF16, "WF")
    w2, t_w2 = rZ.alloc([2], F32, "w2")
    pl, t_pl = rZ.alloc([128], F32, "pl")
    KcT, t_KcT = rZ.alloc([2, 32], BF16, "KcT")
    Vca, t_Vca = rZ.alloc([2, 130], BF16, "Vca")
    moS, t_moS = rZ.alloc([16, SS], BF16, "moS")
    rZ.commit()
    k.ev = dict(gate=gate, t_gate=t_gate, KcT=KcT, t_KcT=t_KcT, Vca=Vca, t_Vca=t_Vca, moS=moS, t_moS=t_moS, WF=WF, t_WF=t_WF)

    P.op("sp", lambda e: e.dma_start(out=vgain, in_=k.gmlp_v_norm[i:i + 1, :].to_broadcast([128, 1024])), W=[t_vgain], dma=True)
    P.op("sp", lambda e: e.dma_start(out=bsbc, in_=k.gmlp_bs[i:i + 1].to_broadcast([128, 8, 128])), W=[t_bsbc], dma=True)
    P.op("sp", lambda e: e.dma_start(out=wsT, in_=k.gmlp_wsT[i]), W=[t_wsT], dma=True)
    P.op("dve", lambda e: e.tensor_tensor(wmT, wsT, k.utri[:, :].unsqueeze(1).to_broadcast([128, 8, 128]), ALU.mult),
         R=[t_wsT, k.t_cst], W=[t_wmT])
    P.op("sp", lambda e: e.dma_start(out=cosT, in_=k.rope_cos), W=[t_cos], dma=True)
    P.op("sp", lambda e: e.dma_start(out=sinT, in_=k.rope_sin), W=[t_sin], dma=True)
    P.op("sp", lambda e: e.dma_start(out=qg, in_=k.q_norm[i:i + 1, :].to_broadcast([128, 128])), W=[t_qg], dma=True)
    P.op("sp", lambda e: e.dma_start(out=kg, in_=k.k_norm[i:i + 1].to_broadcast([128, 3, 128])), W=[t_kg], dma=True)
    P.op("sp", lambda e: e.dma_start(out=pl[0:2, 0:64], in_=k.cmp_pool[i]), W=[t_pl], dma=True)
    P.op("dve", lambda e: e.reduce_max(st8[0:2, 0:1], pl[0:2, 0:64], AX.X), R=[t_pl], W=[t_st8])
    P.op("act", lambda e: e.mul(st8[0:2, 0:1], st8[0:2, 0:1], -1.0), R=[t_st8], W=[t_st8])
    P.op("pool", lambda e: e.memset(st8[0:2, 1:2], 0.0), R=[t_st8], W=[t_st8])
    P.op("act", lambda e: e.activation(pl[0:2, 0:64], pl[0:2, 0:64], AF.Exp, bias=st8[0:2, 0:1], accum_out=st8[0:2, 1:2]), R=[t_pl, t_st8], W=[t_pl, t_st8])
    P.op("dve", lambda e: e.reciprocal(st8[0:2, 1:2], st8[0:2, 1:2]), R=[t_st8], W=[t_st8])
    P.op("dve", lambda e: e.tensor_scalar(pl[0:2, 0:64], pl[0:2, 0:64], st8[0:2, 1:2], None, ALU.mult), R=[t_pl, t_st8], W=[t_pl])
    P.op("dve", lambda e: e.tensor_copy(pl[0:2, 64:128], pl[0:2, 0:64]), R=[t_pl], W=[t_pl])
    P.op("pe", lambda e: e.matmul(pb[7][:, 0:2], pl[0:2, :], k.identf[0:2, 0:2], start=True, stop=True), R=[t_pl, k.t_cst], W=[tpb[7]])
    P.op("act", lambda e: e.copy(w2, pb[7][:, 0:2]), R=[tpb[7]], W=[t_w2])
    for h in range(2):
        P.op("dve", lambda e, h=h: e.tensor_scalar(WF[:, h, :, :], k.ecmp[:, :, :], w2[:, h:h + 1], None, ALU.mult), R=[t_w2, k.t_ecmp], W=[t_WF])

    win = k.w_in_even[i].rearrange("(kc p) n -> p kc n", p=128)
    blocks = [b * 512 for b in range(9)]
    t_uT_scr = [Tok("uT_scr%d" % t) for t in range(17)]
    t_qT_scr = [Tok("qT_scr%d" % t) for t in range(17)]
    t_qrT_scr = [Tok("qrT_scr%d" % t) for t in range(17)]
    t_ksT_scr = [Tok("ksT_scr%d" % t) for t in range(17)]
    t_kwT_scr = [Tok("kwT_scr%d" % t) for t in range(17)]
    t_vs_scr = [Tok("vs_scr%d" % t) for t in range(17)]
    t_vw_scr = [Tok("vw_scr%d" % t) for t in range(17)]
    t_mo_scr = [Tok("mo_scr%d" % t) for t in range(16)]
    k.ev.update(t_qT_scr=t_qT_scr, t_qrT_scr=t_qrT_scr, t_ksT_scr=t_ksT_scr, t_kwT_scr=t_kwT_scr, t_vs_scr=t_vs_scr,
                t_vw_scr=t_vw_scr, t_mo_scr=t_mo_scr, cosT=cosT, sinT=sinT)

    def load_w(bi):
        sl = bi % 4
        c0 = blocks[bi]
        P.op("pool", lambda e, sl=sl, c0=c0: e.dma_start(out=wsl[sl], in_=win[:, :, c0:c0 + 512]), W=[t_w[sl]], dma=True)

    load_w(0)
    load_w(1)
    P.op("pool", lambda e: e.dma_start(out=wgl, in_=win[:, :, 4608:4632]), W=[t_wgl], dma=True)
    P.op("pool", lambda e: e.memset(va16[:, :, 128:130], 1.0), W=[t_va16])
    cnt = [0]

    def proj_tm(sl, tt, bank):
        T = 128 if tt < 16 else SS
        tok0 = tt * 128
        for kc in range(16):
            P.op("pe", lambda e, kc=kc: e.matmul(pb[bank][0:T, :], k.xnT[:, kc, tok0:tok0 + T], wsl[sl][:, kc, :], start=(kc == 0), stop=(kc == 15)),
                 R=[t_w[sl], k.t_xnT], W=[tpb[bank]], sig=(kc == 15))

    def rms_heads(x3, T, nh, gain_ap, t_x, t_gain):
        sq = tmpf[0:T, 0:nh * 128].rearrange("p (h d) -> p h d", h=nh)
        P.op("dve", lambda e: e.tensor_tensor(sq, x3, x3, ALU.mult), R=[t_x], W=[t_tmpf])
        P.op("dve", lambda e: e.reduce_sum(st8[0:T, 0:nh], sq, AX.X), R=[t_tmpf], W=[t_st8])
        P.op("act", lambda e: e.activation(st8[0:T, 0:nh], st8[0:T, 0:nh], AF.Sqrt, scale=1.0 / 128, bias=EPS), R=[t_st8], W=[t_st8])
        P.op("dve", lambda e: e.reciprocal(st8[0:T, 0:nh], st8[0:T, 0:nh]), R=[t_st8], W=[t_st8])
        P.op("dve", lambda e: e.tensor_tensor(x3, x3, st8[0:T, 0:nh].unsqueeze(2).to_broadcast([T, nh, 128]), ALU.mult), R=[t_x, t_st8], W=[t_x])
        P.op("dve", lambda e: e.tensor_tensor(x3, x3, gain_ap.unsqueeze(1).to_broadcast([T, nh, 128]), ALU.mult), R=[t_x, t_gain], W=[t_x])

    def rope_ops(dst3, src3, T, nh, tt, t_dst, t_src):
        c_ = cosT[0:T, tt, :].unsqueeze(1).to_broadcast([T, nh, 64])
        s_ = sinT[0:T, tt, :].unsqueeze(1).to_broadcast([T, nh, 64])
        x1, x2 = src3[:, :, 0:64], src3[:, :, 64:128]
        a, b_ = [rp[j][0:T, 0:nh, :] for j in range(2)]
        P.op("dve", lambda e: e.tensor_tensor(a, x1, c_, ALU.mult), R=[t_src, t_cos], W=[t_rp[0]])
        P.op("pool", lambda e: e.tensor_tensor(b_, x2, s_, ALU.mult), R=[t_src, t_sin], W=[t_rp[1]])
        P.op("dve", lambda e: e.tensor_tensor(dst3[:, :, 0:64], a, b_, ALU.subtract), R=[t_rp[0], t_rp[1]], W=[t_dst])
        P.op("dve", lambda e: e.tensor_tensor(a, x2, c_, ALU.mult), R=[t_src, t_cos], W=[t_rp[0]])
        P.op("pool", lambda e: e.tensor_tensor(b_, x1, s_, ALU.mult), R=[t_src, t_sin], W=[t_rp[1]])
        P.op("dve", lambda e: e.tensor_tensor(dst3[:, :, 64:128], a, b_, ALU.add), R=[t_rp[0], t_rp[1]], W=[t_dst])

    def transp_out(src16, t_src, T, nh, dsts):
        tb_ = cnt[0] % 2
        cnt[0] += 1
        bank2 = 6 + tb_
        pt = pb[bank2][:, 0:256].bitcast(BF16).rearrange("p (a b) -> p a b", a=4)
        for h in range(nh):
            P.op("pe", lambda e, h=h: e.transpose(pt[:, h, 0:T], src16[0:T, h * 128:(h + 1) * 128], k.identb[0:T, 0:T]),
                 R=[t_src, k.t_ident], W=[tpb[bank2]], sig=(h == nh - 1))
        P.op("act", lambda e: e.copy(trs[tb_][:, 0:nh, 0:T], pt[:, 0:nh, 0:T]), R=[tpb[bank2]], W=[t_trs[tb_]])
        for h in range(nh):
            dap, toks = dsts[h]
            P.op("sp", lambda e, h=h, dap=dap: e.dma_start(out=dap, in_=trs[tb_][:, h, 0:T]), R=[t_trs[tb_]], W=toks, dma=True)

    for bi in range(9):
        if bi + 2 < 9:
            load_w(bi + 2)
        sl = bi % 4
        if bi < 2:
            for j in range(4):
                c = bi * 4 + j
                for tb in range(5):
                    N = 512 if tb < 4 else SS
                    c0 = tb * 512
                    bank = cnt[0] % 2
                    u_ = cnt[0] % 2
                    cnt[0] += 1
                    for kc in range(16):
                        P.op("pe", lambda e, bank=bank, j=j, kc=kc, c0=c0, N=N, sl=sl: e.matmul(
                            pb[bank][:, 0:N], wsl[sl][:, kc, j * 128:(j + 1) * 128], k.xnT[:, kc, c0:c0 + N], start=(kc == 0), stop=(kc == 15)),
                            R=[t_w[sl], k.t_xnT], W=[tpb[bank]], sig=(kc == 15))
                    gelu_ops(P, ub[u_][:, 0:N], pb[bank][:, 0:N], tmpf[:, 0:N], [tpb[bank]], [t_ub[u_]], t_tmpf, 128, N)
                    tts = list(range(tb * 4, tb * 4 + 4)) if tb < 4 else [16]
                    P.op("sp", lambda e, c=c, u_=u_, c0=c0, N=N: e.dma_start(out=k.uT_scr[c, :, c0:c0 + N], in_=ub[u_][:, 0:N]),
                         R=[t_ub[u_]], W=[t_uT_scr[t] for t in tts], dma=True)
        elif bi == 2:
            continue
        elif bi == 3:
            def v_tile(tt):
                T = 128 if tt < 16 else SS
                tok0 = tt * 128
                for vb in range(2):
                    proj_tm(2 + vb, tt, vb)
                    gelu_ops(P, gv[0:T, vb * 512:(vb + 1) * 512], pb[vb][0:T, :], tmpf[0:T, :], [tpb[vb]], [t_gv], t_tmpf, T, 512)
                g3 = gv[0:T, :].rearrange("p (g d) -> p g d", g=8)
                v3 = vn[0:T, :].rearrange("p (g d) -> p g d", g=8)
                P.op("dve", lambda e: e.tensor_tensor(v3, g3, g3, ALU.mult), R=[t_gv], W=[t_vn])
                P.op("dve", lambda e: e.reduce_sum(st8[0:T, 0:8], v3, AX.X), R=[t_vn], W=[t_st8])
                P.op("act", lambda e: e.activation(st8[0:T, 0:8], st8[0:T, 0:8], AF.Sqrt, scale=1.0 / 128, bias=EPS), R=[t_st8], W=[t_st8])
                P.op("dve", lambda e: e.reciprocal(st8[0:T, 0:8], st8[0:T, 0:8]), R=[t_st8], W=[t_st8])
                P.op("dve", lambda e: e.tensor_tensor(v3, g3, st8[0:T, 0:8].unsqueeze(2).to_broadcast([T, 8, 128]), ALU.mult), R=[t_gv, t_st8], W=[t_vn])
                P.op("dve", lambda e: e.tensor_tensor(vn[0:T, :], vn[0:T, :], vgain[0:T, :], ALU.mult), R=[t_vn, t_vgain], W=[t_vn])
                if tt == 16:
                    P.op("sp", lambda e: e.dma_start(out=k.o_sv[i], in_=vn[0:SS, :]), R=[t_vn], dma=True)
                P.op("act", lambda e: e.copy(vb16[0:T, :], vn[0:T, :]), R=[t_vn], W=[t_vb16])
                P.op("sp", lambda e: e.dma_start(out=uTt[:, :, 0:T], in_=k.uT_scr[:, :, tok0:tok0 + T].rearrange("g p t -> p g t")),
                     R=[t_uT_scr[tt]], W=[t_uTt], dma=True)
                PSv = [pb[2][:, :].rearrange("p (a b) -> p a b", a=4), pb[3][:, :].rearrange("p (a b) -> p a b", a=4)]
                for g in range(8):
                    P.op("pe", lambda e, g=g: e.matmul(PSv[g // 4][:, g % 4, 0:T], vb16[0:T, g * 128:(g + 1) * 128], wmT[0:T, g, 0:T], start=True, stop=True),
                         R=[t_vb16, t_wmT], W=[tpb[2 + g // 4]], sig=(g % 4 == 3))
                for hf in range(2):
                    P.op("dve", lambda e, hf=hf: e.tensor_tensor(tS[:, hf * 4:(hf + 1) * 4, 0:T], PSv[hf][:, :, 0:T], bsbc[:, hf * 4:(hf + 1) * 4, 0:T], ALU.add),
                         R=[tpb[2 + hf], t_bsbc], W=[t_tS])
                if tt < 16:
                    P.op("dve", lambda e: e.tensor_tensor(aT[:, :, :], tS[:, :, :], uTt[:, :, :], ALU.mult), R=[t_tS, t_uTt], W=[t_aT])
                    P.op("sp", lambda e: e.dma_start(out=k.mo_scr[tt, :, 0:8, :], in_=aT), R=[t_aT], W=[t_mo_scr[tt]], dma=True)
                else:
                    P.op("dve", lambda e: e.tensor_tensor(moS[:, 0:8, :], tS[:, :, 0:SS], uTt[:, :, 0:SS], ALU.mult), R=[t_tS, t_uTt], W=[t_moS])
            for tt in range(17):
                v_tile(tt)
        elif bi < 6:
            def q_tile(tt, sl, hb0):
                T = 128 if tt < 16 else SS
                tok0 = tt * 128
                bank = cnt[0] % 2
                r_ = cnt[0] % 2
                cnt[0] += 1
                proj_tm(sl, tt, bank)
                P.op("act", lambda e: e.copy(raw[r_][0:T, :], pb[bank][0:T, :]), R=[tpb[bank]], W=[t_raw[r_]])
                x3 = raw[r_][0:T, :].rearrange("p (h d) -> p h d", h=4)
                rms_heads(x3, T, 4, qg[0:T, :], t_raw[r_], t_qg)
                P.op("act", lambda e: e.copy(qn16[0:T, :], raw[r_][0:T, :]), R=[t_raw[r_]], W=[t_qn16])
                rope_ops(qr16[0:T, :].rearrange("p (h d) -> p h d", h=4), x3, T, 4, tt, t_qr16, t_raw[r_])
                transp_out(qn16, t_qn16, T, 4, [(k.qT_scr[hb0 + h, :, tok0:tok0 + T], [t_qT_scr[tt]]) for h in range(4)])
                transp_out(qr16, t_qr16, T, 4, [(k.qrT_scr[hb0 + h, :, tok0:tok0 + T], [t_qrT_scr[tt]]) for h in range(4)])
            for tt in range(17):
                q_tile(tt, sl, (bi - 4) * 4)
        else:
            x = bi - 6

            def kv_tile(tt, sl, x):
                T = 128 if tt < 16 else SS
                tok0 = tt * 128
                bank = cnt[0] % 2
                r_ = cnt[0] % 2
                cnt[0] += 1
                proj_tm(sl, tt, bank)
                rw = raw[r_]
                P.op("act", lambda e: e.copy(rw[0:T, :], pb[bank][0:T, :]), R=[tpb[bank]], W=[t_raw[r_]])
                k3 = rw[0:T, 0:256].rearrange("p (h d) -> p h d", h=2)
                rms_heads(k3, T, 2, kg[0:T, x, :], t_raw[r_], t_kg)
                if x > 0:
                    kr = tmpf[0:T, 0:256].rearrange("p (h d) -> p h d", h=2)
                    rope_ops(kr, k3, T, 2, tt, t_tmpf, t_raw[r_])
                    P.op("dve", lambda e: e.tensor_copy(rw[0:T, 0:256], tmpf[0:T, 0:256]), R=[t_tmpf], W=[t_raw[r_]])
                if tt < 16:
                    dst = [k.o_pkc, k.o_pks, k.o_pkw][x]
                    if x < 2:
                        P.op("sp", lambda e, dst=dst: e.dma_start(out=dst[i, tok0:tok0 + T, :], in_=rw[0:T, :]), R=[t_raw[r_]], dma=True)
                    elif tt >= 12:
                        P.op("sp", lambda e, dst=dst: e.dma_start(out=dst[i, tok0 - 1536:tok0 - 1536 + T, :], in_=rw[0:T, :]), R=[t_raw[r_]], dma=True)
                else:
                    dst = [k.o_skc, k.o_sks, k.o_skw][x]
                    P.op("sp", lambda e, dst=dst: e.dma_start(out=dst[i], in_=rw[0:SS, :]), R=[t_raw[r_]], dma=True)
                P.op("act", lambda e: e.copy(kb16[0:T, :, :], rw[0:T, 0:256].rearrange("p (h d) -> p h d", h=2)), R=[t_raw[r_]], W=[t_kb16])
                P.op("act", lambda e: e.copy(va16[0:T, :, 0:128], rw[0:T, 256:512].rearrange("p (h d) -> p h d", h=2)), R=[t_raw[r_]], W=[t_va16])
                if x == 0:
                    if tt < 16:
                        for h in range(2):
                            P.op("pe", lambda e, h=h: e.matmul(pb[2 + h][:, 0:32], kb16[0:T, h, :], WF[0:T, h, tt, :], start=(tt == 0), stop=(tt == 15)),
                                 R=[t_kb16, t_WF], W=[tpb[2 + h]], sig=False)
                            P.op("pe", lambda e, h=h: e.matmul(pb[4 + h][0:32, 0:128], WF[0:T, h, tt, :], va16[0:T, h, 0:128], start=(tt == 0), stop=(tt == 15)),
                                 R=[t_va16, t_WF], W=[tpb[4 + h]], sig=True)
                else:
                    scrT = k.ksT_scr if x == 1 else k.kwT_scr
                    tscT = t_ksT_scr if x == 1 else t_kwT_scr
                    scrV = k.vs_scr if x == 1 else k.vw_scr
                    tscV = t_vs_scr if x == 1 else t_vw_scr
                    transp_out(kb16[:, :, :].rearrange("p h d -> p (h d)"), t_kb16, T, 2, [(scrT[h, :, tok0:tok0 + T], [tscT[tt]]) for h in range(2)])
                    P.op("sp", lambda e, scrV=scrV: e.dma_start(out=scrV[tok0:tok0 + T, :, :], in_=va16[0:T, :, :]), R=[t_va16], W=[tscV[tt]], dma=True)
            for tt in range(17):
                kv_tile(tt, sl, x)
            if x == 0:
                P.op("pool", lambda e: e.memset(Vca[0:32, :, 128:130], 1.0), W=[t_Vca])
                for h in range(2):
                    P.op("act", lambda e, h=h: e.copy(KcT[:, h, :], pb[2 + h][:, 0:32]), R=[tpb[2 + h]], W=[t_KcT])
                    P.op("act", lambda e, h=h: e.copy(Vca[0:32, h, 0:128], pb[4 + h][0:32, 0:128]), R=[tpb[4 + h]], W=[t_Vca])
    for tt in range(17):
        T = 128 if tt < 16 else SS
        tok0 = tt * 128
        bank = cnt[0] % 2
        cnt[0] += 1
        for kc in range(16):
            P.op("pe", lambda e, kc=kc, bank=bank, tok0=tok0, T=T: e.matmul(pb[bank][0:T, 0:24], k.xnT[:, kc, tok0:tok0 + T], wgl[:, kc, :], start=(kc == 0), stop=(kc == 15)),
                 R=[t_wgl, k.t_xnT], W=[tpb[bank]], sig=(kc == 15))
        P.op("act", lambda e, bank=bank, T=T, tt=tt: e.activation(gate[0:T, tt, :], pb[bank][0:T, 0:24], AF.Sigmoid), R=[tpb[bank]], W=[t_gate])
    if k.cfg.get("even_stop", 99) <= 1:
        return
    even_attn_prompt(k, layer)
    if k.cfg.get("even_stop", 99) <= 2:
        return
    if k.cfg.get("even_sample", True):
        even_attn_sample(k, layer)
    out_proj(k, k.w_out_even[i], 16, k.mo_scr, t_mo_scr, moS, t_moS)


def obank(h):
    return 4 + h % 4, 0


def attn_combine(k, T, Oc, Os, Ow, t_O, gate_ap, t_gate, s3, t_s3, tmpb, t_tmpb, bout, t_bout):
    P = k.P
    for x, O in enumerate((Oc, Os, Ow)):
        P.op("dve", lambda e, x=x, O=O: e.tensor_copy(s3[0:T, :, x], O[0:T, :, 128]), R=t_O, W=[t_s3])
    P.op("dve", lambda e: e.tensor_scalar_max(s3[0:T, :, :], s3[0:T, :, :], 1e-30), R=[t_s3], W=[t_s3])
    P.op("dve", lambda e: e.reciprocal(s3[0:T, :, :], s3[0:T, :, :]), R=[t_s3], W=[t_s3])
    P.op("dve", lambda e: e.tensor_tensor(s3[0:T, :, :], s3[0:T, :, :], gate_ap.rearrange("p (h x) -> p h x", h=8), ALU.mult),
         R=[t_s3, t_gate], W=[t_s3])
    for h in range(8):
        eng = "dve"
        P.op(eng, lambda e, h=h: e.tensor_scalar(tmpb[0:T, h, :], Oc[0:T, h, 0:128], s3[0:T, h, 0:1], None, ALU.mult), R=t_O + [t_s3], W=[t_tmpb])
        P.op(eng, lambda e, h=h: e.scalar_tensor_tensor(tmpb[0:T, h, :], Os[0:T, h, 0:128], s3[0:T, h, 1:2], tmpb[0:T, h, :], ALU.mult, ALU.add),
             R=t_O + [t_s3, t_tmpb], W=[t_tmpb])
        P.op(eng, lambda e, h=h: e.scalar_tensor_tensor(bout[0:T, h * 128:(h + 1) * 128], Ow[0:T, h, 0:128], s3[0:T, h, 2:3], tmpb[0:T, h, :], ALU.mult, ALU.add),
             R=t_O + [t_s3, t_tmpb], W=[t_bout])


def even_attn_prompt(k, layer):
    P = k.P
    i = layer // 2
    pb, tpb = k.pb, k.tpb
    ev = k.ev
    gate, t_gate, KcT, t_KcT, Vca, t_Vca = ev["gate"], ev["t_gate"], ev["KcT"], ev["t_KcT"], ev["Vca"], ev["t_Vca"]
    rX, rY = k.rX, k.rY
    rX.reset()
    rY.reset()
    ksT, t_ksT = rX.alloc([2, 2048], BF16, "ksT")
    kwT, t_kwT = rX.alloc([2, 2048], BF16, "kwT")
    vs, t_vs = rX.alloc([16, 2, 130], BF16, "vs")
    vw, t_vw = rX.alloc([16, 2, 130], BF16, "vw")
    qTt, t_qTt, qrTt, t_qrTt = [None, None], [None, None], [None, None], [None, None]
    for a_ in range(2):
        qTt[a_], t_qTt[a_] = rX.alloc([8, 128], BF16, "qTt")
        qrTt[a_], t_qrTt[a_] = rX.alloc([8, 128], BF16, "qrTt")
    Oc, t_Oc = rX.alloc([8, 130], F32, "Oc")
    Os, t_Os = rX.alloc([8, 130], F32, "Os")
    Ow, t_Ow = rX.alloc([8, 130], F32, "Ow")
    tmpb, t_tmpb = rX.alloc([8, 128], F32, "tmpb")
    rX.commit()
    ebig, t_ebig = rY.alloc([2048], BF16, "ebig")
    negc, t_negc = rY.alloc([128], BF16, "negc")
    negu, t_negu = rY.alloc([128], BF16, "negu")
    validq, t_validq = rY.alloc([16, 32], F32, "validq")
    keepm, t_keepm = rY.alloc([16, 32], F32, "keepm")
    addm, t_addm = rY.alloc([16, 32], F32, "addm")
    validT, t_validT = rY.alloc([16, 128], BF16, "validT")
    ee, t_ee = rY.alloc([8, 32], F32, "ee")
    imp, t_imp = rY.alloc([2, 32], F32, "imp")
    cmpb, t_cmpb = rY.alloc([32, 32], F32, "cmpb")
    rank, t_rank = rY.alloc([2, 32], F32, "rank")
    negm, t_negm = rY.alloc([64], BF16, "negm")
    negmT, t_negmT = rY.alloc([128], BF16, "negmT")
    ecf, t_ecf = rY.alloc([512], F32, "ecf")
    eTb, t_eTb = rY.alloc([512], BF16, "eTb")
    PT, t_PT = [None, None], [None, None]
    for a_ in range(2):
        PT[a_], t_PT[a_] = rY.alloc([512], BF16, "PT")
    s8, t_s8 = rY.alloc([8], F32, "s8")
    s3, t_s3 = rY.alloc([8, 3], F32, "s3")
    bout, t_bout = rY.alloc([1024], BF16, "bout")
    boT, t_boT = [None, None], [None, None]
    for a_ in range(2):
        boT[a_], t_boT[a_] = rY.alloc([4, 128], BF16, "boT")
    rY.commit()
    ld = lambda dst, src, R, W: P.op("sp", lambda e: e.dma_start(out=dst, in_=src), R=R, W=W, dma=True)
    ld(ksT, k.ksT_scr[:, :, 0:2048].rearrange("h p t -> p h t"), ev["t_ksT_scr"], [t_ksT])
    ld(kwT, k.kwT_scr[:, :, 0:2048].rearrange("h p t -> p h t"), ev["t_kwT_scr"], [t_kwT])
    ld(vs, k.vs_scr[0:2048].rearrange("(t p) h d -> p t h d", p=128), ev["t_vs_scr"], [t_vs])
    ld(vw, k.vw_scr[0:2048].rearrange("(t p) h d -> p t h d", p=128), ev["t_vw_scr"], [t_vw])
    ld(ebig[0:64, :], k.c_ebig, [], [t_ebig])
    ld(negc, k.c_negb[0], [], [t_negc])
    ld(negu, k.c_negb[1], [], [t_negu])
    ld(validq, k.c_valid, [], [t_validq])
    ld(keepm, k.c_keep, [], [t_keepm])
    ld(addm, k.c_add, [], [t_addm])
    ld(validT[0:32, :, :], k.c_validT, [], [t_validT])
    cnt = [0]
    t_mo_scr = ev["t_mo_scr"]

    def evac(O, t_O, kvh):
        for g in range(4):
            P.op("act", lambda e, g=g: e.copy(O[:, kvh * 4 + g, :], pb[4 + g][:, 0:130]), R=[tpb[4 + g]], W=[t_O])

    def branch(qt, kts, KT, t_KT, V, t_V, qr, t_qr, blockmask, O, t_O):
        for kvh in range(2):
            branch1(qt, kts, KT, t_KT, V, t_V, qr, t_qr, blockmask, kvh)
            evac(O, t_O, kvh)

    def branch1(qt, kts, KT, t_KT, V, t_V, qr, t_qr, blockmask, kvh):
        if True:
            for kt in kts:
                bs_ = 2 + cnt[0] % 2
                p_ = cnt[0] % 2
                cnt[0] += 1
                extra = []
                if blockmask:
                    extra.append("blk")
                if kt == qt:
                    extra.append("caus")
                if (not blockmask) and kt == qt - 4:
                    extra.append("upper")
                P.op("pe", lambda e, kt=kt, kvh=kvh, bs_=bs_, last=(len(extra) == 0): e.matmul(
                    pb[bs_][:, :], KT[:, kvh, kt * 128:(kt + 1) * 128], qr[:, kvh * 4:(kvh + 1) * 4, :], start=True, stop=last),
                    R=[t_KT, t_qr], W=[tpb[bs_]], sig=(len(extra) == 0))
                for j, nm in enumerate(extra):
                    last = (j == len(extra) - 1)
                    if nm == "blk":
                        P.op("pe", lambda e, kt=kt, kvh=kvh, bs_=bs_, last=last: e.matmul(
                            pb[bs_][:, :], ebig[kvh * 32:(kvh + 1) * 32, kt * 128:(kt + 1) * 128],
                            negmT[kvh * 32:(kvh + 1) * 32, :].unsqueeze(1).to_broadcast([32, 4, 128]), start=False, stop=last),
                            R=[t_ebig, t_negmT], W=[tpb[bs_]], sig=last)
                    else:
                        mk, tmk = (negc, t_negc) if nm == "caus" else (negu, t_negu)
                        P.op("pe", lambda e, bs_=bs_, last=last, mk=mk: e.matmul(
                            pb[bs_][:, :], k.identb[:, :], mk[:, :].unsqueeze(1).to_broadcast([128, 4, 128]), start=False, stop=last),
                            R=[k.t_ident, tmk], W=[tpb[bs_]], sig=last)
                P.op("act", lambda e, bs_=bs_, p_=p_: e.activation(PT[p_], pb[bs_][:, :], AF.Exp, scale=ATTN_SCALE), R=[tpb[bs_]], W=[t_PT[p_]])
                for g in range(4):
                    h = kvh * 4 + g
                    bk, col = obank(h)
                    P.op("pe", lambda e, g=g, bk=bk, col=col, kt=kt, kvh=kvh, p_=p_: e.matmul(
                        pb[bk][:, col:col + 130], PT[p_][:, g * 128:(g + 1) * 128], V[:, kt, kvh, :], start=(kt == kts[0]), stop=(kt == kts[-1])),
                        R=[t_PT[p_], t_V], W=[tpb[bk]], sig=True)

    def qtile(qt):
        b = qt % 2
        q_, tq_, qr_, tqr_ = qTt[b], t_qTt[b], qrTt[b], t_qrTt[b]
        ld(q_, k.qT_scr[:, :, qt * 128:(qt + 1) * 128].rearrange("h p t -> p h t"), [ev["t_qT_scr"][qt]], [tq_])
        ld(qr_, k.qrT_scr[:, :, qt * 128:(qt + 1) * 128].rearrange("h p t -> p h t"), [ev["t_qrT_scr"][qt]], [tqr_])
        for h in range(8):
            P.op("pe", lambda e, h=h: e.matmul(pb[0][:, h * 32:(h + 1) * 32], q_[:, h, :], KcT[:, h // 4, :], start=True, stop=True),
                 R=[tq_, t_KcT], W=[tpb[0]], sig=(h == 7))
        P.op("act", lambda e: e.activation(ee, pb[0][:, 0:256].rearrange("p (h n) -> p h n", h=8), AF.Exp, scale=ATTN_SCALE), R=[tpb[0]], W=[t_ee])
        P.op("dve", lambda e: e.tensor_tensor(ee, ee, validq[:, qt, :].unsqueeze(1).to_broadcast([128, 8, 32]), ALU.mult), R=[t_ee, t_validq], W=[t_ee])
        P.op("dve", lambda e: e.reduce_sum(s8, ee, AX.X), R=[t_ee], W=[t_s8])
        P.op("dve", lambda e: e.tensor_scalar_max(s8, s8, 1e-30), R=[t_s8], W=[t_s8])
        P.op("dve", lambda e: e.reciprocal(s8, s8), R=[t_s8], W=[t_s8])
        P.op("dve", lambda e: e.tensor_tensor(ee, ee, s8[:, :].unsqueeze(2).to_broadcast([128, 8, 32]), ALU.mult), R=[t_ee, t_s8], W=[t_ee])
        P.op("dve", lambda e: e.reduce_sum(imp, ee[:, :, :].rearrange("p (k g) n -> p k n g", k=2), AX.X), R=[t_ee], W=[t_imp])
        P.op("dve", lambda e: e.tensor_tensor(imp, imp, keepm[:, qt, :].unsqueeze(1).to_broadcast([128, 2, 32]), ALU.mult), R=[t_imp, t_keepm], W=[t_imp])
        P.op("dve", lambda e: e.tensor_tensor(imp, imp, addm[:, qt, :].unsqueeze(1).to_broadcast([128, 2, 32]), ALU.add), R=[t_imp, t_addm], W=[t_imp])
        for kvh in range(2):
            P.op("dve", lambda e, kvh=kvh: e.tensor_tensor(cmpb, imp[:, kvh, :].unsqueeze(1).to_broadcast([128, 32, 32]),
                                                           imp[:, kvh, :].unsqueeze(2).to_broadcast([128, 32, 32]), ALU.is_gt), R=[t_imp], W=[t_cmpb])
            P.op("dve", lambda e, kvh=kvh: e.reduce_sum(rank[:, kvh, :], cmpb, AX.X), R=[t_cmpb], W=[t_rank])
        P.op("dve", lambda e: e.tensor_scalar(negm, rank[:, :, :].rearrange("p k n -> p (k n)"), 15.5, NEGM, ALU.is_gt, ALU.mult), R=[t_rank], W=[t_negm])
        ptm = pb[0][:, 0:64].bitcast(BF16)
        P.op("pe", lambda e: e.transpose(ptm[0:64, :], negm, k.identb[:, :]), R=[t_negm, k.t_ident], W=[tpb[0]])
        P.op("act", lambda e: e.copy(negmT[0:64, :], ptm[0:64, :]), R=[tpb[0]], W=[t_negmT])
        for kvh in range(2):
            P.op("pe", lambda e, kvh=kvh: e.matmul(pb[1][0:32, :], KcT[:, kvh, :], q_[:, kvh * 4:(kvh + 1) * 4, :], start=True, stop=True),
                 R=[tq_, t_KcT], W=[tpb[1]])
            P.op("act", lambda e: e.activation(ecf[0:32, :], pb[1][0:32, :], AF.Exp, scale=ATTN_SCALE), R=[tpb[1]], W=[t_ecf])
            P.op("dve", lambda e: e.tensor_tensor(eTb[0:32, :].rearrange("p (g q) -> p g q", g=4), ecf[0:32, :].rearrange("p (g q) -> p g q", g=4),
                                                  validT[0:32, qt, :].unsqueeze(1).to_broadcast([32, 4, 128]), ALU.mult), R=[t_ecf, t_validT], W=[t_eTb])
            for g in range(4):
                h = kvh * 4 + g
                bk, col = obank(h)
                P.op("pe", lambda e, g=g, bk=bk, col=col, kvh=kvh: e.matmul(pb[bk][:, col:col + 130], eTb[0:32, g * 128:(g + 1) * 128], Vca[0:32, kvh, :],
                                                                           start=True, stop=True), R=[t_eTb, t_Vca], W=[tpb[bk]], sig=True)
            evac(Oc, t_Oc, kvh)
        branch(qt, list(range(0, qt + 1)), ksT, t_ksT, vs, t_vs, qr_, tqr_, True, Os, t_Os)
        branch(qt, list(range(max(0, qt - 4), qt + 1)), kwT, t_kwT, vw, t_vw, qr_, tqr_, False, Ow, t_Ow)
        attn_combine(k, 128, Oc, Os, Ow, [t_Oc, t_Os, t_Ow], gate[:, qt, :], t_gate, s3, t_s3, tmpb, t_tmpb, bout, t_bout)
        for half in range(2):
            pt = pb[0][:, 0:256].bitcast(BF16).rearrange("p (a b) -> p a b", a=4)
            for j in range(4):
                h = half * 4 + j
                P.op("pe", lambda e, j=j, h=h: e.transpose(pt[:, j, :], bout[:, h * 128:(h + 1) * 128], k.identb[:, :]), R=[t_bout, k.t_ident], W=[tpb[0]], sig=(j == 3))
            P.op("act", lambda e, half=half: e.copy(boT[half], pt), R=[tpb[0]], W=[t_boT[half]])
            P.op("sp", lambda e, half=half: e.dma_start(out=k.mo_scr[qt, :, 8 + half * 4:12 + half * 4, :], in_=boT[half]), R=[t_boT[half]], W=[t_mo_scr[qt]], dma=True)

    for qt in range(k.cfg.get("nqt", 16)):
        qtile(qt)


def even_attn_sample(k, layer):
    P = k.P
    i = layer // 2
    pb, tpb = k.pb, k.tpb
    ev = k.ev
    gate, t_gate, moS, t_moS = ev["gate"], ev["t_gate"], ev["moS"], ev["t_moS"]
    WF, t_WF = ev["WF"], ev["t_WF"]
    rX, rY = k.rX, k.rY
    rX.reset()
    rY.reset()
    G = 16
    ptb, t_ptb = rX.alloc([128], I32, "ptb")
    ptf, t_ptf = rX.alloc([128], F32, "ptf")
    idx, t_idx = rX.alloc([128], I32, "idx")
    iot, t_iot = rX.alloc([8], F32, "iot")
    pgf, t_pgf = [None] * 4, [None] * 4
    for a_ in range(4):
        pgf[a_], t_pgf[a_] = rX.alloc([512], F32, "pgf")
    pgk, t_pgk = [None] * 4, [None] * 4
    vau, t_vau = [None] * 4, [None] * 4
    for a_ in range(4):
        pgk[a_], t_pgk[a_] = rX.alloc([256], BF16, "pgk")
        vau[a_], t_vau[a_] = rX.alloc([2, 130], BF16, "vau")
    KT, t_KT = rX.alloc([2, 512], BF16, "KT")
    PTs, t_PTs = rX.alloc([8, 16], BF16, "PTs")
    KcTs, t_KcTs = rX.alloc([2, 256], BF16, "KcTs")
    VcTs, t_VcTs = rX.alloc([2, 256], BF16, "VcTs")
    Vcs, t_Vcs = rX.alloc([2, 2, 130], BF16, "Vcs")
    qTs, t_qTs = rX.alloc([8, SS], BF16, "qTs")
    qrTs, t_qrTs = rX.alloc([8, SS], BF16, "qrTs")
    knT, t_knT = rX.alloc([2, 2, SS], BF16, "knT")
    vnew, t_vnew = rX.alloc([2, 2, 130], BF16, "vnew")
    rX.commit()
    ef, t_ef = rY.alloc([2, 256], F32, "ef")
    pf, t_pf = rY.alloc([2, 256], F32, "pf")
    eb, t_eb = rY.alloc([2, 256], BF16, "eb")
    impS, t_impS = rY.alloc([2, 256], F32, "impS")
    tt_, t_tt = rY.alloc([2, 255], F32, "tt")
    msk, t_msk = rY.alloc([2, 255], F32, "msk")
    mx, t_mx = rY.alloc([2], F32, "mx")
    sm2, t_sm2 = rY.alloc([2], F32, "sm2")
    negS, t_negS = rY.alloc([2, 256], F32, "negS")
    neg16, t_neg16 = rY.alloc([2, 256], F32, "neg16")
    Sm, t_Sm = rY.alloc([512], F32, "Sm")
    Pb, t_Pb = rY.alloc([512], BF16, "Pb")
    sum16, t_sum16 = rY.alloc([4], F32, "sum16")
    sum16T, t_sum16T = rY.alloc([16], F32, "sum16T")
    negnew, t_negnew = rY.alloc([4], F32, "negnew")
    negwin, t_negwin = rY.alloc([512], F32, "negwin")
    O16, t_O16 = rY.alloc([2, 130], F32, "O16")
    O4 = []
    t_O4 = []
    for a_ in range(3):
        a, t = rY.alloc([8, 130], F32, "O4")
        O4.append(a)
        t_O4.append(t)
    s3, t_s3 = rY.alloc([8, 3], F32, "s3")
    tmpb, t_tmpb = rY.alloc([8, 128], F32, "tmpb")
    bout, t_bout = rY.alloc([1024], BF16, "bout")
    rY.commit()
    ld = lambda dst, src, R, W: P.op("sp", lambda e: e.dma_start(out=dst, in_=src), R=R, W=W, dma=True)
    ld(ptb, k.ptab.to_broadcast([128, 128]), [], [t_ptb])
    ld(iot[:, 0:1], k.c_iota, [], [t_iot])
    ld(sum16[0:16, :], k.c_sum16, [], [t_sum16])
    ld(sum16T[0:4, :], k.c_sum16T, [], [t_sum16T])
    ld(negnew[0:16, :], k.c_negnew, [], [t_negnew])
    ld(negwin[0:16, :], k.c_negwin, [], [t_negwin])
    ld(qTs, k.qT_scr[:, :, 2048:2052].rearrange("h p t -> p h t"), [ev["t_qT_scr"][16]], [t_qTs])
    ld(qrTs, k.qrT_scr[:, :, 2048:2052].rearrange("h p t -> p h t"), [ev["t_qrT_scr"][16]], [t_qrTs])
    ld(knT[:, 0, :, :], k.ksT_scr[:, :, 2048:2052].rearrange("h p t -> p h t"), [ev["t_ksT_scr"][16]], [t_knT])
    ld(knT[:, 1, :, :], k.kwT_scr[:, :, 2048:2052].rearrange("h p t -> p h t"), [ev["t_kwT_scr"][16]], [t_knT])
    ld(vnew[0:SS, 0, :, :], k.vs_scr[2048:2052], [ev["t_vs_scr"][16]], [t_vnew])
    ld(vnew[0:SS, 1, :, :], k.vw_scr[2048:2052], [ev["t_vw_scr"][16]], [t_vnew])
    P.op("dve", lambda e: e.tensor_copy(ptf, ptb), R=[t_ptb], W=[t_ptf])
    P.op("dve", lambda e: e.tensor_scalar(ptf, ptf, 128.0, iot[:, 0:1], ALU.mult, ALU.add), R=[t_ptf, t_iot], W=[t_ptf])
    if i > 0:
        P.op("dve", lambda e: e.tensor_scalar_add(ptf, ptf, float(i * 1280 * 128)), R=[t_ptf], W=[t_ptf])
    P.op("dve", lambda e: e.tensor_copy(idx, ptf), R=[t_ptf], W=[t_idx])
    for a_ in range(4):
        P.op("pool", lambda e, a_=a_: e.memset(vau[a_][:, :, 128:130], 1.0), W=[t_vau[a_]])
    P.op("pool", lambda e: e.memset(Vcs[:, :, :, 128:130], 1.0), W=[t_Vcs])
    cnt = [0]

    def fetch(cache_rows, pg, slot, gather=True):
        if gather:
            P.op("pool", lambda e: e.indirect_dma_start(out=pgf[slot], out_offset=None, in_=cache_rows,
                                                        in_offset=bass.IndirectOffsetOnAxis(ap=idx[:, pg:pg + 1], axis=0)),
                 R=[t_idx], W=[t_pgf[slot]], dma=True)
        else:
            P.op("sp", lambda e: e.dma_start(out=pgf[slot], in_=cache_rows[pg * 128:(pg + 1) * 128, :]), W=[t_pgf[slot]], dma=True)
        P.op("act", lambda e: e.copy(pgk[slot], pgf[slot][:, 0:256]), R=[t_pgf[slot]], W=[t_pgk[slot]])
        P.op("dve", lambda e: e.tensor_copy(vau[slot][:, :, 0:128], pgf[slot][:, 256:512].rearrange("p (h d) -> p h d", h=2)),
             R=[t_pgf[slot]], W=[t_vau[slot]])

    cmp_rows = k.cache_cmp.rearrange("l r c -> (l r) c")
    for pg in range(128):
        sl_ = pg % 4

        def one(pg=pg, sl_=sl_):
            fetch(cmp_rows, pg, sl_)
            for kvh in range(2):
                P.op("pe", lambda e, kvh=kvh: e.matmul(pb[0][:, kvh * 256 + 2 * pg:kvh * 256 + 2 * pg + 2], pgk[sl_][:, kvh * 128:(kvh + 1) * 128],
                                                       WF[:, kvh, 0, 0:2], start=True, stop=True), R=[t_pgk[sl_], t_WF], W=[tpb[0]])
                P.op("pe", lambda e, kvh=kvh: e.matmul(pb[1][:, kvh * 256 + 2 * pg:kvh * 256 + 2 * pg + 2], vau[sl_][:, kvh, 0:128],
                                                       WF[:, kvh, 0, 0:2], start=True, stop=True), R=[t_vau[sl_], t_WF], W=[tpb[1]])
        one()
    P.op("act", lambda e: e.copy(KcTs, pb[0][:, :].rearrange("p (h n) -> p h n", h=2)), R=[tpb[0]], W=[t_KcTs])
    P.op("act", lambda e: e.copy(VcTs, pb[1][:, :].rearrange("p (h n) -> p h n", h=2)), R=[tpb[1]], W=[t_VcTs])
    ptv = pb[2][:, 0:256].bitcast(BF16).rearrange("p (a b) -> p a b", a=4)
    for kvh in range(2):
        for t in range(2):
            P.op("pe", lambda e, kvh=kvh, t=t: e.transpose(ptv[:, kvh * 2 + t, :], VcTs[:, kvh, t * 128:(t + 1) * 128], k.identb[:, :]),
                 R=[t_VcTs, k.t_ident], W=[tpb[2]])
    for kvh in range(2):
        for t in range(2):
            P.op("act", lambda e, kvh=kvh, t=t: e.copy(Vcs[:, t, kvh, 0:128], ptv[:, kvh * 2 + t, :]), R=[tpb[2]], W=[t_Vcs])
    for kvh in range(2):
        P.op("pe", lambda e, kvh=kvh: e.matmul(pb[3][0:G, kvh * 256:(kvh + 1) * 256], qTs[:, kvh * 4:(kvh + 1) * 4, :], KcTs[:, kvh, :], start=True, stop=True),
             R=[t_qTs, t_KcTs], W=[tpb[3]])
    P.op("pool", lambda e: e.memset(sm2[0:G, :], 0.0), W=[t_sm2])
    for kvh in range(2):
        P.op("act", lambda e, kvh=kvh: e.activation(ef[0:G, kvh, :], pb[3][0:G, kvh * 256:(kvh + 1) * 256], AF.Exp, scale=ATTN_SCALE,
                                                    accum_out=sm2[0:G, kvh:kvh + 1]), R=[tpb[3], t_sm2], W=[t_ef, t_sm2])
    P.op("act", lambda e: e.copy(eb[0:G, :, :], ef[0:G, :, :]), R=[t_ef], W=[t_eb])
    P.op("dve", lambda e: e.reciprocal(sm2[0:G, :], sm2[0:G, :]), R=[t_sm2], W=[t_sm2])
    P.op("dve", lambda e: e.tensor_tensor(pf[0:G, :, :], ef[0:G, :, :], sm2[0:G, :].unsqueeze(2).to_broadcast([G, 2, 256]), ALU.mult), R=[t_ef, t_sm2], W=[t_pf])
    P.op("pe", lambda e: e.matmul(pb[3][0:SS, :], sum16[0:G, :], pf[0:G, :, :].rearrange("p h n -> p (h n)"), start=True, stop=True),
         R=[t_sum16, t_pf, t_ef], W=[tpb[3]])
    P.op("act", lambda e: e.copy(impS[0:SS, :, :], pb[3][0:SS, :].rearrange("p (h n) -> p h n", h=2)), R=[tpb[3]], W=[t_impS])
    P.op("dve", lambda e: e.tensor_copy(tt_[0:SS, :, :], impS[0:SS, :, 1:256]), R=[t_impS], W=[t_tt])
    for it in range(14):
        P.op("dve", lambda e: e.reduce_max(mx[0:SS, :], tt_[0:SS, :, :], AX.X), R=[t_tt], W=[t_mx])
        if it < 13:
            P.op("dve", lambda e: e.tensor_tensor(msk[0:SS, :, :], tt_[0:SS, :, :], mx[0:SS, :].unsqueeze(2).to_broadcast([SS, 2, 255]), ALU.is_ge),
                 R=[t_tt, t_mx], W=[t_msk])
            P.op("dve", lambda e: e.scalar_tensor_tensor(tt_[0:SS, :, :], msk[0:SS, :, :], -1e30, tt_[0:SS, :, :], ALU.mult, ALU.add),
                 R=[t_msk, t_tt], W=[t_tt])
    P.op("dve", lambda e: e.tensor_tensor(negS[0:SS, :, :], impS[0:SS, :, :], mx[0:SS, :].unsqueeze(2).to_broadcast([SS, 2, 256]), ALU.is_lt),
         R=[t_impS, t_mx], W=[t_negS])
    P.op("dve", lambda e: e.tensor_scalar(negS[0:SS, :, :], negS[0:SS, :, :], NEGM, None, ALU.mult), R=[t_negS], W=[t_negS])
    P.op("pool", lambda e: e.memset(negS[0:SS, :, 0:1], 0.0), R=[t_negS], W=[t_negS])
    P.op("pe", lambda e: e.matmul(pb[3][0:G, :], sum16T[0:SS, :], negS[0:SS, :, :].rearrange("p h n -> p (h n)"), start=True, stop=True),
         R=[t_sum16T, t_negS, t_impS], W=[tpb[3]])
    P.op("act", lambda e: e.copy(neg16[0:G, :, :], pb[3][0:G, :].rearrange("p (h n) -> p h n", h=2)), R=[tpb[3]], W=[t_neg16])

    state = {"first": [True, True]}

    def pv(kvh, lhsT_ap, t_l, rhs_ap, t_r, last):
        first = state["first"][kvh]
        state["first"][kvh] = False
        P.op("pe", lambda e: e.matmul(pb[6 + kvh][0:G, 0:130], lhsT_ap, rhs_ap, start=first, stop=last), R=[t_l, t_r], W=[tpb[6 + kvh]])

    def finish(x):
        for kvh in range(2):
            P.op("act", lambda e, kvh=kvh: e.copy(O16[0:G, kvh, :], pb[6 + kvh][0:G, 0:130]), R=[tpb[6 + kvh]], W=[t_O16])
        for h in range(8):
            kvh, g = h // 4, h % 4
            bk, col = h // 3, (h % 3) * 130
            P.op("pe", lambda e, kvh=kvh, g=g, bk=bk, col=col: e.matmul(pb[bk][0:SS, col:col + 130], k.identf[0:G, g * 4:(g + 1) * 4], O16[0:G, kvh, :],
                                                                      start=True, stop=True), R=[t_O16, k.t_cst], W=[tpb[bk]])
        for bk, (h0, h1) in enumerate(((0, 3), (3, 6), (6, 8))):
            nh = h1 - h0
            P.op("act", lambda e, bk=bk, h0=h0, h1=h1, nh=nh: e.copy(O4[x][0:SS, h0:h1, :], pb[bk][0:SS, 0:nh * 130].rearrange("p (h d) -> p h d", h=nh)),
                 R=[tpb[bk]], W=[t_O4[x]])
        state["first"] = [True, True]

    def key_group(slots, mask_fn, has_more):
        ktv = pb[2][:, :].bitcast(BF16).rearrange("p (h a b) -> p h a b", h=2, a=4)
        for kvh in range(2):
            for j, sl_ in enumerate(slots):
                P.op("pe", lambda e, kvh=kvh, j=j, sl_=sl_: e.transpose(ktv[:, kvh, j, :], pgk[sl_][:, kvh * 128:(kvh + 1) * 128], k.identb[:, :]),
                     R=[t_pgk[sl_], k.t_ident], W=[tpb[2]])
        P.op("act", lambda e: e.copy(KT, pb[2][:, :].bitcast(BF16).rearrange("p (h n) -> p h n", h=2)), R=[tpb[2]], W=[t_KT])
        ptp = pb[5][:, 0:64].bitcast(BF16).rearrange("p (a b) -> p a b", a=8)
        for kvh in range(2):
            P.op("pe", lambda e, kvh=kvh: e.matmul(pb[3 + kvh][0:G, :], qrTs[:, kvh * 4:(kvh + 1) * 4, :], KT[:, kvh, :], start=True, stop=True),
                 R=[t_qrTs, t_KT], W=[tpb[3 + kvh]])
            mask_fn(kvh)
            P.op("act", lambda e: e.activation(Pb[0:G, :], Sm[0:G, :], AF.Exp, scale=ATTN_SCALE), R=[t_Sm], W=[t_Pb])
            for j in range(4):
                P.op("pe", lambda e, kvh=kvh, j=j: e.transpose(ptp[:, kvh * 4 + j, :], Pb[0:G, j * 128:(j + 1) * 128], k.identb[0:G, 0:G]),
                     R=[t_Pb, k.t_ident], W=[tpb[5]])
        P.op("act", lambda e: e.copy(PTs, ptp), R=[tpb[5]], W=[t_PTs])
        for kvh in range(2):
            for j, sl_ in enumerate(slots):
                pv(kvh, PTs[:, kvh * 4 + j, :], t_PTs, vau[sl_][:, kvh, :], t_vau[sl_], False)

    def new_rows(x):
        ptn = pb[5][:, 0:16].bitcast(BF16).rearrange("p (a b) -> p a b", a=2)
        for kvh in range(2):
            P.op("pe", lambda e, kvh=kvh: e.matmul(pb[3 + kvh][0:G, 0:SS], qrTs[:, kvh * 4:(kvh + 1) * 4, :], knT[:, x, kvh, :], start=True, stop=True),
                 R=[t_qrTs, t_knT], W=[tpb[3 + kvh]])
            P.op("dve", lambda e, kvh=kvh: e.tensor_tensor(Sm[0:G, 0:SS], pb[3 + kvh][0:G, 0:SS], negnew[0:G, :], ALU.add), R=[tpb[3 + kvh], t_negnew], W=[t_Sm])
            P.op("act", lambda e: e.activation(Pb[0:G, 0:SS], Sm[0:G, 0:SS], AF.Exp, scale=ATTN_SCALE), R=[t_Sm], W=[t_Pb])
            P.op("pe", lambda e, kvh=kvh: e.transpose(ptn[0:SS, kvh, :], Pb[0:G, 0:SS], k.identb[0:G, 0:G]), R=[t_Pb, k.t_ident], W=[tpb[5]])
        P.op("act", lambda e: e.copy(PTs[0:SS, 0:2, :], ptn[0:SS, :, :]), R=[tpb[5]], W=[t_PTs])
        for kvh in range(2):
            pv(kvh, PTs[0:SS, kvh, :], t_PTs, vnew[0:SS, x, kvh, :], t_vnew, True)

    ptc = pb[5][:, 0:32].bitcast(BF16).rearrange("p (a b) -> p a b", a=4)
    for kvh in range(2):
        for t in range(2):
            P.op("pe", lambda e, kvh=kvh, t=t: e.transpose(ptc[:, kvh * 2 + t, :], eb[0:G, kvh, t * 128:(t + 1) * 128], k.identb[0:G, 0:G]),
                 R=[t_eb, k.t_ident], W=[tpb[5]])
    P.op("act", lambda e: e.copy(PTs[:, 0:4, :], ptc), R=[tpb[5]], W=[t_PTs])
    for kvh in range(2):
        for t in range(2):
            pv(kvh, PTs[:, kvh * 2 + t, :], t_PTs, Vcs[:, t, kvh, :], t_Vcs, t == 1)
    finish(0)

    sel_rows = k.cache_sel.rearrange("l r c -> (l r) c")
    for grp in range(32):
        def sel_grp(grp=grp):
            for j in range(4):
                fetch(sel_rows, grp * 4 + j, j)

            def mask_fn(kvh):
                P.op("dve", lambda e: e.tensor_tensor(Sm[0:G, :].rearrange("p (b c) -> p b c", b=8), pb[3 + kvh][0:G, :].rearrange("p (b c) -> p b c", b=8),
                                                      neg16[0:G, kvh, grp * 8:(grp + 1) * 8].unsqueeze(2).to_broadcast([G, 8, 64]), ALU.add),
                     R=[tpb[3 + kvh], t_neg16], W=[t_Sm])
            key_group([0, 1, 2, 3], mask_fn, True)
        sel_grp()
    new_rows(0)
    finish(1)

    win_rows = k.cache_win[i]
    for j in range(4):
        fetch(win_rows, j, j, gather=False)

    def mask_win(kvh):
        P.op("dve", lambda e: e.tensor_tensor(Sm[0:G, :], pb[3 + kvh][0:G, :], negwin[0:G, :], ALU.add), R=[tpb[3 + kvh], t_negwin], W=[t_Sm])
    key_group([0, 1, 2, 3], mask_win, True)
    new_rows(1)
    finish(2)

    attn_combine(k, SS, O4[0], O4[1], O4[2], t_O4, gate[0:SS, 16, :], t_gate, s3, t_s3, tmpb, t_tmpb, bout, t_bout)
    ptb_ = pb[5][:, 0:16].bitcast(BF16).rearrange("p (a b) -> p a b", a=8)
    for h in range(8):
        P.op("pe", lambda e, h=h: e.transpose(ptb_[:, h, :], bout[0:SS, h * 128:(h + 1) * 128], k.identb[0:SS, 0:SS]), R=[t_bout, k.t_ident], W=[tpb[5]])
    P.op("act", lambda e: e.copy(moS[:, 8:16, :], ptb_), R=[tpb[5]], W=[t_moS])


def _relayout_ffn(a, last):
    d = a.shape[0]
    return np.ascontiguousarray(a.reshape(d, last, 88, 128).transpose(0, 3, 2, 1))


def _relayout_ch(a, nchunk):
    sh = a.shape
    T = sh[-2]
    b = a.reshape(sh[:-2] + (T, nchunk, 128))
    nd = b.ndim
    perm = tuple(range(nd - 3)) + (nd - 1, nd - 2, nd - 3)
    return np.ascontiguousarray(b.transpose(perm))


def make_core_inputs(inp, c):
    ident = np.eye(128, dtype=np.float32)
    jj, ii = np.meshgrid(np.arange(128), np.arange(128), indexing="ij")
    utri = (jj <= ii).astype(np.float32)
    negtriT = np.where(jj <= ii, 0.0, -30000.0).astype(np.float32)
    cst4 = np.stack([ident, utri, negtriT, np.ones((128, 128), np.float32)])
    d = {
        "x_p": np.ascontiguousarray(inp["x_prompt"][c % 4]),
        "x_s": np.ascontiguousarray(inp["x_sample"][c]),
        "norm_mix": inp["norm_mix"], "norm_ffn": inp["norm_ffn"],
        "ffn_w_up": inp["ffn_w_up"], "ffn_w_down": inp["ffn_w_down"],
        "ffn_cw": _relayout_ch(inp["ffn_conv_w"], 88),
        "ffn_cb": np.ascontiguousarray(_relayout_ch(inp["ffn_conv_b"][:, None, :], 88)[..., 0]),
        "ffn_st": _relayout_ch(inp["state_ffn_conv"][:, c], 88),
        "ident": ident.astype(ml_dtypes.bfloat16), "cst4": cst4,
        "w_in_odd": inp["w_in_odd"], "w_out_odd": inp["w_out_odd"],
        "ssm_cw": _relayout_ch(inp["ssm_conv_w"], 48),
        "ssm_cb": np.ascontiguousarray(_relayout_ch(inp["ssm_conv_b"][:, None, :], 48)[..., 0]),
        "ssm_cst": _relayout_ch(inp["state_ssm_conv"][:, c], 48),
        "ssm_st": np.ascontiguousarray(inp["state_ssm"][:, c].reshape(2, 4096, 128)),
        "ssm_dt_bias": inp["ssm_dt_bias"], "ssm_a_log": inp["ssm_a_log"], "ssm_d": inp["ssm_d"], "ssm_norm": inp["ssm_norm"],
        "w_in_even": inp["w_in_even"], "w_out_even": inp["w_out_even"], "gmlp_v_norm": inp["gmlp_v_norm"],
        "gmlp_wsT": np.ascontiguousarray(inp["gmlp_ws"].transpose(0, 3, 1, 2)), "gmlp_bs": inp["gmlp_bs"],
        "q_norm": inp["q_norm"], "k_norm": inp["k_norm"], "cmp_pool": inp["cmp_pool"],
        "cache_cmp": inp["cache_kv_cmp"].reshape(2, 1280 * 128, 512), "cache_sel": inp["cache_kv_sel"].reshape(2, 1280 * 128, 512),
        "cache_win": np.ascontiguousarray(inp["cache_kv_win"][:, c].reshape(2, 512, 512)),
        "ptab": np.ascontiguousarray(inp["page_table"][c:c + 1]).astype(np.int32),
    }
    d.update(_CONSTS())
    return d


_CC = {}


def _CONSTS():
    if _CC:
        return _CC
    bf = ml_dtypes.bfloat16
    pos = np.concatenate([np.arange(2048), 16384 + np.arange(4), np.zeros(124)]).astype(np.float32)
    inv = (10000.0 ** (-np.arange(64, dtype=np.float32) / 64)).astype(np.float32)
    ang = pos[:, None] * inv[None, :]
    _CC["rope_cos"] = np.ascontiguousarray(np.cos(ang).astype(np.float32).reshape(17, 128, 64).transpose(1, 0, 2))
    _CC["rope_sin"] = np.ascontiguousarray(np.sin(ang).astype(np.float32).reshape(17, 128, 64).transpose(1, 0, 2))
    p = np.arange(128)
    ecmp = np.zeros((128, 16, 32), np.float32)
    for tt in range(16):
        ecmp[p, tt, 2 * tt + p // 64] = 1.0
    _CC["c_ecmp"] = ecmp.astype(bf)
    ebig = np.zeros((64, 2048), np.float32)
    kk = np.arange(2048)
    for h in range(2):
        ebig[h * 32 + kk // 64, kk] = 1.0
    _CC["c_ebig"] = ebig.astype(bf)
    jj, ii = np.meshgrid(np.arange(128), np.arange(128), indexing="ij")
    negc = np.where(jj > ii, -30000.0, 0.0)
    negu = np.where(jj < ii, -30000.0, 0.0)
    _CC["c_negb"] = np.stack([negc, negu]).astype(bf)
    q = np.arange(2048)
    n = np.arange(32)
    valid = (64 * n[None, :] + 63 <= q[:, None]).astype(np.float32)
    cur = q // 64
    forced = (n[None, :] == 0) | (n[None, :] == cur[:, None])
    future = n[None, :] > cur[:, None]
    keep = (~(forced | future)).astype(np.float32)
    add = np.where(forced, 1e4, np.where(future, -1e30, 0.0)).astype(np.float32)
    t3 = lambda a: np.ascontiguousarray(a.reshape(16, 128, 32).transpose(1, 0, 2))
    _CC["c_valid"], _CC["c_keep"], _CC["c_add"] = t3(valid), t3(keep), t3(add)
    _CC["c_validT"] = np.ascontiguousarray(valid.reshape(16, 128, 32).transpose(2, 0, 1)).astype(bf)
    _CC["c_iota"] = np.arange(128, dtype=np.float32)[:, None]
    r = np.arange(16)
    _CC["c_sum16"] = (r[:, None] % 4 == np.arange(4)[None, :]).astype(np.float32)
    _CC["c_sum16T"] = np.ascontiguousarray(_CC["c_sum16"].T)
    _CC["c_negnew"] = np.where(np.arange(4)[None, :] > (r % 4)[:, None], -30000.0, 0.0).astype(np.float32)
    _CC["c_negwin"] = np.where(np.arange(512)[None, :] < (r % 4)[:, None], -30000.0, 0.0).astype(np.float32)
    return _CC


_NC_CACHE = {}


def kernel(**inputs):
    inp = {n: np.asarray(v) for n, v in inputs.items()}
    if "nc" not in _NC_CACHE:
        _NC_CACHE["nc"] = build({"layers": DEPTH})
    nc = _NC_CACHE["nc"]
    in_maps = [make_core_inputs(inp, c) for c in range(8)]
    res = run_bass_kernel_spmd(nc, in_maps, core_ids=list(range(8)))
    r = res.results
    f32 = np.float32
    P4, S8 = range(4), range(8)
    st = lambda key, rng, fn=(lambda a: a): np.stack([fn(np.asarray(r[c][key])) for c in rng], axis=1).astype(f32)
    y_p = np.stack([r[b]["y_p"] for b in P4]).astype(f32)
    y_s = np.stack([r[c]["y_s"] for c in S8]).astype(f32)
    unch = lambda a: a.transpose(0, 3, 2, 1).reshape(a.shape[0], a.shape[3], a.shape[2] * 128)
    rows = lambda a: a.reshape(a.shape[0], a.shape[1], 2, 2, 128)
    sst = lambda a: a.reshape(2, 64, 64, 128)
    return (y_p, y_s,
            st("o_pkc", P4, rows), st("o_pks", P4, rows), st("o_pkw", P4, rows),
            st("o_psc", P4, unch), st("o_pst", P4, sst), st("o_pfc", P4, unch),
            st("o_skc", S8, rows), st("o_sks", S8, rows), st("o_skw", S8, rows), st("o_sv", S8),
            st("o_ssc", S8, unch), st("o_sst", S8, sst), st("o_sfc", S8, unch))
```

```python
import numpy as np
import ml_dtypes
import concourse.bass as bass
import concourse.mybir as mybir
from concourse.bass_utils import run_bass_kernel_spmd
from contextlib import ExitStack

F32 = mybir.dt.float32
BF16 = mybir.dt.bfloat16
I32 = mybir.dt.int32
ALU = mybir.AluOpType
AF = mybir.ActivationFunctionType
AX = mybir.AxisListType

ENGS = ("pe", "act", "dve", "pool", "sp")
SAME_SYNC = {"pe": False, "act": True, "dve": True, "pool": True, "sp": False}
EPOCH = 12000
N_DMA_SEMS = 48

D = 2048
L = 2048
SS = 4
DFF = 5632
NCH = 44
NXC = 48
DEPTH = 4
EPS = 1e-6


class Tok:
    __slots__ = ("name", "lw", "rd")

    def __init__(self, name=""):
        self.name = name
        self.lw = None
        self.rd = []


class Op:
    __slots__ = ("eng", "fn", "waits", "sig", "idx", "dma", "sem", "val")


class Prog:
    def __init__(self, nc, es):
        self.nc = nc
        self.es = es
        self.ops = {e: [] for e in ENGS}
        self.nsig = {e: 0 for e in ENGS}
        self.esems = {e: [] for e in ENGS}
        self.dsems = [es.enter_context(nc.semaphore("dma%d" % i)) for i in range(N_DMA_SEMS)]
        self.dval = [0] * N_DMA_SEMS
        self.dnext = 0
        self.waited = {e: {} for e in ENGS}

    def _esem(self, eng, epoch):
        l = self.esems[eng]
        while len(l) <= epoch:
            l.append(self.es.enter_context(self.nc.semaphore("e_%s_%d" % (eng, len(l)))))
        return l[epoch]

    def op(self, eng, fn, R=(), W=(), sig=True, dma=False):
        o = Op()
        o.eng = eng
        o.fn = fn
        o.dma = dma
        o.sig = sig and not dma
        o.idx = len(self.ops[eng])
        deps = []
        for t in R:
            if t.lw is not None:
                deps.append(t.lw)
        for t in W:
            deps.extend(t.rd)
            if t.lw is not None:
                deps.append(t.lw)
        waits = []
        wd = self.waited[eng]
        for d in deps:
            if d.dma:
                s, v = d.sem, d.val
            else:
                if d.eng == eng and not SAME_SYNC[eng]:
                    continue
                if not d.sig and d.eng == eng:
                    continue
                if not d.sig:
                    lst = self.ops[d.eng]
                    j = d.idx
                    while j < len(lst) and not lst[j].sig:
                        j += 1
                    assert j < len(lst), "dependency on unsignaled op with no later signal"
                    d = lst[j]
                s, v = d.sem, d.val
            if wd.get(s, 0) >= v:
                continue
            wd[s] = v
            waits.append((s, v))
        if dma:
            k = self.dnext
            self.dnext = (self.dnext + 1) % N_DMA_SEMS
            prev = self.dval[k]
            s = self.dsems[k]
            if prev > 0 and wd.get(s, 0) < prev:
                wd[s] = prev
                waits.append((s, prev))
            self.dval[k] = prev + 16
            o.sem = s
            o.val = prev + 16
        elif o.sig:
            n = self.nsig[eng]
            self.nsig[eng] = n + 1
            o.sem = self._esem(eng, n // EPOCH)
            o.val = n % EPOCH + 1
        o.waits = waits
        self.ops[eng].append(o)
        for t in R:
            t.rd.append(o)
        for t in W:
            t.lw = o
            t.rd = []
        return o

    def inherit(self, new_toks, old_toks):
        ops = []
        for t in old_toks:
            ops.extend(t.rd)
            if t.lw is not None:
                ops.append(t.lw)
        for t in new_toks:
            t.rd = list(ops)
            t.lw = None

    def claim(self, region, new_toks):
        if not hasattr(self, "regions"):
            self.regions = {}
        old = self.regions.get(region, [])
        if old is not new_toks:
            self.inherit(new_toks, old)
        self.regions[region] = new_toks

    def emit(self):
        nc = self.nc
        finals = [(self.dsems[k], self.dval[k]) for k in range(N_DMA_SEMS) if self.dval[k] > 0]

        def run(e, name):
            for o in self.ops[name]:
                for (s, v) in o.waits:
                    e.wait_ge(s, v)
                ins = o.fn(e)
                if o.dma:
                    ins.then_inc(o.sem, 16)
                elif o.sig:
                    ins.then_inc(o.sem, 1)

        with nc.Block() as blk:
            @blk.tensor
            def _(e):
                run(e, "pe")

            @blk.scalar
            def _(e):
                run(e, "act")

            @blk.vector
            def _(e):
                run(e, "dve")

            @blk.gpsimd
            def _(e):
                run(e, "pool")

            @blk.sync
            def _(e):
                run(e, "sp")
                for (s, v) in finals:
                    e.wait_ge(s, v)


class K:
    pass


class Region:
    def __init__(self, P, name, t, words):
        self.P, self.name, self.t, self.words = P, name, t, words
        self.off = 0
        self.toks = []

    def reset(self):
        self.off = 0
        self.toks = []

    def alloc(self, shape, dtype, name=""):
        n = 1
        for d in shape:
            n *= d
        w = n if dtype == F32 or dtype == I32 else (n + 1) // 2
        self.off = (self.off + 7) // 8 * 8
        assert self.off + w <= self.words, "region %s overflow (%d + %d > %d)" % (self.name, self.off, w, self.words)
        ap = self.t[:, self.off:self.off + w]
        self.off += w
        if dtype == BF16:
            ap = ap.bitcast(BF16)[:, 0:n]
        elif dtype == I32:
            ap = ap.bitcast(I32)
        if len(shape) == 2:
            ap = ap.rearrange("p (a b) -> p a b", a=shape[0])
        elif len(shape) == 3:
            ap = ap.rearrange("p (a b c) -> p a b c", a=shape[0], b=shape[1])
        t = Tok(name)
        self.toks.append(t)
        return ap, t

    def commit(self):
        self.P.claim(self.name, self.toks)


def build(cfg):
    nc = bass.Bass("TRN2", target_bir_lowering=False)
    k = K()
    k.nc = nc
    k.cfg = cfg
    din = lambda n, s, d=F32: nc.dram_tensor(n, list(s), d, kind="ExternalInput").ap()
    dout = lambda n, s, d=F32: nc.dram_tensor(n, list(s), d, kind="ExternalOutput").ap()
    dscr = lambda n, s, d=F32: nc.dram_tensor(n, list(s), d, kind=("ExternalOutput" if cfg.get("debug") else "Internal")).ap()
    k.x_p = din("x_p", [L, D])
    k.x_s = din("x_s", [SS, D])
    k.norm_mix = din("norm_mix", [DEPTH, D])
    k.norm_ffn = din("norm_ffn", [DEPTH, D])
    k.ffn_w_up = din("ffn_w_up", [DEPTH, D, 2 * DFF])
    k.ffn_w_down = din("ffn_w_down", [DEPTH, DFF, D])
    k.ffn_cw = din("ffn_cw", [DEPTH, 128, 88, 3])
    k.ffn_cb = din("ffn_cb", [DEPTH, 128, 88])
    k.ffn_st = din("ffn_st", [DEPTH, 128, 88, 2])
    k.ident = din("ident", [128, 128], BF16)
    k.cst4 = din("cst4", [4, 128, 128], F32)
    k.w_in_odd = din("w_in_odd", [2, D, 10304])
    k.w_out_odd = din("w_out_odd", [2, 4096, D])
    k.ssm_cw = din("ssm_cw", [2, 128, NXC, 4])
    k.ssm_cb = din("ssm_cb", [2, 128, NXC])
    k.ssm_cst = din("ssm_cst", [2, 128, NXC, 3])
    k.ssm_st = din("ssm_st", [2, 4096, 128])
    k.ssm_dt_bias = din("ssm_dt_bias", [2, 64])
    k.ssm_a_log = din("ssm_a_log", [2, 64])
    k.ssm_d = din("ssm_d", [2, 64])
    k.ssm_norm = din("ssm_norm", [2, 4096])
    k.o_psc = dout("o_psc", [2, 128, NXC, 3])
    k.o_ssc = dout("o_ssc", [2, 128, NXC, 3])
    k.o_pst = dout("o_pst", [2, 4096, 128])
    k.o_sst = dout("o_sst", [2, 4096, 128])
    k.xs_scr = dscr("xs_scr", [2052, 4096], BF16)
    k.bt_scr = dscr("bt_scr", [2052, 1024], BF16)
    k.bT_scr = dscr("bT_scr", [8, 128, 2052], BF16)
    k.cT_scr = dscr("cT_scr", [8, 128, 2052], BF16)
    k.zs_scr = dscr("zs_scr", [2052, 4096], BF16)
    k.yn_scr = dscr("yn_scr", [16, 128, 32, 128], BF16)
    k.w_in_even = din("w_in_even", [2, D, 4632])
    k.w_out_even = din("w_out_even", [2, 2048, D])
    k.gmlp_v_norm = din("gmlp_v_norm", [2, 1024])
    k.gmlp_wsT = din("gmlp_wsT", [2, 128, 8, 128])
    k.gmlp_bs = din("gmlp_bs", [2, 8, 128])
    k.q_norm = din("q_norm", [2, 128])
    k.k_norm = din("k_norm", [2, 3, 128])
    k.cmp_pool = din("cmp_pool", [2, 2, 64])
    k.rope_cos = din("rope_cos", [128, 17, 64])
    k.rope_sin = din("rope_sin", [128, 17, 64])
    k.c_ecmp = din("c_ecmp", [128, 16, 32], BF16)
    k.c_ebig = din("c_ebig", [64, 2048], BF16)
    k.c_negb = din("c_negb", [2, 128, 128], BF16)
    k.c_valid = din("c_valid", [128, 16, 32])
    k.c_keep = din("c_keep", [128, 16, 32])
    k.c_add = din("c_add", [128, 16, 32])
    k.c_validT = din("c_validT", [32, 16, 128], BF16)
    k.cache_cmp = din("cache_cmp", [2, 1280 * 128, 512])
    k.cache_sel = din("cache_sel", [2, 1280 * 128, 512])
    k.cache_win = din("cache_win", [2, 512, 512])
    k.ptab = din("ptab", [1, 128], I32)
    k.c_iota = din("c_iota", [128, 1])
    k.c_sum16 = din("c_sum16", [16, 4])
    k.c_sum16T = din("c_sum16T", [4, 16])
    k.c_negnew = din("c_negnew", [16, 4])
    k.c_negwin = din("c_negwin", [16, 512])
    k.o_pkc = dout("o_pkc", [2, 2048, 512])
    k.o_pks = dout("o_pks", [2, 2048, 512])
    k.o_pkw = dout("o_pkw", [2, 512, 512])
    k.o_skc = dout("o_skc", [2, SS, 512])
    k.o_sks = dout("o_sks", [2, SS, 512])
    k.o_skw = dout("o_skw", [2, SS, 512])
    k.o_sv = dout("o_sv", [2, SS, 1024])
    k.uT_scr = dscr("uT_scr", [8, 128, 2052], BF16)
    k.qT_scr = dscr("qT_scr", [8, 128, 2052], BF16)
    k.qrT_scr = dscr("qrT_scr", [8, 128, 2052], BF16)
    k.ksT_scr = dscr("ksT_scr", [2, 128, 2052], BF16)
    k.kwT_scr = dscr("kwT_scr", [2, 128, 2052], BF16)
    k.vs_scr = dscr("vs_scr", [2052, 2, 130], BF16)
    k.vw_scr = dscr("vw_scr", [2052, 2, 130], BF16)
    k.mo_scr = dscr("mo_scr", [16, 128, 16, 128], BF16)
    k.y_p = dout("y_p", [L, D])
    k.y_s = dout("y_s", [SS, D])
    k.o_pfc = dout("o_pfc", [DEPTH, 128, 88, 2])
    k.o_sfc = dout("o_sfc", [DEPTH, 128, 88, 2])
    k.act_scr = dscr("act_scr", [16, 128, NCH, 128], BF16)

    es = ExitStack()
    with es:
        P = Prog(nc, es)
        k.P = P
        sb = lambda n, s, d: es.enter_context(nc.sbuf_tensor(n, list(s), d))
        k.RX = sb("RX", [128, 16 * 2052 // 2], F32)
        k.RY = sb("RY", [128, 22528], F32)
        k.hs = sb("hs", [SS, D], F32)
        k.identb = sb("identb", [128, 128], BF16)
        k.gbc = k.RY[:, 0:2048]
        k.hbuf = [k.RY[:, 2048:4096], k.RY[:, 4096:6144]]
        k.xnb = [k.RY[:, 6144:7168].bitcast(BF16), k.RY[:, 7168:8192].bitcast(BF16)]
        k.junk = k.RY[:, 8192:9216].bitcast(BF16)
        k.ss = [sb("ss%d" % i, [128, 1], F32) for i in range(2)]
        k.RZt = sb("RZ", [128, 10240], F32)
        k.rX = Region(P, "RX", k.RX, 16 * 2052 // 2)
        k.rY = Region(P, "RY", k.RY, 22528)
        k.rZ = Region(P, "RZ", k.RZt, 10240)
        k.identf = sb("identf", [128, 128], F32)
        k.utri = sb("utri", [128, 128], F32)
        k.negtriT = sb("negtriT", [128, 128], F32)
        k.onesf = sb("onesf", [128, 128], F32)
        k.ecmp = sb("ecmp", [128, 16, 32], BF16)
        k.t_ecmp = Tok("ecmp")
        k.pb = [es.enter_context(nc.psum_tensor("pb%d" % i, [128, 512], F32)) for i in range(8)]
        k.tpb = [Tok("pb%d" % i) for i in range(8)]
        k.t_hp = [Tok("hp%d" % i) for i in range(16)]
        k.t_hs = Tok("hs")
        k.t_xnT = Tok("xnT")
        k.t_ident = Tok("ident")
        k.xnT = k.RX[:, :].bitcast(BF16).rearrange("p (a b) -> p a b", a=16)

        P.op("sp", lambda e: e.dma_start(out=k.identb[:], in_=k.ident), W=[k.t_ident], dma=True)
        k.t_cst = Tok("cst4")
        P.op("sp", lambda e: e.dma_start(out=k.ecmp[:], in_=k.c_ecmp), W=[k.t_ecmp], dma=True)
        for ii, tns in enumerate([k.identf, k.utri, k.negtriT, k.onesf]):
            P.op("sp", lambda e, ii=ii, tns=tns: e.dma_start(out=tns[:], in_=k.cst4[ii]), W=[k.t_cst], dma=True)
        for tt in range(16):
            P.op("sp", lambda e, tt=tt: e.dma_start(out=k.y_p[tt * 128:(tt + 1) * 128, :], in_=k.x_p[tt * 128:(tt + 1) * 128, :]),
                 W=[k.t_hp[tt]], dma=True)
        P.op("sp", lambda e: e.dma_start(out=k.hs[:], in_=k.x_s), W=[k.t_hs], dma=True)

        for layer in range(cfg.get("layers", DEPTH)):
            if layer % 2 == 1 and cfg.get("odd", True):
                odd_phase(k, layer)
            if layer % 2 == 0 and cfg.get("even", True):
                even_phase(k, layer)
            if cfg.get("ffn", True):
                ffn_phase(k, layer)

        P.op("sp", lambda e: e.dma_start(out=k.y_s, in_=k.hs[:]), R=[k.t_hs], dma=True)
        P.emit()
    return nc


def norm_phase(k, gain_ap):
    P = k.P
    t_g = Tok("gbc")
    t_h = [Tok("hbuf0"), Tok("hbuf1")]
    t_xnb = [Tok("xnb0"), Tok("xnb1")]
    t_ss = [Tok("ss0"), Tok("ss1")]
    t_junk = Tok("junk")
    P.claim("RX", [k.t_xnT])
    P.claim("RY", [t_g, t_junk] + t_h + t_xnb)
    P.op("sp", lambda e: e.dma_start(out=k.gbc[:, :], in_=gain_ap.to_broadcast([128, D])), W=[t_g], dma=True)
    tp = k.tpb
    inv = float(D ** -0.5)
    for tt in range(17):
        b = tt % 2
        n = 128 if tt < 16 else SS
        if tt < 16:
            src, tsrc = k.hbuf[b], t_h[b]
            P.op("sp", lambda e, tt=tt, b=b: e.dma_start(out=k.hbuf[b][:], in_=k.y_p[tt * 128:(tt + 1) * 128, :]),
                 R=[k.t_hp[tt]], W=[t_h[b]], dma=True)
        else:
            src, tsrc = k.hs, k.t_hs
        ssb, xnb = k.ss[b], k.xnb[b]
        P.op("pool", lambda e, ssb=ssb: e.memset(ssb[:], 0.0), W=[t_ss[b]])
        P.op("act", lambda e, src=src, n=n, ssb=ssb: e.activation(k.junk[0:n, :], src[0:n, :], AF.Square, scale=inv, accum_out=ssb[0:n, :]),
             R=[tsrc], W=[t_junk, t_ss[b]])
        P.op("act", lambda e, n=n, ssb=ssb: e.activation(ssb[0:n, :], ssb[0:n, :], AF.Sqrt, bias=EPS), R=[t_ss[b]], W=[t_ss[b]])
        P.op("dve", lambda e, n=n, ssb=ssb: e.reciprocal(ssb[0:n, :], ssb[0:n, :]), R=[t_ss[b]], W=[t_ss[b]])
        P.op("dve", lambda e, n=n, src=src, ssb=ssb, xnb=xnb: e.scalar_tensor_tensor(xnb[0:n, :], src[0:n, :], ssb[0:n, 0:1], k.gbc[0:n, :], ALU.mult, ALU.mult),
             R=[tsrc, t_ss[b], t_g], W=[t_xnb[b]])
        if tt < 16:
            for q in range(4):
                bank = 6 + (q % 2)
                pt = k.pb[bank][:, 0:256].bitcast(BF16).rearrange("p (a b) -> p a b", a=4)
                for j in range(4):
                    kc = q * 4 + j
                    P.op("pe", lambda e, pt=pt, j=j, kc=kc, xnb=xnb: e.transpose(pt[:, j, :], xnb[:, kc * 128:(kc + 1) * 128], k.identb[:]),
                         R=[t_xnb[b], k.t_ident], W=[tp[bank]], sig=(j == 3))
                eng = "act" if q % 2 == 0 else "dve"
                if eng == "act":
                    P.op("act", lambda e, pt=pt, q=q, tt=tt: e.copy(k.xnT[:, q * 4:(q + 1) * 4, tt * 128:(tt + 1) * 128], pt),
                         R=[tp[bank]], W=[k.t_xnT])
                else:
                    P.op("dve", lambda e, pt=pt, q=q, tt=tt: e.tensor_copy(k.xnT[:, q * 4:(q + 1) * 4, tt * 128:(tt + 1) * 128], pt),
                         R=[tp[bank]], W=[k.t_xnT])
        else:
            bank = 6
            pt = k.pb[bank][:, 0:32].bitcast(BF16).rearrange("p (a b) -> p a b", a=16)
            for kc in range(16):
                P.op("pe", lambda e, pt=pt, kc=kc, xnb=xnb: e.transpose(pt[:, kc, :], xnb[0:SS, kc * 128:(kc + 1) * 128], k.identb[0:SS, 0:SS]),
                     R=[t_xnb[b], k.t_ident], W=[tp[bank]], sig=(kc == 15))
            P.op("act", lambda e, pt=pt: e.copy(k.xnT[:, :, 2048:2052], pt), R=[tp[bank]], W=[k.t_xnT])


def ffn_phase(k, layer):
    P = k.P
    nc = k.nc
    norm_phase(k, k.norm_ffn[layer:layer + 1, :])
    rZ = k.rZ
    rZ.reset()
    k.cw, t_cw = rZ.alloc([88, 3], F32, "cw")
    k.cbias, t_cb = rZ.alloc([88], F32, "cb")
    k.cst, t_cst = rZ.alloc([88, 2], F32, "cst")
    k.stgp, t_stgp = rZ.alloc([88, 2], F32, "stgp")
    k.stgs, t_stgs = rZ.alloc([88, 2], F32, "stgs")
    k.hub = [[None, None], [None, None]]
    t_hub = [[None, None], [None, None]]
    for a in range(2):
        for b in range(2):
            k.hub[a][b], t_hub[a][b] = rZ.alloc([516], F32, "hub")
    k.cv = [None, None]
    t_cv = [None, None]
    for a in range(2):
        k.cv[a], t_cv[a] = rZ.alloc([512], F32, "cv")
    k.sg, t_sg = rZ.alloc([512], F32, "sg")
    k.aT = [None, None]
    t_aT = [None, None]
    for a in range(2):
        k.aT[a], t_aT[a] = rZ.alloc([512], BF16, "aT")
    k.actS, t_actS = rZ.alloc([NCH, SS], BF16, "actS")
    rZ.commit()
    P.op("sp", lambda e: e.dma_start(out=k.cw[:], in_=k.ffn_cw[layer]), W=[t_cw], dma=True)
    P.op("sp", lambda e: e.dma_start(out=k.cbias[:], in_=k.ffn_cb[layer]), W=[t_cb], dma=True)
    P.op("sp", lambda e: e.dma_start(out=k.cst[:], in_=k.ffn_st[layer]), W=[t_cst], dma=True)
    wv = k.RY[:, :].bitcast(BF16).rearrange("p (s a b) -> p s a b", s=4, a=16)
    t_w = [Tok("wslot%d" % i) for i in range(4)]
    P.claim("RY", t_w)
    t_scr = [Tok("scr%d" % i) for i in range(16)]
    wup = k.ffn_w_up[layer].rearrange("(kc p) n -> p kc n", p=128)
    cnt = 0
    def load_w(cb):
        sg_, su_ = (2 * cb) % 4, (2 * cb + 1) % 4
        P.op("pool", lambda e, cb=cb, s=sg_: e.dma_start(out=wv[:, s, :, 0:512], in_=wup[:, :, cb * 512:(cb + 1) * 512]),
             W=[t_w[sg_]], dma=True)
        P.op("pool", lambda e, cb=cb, s=su_: e.dma_start(out=wv[:, s, :, 0:512], in_=wup[:, :, DFF + cb * 512:DFF + (cb + 1) * 512]),
             W=[t_w[su_]], dma=True)
    load_w(0)
    for cb in range(11):
        sg_, su_ = (2 * cb) % 4, (2 * cb + 1) % 4
        if cb + 1 < 11:
            load_w(cb + 1)
        for j in range(4):
            c = cb * 4 + j
            for tb in range(5):
                N = 512 if tb < 4 else SS
                c0 = tb * 512
                hb = cnt % 2
                cnt += 1
                for gu in range(2):
                    slot = sg_ if gu == 0 else su_
                    bank = (cnt % 2) * 2 + gu
                    ci = c + 44 * gu
                    for kc in range(16):
                        P.op("pe", lambda e, bank=bank, slot=slot, j=j, kc=kc, c0=c0, N=N: e.matmul(
                            k.pb[bank][:, 0:N], wv[:, slot, kc, j * 128:(j + 1) * 128], k.xnT[:, kc, c0:c0 + N],
                            start=(kc == 0), stop=(kc == 15)),
                            R=[t_w[slot], k.t_xnT], W=[k.tpb[bank]], sig=(kc == 15))
                    hub = k.hub[gu][hb]
                    th = t_hub[gu][hb]
                    if tb == 0:
                        P.op("pool", lambda e, hub=hub: e.memset(hub[:, 0:2], 0.0), W=[th])
                    elif tb < 4:
                        prev = k.hub[gu][1 - hb]
                        P.op("pool", lambda e, hub=hub, prev=prev: e.tensor_copy(hub[:, 0:2], prev[:, 512:514]),
                             R=[t_hub[gu][1 - hb]], W=[th])
                    else:
                        P.op("pool", lambda e, hub=hub, ci=ci: e.tensor_copy(hub[:, 0:2], k.cst[:, ci, :]), R=[t_cst], W=[th])
                    P.op("act", lambda e, hub=hub, bank=bank, N=N: e.copy(hub[:, 2:2 + N], k.pb[bank][:, 0:N]),
                         R=[k.tpb[bank]], W=[th])
                    cv = k.cv[gu]
                    P.op("dve", lambda e, cv=cv, hub=hub, ci=ci, N=N: e.tensor_scalar(
                        cv[:, 0:N], hub[:, 0:N], k.cw[:, ci, 0:1], k.cbias[:, ci:ci + 1], ALU.mult, ALU.add),
                        R=[th, t_cw, t_cb], W=[t_cv[gu]])
                    P.op("dve", lambda e, cv=cv, hub=hub, ci=ci, N=N: e.scalar_tensor_tensor(
                        cv[:, 0:N], hub[:, 1:1 + N], k.cw[:, ci, 1:2], cv[:, 0:N], ALU.mult, ALU.add),
                        R=[th, t_cw, t_cv[gu]], W=[t_cv[gu]])
                    P.op("dve", lambda e, cv=cv, hub=hub, ci=ci, N=N: e.scalar_tensor_tensor(
                        cv[:, 0:N], hub[:, 2:2 + N], k.cw[:, ci, 2:3], cv[:, 0:N], ALU.mult, ALU.add),
                        R=[th, t_cw, t_cv[gu]], W=[t_cv[gu]])
                    if tb == 3:
                        P.op("pool", lambda e, hub=hub, ci=ci: e.tensor_copy(k.stgp[:, ci, :], hub[:, 512:514]), R=[th], W=[t_stgp])
                    if tb == 4:
                        P.op("pool", lambda e, hub=hub, ci=ci: e.tensor_copy(k.stgs[:, ci, :], hub[:, 4:6]), R=[th], W=[t_stgs])
                P.op("act", lambda e, N=N: e.activation(k.sg[:, 0:N], k.cv[0][:, 0:N], AF.Silu), R=[t_cv[0]], W=[t_sg])
                if tb < 4:
                    ab = (c * 4 + tb) % 2
                    aT = k.aT[ab]
                    P.op("dve", lambda e, aT=aT: e.tensor_tensor(aT[:, :], k.sg[:, :], k.cv[1][:, :], ALU.mult),
                         R=[t_sg, t_cv[1]], W=[t_aT[ab]])
                    dst = k.act_scr[tb * 4:(tb + 1) * 4, :, c, :].rearrange("t p k -> p t k")
                    P.op("sp", lambda e, aT=aT, dst=dst: e.dma_start(out=dst, in_=aT[:, :].rearrange("p (t k) -> p t k", t=4)),
                         R=[t_aT[ab]], W=t_scr[tb * 4:(tb + 1) * 4], dma=True)
                else:
                    P.op("dve", lambda e, c=c: e.tensor_tensor(k.actS[:, c, :], k.sg[:, 0:SS], k.cv[1][:, 0:SS], ALU.mult),
                         R=[t_sg, t_cv[1]], W=[t_actS])
    P.op("sp", lambda e: e.dma_start(out=k.o_pfc[layer], in_=k.stgp[:]), R=[t_stgp], dma=True)
    P.op("sp", lambda e: e.dma_start(out=k.o_sfc[layer], in_=k.stgs[:]), R=[t_stgs], dma=True)
    out_proj(k, k.ffn_w_down[layer], NCH, k.act_scr, t_scr, k.actS, t_actS)


def out_proj(k, w_dram, nch, scr, t_scr, actS, t_actS, hcol=None, t_hc=None):
    P = k.P
    wd_v = k.RY[:, :].bitcast(BF16).rearrange("p (s a b) -> p s a b", s=2, a=NCH)
    t_wd = [Tok("wd0"), Tok("wd1")]
    P.claim("RY", t_wd)
    at_v = k.RX[:, 0:3 * NCH * 64].bitcast(BF16).rearrange("p (s a b) -> p s a b", s=3, a=NCH)
    t_at = [Tok("at%d" % i) for i in range(3)]
    hcol = [k.RX[:, 8704 + j * 1024:8704 + (j + 1) * 1024] for j in range(3)]
    t_hc = [Tok("hcol%d" % j) for j in range(3)]
    P.claim("RX", t_at + t_hc)
    wdn = w_dram.rearrange("(c p) n -> p c n", p=128)
    it = 0
    for dp in range(2):
        for ws in range(2):
            db = dp * 2 + ws
            P.op("pool", lambda e, db=db, ws=ws: e.dma_start(out=wd_v[:, ws, 0:nch, :], in_=wdn[:, :, db * 512:(db + 1) * 512]),
                 W=[t_wd[ws]], dma=True)
        c0 = dp * 1024
        for tt in range(17):
            b0 = 4 + 2 * (it % 2)
            it += 1
            if tt < 16:
                sl = it % 3
                P.op("sp", lambda e, tt=tt, sl=sl: e.dma_start(out=at_v[:, sl, 0:nch, :], in_=scr[tt]),
                     R=[t_scr[tt]], W=[t_at[sl]], dma=True)
                hc = hcol[sl]
                P.op("sp", lambda e, tt=tt, hc=hc, c0=c0: e.dma_start(out=hc[:, :], in_=k.y_p[tt * 128:(tt + 1) * 128, c0:c0 + 1024]),
                     R=[k.t_hp[tt]], W=[t_hc[sl]], dma=True)
                for ws in range(2):
                    bank = b0 + ws
                    for c in range(nch):
                        P.op("pe", lambda e, bank=bank, sl=sl, ws=ws, c=c: e.matmul(
                            k.pb[bank][:, :], at_v[:, sl, c, :], wd_v[:, ws, c, :], start=(c == 0), stop=(c == nch - 1)),
                            R=[t_at[sl], t_wd[ws]], W=[k.tpb[bank]], sig=(c == nch - 1))
                for ws in range(2):
                    bank = b0 + ws
                    P.op("dve", lambda e, bank=bank, hc=hc, ws=ws: e.tensor_tensor(hc[:, ws * 512:(ws + 1) * 512], k.pb[bank][:, :], hc[:, ws * 512:(ws + 1) * 512], ALU.add),
                         R=[k.tpb[bank], t_hc[sl]], W=[t_hc[sl]])
                P.op("act", lambda e, tt=tt, hc=hc, c0=c0: e.dma_start(out=k.y_p[tt * 128:(tt + 1) * 128, c0:c0 + 1024], in_=hc[:, :]),
                     R=[t_hc[sl]], W=[k.t_hp[tt]], dma=True)
            else:
                for ws in range(2):
                    bank = b0 + ws
                    db = dp * 2 + ws
                    for c in range(nch):
                        P.op("pe", lambda e, bank=bank, ws=ws, c=c: e.matmul(
                            k.pb[bank][0:SS, :], actS[:, c, :], wd_v[:, ws, c, :], start=(c == 0), stop=(c == nch - 1)),
                            R=[t_actS, t_wd[ws]], W=[k.tpb[bank]], sig=(c == nch - 1))
                    P.op("dve", lambda e, bank=bank, db=db: e.tensor_tensor(
                        k.hs[:, db * 512:(db + 1) * 512], k.pb[bank][0:SS, :], k.hs[:, db * 512:(db + 1) * 512], ALU.add),
                        R=[k.tpb[bank], k.t_hs], W=[k.t_hs])


def odd_phase(k, layer):
    P = k.P
    i = layer // 2
    norm_phase(k, k.norm_mix[layer:layer + 1, :])
    rX, rY, rZ = k.rX, k.rY, k.rZ
    rY.reset()
    rZ.reset()
    wsl, t_w = [], []
    for s_ in range(4):
        a, t = rY.alloc([16, 512], BF16, "wsl")
        wsl.append(a)
        t_w.append(t)
    wdt, t_wdt = rY.alloc([16, 64], BF16, "wdt")
    zt, t_zt = [None, None], [None, None]
    trs, t_trs = [None, None], [None, None]
    sv, t_sv = [None, None], [None, None]
    for a_ in range(2):
        zt[a_], t_zt[a_] = rY.alloc([512], BF16, "zt")
        trs[a_], t_trs[a_] = rY.alloc([4, 128], BF16, "trs")
        sv[a_], t_sv[a_] = rY.alloc([512], BF16, "sv")
    rY.commit()
    cw, t_cw = rZ.alloc([NXC, 4], F32, "cw")
    cb, t_cb = rZ.alloc([NXC], F32, "cb")
    cst, t_cst = rZ.alloc([NXC, 3], F32, "cst")
    stgp, t_stgp = rZ.alloc([NXC, 3], F32, "stgp")
    stgs, t_stgs = rZ.alloc([NXC, 3], F32, "stgs")
    hub, t_hub = [None, None], [None, None]
    for a_ in range(2):
        hub[a_], t_hub[a_] = rZ.alloc([520], F32, "hub")
    cv, t_cv = rZ.alloc([512], F32, "cv")
    dt_all, t_dt = rZ.alloc([17, 64], F32, "dt_all")
    da_all, t_da = rZ.alloc([17, 64], F32, "da_all")
    dtb, t_dtb = rZ.alloc([64], F32, "dtb")
    abc, t_abc = rZ.alloc([64], F32, "abc")
    dbc, t_dbc = rZ.alloc([64], F32, "dbc")
    tmp64, t_tmp64 = rZ.alloc([64], F32, "tmp64")
    ynS, t_ynS = rZ.alloc([32, SS], BF16, "ynS")
    cs_sb, t_cs = rZ.alloc([64], F32, "cs")
    ncs, t_ncs = rZ.alloc([64], F32, "ncs")
    ecs, t_ecs = rZ.alloc([64], F32, "ecs")
    dec, t_dec = rZ.alloc([64], F32, "dec")
    etot, t_etot = rZ.alloc([64], F32, "etot")
    ssq, t_ssq = rZ.alloc([2], F32, "ssq")
    hst, t_hst = [None, None], [None, None]
    for a_ in range(2):
        hst[a_], t_hst[a_] = rZ.alloc([128], F32, "hst")
    rZ.commit()

    P.op("sp", lambda e: e.dma_start(out=cw, in_=k.ssm_cw[i]), W=[t_cw], dma=True)
    P.op("sp", lambda e: e.dma_start(out=cb, in_=k.ssm_cb[i]), W=[t_cb], dma=True)
    P.op("sp", lambda e: e.dma_start(out=cst, in_=k.ssm_cst[i]), W=[t_cst], dma=True)
    P.op("sp", lambda e: e.dma_start(out=dtb, in_=k.ssm_dt_bias[i:i + 1, :].to_broadcast([128, 64])), W=[t_dtb], dma=True)
    P.op("sp", lambda e: e.dma_start(out=abc, in_=k.ssm_a_log[i:i + 1, :].to_broadcast([128, 64])), W=[t_abc], dma=True)
    P.op("sp", lambda e: e.dma_start(out=dbc, in_=k.ssm_d[i:i + 1, :].to_broadcast([128, 64])), W=[t_dbc], dma=True)
    P.op("act", lambda e: e.activation(abc, abc, AF.Exp), R=[t_abc], W=[t_abc])
    P.op("act", lambda e: e.mul(abc, abc, -1.0), R=[t_abc], W=[t_abc])

    win = k.w_in_odd[i].rearrange("(kc p) n -> p kc n", p=128)
    blocks = [4096 + b * 512 for b in range(12)] + [b * 512 for b in range(8)]
    t_xs_scr = [Tok("xs_scr%d" % t) for t in range(17)]
    t_bt_scr = [Tok("bt_scr%d" % t) for t in range(17)]
    t_bT_scr = [Tok("bT_scr%d" % t) for t in range(17)]
    t_cT_scr = [Tok("cT_scr%d" % t) for t in range(17)]
    t_zs_scr = [Tok("zs_scr%d" % t) for t in range(17)]

    def load_w(bi):
        sl = bi % 4
        c0 = blocks[bi]
        P.op("pool", lambda e, sl=sl, c0=c0: e.dma_start(out=wsl[sl], in_=win[:, :, c0:c0 + 512]), W=[t_w[sl]], dma=True)

    load_w(0)
    load_w(1)
    P.op("pool", lambda e: e.dma_start(out=wdt, in_=win[:, :, 10240:10304]), W=[t_wdt], dma=True)
    cnt = 0
    pending = []
    for bi in range(20):
        if bi + 2 < 20:
            load_w(bi + 2)
        sl = bi % 4
        if bi < 12:
            for j in range(4):
                c = bi * 4 + j
                for tb in range(5):
                    N = 512 if tb < 4 else SS
                    c0 = tb * 512
                    hb = cnt % 2
                    bank = cnt % 2
                    cnt += 1
                    for kc in range(16):
                        P.op("pe", lambda e, bank=bank, sl=sl, j=j, kc=kc, c0=c0, N=N: e.matmul(
                            k.pb[bank][:, 0:N], wsl[sl][:, kc, j * 128:(j + 1) * 128], k.xnT[:, kc, c0:c0 + N],
                            start=(kc == 0), stop=(kc == 15)),
                            R=[t_w[sl], k.t_xnT], W=[k.tpb[bank]], sig=(kc == 15))
                    while pending:
                        pending.pop(0)()
                    hu, th = hub[hb], t_hub[hb]
                    if tb == 0:
                        P.op("pool", lambda e, hu=hu: e.memset(hu[:, 0:3], 0.0), W=[th])
                    elif tb < 4:
                        prev = hub[1 - hb]
                        P.op("pool", lambda e, hu=hu, prev=prev: e.tensor_copy(hu[:, 0:3], prev[:, 512:515]), R=[t_hub[1 - hb]], W=[th])
                    else:
                        P.op("pool", lambda e, hu=hu, c=c: e.tensor_copy(hu[:, 0:3], cst[:, c, :]), R=[t_cst], W=[th])
                    P.op("act", lambda e, hu=hu, bank=bank, N=N: e.copy(hu[:, 3:3 + N], k.pb[bank][:, 0:N]), R=[k.tpb[bank]], W=[th])
                    P.op("dve", lambda e, hu=hu, c=c, N=N: e.tensor_scalar(cv[:, 0:N], hu[:, 0:N], cw[:, c, 0:1], cb[:, c:c + 1], ALU.mult, ALU.add),
                         R=[th, t_cw, t_cb], W=[t_cv])
                    for tap in range(1, 4):
                        P.op("dve", lambda e, hu=hu, c=c, N=N, tap=tap: e.scalar_tensor_tensor(
                            cv[:, 0:N], hu[:, tap:tap + N], cw[:, c, tap:tap + 1], cv[:, 0:N], ALU.mult, ALU.add),
                            R=[th, t_cw, t_cv], W=[t_cv])
                    if tb == 3:
                        P.op("pool", lambda e, hu=hu, c=c: e.tensor_copy(stgp[:, c, :], hu[:, 512:515]), R=[th], W=[t_stgp])
                    if tb == 4:
                        P.op("pool", lambda e, hu=hu, c=c: e.tensor_copy(stgs[:, c, :], hu[:, 4:7]), R=[th], W=[t_stgs])
                    sb_ = cnt % 2
                    svv, tsv = sv[sb_], t_sv[sb_]
                    P.op("act", lambda e, svv=svv, N=N: e.activation(svv[:, 0:N], cv[:, 0:N], AF.Silu), R=[t_cv], W=[tsv])
                    def post(c=c, tb=tb, c0=c0, N=N, svv=svv, tsv=tsv, sb_=sb_, cnt_=cnt):
                        cnt = cnt_
                        tts = list(range(tb * 4, tb * 4 + 4)) if tb < 4 else [16]
                        if c >= 32:
                            g = (c - 32) % 8
                            scr = k.bT_scr if c < 40 else k.cT_scr
                            tsc = t_bT_scr if c < 40 else t_cT_scr
                            P.op("sp", lambda e, scr=scr, g=g, svv=svv, c0=c0, N=N: e.dma_start(out=scr[g, :, c0:c0 + N], in_=svv[:, 0:N]),
                                 R=[tsv], W=[tsc[t] for t in tts], dma=True)
                        if c < 40:
                            scr = k.xs_scr if c < 32 else k.bt_scr
                            tsc = t_xs_scr if c < 32 else t_bt_scr
                            col0 = c * 128 if c < 32 else (c - 32) * 128
                            trv, ttr = trs[sb_], t_trs[sb_]
                            bank2 = 6 + (cnt % 2)
                            pt = k.pb[bank2][:, 0:256].bitcast(BF16).rearrange("p (a b) -> p a b", a=4)
                            if tb < 4:
                                for q in range(4):
                                    P.op("pe", lambda e, pt=pt, q=q, svv=svv: e.transpose(pt[:, q, :], svv[:, q * 128:(q + 1) * 128], k.identb[:, :]),
                                         R=[tsv, k.t_ident], W=[k.tpb[bank2]], sig=(q == 3))
                                P.op("act", lambda e, pt=pt, trv=trv: e.copy(trv, pt), R=[k.tpb[bank2]], W=[ttr])
                                dst = scr[c0:c0 + 512, col0:col0 + 128].rearrange("(t p) k -> p t k", p=128)
                                P.op("sp", lambda e, dst=dst, trv=trv: e.dma_start(out=dst, in_=trv), R=[ttr], W=[tsc[t] for t in tts], dma=True)
                            else:
                                P.op("pe", lambda e, pt=pt, svv=svv: e.transpose(pt[0:SS, 0, :], svv[:, 0:SS], k.identb[:, :]),
                                     R=[tsv, k.t_ident], W=[k.tpb[bank2]])
                                P.op("act", lambda e, pt=pt, trv=trv: e.copy(trv[0:SS, 0, :], pt[0:SS, 0, :]), R=[k.tpb[bank2]], W=[ttr])
                                P.op("sp", lambda e, scr=scr, col0=col0, trv=trv: e.dma_start(out=scr[2048:2048 + SS, col0:col0 + 128], in_=trv[0:SS, 0, :]),
                                     R=[ttr], W=[tsc[16]], dma=True)

                    pending.append(post)
        else:
            while pending:
                pending.pop(0)()
            zb = bi - 12
            for tt in range(17):
                T = 128 if tt < 16 else SS
                tok0 = tt * 128
                bank = cnt % 2
                zb_ = cnt % 2
                cnt += 1
                for kc in range(16):
                    P.op("pe", lambda e, bank=bank, sl=sl, kc=kc, tok0=tok0, T=T: e.matmul(
                        k.pb[bank][0:T, :], k.xnT[:, kc, tok0:tok0 + T], wsl[sl][:, kc, :], start=(kc == 0), stop=(kc == 15)),
                        R=[t_w[sl], k.t_xnT], W=[k.tpb[bank]], sig=(kc == 15))
                P.op("act", lambda e, bank=bank, zb_=zb_, T=T: e.activation(zt[zb_][0:T, :], k.pb[bank][0:T, :], AF.Silu),
                     R=[k.tpb[bank]], W=[t_zt[zb_]])
                P.op("sp", lambda e, zb_=zb_, T=T, tok0=tok0, zb=zb: e.dma_start(out=k.zs_scr[tok0:tok0 + T, zb * 512:(zb + 1) * 512], in_=zt[zb_][0:T, :]),
                     R=[t_zt[zb_]], W=[t_zs_scr[tt]], dma=True)
    P.op("sp", lambda e: e.dma_start(out=k.o_psc[i], in_=stgp), R=[t_stgp], dma=True)
    P.op("sp", lambda e: e.dma_start(out=k.o_ssc[i], in_=stgs), R=[t_stgs], dma=True)
    for tt in range(17):
        T = 128 if tt < 16 else SS
        tok0 = tt * 128
        bank = cnt % 2
        cnt += 1
        for kc in range(16):
            P.op("pe", lambda e, bank=bank, kc=kc, tok0=tok0, T=T: e.matmul(
                k.pb[bank][0:T, 0:64], k.xnT[:, kc, tok0:tok0 + T], wdt[:, kc, :], start=(kc == 0), stop=(kc == 15)),
                R=[t_wdt, k.t_xnT], W=[k.tpb[bank]], sig=(kc == 15))
        P.op("dve", lambda e, bank=bank, T=T: e.tensor_tensor(tmp64[0:T, :], k.pb[bank][0:T, 0:64], dtb[0:T, :], ALU.add),
             R=[k.tpb[bank], t_dtb], W=[t_tmp64])
        P.op("act", lambda e, T=T: e.activation(tmp64[0:T, :], tmp64[0:T, :], AF.Exp), R=[t_tmp64], W=[t_tmp64])
        P.op("act", lambda e, T=T, tt=tt: e.activation(dt_all[0:T, tt, :], tmp64[0:T, :], AF.Ln, bias=1.0), R=[t_tmp64], W=[t_dt])
        P.op("dve", lambda e, T=T, tt=tt: e.tensor_tensor(da_all[0:T, tt, :], dt_all[0:T, tt, :], abc[0:T, :], ALU.mult),
             R=[t_dt, t_abc], W=[t_da])

    stop = k.cfg.get("odd_stop", 99)
    if stop <= 1:
        return
    rX.reset()
    rY.reset()
    H, t_H = rX.alloc([4096], F32, "H")
    Hb, t_Hb = rX.alloc([4096], BF16, "Hb")
    xs, t_xs = [None, None], [None, None]
    for a_ in range(2):
        xs[a_], t_xs[a_] = rX.alloc([4096], BF16, "xs")
    xdt, t_xdt = rX.alloc([4096], BF16, "xdt")
    xdd, t_xdd = rX.alloc([4096], BF16, "xdd")
    zs, t_zs = rX.alloc([4096], BF16, "zs")
    rX.commit()
    ngbc, t_ng = rY.alloc([4096], F32, "ngbc")
    Dm2, t_Dm2, Lf2, t_Lf2, cbTs2, t_cbTs2, ynb2, t_ynb2 = [None, None], [None, None], [None, None], [None, None], [None, None], [None, None], [None, None], [None, None]
    for a_ in range(2):
        Dm2[a_], t_Dm2[a_] = rY.alloc([8, 128], F32, "Dm")
        Lf2[a_], t_Lf2[a_] = rY.alloc([8, 128], F32, "Lf")
        cbTs2[a_], t_cbTs2[a_] = rY.alloc([128], F32, "cbTs")
        ynb2[a_], t_ynb2[a_] = rY.alloc([512], BF16, "ynb")
    Mb, t_Mb = [None, None], [None, None]
    BT, t_BT, CT, t_CT, Btm, t_Btm = [None, None], [None, None], [None, None], [None, None], [None, None], [None, None]
    for a_ in range(2):
        Mb[a_], t_Mb[a_] = rY.alloc([8, 128], BF16, "Mb")
        BT[a_], t_BT[a_] = rY.alloc([8, 128], BF16, "BT")
        CT[a_], t_CT[a_] = rY.alloc([8, 128], BF16, "CT")
        Btm[a_], t_Btm[a_] = rY.alloc([1024], BF16, "Btm")
    yv, t_yv = rY.alloc([512], F32, "yv")
    tmpv, t_tmpv = rY.alloc([512], F32, "tmpv")
    junk, t_junk = rY.alloc([512], BF16, "junk")
    ynT, t_ynT = [None, None], [None, None]
    for a_ in range(2):
        ynT[a_], t_ynT[a_] = rY.alloc([4, 128], BF16, "ynT")
    rY.commit()
    P.op("sp", lambda e: e.dma_start(out=ngbc, in_=k.ssm_norm[i:i + 1, :].to_broadcast([128, 4096])), W=[t_ng], dma=True)
    t_yn_scr = [Tok("yn_scr%d" % t) for t in range(16)]
    pb, tpb = k.pb, k.tpb
    PCv = [pb[2][:, :].rearrange("p (a b) -> p a b", a=4), pb[3][:, :].rearrange("p (a b) -> p a b", a=4)]

    def state_out(dst):
        for blk in range(32):
            hs_, th_ = hst[blk % 2], t_hst[blk % 2]
            P.op("pe", lambda e, blk=blk: e.matmul(pb[7][:, 0:128], H[:, blk * 128:(blk + 1) * 128], k.identf[:, :], start=True, stop=True),
                 R=[t_H, k.t_cst], W=[tpb[7]])
            P.op("act", lambda e, hs_=hs_: e.copy(hs_, pb[7][:, 0:128]), R=[tpb[7]], W=[th_])
            P.op("sp", lambda e, blk=blk, hs_=hs_: e.dma_start(out=dst[blk * 128:(blk + 1) * 128, :], in_=hs_), R=[th_], dma=True)

    def state_in(src):
        for blk in range(32):
            hs_, th_ = hst[blk % 2], t_hst[blk % 2]
            P.op("sp", lambda e, blk=blk, hs_=hs_: e.dma_start(out=hs_, in_=src[blk * 128:(blk + 1) * 128, :]), W=[th_], dma=True)
            P.op("pe", lambda e, hs_=hs_: e.matmul(pb[7][:, 0:128], hs_, k.identf[:, :], start=True, stop=True),
                 R=[th_, k.t_cst], W=[tpb[7]])
            P.op("act", lambda e, blk=blk: e.copy(H[:, blk * 128:(blk + 1) * 128], pb[7][:, 0:128]), R=[tpb[7]], W=[t_H])
        P.op("dve", lambda e: e.tensor_copy(Hb, H), R=[t_H], W=[t_Hb])

    def chunk(tt, T):
        tok0 = tt * 128
        b = tt % 2
        x_, tx_ = xs[b], t_xs[b]
        P.op("sp", lambda e: e.dma_start(out=x_[0:T, :], in_=k.xs_scr[tok0:tok0 + T, :]), R=[t_xs_scr[tt]], W=[tx_], dma=True)
        P.op("sp", lambda e: e.dma_start(out=zs[0:T, :], in_=k.zs_scr[tok0:tok0 + T, :]), R=[t_zs_scr[tt]], W=[t_zs], dma=True)
        P.op("sp", lambda e: e.dma_start(out=Btm[b][0:T, :], in_=k.bt_scr[tok0:tok0 + T, :]), R=[t_bt_scr[tt]], W=[t_Btm[b]], dma=True)
        P.op("sp", lambda e: e.dma_start(out=BT[b][:, :, 0:T], in_=k.bT_scr[:, :, tok0:tok0 + T].rearrange("g p t -> p g t")),
             R=[t_bT_scr[tt]], W=[t_BT[b]], dma=True)
        P.op("sp", lambda e: e.dma_start(out=CT[b][:, :, 0:T], in_=k.cT_scr[:, :, tok0:tok0 + T].rearrange("g p t -> p g t")),
             R=[t_cT_scr[tt]], W=[t_CT[b]], dma=True)
        da = da_all[0:T, tt, :]
        P.op("pe", lambda e: e.matmul(pb[0][0:T, 0:64], k.utri[0:T, 0:T], da, start=True, stop=True), R=[t_da, k.t_cst], W=[tpb[0]])
        P.op("act", lambda e: e.copy(cs_sb[0:T, :], pb[0][0:T, 0:64]), R=[tpb[0]], W=[t_cs])
        P.op("act", lambda e: e.mul(ncs[0:T, :], cs_sb[0:T, :], -1.0), R=[t_cs], W=[t_ncs])
        P.op("act", lambda e: e.activation(ecs[0:T, :], cs_sb[0:T, :], AF.Exp), R=[t_cs], W=[t_ecs])
        P.op("pe", lambda e: e.matmul(pb[0][:, 64:128], k.onesf[0:T, :], da, start=True, stop=True), R=[t_da, k.t_cst, t_cs, t_ecs, t_ncs], W=[tpb[0]])
        P.op("dve", lambda e: e.tensor_tensor(dec[0:T, :], pb[0][0:T, 64:128], cs_sb[0:T, :], ALU.subtract), R=[tpb[0], t_cs], W=[t_dec])
        P.op("act", lambda e: e.activation(dec[0:T, :], dec[0:T, :], AF.Exp), R=[t_dec], W=[t_dec])
        P.op("act", lambda e: e.activation(etot, pb[0][:, 64:128], AF.Exp), R=[tpb[0]], W=[t_etot])
        x3 = x_[0:T, :].rearrange("p (h d) -> p h d", h=64)
        P.op("pool", lambda e: e.tensor_tensor(xdt[0:T, :].rearrange("p (h d) -> p h d", h=64), x3,
                                               dt_all[0:T, tt, :].unsqueeze(2).to_broadcast([T, 64, 64]), ALU.mult),
             R=[tx_, t_dt], W=[t_xdt])
        P.op("pool", lambda e: e.tensor_tensor(xdd[0:T, :].rearrange("p (h d) -> p h d", h=64), xdt[0:T, :].rearrange("p (h d) -> p h d", h=64),
                                               dec[0:T, :].unsqueeze(2).to_broadcast([T, 64, 64]), ALU.mult),
             R=[t_xdt, t_dec], W=[t_xdd])
        def gA(g):
            mb = g % 2
            Dm, t_Dm, Lf, t_Lf, cbTs, t_cbTs = Dm2[mb], t_Dm2[mb], Lf2[mb], t_Lf2[mb], cbTs2[mb], t_cbTs2[mb]
            P.op("pe", lambda e: e.matmul(pb[1][0:T, 0:T], BT[b][:, g, 0:T], CT[b][:, g, 0:T], start=True, stop=True),
                 R=[t_BT[b], t_CT[b]], W=[tpb[1]])
            P.op("act", lambda e: e.copy(cbTs[0:T, 0:T], pb[1][0:T, 0:T]), R=[tpb[1]], W=[t_cbTs])
            for e8 in range(8):
                h = g * 8 + e8
                P.op("pe", lambda e, e8=e8, h=h: e.matmul(PCv[e8 // 4][0:T, e8 % 4, 0:T], da_all[0:T, tt, h:h + 1].to_broadcast([T, T]),
                                                         k.utri[0:T, 0:T], start=True, stop=True),
                     R=[t_da, k.t_cst], W=[tpb[2 + e8 // 4]], sig=(e8 % 4 == 3))
            for half in range(2):
                P.op("dve", lambda e, half=half: e.tensor_tensor(Dm[0:T, half * 4:(half + 1) * 4, 0:T], PCv[half][0:T, :, 0:T],
                                                                 k.negtriT[0:T, 0:T].unsqueeze(1).to_broadcast([T, 4, T]), ALU.add),
                     R=[tpb[2 + half], k.t_cst], W=[t_Dm])
            for e8 in range(8):
                h = g * 8 + e8
                P.op("act", lambda e, e8=e8, h=h: e.activation(Lf[0:T, e8, 0:T], Dm[0:T, e8, 0:T], AF.Exp, bias=ncs[0:T, h:h + 1]),
                     R=[t_Dm, t_ncs], W=[t_Lf], sig=(e8 == 7))
            P.op("dve", lambda e, mb=mb: e.tensor_tensor(Mb[mb][0:T, :, 0:T], Lf[0:T, :, 0:T],
                                                         cbTs[0:T, 0:T].unsqueeze(1).to_broadcast([T, 8, T]), ALU.mult),
                 R=[t_Lf, t_cbTs], W=[t_Mb[mb]])

        def gB(g):
            mb = g % 2
            ynb, t_ynb = ynb2[mb], t_ynb2[mb]
            for e8 in range(8):
                h = g * 8 + e8
                P.op("pe", lambda e, e8=e8, h=h, mb=mb: e.matmul(pb[4][0:T, e8 * 64:(e8 + 1) * 64], Mb[mb][0:T, e8, 0:T], xdt[0:T, h * 64:(h + 1) * 64],
                                                             start=True, stop=True),
                     R=[t_Mb[mb], t_xdt], W=[tpb[4]], sig=(e8 == 7))
            P.op("pe", lambda e: e.matmul(pb[5][0:T, :], CT[b][:, g, 0:T], Hb[:, g * 512:(g + 1) * 512], start=True, stop=True),
                 R=[t_CT[b], t_Hb], W=[tpb[5]])
            P.op("pe", lambda e: e.matmul(pb[6][:, :], Btm[b][0:T, g * 128:(g + 1) * 128], xdd[0:T, g * 512:(g + 1) * 512], start=True, stop=True),
                 R=[t_Btm[b], t_xdd], W=[tpb[6]])
            Hg = H[:, g * 512:(g + 1) * 512]
            P.op("pool", lambda e: e.tensor_tensor(Hg.rearrange("p (h d) -> p h d", h=8), Hg.rearrange("p (h d) -> p h d", h=8),
                                                   etot[:, g * 8:(g + 1) * 8].unsqueeze(2).to_broadcast([128, 8, 64]), ALU.mult),
                 R=[t_H, t_etot, tpb[5]], W=[t_H])
            P.op("dve", lambda e: e.tensor_tensor(Hg, pb[6][:, :], Hg, ALU.add), R=[tpb[6], t_H], W=[t_H])
            P.op("act", lambda e: e.copy(Hb[:, g * 512:(g + 1) * 512], Hg), R=[t_H, tpb[5]], W=[t_Hb])

            y3 = yv[0:T, :].rearrange("p (h d) -> p h d", h=8)
            t3 = tmpv[0:T, :].rearrange("p (h d) -> p h d", h=8)
            P.op("dve", lambda e: e.tensor_tensor(t3, pb[5][0:T, :].rearrange("p (h d) -> p h d", h=8),
                                                  ecs[0:T, g * 8:(g + 1) * 8].unsqueeze(2).to_broadcast([T, 8, 64]), ALU.mult),
                 R=[tpb[5], t_ecs], W=[t_tmpv])
            P.op("dve", lambda e: e.tensor_tensor(yv[0:T, :], pb[4][0:T, :], tmpv[0:T, :], ALU.add), R=[tpb[4], t_tmpv], W=[t_yv])
            P.op("pool", lambda e: e.tensor_tensor(t3, x_[0:T, g * 512:(g + 1) * 512].rearrange("p (h d) -> p h d", h=8),
                                                   dbc[0:T, g * 8:(g + 1) * 8].unsqueeze(2).to_broadcast([T, 8, 64]), ALU.mult),
                 R=[tx_, t_dbc, t_yv], W=[t_tmpv])
            P.op("dve", lambda e: e.tensor_tensor(yv[0:T, :], yv[0:T, :], tmpv[0:T, :], ALU.add), R=[t_yv, t_tmpv], W=[t_yv])
            P.op("dve", lambda e: e.tensor_tensor(yv[0:T, :], yv[0:T, :], zs[0:T, g * 512:(g + 1) * 512], ALU.mult), R=[t_yv, t_zs], W=[t_yv])
            P.op("pool", lambda e: e.memset(ssq[0:T, 0:1], 0.0), W=[t_ssq])
            P.op("act", lambda e: e.activation(junk[0:T, :], yv[0:T, :], AF.Square, scale=float(512 ** -0.5), accum_out=ssq[0:T, 0:1]),
                 R=[t_yv], W=[t_junk, t_ssq])
            P.op("act", lambda e: e.activation(ssq[0:T, 0:1], ssq[0:T, 0:1], AF.Sqrt, bias=EPS), R=[t_ssq], W=[t_ssq])
            P.op("dve", lambda e: e.reciprocal(ssq[0:T, 0:1], ssq[0:T, 0:1]), R=[t_ssq], W=[t_ssq])
            P.op("dve", lambda e: e.scalar_tensor_tensor(ynb[0:T, :], yv[0:T, :], ssq[0:T, 0:1], ngbc[0:T, g * 512:(g + 1) * 512], ALU.mult, ALU.mult),
                 R=[t_yv, t_ssq, t_ng], W=[t_ynb])

        def gC(g):
            mb = g % 2
            ynb, t_ynb = ynb2[mb], t_ynb2[mb]
            ptv = pb[7][:, 0:256].bitcast(BF16).rearrange("p (a b) -> p a b", a=4)
            for q in range(4):
                P.op("pe", lambda e, q=q: e.transpose(ptv[:, q, 0:T], ynb[0:T, q * 128:(q + 1) * 128], k.identb[0:T, 0:T]),
                     R=[t_ynb, k.t_ident], W=[tpb[7]], sig=(q == 3))
            if T == 128:
                yb = g % 2
                P.op("act", lambda e, yb=yb: e.copy(ynT[yb], ptv), R=[tpb[7]], W=[t_ynT[yb]])
                P.op("sp", lambda e, yb=yb: e.dma_start(out=k.yn_scr[tt, :, g * 4:(g + 1) * 4, :], in_=ynT[yb]), R=[t_ynT[yb]], W=[t_yn_scr[tt]], dma=True)
            else:
                P.op("act", lambda e: e.copy(ynS[:, g * 4:(g + 1) * 4, :], ptv[:, :, 0:T]), R=[tpb[7]], W=[t_ynS])

        gA(0)
        for g in range(8):
            if g + 1 < 8:
                gA(g + 1)
            gB(g)
            if g >= 1:
                gC(g - 1)
        gC(7)

    P.op("pool", lambda e: e.memset(H, 0.0), W=[t_H])
    P.op("pool", lambda e: e.memset(Hb, 0.0), W=[t_Hb])
    for tt in range(16 if stop > 2 else 1):
        chunk(tt, 128)
    if stop <= 3:
        return
    state_out(k.o_pst[i])
    if stop <= 4:
        return
    state_in(k.ssm_st[i])
    chunk(16, SS)
    state_out(k.o_sst[i])
    if stop <= 5:
        return
    out_proj(k, k.w_out_odd[i], 32, k.yn_scr, t_yn_scr, ynS, t_ynS)


ATTN_SCALE = 128 ** -0.5
NEGM = -30000.0


def gelu_ops(P, dst, src_ps, tmp, R, W_dst, t_tmp, T, N):
    P.op("act", lambda e: e.activation(tmp, src_ps, AF.Square), R=R, W=[t_tmp])
    P.op("dve", lambda e: e.tensor_scalar(tmp, tmp, 0.044715, 1.0, ALU.mult, ALU.add), R=[t_tmp], W=[t_tmp])
    P.op("dve", lambda e: e.tensor_tensor(tmp, tmp, src_ps, ALU.mult), R=[t_tmp] + R, W=[t_tmp])
    P.op("act", lambda e: e.activation(tmp, tmp, AF.Sigmoid, scale=1.5957691216057308), R=[t_tmp], W=[t_tmp])
    P.op("dve", lambda e: e.tensor_tensor(dst, tmp, src_ps, ALU.mult), R=[t_tmp] + R, W=W_dst)


def even_phase(k, layer):
    P = k.P
    i = layer // 2
    pb, tpb = k.pb, k.tpb
    norm_phase(k, k.norm_mix[layer:layer + 1, :])
    rX, rY, rZ = k.rX, k.rY, k.rZ
    rY.reset()
    rZ.reset()
    wsl, t_w = [], []
    for s_ in range(4):
        a, t = rY.alloc([16, 512], BF16, "wsl")
        wsl.append(a)
        t_w.append(t)
    wgl, t_wgl = rY.alloc([16, 24], BF16, "wgl")
    vgain, t_vgain = rY.alloc([1024], F32, "vgain")
    bsbc, t_bsbc = rY.alloc([8, 128], F32, "bsbc")
    wmT, t_wmT = rY.alloc([8, 128], BF16, "wmT")
    wsT, t_wsT = rY.alloc([8, 128], F32, "wsT")
    gv, t_gv = rY.alloc([1024], F32, "gv")
    vn, t_vn = rY.alloc([1024], F32, "vn")
    rY.commit()
    vb16, t_vb16 = rZ.alloc([1024], BF16, "vb16")
    uTt, t_uTt = rZ.alloc([8, 128], BF16, "uTt")
    tS, t_tS = rZ.alloc([8, 128], F32, "tS")
    aT, t_aT = rZ.alloc([8, 128], BF16, "aT")
    raw, t_raw = [None, None], [None, None]
    for a_ in range(2):
        raw[a_], t_raw[a_] = rZ.alloc([512], F32, "raw")
    tmpf, t_tmpf = rZ.alloc([512], F32, "tmpf")
    rp, t_rp = [None] * 2, [None] * 2
    for a_ in range(2):
        rp[a_], t_rp[a_] = rZ.alloc([4, 64], F32, "rp")
    qn16, t_qn16 = rZ.alloc([512], BF16, "qn16")
    qr16, t_qr16 = rZ.alloc([512], BF16, "qr16")
    kb16, t_kb16 = rZ.alloc([2, 128], BF16, "kb16")
    va16, t_va16 = rZ.alloc([2, 130], BF16, "va16")
    trs, t_trs = [None, None], [None, None]
    for a_ in range(2):
        trs[a_], t_trs[a_] = rZ.alloc([4, 128], BF16, "trs")
    ub = [trs[a_][:, :, :].rearrange("p a b -> p (a b)") for a_ in range(2)]
    t_ub = t_trs
    cosT, t_cos = rZ.alloc([17, 64], F32, "cos")
    sinT, t_sin = rZ.alloc([17, 64], F32, "sin")
    gate, t_gate = rZ.alloc([17, 24], F32, "gate")
    st8, t_st8 = rZ.alloc([8], F32, "st8")
    qg, t_qg = rZ.alloc([128], F32, "qg")
    kg, t_kg = rZ.alloc([3, 128], F32, "kg")
    WF, t_WF = rZ.alloc([2, 16, 32], BF16, "WF")
    w2, t_w2 = rZ.alloc([2], F32, "w2")
    pl, t_pl = rZ.alloc([128], F32, "pl")
    KcT, t_KcT = rZ.alloc([2, 32], BF16, "KcT")
    Vca, t_Vca = rZ.alloc([2, 130], BF16, "Vca")
    moS, t_moS = rZ.alloc([16, SS], BF16, "moS")
    rZ.commit()
    k.ev = dict(gate=gate, t_gate=t_gate, KcT=KcT, t_KcT=t_KcT, Vca=Vca, t_Vca=t_Vca, moS=moS, t_moS=t_moS, WF=WF, t_WF=t_WF)

    P.op("sp", lambda e: e.dma_start(out=vgain, in_=k.gmlp_v_norm[i:i + 1, :].to_broadcast([128, 1024])), W=[t_vgain], dma=True)
    P.op("sp", lambda e: e.dma_start(out=bsbc, in_=k.gmlp_bs[i:i + 1].to_broadcast([128, 8, 128])), W=[t_bsbc], dma=True)
    P.op("sp", lambda e: e.dma_start(out=wsT, in_=k.gmlp_wsT[i]), W=[t_wsT], dma=True)
    P.op("dve", lambda e: e.tensor_tensor(wmT, wsT, k.utri[:, :].unsqueeze(1).to_broadcast([128, 8, 128]), ALU.mult),
         R=[t_wsT, k.t_cst], W=[t_wmT])
    P.op("sp", lambda e: e.dma_start(out=cosT, in_=k.rope_cos), W=[t_cos], dma=True)
    P.op("sp", lambda e: e.dma_start(out=sinT, in_=k.rope_sin), W=[t_sin], dma=True)
    P.op("sp", lambda e: e.dma_start(out=qg, in_=k.q_norm[i:i + 1, :].to_broadcast([128, 128])), W=[t_qg], dma=True)
    P.op("sp", lambda e: e.dma_start(out=kg, in_=k.k_norm[i:i + 1].to_broadcast([128, 3, 128])), W=[t_kg], dma=True)
    P.op("sp", lambda e: e.dma_start(out=pl[0:2, 0:64], in_=k.cmp_pool[i]), W=[t_pl], dma=True)
    P.op("dve", lambda e: e.reduce_max(st8[0:2, 0:1], pl[0:2, 0:64], AX.X), R=[t_pl], W=[t_st8])
    P.op("act", lambda e: e.mul(st8[0:2, 0:1], st8[0:2, 0:1], -1.0), R=[t_st8], W=[t_st8])
    P.op("pool", lambda e: e.memset(st8[0:2, 1:2], 0.0), R=[t_st8], W=[t_st8])
    P.op("act", lambda e: e.activation(pl[0:2, 0:64], pl[0:2, 0:64], AF.Exp, bias=st8[0:2, 0:1], accum_out=st8[0:2, 1:2]), R=[t_pl, t_st8], W=[t_pl, t_st8])
    P.op("dve", lambda e: e.reciprocal(st8[0:2, 1:2], st8[0:2, 1:2]), R=[t_st8], W=[t_st8])
    P.op("dve", lambda e: e.tensor_scalar(pl[0:2, 0:64], pl[0:2, 0:64], st8[0:2, 1:2], None, ALU.mult), R=[t_pl, t_st8], W=[t_pl])
    P.op("dve", lambda e: e.tensor_copy(pl[0:2, 64:128], pl[0:2, 0:64]), R=[t_pl], W=[t_pl])
    P.op("pe", lambda e: e.matmul(pb[7][:, 0:2], pl[0:2, :], k.identf[0:2, 0:2], start=True, stop=True), R=[t_pl, k.t_cst], W=[tpb[7]])
    P.op("act", lambda e: e.copy(w2, pb[7][:, 0:2]), R=[tpb[7]], W=[t_w2])
    for h in range(2):
        P.op("dve", lambda e, h=h: e.tensor_scalar(WF[:, h, :, :], k.ecmp[:, :, :], w2[:, h:h + 1], None, ALU.mult), R=[t_w2, k.t_ecmp], W=[t_WF])

    win = k.w_in_even[i].rearrange("(kc p) n -> p kc n", p=128)
    blocks = [b * 512 for b in range(9)]
    t_uT_scr = [Tok("uT_scr%d" % t) for t in range(17)]
    t_qT_scr = [Tok("qT_scr%d" % t) for t in range(17)]
    t_qrT_scr = [Tok("qrT_scr%d" % t) for t in range(17)]
    t_ksT_scr = [Tok("ksT_scr%d" % t) for t in range(17)]
    t_kwT_scr = [Tok("kwT_scr%d" % t) for t in range(17)]
    t_vs_scr = [Tok("vs_scr%d" % t) for t in range(17)]
    t_vw_scr = [Tok("vw_scr%d" % t) for t in range(17)]
    t_mo_scr = [Tok("mo_scr%d" % t) for t in range(16)]
    k.ev.update(t_qT_scr=t_qT_scr, t_qrT_scr=t_qrT_scr, t_ksT_scr=t_ksT_scr, t_kwT_scr=t_kwT_scr, t_vs_scr=t_vs_scr,
                t_vw_scr=t_vw_scr, t_mo_scr=t_mo_scr, cosT=cosT, sinT=sinT)

    def load_w(bi):
        sl = bi % 4
        c0 = blocks[bi]
        P.op("pool", lambda e, sl=sl, c0=c0: e.dma_start(out=wsl[sl], in_=win[:, :, c0:c0 + 512]), W=[t_w[sl]], dma=True)

    load_w(0)
    load_w(1)
    P.op("pool", lambda e: e.dma_start(out=wgl, in_=win[:, :, 4608:4632]), W=[t_wgl], dma=True)
    P.op("pool", lambda e: e.memset(va16[:, :, 128:130], 1.0), W=[t_va16])
    cnt = [0]

    def proj_tm(sl, tt, bank):
        T = 128 if tt < 16 else SS
        tok0 = tt * 128
        for kc in range(16):
            P.op("pe", lambda e, kc=kc: e.matmul(pb[bank][0:T, :], k.xnT[:, kc, tok0:tok0 + T], wsl[sl][:, kc, :], start=(kc == 0), stop=(kc == 15)),
                 R=[t_w[sl], k.t_xnT], W=[tpb[bank]], sig=(kc == 15))

    def rms_heads(x3, T, nh, gain_ap, t_x, t_gain):
        sq = tmpf[0:T, 0:nh * 128].rearrange("p (h d) -> p h d", h=nh)
        P.op("dve", lambda e: e.tensor_tensor(sq, x3, x3, ALU.mult), R=[t_x], W=[t_tmpf])
        P.op("dve", lambda e: e.reduce_sum(st8[0:T, 0:nh], sq, AX.X), R=[t_tmpf], W=[t_st8])
        P.op("act", lambda e: e.activation(st8[0:T, 0:nh], st8[0:T, 0:nh], AF.Sqrt, scale=1.0 / 128, bias=EPS), R=[t_st8], W=[t_st8])
        P.op("dve", lambda e: e.reciprocal(st8[0:T, 0:nh], st8[0:T, 0:nh]), R=[t_st8], W=[t_st8])
        P.op("dve", lambda e: e.tensor_tensor(x3, x3, st8[0:T, 0:nh].unsqueeze(2).to_broadcast([T, nh, 128]), ALU.mult), R=[t_x, t_st8], W=[t_x])
        P.op("dve", lambda e: e.tensor_tensor(x3, x3, gain_ap.unsqueeze(1).to_broadcast([T, nh, 128]), ALU.mult), R=[t_x, t_gain], W=[t_x])

    def rope_ops(dst3, src3, T, nh, tt, t_dst, t_src):
        c_ = cosT[0:T, tt, :].unsqueeze(1).to_broadcast([T, nh, 64])
        s_ = sinT[0:T, tt, :].unsqueeze(1).to_broadcast([T, nh, 64])
        x1, x2 = src3[:, :, 0:64], src3[:, :, 64:128]
        a, b_ = [rp[j][0:T, 0:nh, :] for j in range(2)]
        P.op("dve", lambda e: e.tensor_tensor(a, x1, c_, ALU.mult), R=[t_src, t_cos], W=[t_rp[0]])
        P.op("pool", lambda e: e.tensor_tensor(b_, x2, s_, ALU.mult), R=[t_src, t_sin], W=[t_rp[1]])
        P.op("dve", lambda e: e.tensor_tensor(dst3[:, :, 0:64], a, b_, ALU.subtract), R=[t_rp[0], t_rp[1]], W=[t_dst])
        P.op("dve", lambda e: e.tensor_tensor(a, x2, c_, ALU.mult), R=[t_src, t_cos], W=[t_rp[0]])
        P.op("pool", lambda e: e.tensor_tensor(b_, x1, s_, ALU.mult), R=[t_src, t_sin], W=[t_rp[1]])
        P.op("dve", lambda e: e.tensor_tensor(dst3[:, :, 64:128], a, b_, ALU.add), R=[t_rp[0], t_rp[1]], W=[t_dst])

    def transp_out(src16, t_src, T, nh, dsts):
        tb_ = cnt[0] % 2
        cnt[0] += 1
        bank2 = 6 + tb_
        pt = pb[bank2][:, 0:256].bitcast(BF16).rearrange("p (a b) -> p a b", a=4)
        for h in range(nh):
            P.op("pe", lambda e, h=h: e.transpose(pt[:, h, 0:T], src16[0:T, h * 128:(h + 1) * 128], k.identb[0:T, 0:T]),
                 R=[t_src, k.t_ident], W=[tpb[bank2]], sig=(h == nh - 1))
        P.op("act", lambda e: e.copy(trs[tb_][:, 0:nh, 0:T], pt[:, 0:nh, 0:T]), R=[tpb[bank2]], W=[t_trs[tb_]])
        for h in range(nh):
            dap, toks = dsts[h]
            P.op("sp", lambda e, h=h, dap=dap: e.dma_start(out=dap, in_=trs[tb_][:, h, 0:T]), R=[t_trs[tb_]], W=toks, dma=True)

    for bi in range(9):
        if bi + 2 < 9:
            load_w(bi + 2)
        sl = bi % 4
        if bi < 2:
            for j in range(4):
                c = bi * 4 + j
                for tb in range(5):
                    N = 512 if tb < 4 else SS
                    c0 = tb * 512
                    bank = cnt[0] % 2
                    u_ = cnt[0] % 2
                    cnt[0] += 1
                    for kc in range(16):
                        P.op("pe", lambda e, bank=bank, j=j, kc=kc, c0=c0, N=N, sl=sl: e.matmul(
                            pb[bank][:, 0:N], wsl[sl][:, kc, j * 128:(j + 1) * 128], k.xnT[:, kc, c0:c0 + N], start=(kc == 0), stop=(kc == 15)),
                            R=[t_w[sl], k.t_xnT], W=[tpb[bank]], sig=(kc == 15))
                    gelu_ops(P, ub[u_][:, 0:N], pb[bank][:, 0:N], tmpf[:, 0:N], [tpb[bank]], [t_ub[u_]], t_tmpf, 128, N)
                    tts = list(range(tb * 4, tb * 4 + 4)) if tb < 4 else [16]
                    P.op("sp", lambda e, c=c, u_=u_, c0=c0, N=N: e.dma_start(out=k.uT_scr[c, :, c0:c0 + N], in_=ub[u_][:, 0:N]),
                         R=[t_ub[u_]], W=[t_uT_scr[t] for t in tts], dma=True)
        elif bi == 2:
            continue
        elif bi == 3:
            def v_tile(tt):
                T = 128 if tt < 16 else SS
                tok0 = tt * 128
                for vb in range(2):
                    proj_tm(2 + vb, tt, vb)
                    gelu_ops(P, gv[0:T, vb * 512:(vb + 1) * 512], pb[vb][0:T, :], tmpf[0:T, :], [tpb[vb]], [t_gv], t_tmpf, T, 512)
                g3 = gv[0:T, :].rearrange("p (g d) -> p g d", g=8)
                v3 = vn[0:T, :].rearrange("p (g d) -> p g d", g=8)
                P.op("dve", lambda e: e.tensor_tensor(v3, g3, g3, ALU.mult), R=[t_gv], W=[t_vn])
                P.op("dve", lambda e: e.reduce_sum(st8[0:T, 0:8], v3, AX.X), R=[t_vn], W=[t_st8])
                P.op("act", lambda e: e.activation(st8[0:T, 0:8], st8[0:T, 0:8], AF.Sqrt, scale=1.0 / 128, bias=EPS), R=[t_st8], W=[t_st8])
                P.op("dve", lambda e: e.reciprocal(st8[0:T, 0:8], st8[0:T, 0:8]), R=[t_st8], W=[t_st8])
                P.op("dve", lambda e: e.tensor_tensor(v3, g3, st8[0:T, 0:8].unsqueeze(2).to_broadcast([T, 8, 128]), ALU.mult), R=[t_gv, t_st8], W=[t_vn])
                P.op("dve", lambda e: e.tensor_tensor(vn[0:T, :], vn[0:T, :], vgain[0:T, :], ALU.mult), R=[t_vn, t_vgain], W=[t_vn])
                if tt == 16:
                    P.op("sp", lambda e: e.dma_start(out=k.o_sv[i], in_=vn[0:SS, :]), R=[t_vn], dma=True)
                P.op("act", lambda e: e.copy(vb16[0:T, :], vn[0:T, :]), R=[t_vn], W=[t_vb16])
                P.op("sp", lambda e: e.dma_start(out=uTt[:, :, 0:T], in_=k.uT_scr[:, :, tok0:tok0 + T].rearrange("g p t -> p g t")),
                     R=[t_uT_scr[tt]], W=[t_uTt], dma=True)
                PSv = [pb[2][:, :].rearrange("p (a b) -> p a b", a=4), pb[3][:, :].rearrange("p (a b) -> p a b", a=4)]
                for g in range(8):
                    P.op("pe", lambda e, g=g: e.matmul(PSv[g // 4][:, g % 4, 0:T], vb16[0:T, g * 128:(g + 1) * 128], wmT[0:T, g, 0:T], start=True, stop=True),
                         R=[t_vb16, t_wmT], W=[tpb[2 + g // 4]], sig=(g % 4 == 3))
                for hf in range(2):
                    P.op("dve", lambda e, hf=hf: e.tensor_tensor(tS[:, hf * 4:(hf + 1) * 4, 0:T], PSv[hf][:, :, 0:T], bsbc[:, hf * 4:(hf + 1) * 4, 0:T], ALU.add),
                         R=[tpb[2 + hf], t_bsbc], W=[t_tS])
                if tt < 16:
                    P.op("dve", lambda e: e.tensor_tensor(aT[:, :, :], tS[:, :, :], uTt[:, :, :], ALU.mult), R=[t_tS, t_uTt], W=[t_aT])
                    P.op("sp", lambda e: e.dma_start(out=k.mo_scr[tt, :, 0:8, :], in_=aT), R=[t_aT], W=[t_mo_scr[tt]], dma=True)
                else:
                    P.op("dve", lambda e: e.tensor_tensor(moS[:, 0:8, :], tS[:, :, 0:SS], uTt[:, :, 0:SS], ALU.mult), R=[t_tS, t_uTt], W=[t_moS])
            for tt in range(17):
                v_tile(tt)
        elif bi < 6:
            def q_tile(tt, sl, hb0):
                T = 128 if tt < 16 else SS
                tok0 = tt * 128
                bank = cnt[0] % 2
                r_ = cnt[0] % 2
                cnt[0] += 1
                proj_tm(sl, tt, bank)
                P.op("act", lambda e: e.copy(raw[r_][0:T, :], pb[bank][0:T, :]), R=[tpb[bank]], W=[t_raw[r_]])
                x3 = raw[r_][0:T, :].rearrange("p (h d) -> p h d", h=4)
                rms_heads(x3, T, 4, qg[0:T, :], t_raw[r_], t_qg)
                P.op("act", lambda e: e.copy(qn16[0:T, :], raw[r_][0:T, :]), R=[t_raw[r_]], W=[t_qn16])
                rope_ops(qr16[0:T, :].rearrange("p (h d) -> p h d", h=4), x3, T, 4, tt, t_qr16, t_raw[r_])
                transp_out(qn16, t_qn16, T, 4, [(k.qT_scr[hb0 + h, :, tok0:tok0 + T], [t_qT_scr[tt]]) for h in range(4)])
                transp_out(qr16, t_qr16, T, 4, [(k.qrT_scr[hb0 + h, :, tok0:tok0 + T], [t_qrT_scr[tt]]) for h in range(4)])
            for tt in range(17):
                q_tile(tt, sl, (bi - 4) * 4)
        else:
            x = bi - 6

            def kv_tile(tt, sl, x):
                T = 128 if tt < 16 else SS
                tok0 = tt * 128
                bank = cnt[0] % 2
                r_ = cnt[0] % 2
                cnt[0] += 1
                proj_tm(sl, tt, bank)
                rw = raw[r_]
                P.op("act", lambda e: e.copy(rw[0:T, :], pb[bank][0:T, :]), R=[tpb[bank]], W=[t_raw[r_]])
                k3 = rw[0:T, 0:256].rearrange("p (h d) -> p h d", h=2)
                rms_heads(k3, T, 2, kg[0:T, x, :], t_raw[r_], t_kg)
                if x > 0:
                    kr = tmpf[0:T, 0:256].rearrange("p (h d) -> p h d", h=2)
                    rope_ops(kr, k3, T, 2, tt, t_tmpf, t_raw[r_])
                    P.op("dve", lambda e: e.tensor_copy(rw[0:T, 0:256], tmpf[0:T, 0:256]), R=[t_tmpf], W=[t_raw[r_]])
                if tt < 16:
                    dst = [k.o_pkc, k.o_pks, k.o_pkw][x]
                    if x < 2:
                        P.op("sp", lambda e, dst=dst: e.dma_start(out=dst[i, tok0:tok0 + T, :], in_=rw[0:T, :]), R=[t_raw[r_]], dma=True)
                    elif tt >= 12:
                        P.op("sp", lambda e, dst=dst: e.dma_start(out=dst[i, tok0 - 1536:tok0 - 1536 + T, :], in_=rw[0:T, :]), R=[t_raw[r_]], dma=True)
                else:
                    dst = [k.o_skc, k.o_sks, k.o_skw][x]
                    P.op("sp", lambda e, dst=dst: e.dma_start(out=dst[i], in_=rw[0:SS, :]), R=[t_raw[r_]], dma=True)
                P.op("act", lambda e: e.copy(kb16[0:T, :, :], rw[0:T, 0:256].rearrange("p (h d) -> p h d", h=2)), R=[t_raw[r_]], W=[t_kb16])
                P.op("act", lambda e: e.copy(va16[0:T, :, 0:128], rw[0:T, 256:512].rearrange("p (h d) -> p h d", h=2)), R=[t_raw[r_]], W=[t_va16])
                if x == 0:
                    if tt < 16:
                        for h in range(2):
                            P.op("pe", lambda e, h=h: e.matmul(pb[2 + h][:, 0:32], kb16[0:T, h, :], WF[0:T, h, tt, :], start=(tt == 0), stop=(tt == 15)),
                                 R=[t_kb16, t_WF], W=[tpb[2 + h]], sig=False)
                            P.op("pe", lambda e, h=h: e.matmul(pb[4 + h][0:32, 0:128], WF[0:T, h, tt, :], va16[0:T, h, 0:128], start=(tt == 0), stop=(tt == 15)),
                                 R=[t_va16, t_WF], W=[tpb[4 + h]], sig=True)
                else:
                    scrT = k.ksT_scr if x == 1 else k.kwT_scr
                    tscT = t_ksT_scr if x == 1 else t_kwT_scr
                    scrV = k.vs_scr if x == 1 else k.vw_scr
                    tscV = t_vs_scr if x == 1 else t_vw_scr
                    transp_out(kb16[:, :, :].rearrange("p h d -> p (h d)"), t_kb16, T, 2, [(scrT[h, :, tok0:tok0 + T], [tscT[tt]]) for h in range(2)])
                    P.op("sp", lambda e, scrV=scrV: e.dma_start(out=scrV[tok0:tok0 + T, :, :], in_=va16[0:T, :, :]), R=[t_va16], W=[tscV[tt]], dma=True)
            for tt in range(17):
                kv_tile(tt, sl, x)
            if x == 0:
                P.op("pool", lambda e: e.memset(Vca[0:32, :, 128:130], 1.0), W=[t_Vca])
                for h in range(2):
                    P.op("act", lambda e, h=h: e.copy(KcT[:, h, :], pb[2 + h][:, 0:32]), R=[tpb[2 + h]], W=[t_KcT])
                    P.op("act", lambda e, h=h: e.copy(Vca[0:32, h, 0:128], pb[4 + h][0:32, 0:128]), R=[tpb[4 + h]], W=[t_Vca])
    for tt in range(17):
        T = 128 if tt < 16 else SS
        tok0 = tt * 128
        bank = cnt[0] % 2
        cnt[0] += 1
        for kc in range(16):
            P.op("pe", lambda e, kc=kc, bank=bank, tok0=tok0, T=T: e.matmul(pb[bank][0:T, 0:24], k.xnT[:, kc, tok0:tok0 + T], wgl[:, kc, :], start=(kc == 0), stop=(kc == 15)),
                 R=[t_wgl, k.t_xnT], W=[tpb[bank]], sig=(kc == 15))
        P.op("act", lambda e, bank=bank, T=T, tt=tt: e.activation(gate[0:T, tt, :], pb[bank][0:T, 0:24], AF.Sigmoid), R=[tpb[bank]], W=[t_gate])
    if k.cfg.get("even_stop", 99) <= 1:
        return
    even_attn_prompt(k, layer)
    if k.cfg.get("even_stop", 99) <= 2:
        return
    if k.cfg.get("even_sample", True):
        even_attn_sample(k, layer)
    out_proj(k, k.w_out_even[i], 16, k.mo_scr, t_mo_scr, moS, t_moS)


def obank(h):
    return 4 + h % 4, 0


def attn_combine(k, T, Oc, Os, Ow, t_O, gate_ap, t_gate, s3, t_s3, tmpb, t_tmpb, bout, t_bout):
    P = k.P
    for x, O in enumerate((Oc, Os, Ow)):
        P.op("dve", lambda e, x=x, O=O: e.tensor_copy(s3[0:T, :, x], O[0:T, :, 128]), R=t_O, W=[t_s3])
    P.op("dve", lambda e: e.tensor_scalar_max(s3[0:T, :, :], s3[0:T, :, :], 1e-30), R=[t_s3], W=[t_s3])
    P.op("dve", lambda e: e.reciprocal(s3[0:T, :, :], s3[0:T, :, :]), R=[t_s3], W=[t_s3])
    P.op("dve", lambda e: e.tensor_tensor(s3[0:T, :, :], s3[0:T, :, :], gate_ap.rearrange("p (h x) -> p h x", h=8), ALU.mult),
         R=[t_s3, t_gate], W=[t_s3])
    for h in range(8):
        eng = "dve"
        P.op(eng, lambda e, h=h: e.tensor_scalar(tmpb[0:T, h, :], Oc[0:T, h, 0:128], s3[0:T, h, 0:1], None, ALU.mult), R=t_O + [t_s3], W=[t_tmpb])
        P.op(eng, lambda e, h=h: e.scalar_tensor_tensor(tmpb[0:T, h, :], Os[0:T, h, 0:128], s3[0:T, h, 1:2], tmpb[0:T, h, :], ALU.mult, ALU.add),
             R=t_O + [t_s3, t_tmpb], W=[t_tmpb])
        P.op(eng, lambda e, h=h: e.scalar_tensor_tensor(bout[0:T, h * 128:(h + 1) * 128], Ow[0:T, h, 0:128], s3[0:T, h, 2:3], tmpb[0:T, h, :], ALU.mult, ALU.add),
             R=t_O + [t_s3, t_tmpb], W=[t_bout])


def even_attn_prompt(k, layer):
    P = k.P
    i = layer // 2
    pb, tpb = k.pb, k.tpb
    ev = k.ev
    gate, t_gate, KcT, t_KcT, Vca, t_Vca = ev["gate"], ev["t_gate"], ev["KcT"], ev["t_KcT"], ev["Vca"], ev["t_Vca"]
    rX, rY = k.rX, k.rY
    rX.reset()
    rY.reset()
    ksT, t_ksT = rX.alloc([2, 2048], BF16, "ksT")
    kwT, t_kwT = rX.alloc([2, 2048], BF16, "kwT")
    vs, t_vs = rX.alloc([16, 2, 130], BF16, "vs")
    vw, t_vw = rX.alloc([16, 2, 130], BF16, "vw")
    qTt, t_qTt, qrTt, t_qrTt = [None, None], [None, None], [None, None], [None, None]
    for a_ in range(2):
        qTt[a_], t_qTt[a_] = rX.alloc([8, 128], BF16, "qTt")
        qrTt[a_], t_qrTt[a_] = rX.alloc([8, 128], BF16, "qrTt")
    Oc, t_Oc = rX.alloc([8, 130], F32, "Oc")
    Os, t_Os = rX.alloc([8, 130], F32, "Os")
    Ow, t_Ow = rX.alloc([8, 130], F32, "Ow")
    tmpb, t_tmpb = rX.alloc([8, 128], F32, "tmpb")
    rX.commit()
    ebig, t_ebig = rY.alloc([2048], BF16, "ebig")
    negc, t_negc = rY.alloc([128], BF16, "negc")
    negu, t_negu = rY.alloc([128], BF16, "negu")
    validq, t_validq = rY.alloc([16, 32], F32, "validq")
    keepm, t_keepm = rY.alloc([16, 32], F32, "keepm")
    addm, t_addm = rY.alloc([16, 32], F32, "addm")
    validT, t_validT = rY.alloc([16, 128], BF16, "validT")
    ee, t_ee = rY.alloc([8, 32], F32, "ee")
    imp, t_imp = rY.alloc([2, 32], F32, "imp")
    cmpb, t_cmpb = rY.alloc([32, 32], F32, "cmpb")
    rank, t_rank = rY.alloc([2, 32], F32, "rank")
    negm, t_negm = rY.alloc([64], BF16, "negm")
    negmT, t_negmT = rY.alloc([128], BF16, "negmT")
    ecf, t_ecf = rY.alloc([512], F32, "ecf")
    eTb, t_eTb = rY.alloc([512], BF16, "eTb")
    PT, t_PT = [None, None], [None, None]
    for a_ in range(2):
        PT[a_], t_PT[a_] = rY.alloc([512], BF16, "PT")
    s8, t_s8 = rY.alloc([8], F32, "s8")
    s3, t_s3 = rY.alloc([8, 3], F32, "s3")
    bout, t_bout = rY.alloc([1024], BF16, "bout")
    boT, t_boT = [None, None], [None, None]
    for a_ in range(2):
        boT[a_], t_boT[a_] = rY.alloc([4, 128], BF16, "boT")
    rY.commit()
    ld = lambda dst, src, R, W: P.op("sp", lambda e: e.dma_start(out=dst, in_=src), R=R, W=W, dma=True)
    ld(ksT, k.ksT_scr[:, :, 0:2048].rearrange("h p t -> p h t"), ev["t_ksT_scr"], [t_ksT])
    ld(kwT, k.kwT_scr[:, :, 0:2048].rearrange("h p t -> p h t"), ev["t_kwT_scr"], [t_kwT])
    ld(vs, k.vs_scr[0:2048].rearrange("(t p) h d -> p t h d", p=128), ev["t_vs_scr"], [t_vs])
    ld(vw, k.vw_scr[0:2048].rearrange("(t p) h d -> p t h d", p=128), ev["t_vw_scr"], [t_vw])
    ld(ebig[0:64, :], k.c_ebig, [], [t_ebig])
    ld(negc, k.c_negb[0], [], [t_negc])
    ld(negu, k.c_negb[1], [], [t_negu])
    ld(validq, k.c_valid, [], [t_validq])
    ld(keepm, k.c_keep, [], [t_keepm])
    ld(addm, k.c_add, [], [t_addm])
    ld(validT[0:32, :, :], k.c_validT, [], [t_validT])
    cnt = [0]
    t_mo_scr = ev["t_mo_scr"]

    def evac(O, t_O, kvh):
        for g in range(4):
            P.op("act", lambda e, g=g: e.copy(O[:, kvh * 4 + g, :], pb[4 + g][:, 0:130]), R=[tpb[4 + g]], W=[t_O])

    def branch(qt, kts, KT, t_KT, V, t_V, qr, t_qr, blockmask, O, t_O):
        for kvh in range(2):
            branch1(qt, kts, KT, t_KT, V, t_V, qr, t_qr, blockmask, kvh)
            evac(O, t_O, kvh)

    def branch1(qt, kts, KT, t_KT, V, t_V, qr, t_qr, blockmask, kvh):
        def scores(kt):
            bs_ = 2 + cnt[0] % 2
            p_ = cnt[0] % 2
            cnt[0] += 1
            extra = []
            if blockmask:
                extra.append("blk")
            if kt == qt:
                extra.append("caus")
            if (not blockmask) and kt == qt - 4:
                extra.append("upper")
            P.op("pe", lambda e, last=(len(extra) == 0): e.matmul(
                pb[bs_][:, :], KT[:, kvh, kt * 128:(kt + 1) * 128], qr[:, kvh * 4:(kvh + 1) * 4, :], start=True, stop=last),
                R=[t_KT, t_qr], W=[tpb[bs_]], sig=(len(extra) == 0))
            for j, nm in enumerate(extra):
                last = (j == len(extra) - 1)
                if nm == "blk":
                    P.op("pe", lambda e, last=last: e.matmul(
                        pb[bs_][:, :], ebig[kvh * 32:(kvh + 1) * 32, kt * 128:(kt + 1) * 128],
                        negmT[kvh * 32:(kvh + 1) * 32, :].unsqueeze(1).to_broadcast([32, 4, 128]), start=False, stop=last),
                        R=[t_ebig, t_negmT], W=[tpb[bs_]], sig=last)
                else:
                    mk, tmk = (negc, t_negc) if nm == "caus" else (negu, t_negu)
                    P.op("pe", lambda e, last=last, mk=mk: e.matmul(
                        pb[bs_][:, :], k.identb[:, :], mk[:, :].unsqueeze(1).to_broadcast([128, 4, 128]), start=False, stop=last),
                        R=[k.t_ident, tmk], W=[tpb[bs_]], sig=last)
            P.op("act", lambda e: e.activation(PT[p_], pb[bs_][:, :], AF.Exp, scale=ATTN_SCALE), R=[tpb[bs_]], W=[t_PT[p_]])
            return p_

        def pv(kt, p_):
            for g in range(4):
                h = kvh * 4 + g
                bk, col = obank(h)
                P.op("pe", lambda e, g=g, bk=bk, col=col: e.matmul(
                    pb[bk][:, col:col + 130], PT[p_][:, g * 128:(g + 1) * 128], V[:, kt, kvh, :], start=(kt == kts[0]), stop=(kt == kts[-1])),
                    R=[t_PT[p_], t_V], W=[tpb[bk]], sig=True)

        pend = scores(kts[0])
        for idx_, kt in enumerate(kts):
            nxt = scores(kts[idx_ + 1]) if idx_ + 1 < len(kts) else None
            pv(kt, pend)
            pend = nxt

    def qtile(qt):
        b = qt % 2
        q_, tq_, qr_, tqr_ = qTt[b], t_qTt[b], qrTt[b], t_qrTt[b]
        ld(q_, k.qT_scr[:, :, qt * 128:(qt + 1) * 128].rearrange("h p t -> p h t"), [ev["t_qT_scr"][qt]], [tq_])
        ld(qr_, k.qrT_scr[:, :, qt * 128:(qt + 1) * 128].rearrange("h p t -> p h t"), [ev["t_qrT_scr"][qt]], [tqr_])
        for h in range(8):
            P.op("pe", lambda e, h=h: e.matmul(pb[0][:, h * 32:(h + 1) * 32], q_[:, h, :], KcT[:, h // 4, :], start=True, stop=True),
                 R=[tq_, t_KcT], W=[tpb[0]], sig=(h == 7))
        P.op("act", lambda e: e.activation(ee, pb[0][:, 0:256].rearrange("p (h n) -> p h n", h=8), AF.Exp, scale=ATTN_SCALE), R=[tpb[0]], W=[t_ee])
        P.op("dve", lambda e: e.tensor_tensor(ee, ee, validq[:, qt, :].unsqueeze(1).to_broadcast([128, 8, 32]), ALU.mult), R=[t_ee, t_validq], W=[t_ee])
        P.op("dve", lambda e: e.reduce_sum(s8, ee, AX.X), R=[t_ee], W=[t_s8])
        P.op("dve", lambda e: e.tensor_scalar_max(s8, s8, 1e-30), R=[t_s8], W=[t_s8])
        P.op("dve", lambda e: e.reciprocal(s8, s8), R=[t_s8], W=[t_s8])
        P.op("dve", lambda e: e.tensor_tensor(ee, ee, s8[:, :].unsqueeze(2).to_broadcast([128, 8, 32]), ALU.mult), R=[t_ee, t_s8], W=[t_ee])
        P.op("dve", lambda e: e.reduce_sum(imp, ee[:, :, :].rearrange("p (k g) n -> p k n g", k=2), AX.X), R=[t_ee], W=[t_imp])
        P.op("dve", lambda e: e.tensor_tensor(imp, imp, keepm[:, qt, :].unsqueeze(1).to_broadcast([128, 2, 32]), ALU.mult), R=[t_imp, t_keepm], W=[t_imp])
        P.op("dve", lambda e: e.tensor_tensor(imp, imp, addm[:, qt, :].unsqueeze(1).to_broadcast([128, 2, 32]), ALU.add), R=[t_imp, t_addm], W=[t_imp])
        for kvh in range(2):
            P.op("dve", lambda e, kvh=kvh: e.tensor_tensor(cmpb, imp[:, kvh, :].unsqueeze(1).to_broadcast([128, 32, 32]),
                                                           imp[:, kvh, :].unsqueeze(2).to_broadcast([128, 32, 32]), ALU.is_gt), R=[t_imp], W=[t_cmpb])
            P.op("dve", lambda e, kvh=kvh: e.reduce_sum(rank[:, kvh, :], cmpb, AX.X), R=[t_cmpb], W=[t_rank])
        P.op("dve", lambda e: e.tensor_scalar(negm, rank[:, :, :].rearrange("p k n -> p (k n)"), 15.5, NEGM, ALU.is_gt, ALU.mult), R=[t_rank], W=[t_negm])
        ptm = pb[0][:, 0:64].bitcast(BF16)
        P.op("pe", lambda e: e.transpose(ptm[0:64, :], negm, k.identb[:, :]), R=[t_negm, k.t_ident], W=[tpb[0]])
        P.op("act", lambda e: e.copy(negmT[0:64, :], ptm[0:64, :]), R=[tpb[0]], W=[t_negmT])
        for kvh in range(2):
            P.op("pe", lambda e, kvh=kvh: e.matmul(pb[1][0:32, :], KcT[:, kvh, :], q_[:, kvh * 4:(kvh + 1) * 4, :], start=True, stop=True),
                 R=[tq_, t_KcT], W=[tpb[1]])
            P.op("act", lambda e: e.activation(ecf[0:32, :], pb[1][0:32, :], AF.Exp, scale=ATTN_SCALE), R=[tpb[1]], W=[t_ecf])
            P.op("dve", lambda e: e.tensor_tensor(eTb[0:32, :].rearrange("p (g q) -> p g q", g=4), ecf[0:32, :].rearrange("p (g q) -> p g q", g=4),
                                                  validT[0:32, qt, :].unsqueeze(1).to_broadcast([32, 4, 128]), ALU.mult), R=[t_ecf, t_validT], W=[t_eTb])
            for g in range(4):
                h = kvh * 4 + g
                bk, col = obank(h)
                P.op("pe", lambda e, g=g, bk=bk, col=col, kvh=kvh: e.matmul(pb[bk][:, col:col + 130], eTb[0:32, g * 128:(g + 1) * 128], Vca[0:32, kvh, :],
                                                                           start=True, stop=True), R=[t_eTb, t_Vca], W=[tpb[bk]], sig=True)
            evac(Oc, t_Oc, kvh)
        branch(qt, list(range(0, qt + 1)), ksT, t_ksT, vs, t_vs, qr_, tqr_, True, Os, t_Os)
        branch(qt, list(range(max(0, qt - 4), qt + 1)), kwT, t_kwT, vw, t_vw, qr_, tqr_, False, Ow, t_Ow)
        attn_combine(k, 128, Oc, Os, Ow, [t_Oc, t_Os, t_Ow], gate[:, qt, :], t_gate, s3, t_s3, tmpb, t_tmpb, bout, t_bout)
        for half in range(2):
            pt = pb[0][:, 0:256].bitcast(BF16).rearrange("p (a b) -> p a b", a=4)
            for j in range(4):
                h = half * 4 + j
                P.op("pe", lambda e, j=j, h=h: e.transpose(pt[:, j, :], bout[:, h * 128:(h + 1) * 128], k.identb[:, :]), R=[t_bout, k.t_ident], W=[tpb[0]], sig=(j == 3))
            P.op("act", lambda e, half=half: e.copy(boT[half], pt), R=[tpb[0]], W=[t_boT[half]])
            P.op("sp", lambda e, half=half: e.dma_start(out=k.mo_scr[qt, :, 8 + half * 4:12 + half * 4, :], in_=boT[half]), R=[t_boT[half]], W=[t_mo_scr[qt]], dma=True)

    for qt in range(k.cfg.get("nqt", 16)):
        qtile(qt)


def even_attn_sample(k, layer):
    P = k.P
    i = layer // 2
    pb, tpb = k.pb, k.tpb
    ev = k.ev
    gate, t_gate, moS, t_moS = ev["gate"], ev["t_gate"], ev["moS"], ev["t_moS"]
    WF, t_WF = ev["WF"], ev["t_WF"]
    rX, rY = k.rX, k.rY
    rX.reset()
    rY.reset()
    G = 16
    ptb, t_ptb = rX.alloc([128], I32, "ptb")
    ptf, t_ptf = rX.alloc([128], F32, "ptf")
    idx, t_idx = rX.alloc([128], I32, "idx")
    iot, t_iot = rX.alloc([8], F32, "iot")
    pgf, t_pgf = [None] * 4, [None] * 4
    for a_ in range(4):
        pgf[a_], t_pgf[a_] = rX.alloc([512], F32, "pgf")
    pgk, t_pgk = [None] * 4, [None] * 4
    vau, t_vau = [None] * 4, [None] * 4
    for a_ in range(4):
        pgk[a_], t_pgk[a_] = rX.alloc([256], BF16, "pgk")
        vau[a_], t_vau[a_] = rX.alloc([2, 130], BF16, "vau")
    KT, t_KT = rX.alloc([2, 512], BF16, "KT")
    PTs, t_PTs = rX.alloc([8, 16], BF16, "PTs")
    KcTs, t_KcTs = rX.alloc([2, 256], BF16, "KcTs")
    VcTs, t_VcTs = rX.alloc([2, 256], BF16, "VcTs")
    Vcs, t_Vcs = rX.alloc([2, 2, 130], BF16, "Vcs")
    qTs, t_qTs = rX.alloc([8, SS], BF16, "qTs")
    qrTs, t_qrTs = rX.alloc([8, SS], BF16, "qrTs")
    knT, t_knT = rX.alloc([2, 2, SS], BF16, "knT")
    vnew, t_vnew = rX.alloc([2, 2, 130], BF16, "vnew")
    rX.commit()
    ef, t_ef = rY.alloc([2, 256], F32, "ef")
    pf, t_pf = rY.alloc([2, 256], F32, "pf")
    eb, t_eb = rY.alloc([2, 256], BF16, "eb")
    impS, t_impS = rY.alloc([2, 256], F32, "impS")
    tt_, t_tt = rY.alloc([2, 255], F32, "tt")
    msk, t_msk = rY.alloc([2, 255], F32, "msk")
    mx, t_mx = rY.alloc([2], F32, "mx")
    sm2, t_sm2 = rY.alloc([2], F32, "sm2")
    negS, t_negS = rY.alloc([2, 256], F32, "negS")
    neg16, t_neg16 = rY.alloc([2, 256], F32, "neg16")
    Sm, t_Sm = rY.alloc([512], F32, "Sm")
    Pb, t_Pb = rY.alloc([512], BF16, "Pb")
    sum16, t_sum16 = rY.alloc([4], F32, "sum16")
    sum16T, t_sum16T = rY.alloc([16], F32, "sum16T")
    negnew, t_negnew = rY.alloc([4], F32, "negnew")
    negwin, t_negwin = rY.alloc([512], F32, "negwin")
    O16, t_O16 = rY.alloc([2, 130], F32, "O16")
    O4 = []
    t_O4 = []
    for a_ in range(3):
        a, t = rY.alloc([8, 130], F32, "O4")
        O4.append(a)
        t_O4.append(t)
    s3, t_s3 = rY.alloc([8, 3], F32, "s3")
    tmpb, t_tmpb = rY.alloc([8, 128], F32, "tmpb")
    bout, t_bout = rY.alloc([1024], BF16, "bout")
    rY.commit()
    ld = lambda dst, src, R, W: P.op("sp", lambda e: e.dma_start(out=dst, in_=src), R=R, W=W, dma=True)
    ld(ptb, k.ptab.to_broadcast([128, 128]), [], [t_ptb])
    ld(iot[:, 0:1], k.c_iota, [], [t_iot])
    ld(sum16[0:16, :], k.c_sum16, [], [t_sum16])
    ld(sum16T[0:4, :], k.c_sum16T, [], [t_sum16T])
    ld(negnew[0:16, :], k.c_negnew, [], [t_negnew])
    ld(negwin[0:16, :], k.c_negwin, [], [t_negwin])
    ld(qTs, k.qT_scr[:, :, 2048:2052].rearrange("h p t -> p h t"), [ev["t_qT_scr"][16]], [t_qTs])
    ld(qrTs, k.qrT_scr[:, :, 2048:2052].rearrange("h p t -> p h t"), [ev["t_qrT_scr"][16]], [t_qrTs])
    ld(knT[:, 0, :, :], k.ksT_scr[:, :, 2048:2052].rearrange("h p t -> p h t"), [ev["t_ksT_scr"][16]], [t_knT])
    ld(knT[:, 1, :, :], k.kwT_scr[:, :, 2048:2052].rearrange("h p t -> p h t"), [ev["t_kwT_scr"][16]], [t_knT])
    ld(vnew[0:SS, 0, :, :], k.vs_scr[2048:2052], [ev["t_vs_scr"][16]], [t_vnew])
    ld(vnew[0:SS, 1, :, :], k.vw_scr[2048:2052], [ev["t_vw_scr"][16]], [t_vnew])
    P.op("dve", lambda e: e.tensor_copy(ptf, ptb), R=[t_ptb], W=[t_ptf])
    P.op("dve", lambda e: e.tensor_scalar(ptf, ptf, 128.0, iot[:, 0:1], ALU.mult, ALU.add), R=[t_ptf, t_iot], W=[t_ptf])
    if i > 0:
        P.op("dve", lambda e: e.tensor_scalar_add(ptf, ptf, float(i * 1280 * 128)), R=[t_ptf], W=[t_ptf])
    P.op("dve", lambda e: e.tensor_copy(idx, ptf), R=[t_ptf], W=[t_idx])
    for a_ in range(4):
        P.op("pool", lambda e, a_=a_: e.memset(vau[a_][:, :, 128:130], 1.0), W=[t_vau[a_]])
    P.op("pool", lambda e: e.memset(Vcs[:, :, :, 128:130], 1.0), W=[t_Vcs])
    cnt = [0]

    def fetch(cache_rows, pg, slot, gather=True):
        if gather:
            P.op("pool", lambda e: e.indirect_dma_start(out=pgf[slot], out_offset=None, in_=cache_rows,
                                                        in_offset=bass.IndirectOffsetOnAxis(ap=idx[:, pg:pg + 1], axis=0)),
                 R=[t_idx], W=[t_pgf[slot]], dma=True)
        else:
            P.op("sp", lambda e: e.dma_start(out=pgf[slot], in_=cache_rows[pg * 128:(pg + 1) * 128, :]), W=[t_pgf[slot]], dma=True)
        P.op("act", lambda e: e.copy(pgk[slot], pgf[slot][:, 0:256]), R=[t_pgf[slot]], W=[t_pgk[slot]])
        P.op("dve", lambda e: e.tensor_copy(vau[slot][:, :, 0:128], pgf[slot][:, 256:512].rearrange("p (h d) -> p h d", h=2)),
             R=[t_pgf[slot]], W=[t_vau[slot]])

    cmp_rows = k.cache_cmp.rearrange("l r c -> (l r) c")
    for pg in range(128):
        sl_ = pg % 4

        def one(pg=pg, sl_=sl_):
            fetch(cmp_rows, pg, sl_)
            for kvh in range(2):
                P.op("pe", lambda e, kvh=kvh: e.matmul(pb[0][:, kvh * 256 + 2 * pg:kvh * 256 + 2 * pg + 2], pgk[sl_][:, kvh * 128:(kvh + 1) * 128],
                                                       WF[:, kvh, 0, 0:2], start=True, stop=True), R=[t_pgk[sl_], t_WF], W=[tpb[0]])
                P.op("pe", lambda e, kvh=kvh: e.matmul(pb[1][:, kvh * 256 + 2 * pg:kvh * 256 + 2 * pg + 2], vau[sl_][:, kvh, 0:128],
                                                       WF[:, kvh, 0, 0:2], start=True, stop=True), R=[t_vau[sl_], t_WF], W=[tpb[1]])
        one()
    P.op("act", lambda e: e.copy(KcTs, pb[0][:, :].rearrange("p (h n) -> p h n", h=2)), R=[tpb[0]], W=[t_KcTs])
    P.op("act", lambda e: e.copy(VcTs, pb[1][:, :].rearrange("p (h n) -> p h n", h=2)), R=[tpb[1]], W=[t_VcTs])
    ptv = pb[2][:, 0:256].bitcast(BF16).rearrange("p (a b) -> p a b", a=4)
    for kvh in range(2):
        for t in range(2):
            P.op("pe", lambda e, kvh=kvh, t=t: e.transpose(ptv[:, kvh * 2 + t, :], VcTs[:, kvh, t * 128:(t + 1) * 128], k.identb[:, :]),
                 R=[t_VcTs, k.t_ident], W=[tpb[2]])
    for kvh in range(2):
        for t in range(2):
            P.op("act", lambda e, kvh=kvh, t=t: e.copy(Vcs[:, t, kvh, 0:128], ptv[:, kvh * 2 + t, :]), R=[tpb[2]], W=[t_Vcs])
    for kvh in range(2):
        P.op("pe", lambda e, kvh=kvh: e.matmul(pb[3][0:G, kvh * 256:(kvh + 1) * 256], qTs[:, kvh * 4:(kvh + 1) * 4, :], KcTs[:, kvh, :], start=True, stop=True),
             R=[t_qTs, t_KcTs], W=[tpb[3]])
    P.op("pool", lambda e: e.memset(sm2[0:G, :], 0.0), W=[t_sm2])
    for kvh in range(2):
        P.op("act", lambda e, kvh=kvh: e.activation(ef[0:G, kvh, :], pb[3][0:G, kvh * 256:(kvh + 1) * 256], AF.Exp, scale=ATTN_SCALE,
                                                    accum_out=sm2[0:G, kvh:kvh + 1]), R=[tpb[3], t_sm2], W=[t_ef, t_sm2])
    P.op("act", lambda e: e.copy(eb[0:G, :, :], ef[0:G, :, :]), R=[t_ef], W=[t_eb])
    P.op("dve", lambda e: e.reciprocal(sm2[0:G, :], sm2[0:G, :]), R=[t_sm2], W=[t_sm2])
    P.op("dve", lambda e: e.tensor_tensor(pf[0:G, :, :], ef[0:G, :, :], sm2[0:G, :].unsqueeze(2).to_broadcast([G, 2, 256]), ALU.mult), R=[t_ef, t_sm2], W=[t_pf])
    P.op("pe", lambda e: e.matmul(pb[3][0:SS, :], sum16[0:G, :], pf[0:G, :, :].rearrange("p h n -> p (h n)"), start=True, stop=True),
         R=[t_sum16, t_pf, t_ef], W=[tpb[3]])
    P.op("act", lambda e: e.copy(impS[0:SS, :, :], pb[3][0:SS, :].rearrange("p (h n) -> p h n", h=2)), R=[tpb[3]], W=[t_impS])
    P.op("dve", lambda e: e.tensor_copy(tt_[0:SS, :, :], impS[0:SS, :, 1:256]), R=[t_impS], W=[t_tt])
    for it in range(14):
        P.op("dve", lambda e: e.reduce_max(mx[0:SS, :], tt_[0:SS, :, :], AX.X), R=[t_tt], W=[t_mx])
        if it < 13:
            P.op("dve", lambda e: e.tensor_tensor(msk[0:SS, :, :], tt_[0:SS, :, :], mx[0:SS, :].unsqueeze(2).to_broadcast([SS, 2, 255]), ALU.is_ge),
                 R=[t_tt, t_mx], W=[t_msk])
            P.op("dve", lambda e: e.scalar_tensor_tensor(tt_[0:SS, :, :], msk[0:SS, :, :], -1e30, tt_[0:SS, :, :], ALU.mult, ALU.add),
                 R=[t_msk, t_tt], W=[t_tt])
    P.op("dve", lambda e: e.tensor_tensor(negS[0:SS, :, :], impS[0:SS, :, :], mx[0:SS, :].unsqueeze(2).to_broadcast([SS, 2, 256]), ALU.is_lt),
         R=[t_impS, t_mx], W=[t_negS])
    P.op("dve", lambda e: e.tensor_scalar(negS[0:SS, :, :], negS[0:SS, :, :], NEGM, None, ALU.mult), R=[t_negS], W=[t_negS])
    P.op("pool", lambda e: e.memset(negS[0:SS, :, 0:1], 0.0), R=[t_negS], W=[t_negS])
    P.op("pe", lambda e: e.matmul(pb[3][0:G, :], sum16T[0:SS, :], negS[0:SS, :, :].rearrange("p h n -> p (h n)"), start=True, stop=True),
         R=[t_sum16T, t_negS, t_impS], W=[tpb[3]])
    P.op("act", lambda e: e.copy(neg16[0:G, :, :], pb[3][0:G, :].rearrange("p (h n) -> p h n", h=2)), R=[tpb[3]], W=[t_neg16])

    state = {"first": [True, True]}

    def pv(kvh, lhsT_ap, t_l, rhs_ap, t_r, last):
        first = state["first"][kvh]
        state["first"][kvh] = False
        P.op("pe", lambda e: e.matmul(pb[6 + kvh][0:G, 0:130], lhsT_ap, rhs_ap, start=first, stop=last), R=[t_l, t_r], W=[tpb[6 + kvh]])

    def finish(x):
        for kvh in range(2):
            P.op("act", lambda e, kvh=kvh: e.copy(O16[0:G, kvh, :], pb[6 + kvh][0:G, 0:130]), R=[tpb[6 + kvh]], W=[t_O16])
        for h in range(8):
            kvh, g = h // 4, h % 4
            bk, col = h // 3, (h % 3) * 130
            P.op("pe", lambda e, kvh=kvh, g=g, bk=bk, col=col: e.matmul(pb[bk][0:SS, col:col + 130], k.identf[0:G, g * 4:(g + 1) * 4], O16[0:G, kvh, :],
                                                                      start=True, stop=True), R=[t_O16, k.t_cst], W=[tpb[bk]])
        for bk, (h0, h1) in enumerate(((0, 3), (3, 6), (6, 8))):
            nh = h1 - h0
            P.op("act", lambda e, bk=bk, h0=h0, h1=h1, nh=nh: e.copy(O4[x][0:SS, h0:h1, :], pb[bk][0:SS, 0:nh * 130].rearrange("p (h d) -> p h d", h=nh)),
                 R=[tpb[bk]], W=[t_O4[x]])
        state["first"] = [True, True]

    def key_group(slots, mask_fn, has_more):
        ktv = pb[2][:, :].bitcast(BF16).rearrange("p (h a b) -> p h a b", h=2, a=4)
        for kvh in range(2):
            for j, sl_ in enumerate(slots):
                P.op("pe", lambda e, kvh=kvh, j=j, sl_=sl_: e.transpose(ktv[:, kvh, j, :], pgk[sl_][:, kvh * 128:(kvh + 1) * 128], k.identb[:, :]),
                     R=[t_pgk[sl_], k.t_ident], W=[tpb[2]])
        P.op("act", lambda e: e.copy(KT, pb[2][:, :].bitcast(BF16).rearrange("p (h n) -> p h n", h=2)), R=[tpb[2]], W=[t_KT])
        ptp = pb[5][:, 0:64].bitcast(BF16).rearrange("p (a b) -> p a b", a=8)
        for kvh in range(2):
            P.op("pe", lambda e, kvh=kvh: e.matmul(pb[3 + kvh][0:G, :], qrTs[:, kvh * 4:(kvh + 1) * 4, :], KT[:, kvh, :], start=True, stop=True),
                 R=[t_qrTs, t_KT], W=[tpb[3 + kvh]])
            mask_fn(kvh)
            P.op("act", lambda e: e.activation(Pb[0:G, :], Sm[0:G, :], AF.Exp, scale=ATTN_SCALE), R=[t_Sm], W=[t_Pb])
            for j in range(4):
                P.op("pe", lambda e, kvh=kvh, j=j: e.transpose(ptp[:, kvh * 4 + j, :], Pb[0:G, j * 128:(j + 1) * 128], k.identb[0:G, 0:G]),
                     R=[t_Pb, k.t_ident], W=[tpb[5]])
        P.op("act", lambda e: e.copy(PTs, ptp), R=[tpb[5]], W=[t_PTs])
        for kvh in range(2):
            for j, sl_ in enumerate(slots):
                pv(kvh, PTs[:, kvh * 4 + j, :], t_PTs, vau[sl_][:, kvh, :], t_vau[sl_], False)

    def new_rows(x):
        ptn = pb[5][:, 0:16].bitcast(BF16).rearrange("p (a b) -> p a b", a=2)
        for kvh in range(2):
            P.op("pe", lambda e, kvh=kvh: e.matmul(pb[3 + kvh][0:G, 0:SS], qrTs[:, kvh * 4:(kvh + 1) * 4, :], knT[:, x, kvh, :], start=True, stop=True),
                 R=[t_qrTs, t_knT], W=[tpb[3 + kvh]])
            P.op("dve", lambda e, kvh=kvh: e.tensor_tensor(Sm[0:G, 0:SS], pb[3 + kvh][0:G, 0:SS], negnew[0:G, :], ALU.add), R=[tpb[3 + kvh], t_negnew], W=[t_Sm])
            P.op("act", lambda e: e.activation(Pb[0:G, 0:SS], Sm[0:G, 0:SS], AF.Exp, scale=ATTN_SCALE), R=[t_Sm], W=[t_Pb])
            P.op("pe", lambda e, kvh=kvh: e.transpose(ptn[0:SS, kvh, :], Pb[0:G, 0:SS], k.identb[0:G, 0:G]), R=[t_Pb, k.t_ident], W=[tpb[5]])
        P.op("act", lambda e: e.copy(PTs[0:SS, 0:2, :], ptn[0:SS, :, :]), R=[tpb[5]], W=[t_PTs])
        for kvh in range(2):
            pv(kvh, PTs[0:SS, kvh, :], t_PTs, vnew[0:SS, x, kvh, :], t_vnew, True)

    ptc = pb[5][:, 0:32].bitcast(BF16).rearrange("p (a b) -> p a b", a=4)
    for kvh in range(2):
        for t in range(2):
            P.op("pe", lambda e, kvh=kvh, t=t: e.transpose(ptc[:, kvh * 2 + t, :], eb[0:G, kvh, t * 128:(t + 1) * 128], k.identb[0:G, 0:G]),
                 R=[t_eb, k.t_ident], W=[tpb[5]])
    P.op("act", lambda e: e.copy(PTs[:, 0:4, :], ptc), R=[tpb[5]], W=[t_PTs])
    for kvh in range(2):
        for t in range(2):
            pv(kvh, PTs[:, kvh * 2 + t, :], t_PTs, Vcs[:, t, kvh, :], t_Vcs, t == 1)
    finish(0)

    sel_rows = k.cache_sel.rearrange("l r c -> (l r) c")
    for grp in range(32):
        def sel_grp(grp=grp):
            for j in range(4):
                fetch(sel_rows, grp * 4 + j, j)

            def mask_fn(kvh):
                P.op("dve", lambda e: e.tensor_tensor(Sm[0:G, :].rearrange("p (b c) -> p b c", b=8), pb[3 + kvh][0:G, :].rearrange("p (b c) -> p b c", b=8),
                                                      neg16[0:G, kvh, grp * 8:(grp + 1) * 8].unsqueeze(2).to_broadcast([G, 8, 64]), ALU.add),
                     R=[tpb[3 + kvh], t_neg16], W=[t_Sm])
            key_group([0, 1, 2, 3], mask_fn, True)
        sel_grp()
    new_rows(0)
    finish(1)

    win_rows = k.cache_win[i]
    for j in range(4):
        fetch(win_rows, j, j, gather=False)

    def mask_win(kvh):
        P.op("dve", lambda e: e.tensor_tensor(Sm[0:G, :], pb[3 + kvh][0:G, :], negwin[0:G, :], ALU.add), R=[tpb[3 + kvh], t_negwin], W=[t_Sm])
    key_group([0, 1, 2, 3], mask_win, True)
    new_rows(1)
    finish(2)

    attn_combine(k, SS, O4[0], O4[1], O4[2], t_O4, gate[0:SS, 16, :], t_gate, s3, t_s3, tmpb, t_tmpb, bout, t_bout)
    ptb_ = pb[5][:, 0:16].bitcast(BF16).rearrange("p (a b) -> p a b", a=8)
    for h in range(8):
        P.op("pe", lambda e, h=h: e.transpose(ptb_[:, h, :], bout[0:SS, h * 128:(h + 1) * 128], k.identb[0:SS, 0:SS]), R=[t_bout, k.t_ident], W=[tpb[5]])
    P.op("act", lambda e: e.copy(moS[:, 8:16, :], ptb_), R=[tpb[5]], W=[t_moS])


def _relayout_ffn(a, last):
    d = a.shape[0]
    return np.ascontiguousarray(a.reshape(d, last, 88, 128).transpose(0, 3, 2, 1))


def _relayout_ch(a, nchunk):
    sh = a.shape
    T = sh[-2]
    b = a.reshape(sh[:-2] + (T, nchunk, 128))
    nd = b.ndim
    perm = tuple(range(nd - 3)) + (nd - 1, nd - 2, nd - 3)
    return np.ascontiguousarray(b.transpose(perm))


def make_core_inputs(inp, c):
    ident = np.eye(128, dtype=np.float32)
    jj, ii = np.meshgrid(np.arange(128), np.arange(128), indexing="ij")
    utri = (jj <= ii).astype(np.float32)
    negtriT = np.where(jj <= ii, 0.0, -30000.0).astype(np.float32)
    cst4 = np.stack([ident, utri, negtriT, np.ones((128, 128), np.float32)])
    d = {
        "x_p": np.ascontiguousarray(inp["x_prompt"][c % 4]),
        "x_s": np.ascontiguousarray(inp["x_sample"][c]),
        "norm_mix": inp["norm_mix"], "norm_ffn": inp["norm_ffn"],
        "ffn_w_up": inp["ffn_w_up"], "ffn_w_down": inp["ffn_w_down"],
        "ffn_cw": _relayout_ch(inp["ffn_conv_w"], 88),
        "ffn_cb": np.ascontiguousarray(_relayout_ch(inp["ffn_conv_b"][:, None, :], 88)[..., 0]),
        "ffn_st": _relayout_ch(inp["state_ffn_conv"][:, c], 88),
        "ident": ident.astype(ml_dtypes.bfloat16), "cst4": cst4,
        "w_in_odd": inp["w_in_odd"], "w_out_odd": inp["w_out_odd"],
        "ssm_cw": _relayout_ch(inp["ssm_conv_w"], 48),
        "ssm_cb": np.ascontiguousarray(_relayout_ch(inp["ssm_conv_b"][:, None, :], 48)[..., 0]),
        "ssm_cst": _relayout_ch(inp["state_ssm_conv"][:, c], 48),
        "ssm_st": np.ascontiguousarray(inp["state_ssm"][:, c].reshape(2, 4096, 128)),
        "ssm_dt_bias": inp["ssm_dt_bias"], "ssm_a_log": inp["ssm_a_log"], "ssm_d": inp["ssm_d"], "ssm_norm": inp["ssm_norm"],
        "w_in_even": inp["w_in_even"], "w_out_even": inp["w_out_even"], "gmlp_v_norm": inp["gmlp_v_norm"],
        "gmlp_wsT": np.ascontiguousarray(inp["gmlp_ws"].transpose(0, 3, 1, 2)), "gmlp_bs": inp["gmlp_bs"],
        "q_norm": inp["q_norm"], "k_norm": inp["k_norm"], "cmp_pool": inp["cmp_pool"],
        "cache_cmp": inp["cache_kv_cmp"].reshape(2, 1280 * 128, 512), "cache_sel": inp["cache_kv_sel"].reshape(2, 1280 * 128, 512),
        "cache_win": np.ascontiguousarray(inp["cache_kv_win"][:, c].reshape(2, 512, 512)),
        "ptab": np.ascontiguousarray(inp["page_table"][c:c + 1]).astype(np.int32),
    }
    d.update(_CONSTS())
    return d


_CC = {}


def _CONSTS():
    if _CC:
        return _CC
    bf = ml_dtypes.bfloat16
    pos = np.concatenate([np.arange(2048), 16384 + np.arange(4), np.zeros(124)]).astype(np.float32)
    inv = (10000.0 ** (-np.arange(64, dtype=np.float32) / 64)).astype(np.float32)
    ang = pos[:, None] * inv[None, :]
    _CC["rope_cos"] = np.ascontiguousarray(np.cos(ang).astype(np.float32).reshape(17, 128, 64).transpose(1, 0, 2))
    _CC["rope_sin"] = np.ascontiguousarray(np.sin(ang).astype(np.float32).reshape(17, 128, 64).transpose(1, 0, 2))
    p = np.arange(128)
    ecmp = np.zeros((128, 16, 32), np.float32)
    for tt in range(16):
        ecmp[p, tt, 2 * tt + p // 64] = 1.0
    _CC["c_ecmp"] = ecmp.astype(bf)
    ebig = np.zeros((64, 2048), np.float32)
    kk = np.arange(2048)
    for h in range(2):
        ebig[h * 32 + kk // 64, kk] = 1.0
    _CC["c_ebig"] = ebig.astype(bf)
    jj, ii = np.meshgrid(np.arange(128), np.arange(128), indexing="ij")
    negc = np.where(jj > ii, -30000.0, 0.0)
    negu = np.where(jj < ii, -30000.0, 0.0)
    _CC["c_negb"] = np.stack([negc, negu]).astype(bf)
    q = np.arange(2048)
    n = np.arange(32)
    valid = (64 * n[None, :] + 63 <= q[:, None]).astype(np.float32)
    cur = q // 64
    forced = (n[None, :] == 0) | (n[None, :] == cur[:, None])
    future = n[None, :] > cur[:, None]
    keep = (~(forced | future)).astype(np.float32)
    add = np.where(forced, 1e4, np.where(future, -1e30, 0.0)).astype(np.float32)
    t3 = lambda a: np.ascontiguousarray(a.reshape(16, 128, 32).transpose(1, 0, 2))
    _CC["c_valid"], _CC["c_keep"], _CC["c_add"] = t3(valid), t3(keep), t3(add)
    _CC["c_validT"] = np.ascontiguousarray(valid.reshape(16, 128, 32).transpose(2, 0, 1)).astype(bf)
    _CC["c_iota"] = np.arange(128, dtype=np.float32)[:, None]
    r = np.arange(16)
    _CC["c_sum16"] = (r[:, None] % 4 == np.arange(4)[None, :]).astype(np.float32)
    _CC["c_sum16T"] = np.ascontiguousarray(_CC["c_sum16"].T)
    _CC["c_negnew"] = np.where(np.arange(4)[None, :] > (r % 4)[:, None], -30000.0, 0.0).astype(np.float32)
    _CC["c_negwin"] = np.where(np.arange(512)[None, :] < (r % 4)[:, None], -30000.0, 0.0).astype(np.float32)
    return _CC


_NC_CACHE = {}


def kernel(**inputs):
    inp = {n: np.asarray(v) for n, v in inputs.items()}
    if "nc" not in _NC_CACHE:
        _NC_CACHE["nc"] = build({"layers": DEPTH})
    nc = _NC_CACHE["nc"]
    in_maps = [make_core_inputs(inp, c) for c in range(8)]
    res = run_bass_kernel_spmd(nc, in_maps, core_ids=list(range(8)))
    r = res.results
    f32 = np.float32
    P4, S8 = range(4), range(8)
    st = lambda key, rng, fn=(lambda a: a): np.stack([fn(np.asarray(r[c][key])) for c in rng], axis=1).astype(f32)
    y_p = np.stack([r[b]["y_p"] for b in P4]).astype(f32)
    y_s = np.stack([r[c]["y_s"] for c in S8]).astype(f32)
    unch = lambda a: a.transpose(0, 3, 2, 1).reshape(a.shape[0], a.shape[3], a.shape[2] * 128)
    rows = lambda a: a.reshape(a.shape[0], a.shape[1], 2, 2, 128)
    sst = lambda a: a.reshape(2, 64, 64, 128)
    return (y_p, y_s,
            st("o_pkc", P4, rows), st("o_pks", P4, rows), st("o_pkw", P4, rows),
            st("o_psc", P4, unch), st("o_pst", P4, sst), st("o_pfc", P4, unch),
            st("o_skc", S8, rows), st("o_sks", S8, rows), st("o_skw", S8, rows), st("o_sv", S8),
            st("o_ssc", S8, unch), st("o_sst", S8, sst), st("o_sfc", S8, unch))
```

```python
import numpy as np
import ml_dtypes
import concourse.bass as bass
import concourse.mybir as mybir
from concourse.bass_utils import run_bass_kernel_spmd
from contextlib import ExitStack

F32 = mybir.dt.float32
BF16 = mybir.dt.bfloat16
I32 = mybir.dt.int32
ALU = mybir.AluOpType
AF = mybir.ActivationFunctionType
AX = mybir.AxisListType

ENGS = ("pe", "act", "dve", "pool", "sp")
SAME_SYNC = {"pe": False, "act": True, "dve": True, "pool": True, "sp": False}
EPOCH = 12000
N_DMA_SEMS = 48

D = 2048
L = 2048
SS = 4
DFF = 5632
NCH = 44
NXC = 48
DEPTH = 4
EPS = 1e-6


class Tok:
    __slots__ = ("name", "lw", "rd")

    def __init__(self, name=""):
        self.name = name
        self.lw = None
        self.rd = []


class Op:
    __slots__ = ("eng", "fn", "waits", "sig", "idx", "dma", "sem", "val")


class Prog:
    def __init__(self, nc, es):
        self.nc = nc
        self.es = es
        self.ops = {e: [] for e in ENGS}
        self.nsig = {e: 0 for e in ENGS}
        self.esems = {e: [] for e in ENGS}
        self.dsems = [es.enter_context(nc.semaphore("dma%d" % i)) for i in range(N_DMA_SEMS)]
        self.dval = [0] * N_DMA_SEMS
        self.dnext = 0
        self.waited = {e: {} for e in ENGS}

    def _esem(self, eng, epoch):
        l = self.esems[eng]
        while len(l) <= epoch:
            l.append(self.es.enter_context(self.nc.semaphore("e_%s_%d" % (eng, len(l)))))
        return l[epoch]

    def op(self, eng, fn, R=(), W=(), sig=True, dma=False):
        o = Op()
        o.eng = eng
        o.fn = fn
        o.dma = dma
        o.sig = sig and not dma
        o.idx = len(self.ops[eng])
        deps = []
        for t in R:
            if t.lw is not None:
                deps.append(t.lw)
        for t in W:
            deps.extend(t.rd)
            if t.lw is not None:
                deps.append(t.lw)
        waits = []
        wd = self.waited[eng]
        for d in deps:
            if d.dma:
                s, v = d.sem, d.val
            else:
                if d.eng == eng and not SAME_SYNC[eng]:
                    continue
                if not d.sig and d.eng == eng:
                    continue
                if not d.sig:
                    lst = self.ops[d.eng]
                    j = d.idx
                    while j < len(lst) and not lst[j].sig:
                        j += 1
                    assert j < len(lst), "dependency on unsignaled op with no later signal"
                    d = lst[j]
                s, v = d.sem, d.val
            if wd.get(s, 0) >= v:
                continue
            wd[s] = v
            waits.append((s, v))
        if dma:
            k = self.dnext
            self.dnext = (self.dnext + 1) % N_DMA_SEMS
            prev = self.dval[k]
            s = self.dsems[k]
            if prev > 0 and wd.get(s, 0) < prev:
                wd[s] = prev
                waits.append((s, prev))
            self.dval[k] = prev + 16
            o.sem = s
            o.val = prev + 16
        elif o.sig:
            n = self.nsig[eng]
            self.nsig[eng] = n + 1
            o.sem = self._esem(eng, n // EPOCH)
            o.val = n % EPOCH + 1
        o.waits = waits
        self.ops[eng].append(o)
        for t in R:
            t.rd.append(o)
        for t in W:
            t.lw = o
            t.rd = []
        return o

    def inherit(self, new_toks, old_toks):
        ops = []
        for t in old_toks:
            ops.extend(t.rd)
            if t.lw is not None:
                ops.append(t.lw)
        for t in new_toks:
            t.rd = list(ops)
            t.lw = None

    def claim(self, region, new_toks):
        if not hasattr(self, "regions"):
            self.regions = {}
        old = self.regions.get(region, [])
        if old is not new_toks:
            self.inherit(new_toks, old)
        self.regions[region] = new_toks

    def emit(self):
        nc = self.nc
        finals = [(self.dsems[k], self.dval[k]) for k in range(N_DMA_SEMS) if self.dval[k] > 0]

        def run(e, name):
            for o in self.ops[name]:
                for (s, v) in o.waits:
                    e.wait_ge(s, v)
                ins = o.fn(e)
                if o.dma:
                    ins.then_inc(o.sem, 16)
                elif o.sig:
                    ins.then_inc(o.sem, 1)

        with nc.Block() as blk:
            @blk.tensor
            def _(e):
                run(e, "pe")

            @blk.scalar
            def _(e):
                run(e, "act")

            @blk.vector
            def _(e):
                run(e, "dve")

            @blk.gpsimd
            def _(e):
                run(e, "pool")

            @blk.sync
            def _(e):
                run(e, "sp")
                for (s, v) in finals:
                    e.wait_ge(s, v)


class K:
    pass


class Region:
    def __init__(self, P, name, t, words):
        self.P, self.name, self.t, self.words = P, name, t, words
        self.off = 0
        self.toks = []

    def reset(self):
        self.off = 0
        self.toks = []

    def alloc(self, shape, dtype, name=""):
        n = 1
        for d in shape:
            n *= d
        w = n if dtype == F32 or dtype == I32 else (n + 1) // 2
        self.off = (self.off + 7) // 8 * 8
        assert self.off + w <= self.words, "region %s overflow (%d + %d > %d)" % (self.name, self.off, w, self.words)
        ap = self.t[:, self.off:self.off + w]
        self.off += w
        if dtype == BF16:
            ap = ap.bitcast(BF16)[:, 0:n]
        elif dtype == I32:
            ap = ap.bitcast(I32)
        if len(shape) == 2:
            ap = ap.rearrange("p (a b) -> p a b", a=shape[0])
        elif len(shape) == 3:
            ap = ap.rearrange("p (a b c) -> p a b c", a=shape[0], b=shape[1])
        t = Tok(name)
        self.toks.append(t)
        return ap, t

    def commit(self):
        self.P.claim(self.name, self.toks)


def build(cfg):
    nc = bass.Bass("TRN2", target_bir_lowering=False)
    k = K()
    k.nc = nc
    k.cfg = cfg
    din = lambda n, s, d=F32: nc.dram_tensor(n, list(s), d, kind="ExternalInput").ap()
    dout = lambda n, s, d=F32: nc.dram_tensor(n, list(s), d, kind="ExternalOutput").ap()
    dscr = lambda n, s, d=F32: nc.dram_tensor(n, list(s), d, kind=("ExternalOutput" if cfg.get("debug") else "Internal")).ap()
    k.x_p = din("x_p", [L, D])
    k.x_s = din("x_s", [SS, D])
    k.norm_mix = din("norm_mix", [DEPTH, D])
    k.norm_ffn = din("norm_ffn", [DEPTH, D])
    k.ffn_w_up = din("ffn_w_up", [DEPTH, D, 2 * DFF])
    k.ffn_w_down = din("ffn_w_down", [DEPTH, DFF, D])
    k.ffn_cw = din("ffn_cw", [DEPTH, 128, 88, 3])
    k.ffn_cb = din("ffn_cb", [DEPTH, 128, 88])
    k.ffn_st = din("ffn_st", [DEPTH, 128, 88, 2])
    k.ident = din("ident", [128, 128], BF16)
    k.cst4 = din("cst4", [4, 128, 128], F32)
    k.w_in_odd = din("w_in_odd", [2, D, 10304])
    k.w_out_odd = din("w_out_odd", [2, 4096, D])
    k.ssm_cw = din("ssm_cw", [2, 128, NXC, 4])
    k.ssm_cb = din("ssm_cb", [2, 128, NXC])
    k.ssm_cst = din("ssm_cst", [2, 128, NXC, 3])
    k.ssm_st = din("ssm_st", [2, 4096, 128])
    k.ssm_dt_bias = din("ssm_dt_bias", [2, 64])
    k.ssm_a_log = din("ssm_a_log", [2, 64])
    k.ssm_d = din("ssm_d", [2, 64])
    k.ssm_norm = din("ssm_norm", [2, 4096])
    k.o_psc = dout("o_psc", [2, 128, NXC, 3])
    k.o_ssc = dout("o_ssc", [2, 128, NXC, 3])
    k.o_pst = dout("o_pst", [2, 4096, 128])
    k.o_sst = dout("o_sst", [2, 4096, 128])
    k.xs_scr = dscr("xs_scr", [2052, 4096], BF16)
    k.bt_scr = dscr("bt_scr", [2052, 1024], BF16)
    k.bT_scr = dscr("bT_scr", [8, 128, 2052], BF16)
    k.cT_scr = dscr("cT_scr", [8, 128, 2052], BF16)
    k.zs_scr = dscr("zs_scr", [2052, 4096], BF16)
    k.yn_scr = dscr("yn_scr", [16, 128, 32, 128], BF16)
    k.w_in_even = din("w_in_even", [2, D, 4632])
    k.w_out_even = din("w_out_even", [2, 2048, D])
    k.gmlp_v_norm = din("gmlp_v_norm", [2, 1024])
    k.gmlp_wsT = din("gmlp_wsT", [2, 128, 8, 128])
    k.gmlp_bs = din("gmlp_bs", [2, 8, 128])
    k.q_norm = din("q_norm", [2, 128])
    k.k_norm = din("k_norm", [2, 3, 128])
    k.cmp_pool = din("cmp_pool", [2, 2, 64])
    k.rope_cos = din("rope_cos", [128, 17, 64])
    k.rope_sin = din("rope_sin", [128, 17, 64])
    k.c_ecmp = din("c_ecmp", [128, 16, 32], BF16)
    k.c_ebig = din("c_ebig", [64, 2048], BF16)
    k.c_negb = din("c_negb", [2, 128, 128], BF16)
    k.c_valid = din("c_valid", [128, 16, 32])
    k.c_keep = din("c_keep", [128, 16, 32])
    k.c_add = din("c_add", [128, 16, 32])
    k.c_validT = din("c_validT", [32, 16, 128], BF16)
    k.cache_cmp = din("cache_cmp", [2, 1280 * 128, 512])
    k.cache_sel = din("cache_sel", [2, 1280 * 128, 512])
    k.cache_win = din("cache_win", [2, 512, 512])
    k.ptab = din("ptab", [1, 128], I32)
    k.c_iota = din("c_iota", [128, 1])
    k.c_sum16 = din("c_sum16", [16, 4])
    k.c_sum16T = din("c_sum16T", [4, 16])
    k.c_negnew = din("c_negnew", [16, 4])
    k.c_negwin = din("c_negwin", [16, 512])
    k.o_pkc = dout("o_pkc", [2, 2048, 512])
    k.o_pks = dout("o_pks", [2, 2048, 512])
    k.o_pkw = dout("o_pkw", [2, 512, 512])
    k.o_skc = dout("o_skc", [2, SS, 512])
    k.o_sks = dout("o_sks", [2, SS, 512])
    k.o_skw = dout("o_skw", [2, SS, 512])
    k.o_sv = dout("o_sv", [2, SS, 1024])
    k.uT_scr = dscr("uT_scr", [8, 128, 2052], BF16)
    k.qT_scr = dscr("qT_scr", [8, 128, 2052], BF16)
    k.qrT_scr = dscr("qrT_scr", [8, 128, 2052], BF16)
    k.ksT_scr = dscr("ksT_scr", [2, 128, 2052], BF16)
    k.kwT_scr = dscr("kwT_scr", [2, 128, 2052], BF16)
    k.vs_scr = dscr("vs_scr", [2052, 2, 130], BF16)
    k.vw_scr = dscr("vw_scr", [2052, 2, 130], BF16)
    k.mo_scr = dscr("mo_scr", [16, 128, 16, 128], BF16)
    k.y_p = dout("y_p", [L, D])
    k.y_s = dout("y_s", [SS, D])
    k.o_pfc = dout("o_pfc", [DEPTH, 128, 88, 2])
    k.o_sfc = dout("o_sfc", [DEPTH, 128, 88, 2])
    k.act_scr = dscr("act_scr", [16, 128, NCH, 128], BF16)

    es = ExitStack()
    with es:
        P = Prog(nc, es)
        k.P = P
        sb = lambda n, s, d: es.enter_context(nc.sbuf_tensor(n, list(s), d))
        k.RX = sb("RX", [128, 16 * 2052 // 2], F32)
        k.RY = sb("RY", [128, 22528], F32)
        k.hs = sb("hs", [SS, D], F32)
        k.identb = sb("identb", [128, 128], BF16)
        k.gbc = k.RY[:, 0:2048]
        k.hbuf = [k.RY[:, 2048:4096], k.RY[:, 4096:6144]]
        k.xnb = [k.RY[:, 6144:7168].bitcast(BF16), k.RY[:, 7168:8192].bitcast(BF16)]
        k.junk = k.RY[:, 8192:9216].bitcast(BF16)
        k.ss = [sb("ss%d" % i, [128, 1], F32) for i in range(2)]
        k.RZt = sb("RZ", [128, 10240], F32)
        k.rX = Region(P, "RX", k.RX, 16 * 2052 // 2)
        k.rY = Region(P, "RY", k.RY, 22528)
        k.rZ = Region(P, "RZ", k.RZt, 10240)
        k.identf = sb("identf", [128, 128], F32)
        k.utri = sb("utri", [128, 128], F32)
        k.negtriT = sb("negtriT", [128, 128], F32)
        k.onesf = sb("onesf", [128, 128], F32)
        k.ecmp = sb("ecmp", [128, 16, 32], BF16)
        k.t_ecmp = Tok("ecmp")
        k.pb = [es.enter_context(nc.psum_tensor("pb%d" % i, [128, 512], F32)) for i in range(8)]
        k.tpb = [Tok("pb%d" % i) for i in range(8)]
        k.t_hp = [Tok("hp%d" % i) for i in range(16)]
        k.t_hs = Tok("hs")
        k.t_xnT = Tok("xnT")
        k.t_ident = Tok("ident")
        k.xnT = k.RX[:, :].bitcast(BF16).rearrange("p (a b) -> p a b", a=16)

        P.op("sp", lambda e: e.dma_start(out=k.identb[:], in_=k.ident), W=[k.t_ident], dma=True)
        k.t_cst = Tok("cst4")
        P.op("sp", lambda e: e.dma_start(out=k.ecmp[:], in_=k.c_ecmp), W=[k.t_ecmp], dma=True)
        for ii, tns in enumerate([k.identf, k.utri, k.negtriT, k.onesf]):
            P.op("sp", lambda e, ii=ii, tns=tns: e.dma_start(out=tns[:], in_=k.cst4[ii]), W=[k.t_cst], dma=True)
        for tt in range(16):
            P.op("sp", lambda e, tt=tt: e.dma_start(out=k.y_p[tt * 128:(tt + 1) * 128, :], in_=k.x_p[tt * 128:(tt + 1) * 128, :]),
                 W=[k.t_hp[tt]], dma=True)
        P.op("sp", lambda e: e.dma_start(out=k.hs[:], in_=k.x_s), W=[k.t_hs], dma=True)

        for layer in range(cfg.get("layers", DEPTH)):
            if layer % 2 == 1 and cfg.get("odd", True):
                odd_phase(k, layer)
            if layer % 2 == 0 and cfg.get("even", True):
                even_phase(k, layer)
            if cfg.get("ffn", True):
                ffn_phase(k, layer)

        P.op("sp", lambda e: e.dma_start(out=k.y_s, in_=k.hs[:]), R=[k.t_hs], dma=True)
        P.emit()
    return nc


def norm_phase(k, gain_ap):
    P = k.P
    t_g = Tok("gbc")
    t_h = [Tok("hbuf0"), Tok("hbuf1")]
    t_xnb = [Tok("xnb0"), Tok("xnb1")]
    t_ss = [Tok("ss0"), Tok("ss1")]
    t_junk = Tok("junk")
    P.claim("RX", [k.t_xnT])
    P.claim("RY", [t_g, t_junk] + t_h + t_xnb)
    P.op("sp", lambda e: e.dma_start(out=k.gbc[:, :], in_=gain_ap.to_broadcast([128, D])), W=[t_g], dma=True)
    tp = k.tpb
    inv = float(D ** -0.5)
    evs_late = []
    for tt in range(17):
        b = tt % 2
        n = 128 if tt < 16 else SS
        if tt < 16:
            src, tsrc = k.hbuf[b], t_h[b]
            P.op("sp", lambda e, tt=tt, b=b: e.dma_start(out=k.hbuf[b][:], in_=k.y_p[tt * 128:(tt + 1) * 128, :]),
                 R=[k.t_hp[tt]], W=[t_h[b]], dma=True)
        else:
            src, tsrc = k.hs, k.t_hs
        ssb, xnb = k.ss[b], k.xnb[b]
        P.op("pool", lambda e, ssb=ssb: e.memset(ssb[:], 0.0), W=[t_ss[b]])
        P.op("act", lambda e, src=src, n=n, ssb=ssb: e.activation(k.junk[0:n, :], src[0:n, :], AF.Square, scale=inv, accum_out=ssb[0:n, :]),
             R=[tsrc], W=[t_junk, t_ss[b]])
        P.op("act", lambda e, n=n, ssb=ssb: e.activation(ssb[0:n, :], ssb[0:n, :], AF.Sqrt, bias=EPS), R=[t_ss[b]], W=[t_ss[b]])
        P.op("dve", lambda e, n=n, ssb=ssb: e.reciprocal(ssb[0:n, :], ssb[0:n, :]), R=[t_ss[b]], W=[t_ss[b]])
        P.op("dve", lambda e, n=n, src=src, ssb=ssb, xnb=xnb: e.scalar_tensor_tensor(xnb[0:n, :], src[0:n, :], ssb[0:n, 0:1], k.gbc[0:n, :], ALU.mult, ALU.mult),
             R=[tsrc, t_ss[b], t_g], W=[t_xnb[b]])
        if tt < 16:
            late = list(evs_late)
            del evs_late[:]
            for q in range(4):
                bank = 4 * b + q
                pt = k.pb[bank][:, 0:256].bitcast(BF16).rearrange("p (a b) -> p a b", a=4)
                for j in range(4):
                    kc = q * 4 + j
                    P.op("pe", lambda e, pt=pt, j=j, kc=kc, xnb=xnb: e.transpose(pt[:, j, :], xnb[:, kc * 128:(kc + 1) * 128], k.identb[:]),
                         R=[t_xnb[b], k.t_ident], W=[tp[bank]], sig=(j == 3))

                def ev(q=q, bank=bank, pt=pt, tt=tt):
                    dst = k.xnT[:, q * 4:(q + 1) * 4, tt * 128:(tt + 1) * 128]
                    if q % 2 == 0:
                        P.op("act", lambda e: e.copy(dst, pt), R=[tp[bank]], W=[k.t_xnT])
                    else:
                        P.op("dve", lambda e: e.tensor_copy(dst, pt), R=[tp[bank]], W=[k.t_xnT])
                evs_late.append(ev)
            for f in late:
                f()
        else:
            bank = 4 * b
            pt = k.pb[bank][:, 0:32].bitcast(BF16).rearrange("p (a b) -> p a b", a=16)
            for kc in range(16):
                P.op("pe", lambda e, pt=pt, kc=kc, xnb=xnb: e.transpose(pt[:, kc, :], xnb[0:SS, kc * 128:(kc + 1) * 128], k.identb[0:SS, 0:SS]),
                     R=[t_xnb[b], k.t_ident], W=[tp[bank]], sig=(kc == 15))
            for f in evs_late:
                f()
            del evs_late[:]
            P.op("act", lambda e, pt=pt: e.copy(k.xnT[:, :, 2048:2052], pt), R=[tp[bank]], W=[k.t_xnT])


def ffn_phase(k, layer):
    P = k.P
    nc = k.nc
    norm_phase(k, k.norm_ffn[layer:layer + 1, :])
    rZ = k.rZ
    rZ.reset()
    k.cw, t_cw = rZ.alloc([88, 3], F32, "cw")
    k.cbias, t_cb = rZ.alloc([88], F32, "cb")
    k.cst, t_cst = rZ.alloc([88, 2], F32, "cst")
    k.stgp, t_stgp = rZ.alloc([88, 2], F32, "stgp")
    k.stgs, t_stgs = rZ.alloc([88, 2], F32, "stgs")
    k.hub = [[None, None], [None, None]]
    t_hub = [[None, None], [None, None]]
    for a in range(2):
        for b in range(2):
            k.hub[a][b], t_hub[a][b] = rZ.alloc([516], F32, "hub")
    k.cv = [None, None]
    t_cv = [None, None]
    for a in range(2):
        k.cv[a], t_cv[a] = rZ.alloc([512], F32, "cv")
    k.sg, t_sg = rZ.alloc([512], F32, "sg")
    k.aT = [None, None]
    t_aT = [None, None]
    for a in range(2):
        k.aT[a], t_aT[a] = rZ.alloc([512], BF16, "aT")
    k.actS, t_actS = rZ.alloc([NCH, SS], BF16, "actS")
    rZ.commit()
    P.op("sp", lambda e: e.dma_start(out=k.cw[:], in_=k.ffn_cw[layer]), W=[t_cw], dma=True)
    P.op("sp", lambda e: e.dma_start(out=k.cbias[:], in_=k.ffn_cb[layer]), W=[t_cb], dma=True)
    P.op("sp", lambda e: e.dma_start(out=k.cst[:], in_=k.ffn_st[layer]), W=[t_cst], dma=True)
    wv = k.RY[:, :].bitcast(BF16).rearrange("p (s a b) -> p s a b", s=4, a=16)
    t_w = [Tok("wslot%d" % i) for i in range(4)]
    P.claim("RY", t_w)
    t_scr = [Tok("scr%d" % i) for i in range(16)]
    wup = k.ffn_w_up[layer].rearrange("(kc p) n -> p kc n", p=128)
    cnt = 0
    def load_w(cb):
        sg_, su_ = (2 * cb) % 4, (2 * cb + 1) % 4
        P.op("pool", lambda e, cb=cb, s=sg_: e.dma_start(out=wv[:, s, :, 0:512], in_=wup[:, :, cb * 512:(cb + 1) * 512]),
             W=[t_w[sg_]], dma=True)
        P.op("pool", lambda e, cb=cb, s=su_: e.dma_start(out=wv[:, s, :, 0:512], in_=wup[:, :, DFF + cb * 512:DFF + (cb + 1) * 512]),
             W=[t_w[su_]], dma=True)
    load_w(0)
    for cb in range(11):
        sg_, su_ = (2 * cb) % 4, (2 * cb + 1) % 4
        if cb + 1 < 11:
            load_w(cb + 1)
        for j in range(4):
            c = cb * 4 + j
            for tb in range(5):
                N = 512 if tb < 4 else SS
                c0 = tb * 512
                hb = cnt % 2
                cnt += 1
                for gu in range(2):
                    slot = sg_ if gu == 0 else su_
                    bank = (cnt % 2) * 2 + gu
                    ci = c + 44 * gu
                    for kc in range(16):
                        P.op("pe", lambda e, bank=bank, slot=slot, j=j, kc=kc, c0=c0, N=N: e.matmul(
                            k.pb[bank][:, 0:N], wv[:, slot, kc, j * 128:(j + 1) * 128], k.xnT[:, kc, c0:c0 + N],
                            start=(kc == 0), stop=(kc == 15)),
                            R=[t_w[slot], k.t_xnT], W=[k.tpb[bank]], sig=(kc == 15))
                    hub = k.hub[gu][hb]
                    th = t_hub[gu][hb]
                    if tb == 0:
                        P.op("pool", lambda e, hub=hub: e.memset(hub[:, 0:2], 0.0), W=[th])
                    elif tb < 4:
                        prev = k.hub[gu][1 - hb]
                        P.op("pool", lambda e, hub=hub, prev=prev: e.tensor_copy(hub[:, 0:2], prev[:, 512:514]),
                             R=[t_hub[gu][1 - hb]], W=[th])
                    else:
                        P.op("pool", lambda e, hub=hub, ci=ci: e.tensor_copy(hub[:, 0:2], k.cst[:, ci, :]), R=[t_cst], W=[th])
                    P.op("act", lambda e, hub=hub, bank=bank, N=N: e.copy(hub[:, 2:2 + N], k.pb[bank][:, 0:N]),
                         R=[k.tpb[bank]], W=[th])
                    cv = k.cv[gu]
                    P.op("dve", lambda e, cv=cv, hub=hub, ci=ci, N=N: e.tensor_scalar(
                        cv[:, 0:N], hub[:, 0:N], k.cw[:, ci, 0:1], k.cbias[:, ci:ci + 1], ALU.mult, ALU.add),
                        R=[th, t_cw, t_cb], W=[t_cv[gu]])
                    P.op("dve", lambda e, cv=cv, hub=hub, ci=ci, N=N: e.scalar_tensor_tensor(
                        cv[:, 0:N], hub[:, 1:1 + N], k.cw[:, ci, 1:2], cv[:, 0:N], ALU.mult, ALU.add),
                        R=[th, t_cw, t_cv[gu]], W=[t_cv[gu]])
                    P.op("dve", lambda e, cv=cv, hub=hub, ci=ci, N=N: e.scalar_tensor_tensor(
                        cv[:, 0:N], hub[:, 2:2 + N], k.cw[:, ci, 2:3], cv[:, 0:N], ALU.mult, ALU.add),
                        R=[th, t_cw, t_cv[gu]], W=[t_cv[gu]])
                    if tb == 3:
                        P.op("pool", lambda e, hub=hub, ci=ci: e.tensor_copy(k.stgp[:, ci, :], hub[:, 512:514]), R=[th], W=[t_stgp])
                    if tb == 4:
                        P.op("pool", lambda e, hub=hub, ci=ci: e.tensor_copy(k.stgs[:, ci, :], hub[:, 4:6]), R=[th], W=[t_stgs])
                P.op("act", lambda e, N=N: e.activation(k.sg[:, 0:N], k.cv[0][:, 0:N], AF.Silu), R=[t_cv[0]], W=[t_sg])
                if tb < 4:
                    ab = (c * 4 + tb) % 2
                    aT = k.aT[ab]
                    P.op("dve", lambda e, aT=aT: e.tensor_tensor(aT[:, :], k.sg[:, :], k.cv[1][:, :], ALU.mult),
                         R=[t_sg, t_cv[1]], W=[t_aT[ab]])
                    dst = k.act_scr[tb * 4:(tb + 1) * 4, :, c, :].rearrange("t p k -> p t k")
                    P.op("sp", lambda e, aT=aT, dst=dst: e.dma_start(out=dst, in_=aT[:, :].rearrange("p (t k) -> p t k", t=4)),
                         R=[t_aT[ab]], W=t_scr[tb * 4:(tb + 1) * 4], dma=True)
                else:
                    P.op("dve", lambda e, c=c: e.tensor_tensor(k.actS[:, c, :], k.sg[:, 0:SS], k.cv[1][:, 0:SS], ALU.mult),
                         R=[t_sg, t_cv[1]], W=[t_actS])
    P.op("sp", lambda e: e.dma_start(out=k.o_pfc[layer], in_=k.stgp[:]), R=[t_stgp], dma=True)
    P.op("sp", lambda e: e.dma_start(out=k.o_sfc[layer], in_=k.stgs[:]), R=[t_stgs], dma=True)
    out_proj(k, k.ffn_w_down[layer], NCH, k.act_scr, t_scr, k.actS, t_actS)


def out_proj(k, w_dram, nch, scr, t_scr, actS, t_actS, hcol=None, t_hc=None):
    P = k.P
    wd_v = k.RY[:, :].bitcast(BF16).rearrange("p (s a b) -> p s a b", s=2, a=NCH)
    t_wd = [Tok("wd0"), Tok("wd1")]
    P.claim("RY", t_wd)
    at_v = k.RX[:, 0:3 * NCH * 64].bitcast(BF16).rearrange("p (s a b) -> p s a b", s=3, a=NCH)
    t_at = [Tok("at%d" % i) for i in range(3)]
    hcol = [k.RX[:, 8704 + j * 1024:8704 + (j + 1) * 1024] for j in range(3)]
    t_hc = [Tok("hcol%d" % j) for j in range(3)]
    P.claim("RX", t_at + t_hc)
    wdn = w_dram.rearrange("(c p) n -> p c n", p=128)
    it = 0
    for dp in range(2):
        for ws in range(2):
            db = dp * 2 + ws
            P.op("pool", lambda e, db=db, ws=ws: e.dma_start(out=wd_v[:, ws, 0:nch, :], in_=wdn[:, :, db * 512:(db + 1) * 512]),
                 W=[t_wd[ws]], dma=True)
        c0 = dp * 1024
        for tt in range(17):
            b0 = 4 + 2 * (it % 2)
            it += 1
            if tt < 16:
                sl = it % 3
                P.op("sp", lambda e, tt=tt, sl=sl: e.dma_start(out=at_v[:, sl, 0:nch, :], in_=scr[tt]),
                     R=[t_scr[tt]], W=[t_at[sl]], dma=True)
                hc = hcol[sl]
                P.op("sp", lambda e, tt=tt, hc=hc, c0=c0: e.dma_start(out=hc[:, :], in_=k.y_p[tt * 128:(tt + 1) * 128, c0:c0 + 1024]),
                     R=[k.t_hp[tt]], W=[t_hc[sl]], dma=True)
                for ws in range(2):
                    bank = b0 + ws
                    for c in range(nch):
                        P.op("pe", lambda e, bank=bank, sl=sl, ws=ws, c=c: e.matmul(
                            k.pb[bank][:, :], at_v[:, sl, c, :], wd_v[:, ws, c, :], start=(c == 0), stop=(c == nch - 1)),
                            R=[t_at[sl], t_wd[ws]], W=[k.tpb[bank]], sig=(c == nch - 1))
                for ws in range(2):
                    bank = b0 + ws
                    P.op("dve", lambda e, bank=bank, hc=hc, ws=ws: e.tensor_tensor(hc[:, ws * 512:(ws + 1) * 512], k.pb[bank][:, :], hc[:, ws * 512:(ws + 1) * 512], ALU.add),
                         R=[k.tpb[bank], t_hc[sl]], W=[t_hc[sl]])
                P.op("act", lambda e, tt=tt, hc=hc, c0=c0: e.dma_start(out=k.y_p[tt * 128:(tt + 1) * 128, c0:c0 + 1024], in_=hc[:, :]),
                     R=[t_hc[sl]], W=[k.t_hp[tt]], dma=True)
            else:
                for ws in range(2):
                    bank = b0 + ws
                    db = dp * 2 + ws
                    for c in range(nch):
                        P.op("pe", lambda e, bank=bank, ws=ws, c=c: e.matmul(
                            k.pb[bank][0:SS, :], actS[:, c, :], wd_v[:, ws, c, :], start=(c == 0), stop=(c == nch - 1)),
                            R=[t_actS, t_wd[ws]], W=[k.tpb[bank]], sig=(c == nch - 1))
                    P.op("dve", lambda e, bank=bank, db=db: e.tensor_tensor(
                        k.hs[:, db * 512:(db + 1) * 512], k.pb[bank][0:SS, :], k.hs[:, db * 512:(db + 1) * 512], ALU.add),
                        R=[k.tpb[bank], k.t_hs], W=[k.t_hs])


def odd_phase(k, layer):
    P = k.P
    i = layer // 2
    norm_phase(k, k.norm_mix[layer:layer + 1, :])
    rX, rY, rZ = k.rX, k.rY, k.rZ
    rY.reset()
    rZ.reset()
    wsl, t_w = [], []
    for s_ in range(4):
        a, t = rY.alloc([16, 512], BF16, "wsl")
        wsl.append(a)
        t_w.append(t)
    wdt, t_wdt = rY.alloc([16, 64], BF16, "wdt")
    zt, t_zt = [None, None], [None, None]
    trs, t_trs = [None, None], [None, None]
    sv, t_sv = [None, None], [None, None]
    for a_ in range(2):
        zt[a_], t_zt[a_] = rY.alloc([512], BF16, "zt")
        trs[a_], t_trs[a_] = rY.alloc([4, 128], BF16, "trs")
        sv[a_], t_sv[a_] = rY.alloc([512], BF16, "sv")
    rY.commit()
    cw, t_cw = rZ.alloc([NXC, 4], F32, "cw")
    cb, t_cb = rZ.alloc([NXC], F32, "cb")
    cst, t_cst = rZ.alloc([NXC, 3], F32, "cst")
    stgp, t_stgp = rZ.alloc([NXC, 3], F32, "stgp")
    stgs, t_stgs = rZ.alloc([NXC, 3], F32, "stgs")
    hub, t_hub = [None, None], [None, None]
    for a_ in range(2):
        hub[a_], t_hub[a_] = rZ.alloc([520], F32, "hub")
    cv, t_cv = rZ.alloc([512], F32, "cv")
    dt_all, t_dt = rZ.alloc([17, 64], F32, "dt_all")
    da_all, t_da = rZ.alloc([17, 64], F32, "da_all")
    dtb, t_dtb = rZ.alloc([64], F32, "dtb")
    abc, t_abc = rZ.alloc([64], F32, "abc")
    dbc, t_dbc = rZ.alloc([64], F32, "dbc")
    tmp64, t_tmp64 = rZ.alloc([64], F32, "tmp64")
    ynS, t_ynS = rZ.alloc([32, SS], BF16, "ynS")
    cs_sb, t_cs = rZ.alloc([64], F32, "cs")
    ncs, t_ncs = rZ.alloc([64], F32, "ncs")
    ecs, t_ecs = rZ.alloc([64], F32, "ecs")
    dec, t_dec = rZ.alloc([64], F32, "dec")
    etot, t_etot = rZ.alloc([64], F32, "etot")
    ssq, t_ssq = rZ.alloc([2], F32, "ssq")
    hst, t_hst = [None, None], [None, None]
    for a_ in range(2):
        hst[a_], t_hst[a_] = rZ.alloc([128], F32, "hst")
    rZ.commit()

    P.op("sp", lambda e: e.dma_start(out=cw, in_=k.ssm_cw[i]), W=[t_cw], dma=True)
    P.op("sp", lambda e: e.dma_start(out=cb, in_=k.ssm_cb[i]), W=[t_cb], dma=True)
    P.op("sp", lambda e: e.dma_start(out=cst, in_=k.ssm_cst[i]), W=[t_cst], dma=True)
    P.op("sp", lambda e: e.dma_start(out=dtb, in_=k.ssm_dt_bias[i:i + 1, :].to_broadcast([128, 64])), W=[t_dtb], dma=True)
    P.op("sp", lambda e: e.dma_start(out=abc, in_=k.ssm_a_log[i:i + 1, :].to_broadcast([128, 64])), W=[t_abc], dma=True)
    P.op("sp", lambda e: e.dma_start(out=dbc, in_=k.ssm_d[i:i + 1, :].to_broadcast([128, 64])), W=[t_dbc], dma=True)
    P.op("act", lambda e: e.activation(abc, abc, AF.Exp), R=[t_abc], W=[t_abc])
    P.op("act", lambda e: e.mul(abc, abc, -1.0), R=[t_abc], W=[t_abc])

    win = k.w_in_odd[i].rearrange("(kc p) n -> p kc n", p=128)
    blocks = [4096 + b * 512 for b in range(12)] + [b * 512 for b in range(8)]
    t_xs_scr = [Tok("xs_scr%d" % t) for t in range(17)]
    t_bt_scr = [Tok("bt_scr%d" % t) for t in range(17)]
    t_bT_scr = [Tok("bT_scr%d" % t) for t in range(17)]
    t_cT_scr = [Tok("cT_scr%d" % t) for t in range(17)]
    t_zs_scr = [Tok("zs_scr%d" % t) for t in range(17)]

    def load_w(bi):
        sl = bi % 4
        c0 = blocks[bi]
        P.op("pool", lambda e, sl=sl, c0=c0: e.dma_start(out=wsl[sl], in_=win[:, :, c0:c0 + 512]), W=[t_w[sl]], dma=True)

    load_w(0)
    load_w(1)
    P.op("pool", lambda e: e.dma_start(out=wdt, in_=win[:, :, 10240:10304]), W=[t_wdt], dma=True)
    cnt = 0
    pending = []
    for bi in range(20):
        if bi + 2 < 20:
            load_w(bi + 2)
        sl = bi % 4
        if bi < 12:
            for j in range(4):
                c = bi * 4 + j
                for tb in range(5):
                    N = 512 if tb < 4 else SS
                    c0 = tb * 512
                    hb = cnt % 2
                    bank = cnt % 2
                    cnt += 1
                    for kc in range(16):
                        P.op("pe", lambda e, bank=bank, sl=sl, j=j, kc=kc, c0=c0, N=N: e.matmul(
                            k.pb[bank][:, 0:N], wsl[sl][:, kc, j * 128:(j + 1) * 128], k.xnT[:, kc, c0:c0 + N],
                            start=(kc == 0), stop=(kc == 15)),
                            R=[t_w[sl], k.t_xnT], W=[k.tpb[bank]], sig=(kc == 15))
                    while pending:
                        pending.pop(0)()
                    hu, th = hub[hb], t_hub[hb]
                    if tb == 0:
                        P.op("pool", lambda e, hu=hu: e.memset(hu[:, 0:3], 0.0), W=[th])
                    elif tb < 4:
                        prev = hub[1 - hb]
                        P.op("pool", lambda e, hu=hu, prev=prev: e.tensor_copy(hu[:, 0:3], prev[:, 512:515]), R=[t_hub[1 - hb]], W=[th])
                    else:
                        P.op("pool", lambda e, hu=hu, c=c: e.tensor_copy(hu[:, 0:3], cst[:, c, :]), R=[t_cst], W=[th])
                    P.op("act", lambda e, hu=hu, bank=bank, N=N: e.copy(hu[:, 3:3 + N], k.pb[bank][:, 0:N]), R=[k.tpb[bank]], W=[th])
                    P.op("dve", lambda e, hu=hu, c=c, N=N: e.tensor_scalar(cv[:, 0:N], hu[:, 0:N], cw[:, c, 0:1], cb[:, c:c + 1], ALU.mult, ALU.add),
                         R=[th, t_cw, t_cb], W=[t_cv])
                    for tap in range(1, 4):
                        P.op("dve", lambda e, hu=hu, c=c, N=N, tap=tap: e.scalar_tensor_tensor(
                            cv[:, 0:N], hu[:, tap:tap + N], cw[:, c, tap:tap + 1], cv[:, 0:N], ALU.mult, ALU.add),
                            R=[th, t_cw, t_cv], W=[t_cv])
                    if tb == 3:
                        P.op("pool", lambda e, hu=hu, c=c: e.tensor_copy(stgp[:, c, :], hu[:, 512:515]), R=[th], W=[t_stgp])
                    if tb == 4:
                        P.op("pool", lambda e, hu=hu, c=c: e.tensor_copy(stgs[:, c, :], hu[:, 4:7]), R=[th], W=[t_stgs])
                    sb_ = cnt % 2
                    svv, tsv = sv[sb_], t_sv[sb_]
                    P.op("act", lambda e, svv=svv, N=N: e.activation(svv[:, 0:N], cv[:, 0:N], AF.Silu), R=[t_cv], W=[tsv])
                    def post(c=c, tb=tb, c0=c0, N=N, svv=svv, tsv=tsv, sb_=sb_, cnt_=cnt):
                        cnt = cnt_
                        tts = list(range(tb * 4, tb * 4 + 4)) if tb < 4 else [16]
                        if c >= 32:
                            g = (c - 32) % 8
                            scr = k.bT_scr if c < 40 else k.cT_scr
                            tsc = t_bT_scr if c < 40 else t_cT_scr
                            P.op("sp", lambda e, scr=scr, g=g, svv=svv, c0=c0, N=N: e.dma_start(out=scr[g, :, c0:c0 + N], in_=svv[:, 0:N]),
                                 R=[tsv], W=[tsc[t] for t in tts], dma=True)
                        if c < 40:
                            scr = k.xs_scr if c < 32 else k.bt_scr
                            tsc = t_xs_scr if c < 32 else t_bt_scr
                            col0 = c * 128 if c < 32 else (c - 32) * 128
                            trv, ttr = trs[sb_], t_trs[sb_]
                            bank2 = 6 + (cnt % 2)
                            pt = k.pb[bank2][:, 0:256].bitcast(BF16).rearrange("p (a b) -> p a b", a=4)
                            if tb < 4:
                                for q in range(4):
                                    P.op("pe", lambda e, pt=pt, q=q, svv=svv: e.transpose(pt[:, q, :], svv[:, q * 128:(q + 1) * 128], k.identb[:, :]),
                                         R=[tsv, k.t_ident], W=[k.tpb[bank2]], sig=(q == 3))
                                P.op("act", lambda e, pt=pt, trv=trv: e.copy(trv, pt), R=[k.tpb[bank2]], W=[ttr])
                                dst = scr[c0:c0 + 512, col0:col0 + 128].rearrange("(t p) k -> p t k", p=128)
                                P.op("sp", lambda e, dst=dst, trv=trv: e.dma_start(out=dst, in_=trv), R=[ttr], W=[tsc[t] for t in tts], dma=True)
                            else:
                                P.op("pe", lambda e, pt=pt, svv=svv: e.transpose(pt[0:SS, 0, :], svv[:, 0:SS], k.identb[:, :]),
                                     R=[tsv, k.t_ident], W=[k.tpb[bank2]])
                                P.op("act", lambda e, pt=pt, trv=trv: e.copy(trv[0:SS, 0, :], pt[0:SS, 0, :]), R=[k.tpb[bank2]], W=[ttr])
                                P.op("sp", lambda e, scr=scr, col0=col0, trv=trv: e.dma_start(out=scr[2048:2048 + SS, col0:col0 + 128], in_=trv[0:SS, 0, :]),
                                     R=[ttr], W=[tsc[16]], dma=True)

                    pending.append(post)
        else:
            while pending:
                pending.pop(0)()
            zb = bi - 12
            for tt in range(17):
                T = 128 if tt < 16 else SS
                tok0 = tt * 128
                bank = cnt % 2
                zb_ = cnt % 2
                cnt += 1
                for kc in range(16):
                    P.op("pe", lambda e, bank=bank, sl=sl, kc=kc, tok0=tok0, T=T: e.matmul(
                        k.pb[bank][0:T, :], k.xnT[:, kc, tok0:tok0 + T], wsl[sl][:, kc, :], start=(kc == 0), stop=(kc == 15)),
                        R=[t_w[sl], k.t_xnT], W=[k.tpb[bank]], sig=(kc == 15))
                P.op("act", lambda e, bank=bank, zb_=zb_, T=T: e.activation(zt[zb_][0:T, :], k.pb[bank][0:T, :], AF.Silu),
                     R=[k.tpb[bank]], W=[t_zt[zb_]])
                P.op("sp", lambda e, zb_=zb_, T=T, tok0=tok0, zb=zb: e.dma_start(out=k.zs_scr[tok0:tok0 + T, zb * 512:(zb + 1) * 512], in_=zt[zb_][0:T, :]),
                     R=[t_zt[zb_]], W=[t_zs_scr[tt]], dma=True)
    P.op("sp", lambda e: e.dma_start(out=k.o_psc[i], in_=stgp), R=[t_stgp], dma=True)
    P.op("sp", lambda e: e.dma_start(out=k.o_ssc[i], in_=stgs), R=[t_stgs], dma=True)
    for tt in range(17):
        T = 128 if tt < 16 else SS
        tok0 = tt * 128
        bank = cnt % 2
        cnt += 1
        for kc in range(16):
            P.op("pe", lambda e, bank=bank, kc=kc, tok0=tok0, T=T: e.matmul(
                k.pb[bank][0:T, 0:64], k.xnT[:, kc, tok0:tok0 + T], wdt[:, kc, :], start=(kc == 0), stop=(kc == 15)),
                R=[t_wdt, k.t_xnT], W=[k.tpb[bank]], sig=(kc == 15))
        P.op("dve", lambda e, bank=bank, T=T: e.tensor_tensor(tmp64[0:T, :], k.pb[bank][0:T, 0:64], dtb[0:T, :], ALU.add),
             R=[k.tpb[bank], t_dtb], W=[t_tmp64])
        P.op("act", lambda e, T=T: e.activation(tmp64[0:T, :], tmp64[0:T, :], AF.Exp), R=[t_tmp64], W=[t_tmp64])
        P.op("act", lambda e, T=T, tt=tt: e.activation(dt_all[0:T, tt, :], tmp64[0:T, :], AF.Ln, bias=1.0), R=[t_tmp64], W=[t_dt])
        P.op("dve", lambda e, T=T, tt=tt: e.tensor_tensor(da_all[0:T, tt, :], dt_all[0:T, tt, :], abc[0:T, :], ALU.mult),
             R=[t_dt, t_abc], W=[t_da])

    stop = k.cfg.get("odd_stop", 99)
    if stop <= 1:
        return
    rX.reset()
    rY.reset()
    H, t_H = rX.alloc([4096], F32, "H")
    Hb, t_Hb = rX.alloc([4096], BF16, "Hb")
    xs, t_xs = [None, None], [None, None]
    for a_ in range(2):
        xs[a_], t_xs[a_] = rX.alloc([4096], BF16, "xs")
    xdt, t_xdt = rX.alloc([4096], BF16, "xdt")
    xdd, t_xdd = rX.alloc([4096], BF16, "xdd")
    zs, t_zs = rX.alloc([4096], BF16, "zs")
    rX.commit()
    ngbc, t_ng = rY.alloc([4096], F32, "ngbc")
    Dm2, t_Dm2, Lf2, t_Lf2, cbTs2, t_cbTs2, ynb2, t_ynb2 = [None, None], [None, None], [None, None], [None, None], [None, None], [None, None], [None, None], [None, None]
    for a_ in range(2):
        Dm2[a_], t_Dm2[a_] = rY.alloc([8, 128], F32, "Dm")
        Lf2[a_], t_Lf2[a_] = rY.alloc([8, 128], F32, "Lf")
        cbTs2[a_], t_cbTs2[a_] = rY.alloc([128], F32, "cbTs")
        ynb2[a_], t_ynb2[a_] = rY.alloc([512], BF16, "ynb")
    Mb, t_Mb = [None, None], [None, None]
    BT, t_BT, CT, t_CT, Btm, t_Btm = [None, None], [None, None], [None, None], [None, None], [None, None], [None, None]
    for a_ in range(2):
        Mb[a_], t_Mb[a_] = rY.alloc([8, 128], BF16, "Mb")
        BT[a_], t_BT[a_] = rY.alloc([8, 128], BF16, "BT")
        CT[a_], t_CT[a_] = rY.alloc([8, 128], BF16, "CT")
        Btm[a_], t_Btm[a_] = rY.alloc([1024], BF16, "Btm")
    yv, t_yv = rY.alloc([512], F32, "yv")
    tmpv, t_tmpv = rY.alloc([512], F32, "tmpv")
    junk, t_junk = rY.alloc([512], BF16, "junk")
    ynT, t_ynT = [None, None], [None, None]
    for a_ in range(2):
        ynT[a_], t_ynT[a_] = rY.alloc([4, 128], BF16, "ynT")
    rY.commit()
    P.op("sp", lambda e: e.dma_start(out=ngbc, in_=k.ssm_norm[i:i + 1, :].to_broadcast([128, 4096])), W=[t_ng], dma=True)
    t_yn_scr = [Tok("yn_scr%d" % t) for t in range(16)]
    pb, tpb = k.pb, k.tpb
    PCv = [pb[2][:, :].rearrange("p (a b) -> p a b", a=4), pb[3][:, :].rearrange("p (a b) -> p a b", a=4)]

    def state_out(dst):
        for blk in range(32):
            hs_, th_ = hst[blk % 2], t_hst[blk % 2]
            P.op("pe", lambda e, blk=blk: e.matmul(pb[7][:, 0:128], H[:, blk * 128:(blk + 1) * 128], k.identf[:, :], start=True, stop=True),
                 R=[t_H, k.t_cst], W=[tpb[7]])
            P.op("act", lambda e, hs_=hs_: e.copy(hs_, pb[7][:, 0:128]), R=[tpb[7]], W=[th_])
            P.op("sp", lambda e, blk=blk, hs_=hs_: e.dma_start(out=dst[blk * 128:(blk + 1) * 128, :], in_=hs_), R=[th_], dma=True)

    def state_in(src):
        for blk in range(32):
            hs_, th_ = hst[blk % 2], t_hst[blk % 2]
            P.op("sp", lambda e, blk=blk, hs_=hs_: e.dma_start(out=hs_, in_=src[blk * 128:(blk + 1) * 128, :]), W=[th_], dma=True)
            P.op("pe", lambda e, hs_=hs_: e.matmul(pb[7][:, 0:128], hs_, k.identf[:, :], start=True, stop=True),
                 R=[th_, k.t_cst], W=[tpb[7]])
            P.op("act", lambda e, blk=blk: e.copy(H[:, blk * 128:(blk + 1) * 128], pb[7][:, 0:128]), R=[tpb[7]], W=[t_H])
        P.op("dve", lambda e: e.tensor_copy(Hb, H), R=[t_H], W=[t_Hb])

    def chunk(tt, T):
        tok0 = tt * 128
        b = tt % 2
        x_, tx_ = xs[b], t_xs[b]
        P.op("sp", lambda e: e.dma_start(out=x_[0:T, :], in_=k.xs_scr[tok0:tok0 + T, :]), R=[t_xs_scr[tt]], W=[tx_], dma=True)
        P.op("sp", lambda e: e.dma_start(out=zs[0:T, :], in_=k.zs_scr[tok0:tok0 + T, :]), R=[t_zs_scr[tt]], W=[t_zs], dma=True)
        P.op("sp", lambda e: e.dma_start(out=Btm[b][0:T, :], in_=k.bt_scr[tok0:tok0 + T, :]), R=[t_bt_scr[tt]], W=[t_Btm[b]], dma=True)
        P.op("sp", lambda e: e.dma_start(out=BT[b][:, :, 0:T], in_=k.bT_scr[:, :, tok0:tok0 + T].rearrange("g p t -> p g t")),
             R=[t_bT_scr[tt]], W=[t_BT[b]], dma=True)
        P.op("sp", lambda e: e.dma_start(out=CT[b][:, :, 0:T], in_=k.cT_scr[:, :, tok0:tok0 + T].rearrange("g p t -> p g t")),
             R=[t_cT_scr[tt]], W=[t_CT[b]], dma=True)
        da = da_all[0:T, tt, :]
        P.op("pe", lambda e: e.matmul(pb[0][0:T, 0:64], k.utri[0:T, 0:T], da, start=True, stop=True), R=[t_da, k.t_cst], W=[tpb[0]])
        P.op("act", lambda e: e.copy(cs_sb[0:T, :], pb[0][0:T, 0:64]), R=[tpb[0]], W=[t_cs])
        P.op("act", lambda e: e.mul(ncs[0:T, :], cs_sb[0:T, :], -1.0), R=[t_cs], W=[t_ncs])
        P.op("act", lambda e: e.activation(ecs[0:T, :], cs_sb[0:T, :], AF.Exp), R=[t_cs], W=[t_ecs])
        P.op("pe", lambda e: e.matmul(pb[0][:, 64:128], k.onesf[0:T, :], da, start=True, stop=True), R=[t_da, k.t_cst, t_cs, t_ecs, t_ncs], W=[tpb[0]])
        P.op("dve", lambda e: e.tensor_tensor(dec[0:T, :], pb[0][0:T, 64:128], cs_sb[0:T, :], ALU.subtract), R=[tpb[0], t_cs], W=[t_dec])
        P.op("act", lambda e: e.activation(dec[0:T, :], dec[0:T, :], AF.Exp), R=[t_dec], W=[t_dec])
        P.op("act", lambda e: e.activation(etot, pb[0][:, 64:128], AF.Exp), R=[tpb[0]], W=[t_etot])
        x3 = x_[0:T, :].rearrange("p (h d) -> p h d", h=64)
        P.op("pool", lambda e: e.tensor_tensor(xdt[0:T, :].rearrange("p (h d) -> p h d", h=64), x3,
                                               dt_all[0:T, tt, :].unsqueeze(2).to_broadcast([T, 64, 64]), ALU.mult),
             R=[tx_, t_dt], W=[t_xdt])
        P.op("pool", lambda e: e.tensor_tensor(xdd[0:T, :].rearrange("p (h d) -> p h d", h=64), xdt[0:T, :].rearrange("p (h d) -> p h d", h=64),
                                               dec[0:T, :].unsqueeze(2).to_broadcast([T, 64, 64]), ALU.mult),
             R=[t_xdt, t_dec], W=[t_xdd])
        def gA(g):
            mb = g % 2
            Dm, t_Dm, Lf, t_Lf, cbTs, t_cbTs = Dm2[mb], t_Dm2[mb], Lf2[mb], t_Lf2[mb], cbTs2[mb], t_cbTs2[mb]
            P.op("pe", lambda e: e.matmul(pb[1][0:T, 0:T], BT[b][:, g, 0:T], CT[b][:, g, 0:T], start=True, stop=True),
                 R=[t_BT[b], t_CT[b]], W=[tpb[1]])
            P.op("act", lambda e: e.copy(cbTs[0:T, 0:T], pb[1][0:T, 0:T]), R=[tpb[1]], W=[t_cbTs])
            for e8 in range(8):
                h = g * 8 + e8
                P.op("pe", lambda e, e8=e8, h=h: e.matmul(PCv[e8 // 4][0:T, e8 % 4, 0:T], da_all[0:T, tt, h:h + 1].to_broadcast([T, T]),
                                                         k.utri[0:T, 0:T], start=True, stop=True),
                     R=[t_da, k.t_cst], W=[tpb[2 + e8 // 4]], sig=(e8 % 4 == 3))
            for half in range(2):
                P.op("dve", lambda e, half=half: e.tensor_tensor(Dm[0:T, half * 4:(half + 1) * 4, 0:T], PCv[half][0:T, :, 0:T],
                                                                 k.negtriT[0:T, 0:T].unsqueeze(1).to_broadcast([T, 4, T]), ALU.add),
                     R=[tpb[2 + half], k.t_cst], W=[t_Dm])
            for e8 in range(8):
                h = g * 8 + e8
                P.op("act", lambda e, e8=e8, h=h: e.activation(Lf[0:T, e8, 0:T], Dm[0:T, e8, 0:T], AF.Exp, bias=ncs[0:T, h:h + 1]),
                     R=[t_Dm, t_ncs], W=[t_Lf], sig=(e8 == 7))
            P.op("dve", lambda e, mb=mb: e.tensor_tensor(Mb[mb][0:T, :, 0:T], Lf[0:T, :, 0:T],
                                                         cbTs[0:T, 0:T].unsqueeze(1).to_broadcast([T, 8, T]), ALU.mult),
                 R=[t_Lf, t_cbTs], W=[t_Mb[mb]])

        def gB(g):
            mb = g % 2
            ynb, t_ynb = ynb2[mb], t_ynb2[mb]
            for e8 in range(8):
                h = g * 8 + e8
                P.op("pe", lambda e, e8=e8, h=h, mb=mb: e.matmul(pb[4][0:T, e8 * 64:(e8 + 1) * 64], Mb[mb][0:T, e8, 0:T], xdt[0:T, h * 64:(h + 1) * 64],
                                                             start=True, stop=True),
                     R=[t_Mb[mb], t_xdt], W=[tpb[4]], sig=(e8 == 7))
            P.op("pe", lambda e: e.matmul(pb[5][0:T, :], CT[b][:, g, 0:T], Hb[:, g * 512:(g + 1) * 512], start=True, stop=True),
                 R=[t_CT[b], t_Hb], W=[tpb[5]])
            P.op("pe", lambda e: e.matmul(pb[6][:, :], Btm[b][0:T, g * 128:(g + 1) * 128], xdd[0:T, g * 512:(g + 1) * 512], start=True, stop=True),
                 R=[t_Btm[b], t_xdd], W=[tpb[6]])
            Hg = H[:, g * 512:(g + 1) * 512]
            P.op("pool", lambda e: e.tensor_tensor(Hg.rearrange("p (h d) -> p h d", h=8), Hg.rearrange("p (h d) -> p h d", h=8),
                                                   etot[:, g * 8:(g + 1) * 8].unsqueeze(2).to_broadcast([128, 8, 64]), ALU.mult),
                 R=[t_H, t_etot, tpb[5]], W=[t_H])
            P.op("dve", lambda e: e.tensor_tensor(Hg, pb[6][:, :], Hg, ALU.add), R=[tpb[6], t_H], W=[t_H])
            P.op("act", lambda e: e.copy(Hb[:, g * 512:(g + 1) * 512], Hg), R=[t_H, tpb[5]], W=[t_Hb])

            y3 = yv[0:T, :].rearrange("p (h d) -> p h d", h=8)
            t3 = tmpv[0:T, :].rearrange("p (h d) -> p h d", h=8)
            P.op("dve", lambda e: e.tensor_tensor(t3, pb[5][0:T, :].rearrange("p (h d) -> p h d", h=8),
                                                  ecs[0:T, g * 8:(g + 1) * 8].unsqueeze(2).to_broadcast([T, 8, 64]), ALU.mult),
                 R=[tpb[5], t_ecs], W=[t_tmpv])
            P.op("dve", lambda e: e.tensor_tensor(yv[0:T, :], pb[4][0:T, :], tmpv[0:T, :], ALU.add), R=[tpb[4], t_tmpv], W=[t_yv])
            P.op("pool", lambda e: e.tensor_tensor(t3, x_[0:T, g * 512:(g + 1) * 512].rearrange("p (h d) -> p h d", h=8),
                                                   dbc[0:T, g * 8:(g + 1) * 8].unsqueeze(2).to_broadcast([T, 8, 64]), ALU.mult),
                 R=[tx_, t_dbc, t_yv], W=[t_tmpv])
            P.op("dve", lambda e: e.tensor_tensor(yv[0:T, :], yv[0:T, :], tmpv[0:T, :], ALU.add), R=[t_yv, t_tmpv], W=[t_yv])
            P.op("dve", lambda e: e.tensor_tensor(yv[0:T, :], yv[0:T, :], zs[0:T, g * 512:(g + 1) * 512], ALU.mult), R=[t_yv, t_zs], W=[t_yv])
            P.op("pool", lambda e: e.memset(ssq[0:T, 0:1], 0.0), W=[t_ssq])
            P.op("act", lambda e: e.activation(junk[0:T, :], yv[0:T, :], AF.Square, scale=float(512 ** -0.5), accum_out=ssq[0:T, 0:1]),
                 R=[t_yv], W=[t_junk, t_ssq])
            P.op("act", lambda e: e.activation(ssq[0:T, 0:1], ssq[0:T, 0:1], AF.Sqrt, bias=EPS), R=[t_ssq], W=[t_ssq])
            P.op("dve", lambda e: e.reciprocal(ssq[0:T, 0:1], ssq[0:T, 0:1]), R=[t_ssq], W=[t_ssq])
            P.op("dve", lambda e: e.scalar_tensor_tensor(ynb[0:T, :], yv[0:T, :], ssq[0:T, 0:1], ngbc[0:T, g * 512:(g + 1) * 512], ALU.mult, ALU.mult),
                 R=[t_yv, t_ssq, t_ng], W=[t_ynb])

        def gC(g):
            mb = g % 2
            ynb, t_ynb = ynb2[mb], t_ynb2[mb]
            ptv = pb[7][:, 0:256].bitcast(BF16).rearrange("p (a b) -> p a b", a=4)
            for q in range(4):
                P.op("pe", lambda e, q=q: e.transpose(ptv[:, q, 0:T], ynb[0:T, q * 128:(q + 1) * 128], k.identb[0:T, 0:T]),
                     R=[t_ynb, k.t_ident], W=[tpb[7]], sig=(q == 3))
            if T == 128:
                yb = g % 2
                P.op("act", lambda e, yb=yb: e.copy(ynT[yb], ptv), R=[tpb[7]], W=[t_ynT[yb]])
                P.op("sp", lambda e, yb=yb: e.dma_start(out=k.yn_scr[tt, :, g * 4:(g + 1) * 4, :], in_=ynT[yb]), R=[t_ynT[yb]], W=[t_yn_scr[tt]], dma=True)
            else:
                P.op("act", lambda e: e.copy(ynS[:, g * 4:(g + 1) * 4, :], ptv[:, :, 0:T]), R=[tpb[7]], W=[t_ynS])

        gA(0)
        for g in range(8):
            if g + 1 < 8:
                gA(g + 1)
            gB(g)
            if g >= 1:
                gC(g - 1)
        gC(7)

    P.op("pool", lambda e: e.memset(H, 0.0), W=[t_H])
    P.op("pool", lambda e: e.memset(Hb, 0.0), W=[t_Hb])
    for tt in range(16 if stop > 2 else 1):
        chunk(tt, 128)
    if stop <= 3:
        return
    state_out(k.o_pst[i])
    if stop <= 4:
        return
    state_in(k.ssm_st[i])
    chunk(16, SS)
    state_out(k.o_sst[i])
    if stop <= 5:
        return
    out_proj(k, k.w_out_odd[i], 32, k.yn_scr, t_yn_scr, ynS, t_ynS)


ATTN_SCALE = 128 ** -0.5
NEGM = -30000.0


def gelu_ops(P, dst, src_ps, tmp, R, W_dst, t_tmp, T, N):
    P.op("act", lambda e: e.activation(tmp, src_ps, AF.Square), R=R, W=[t_tmp])
    P.op("dve", lambda e: e.tensor_scalar(tmp, tmp, 0.044715, 1.0, ALU.mult, ALU.add), R=[t_tmp], W=[t_tmp])
    P.op("dve", lambda e: e.tensor_tensor(tmp, tmp, src_ps, ALU.mult), R=[t_tmp] + R, W=[t_tmp])
    P.op("act", lambda e: e.activation(tmp, tmp, AF.Sigmoid, scale=1.5957691216057308), R=[t_tmp], W=[t_tmp])
    P.op("dve", lambda e: e.tensor_tensor(dst, tmp, src_ps, ALU.mult), R=[t_tmp] + R, W=W_dst)


def even_phase(k, layer):
    P = k.P
    i = layer // 2
    pb, tpb = k.pb, k.tpb
    norm_phase(k, k.norm_mix[layer:layer + 1, :])
    rX, rY, rZ = k.rX, k.rY, k.rZ
    rY.reset()
    rZ.reset()
    wsl, t_w = [], []
    for s_ in range(4):
        a, t = rY.alloc([16, 512], BF16, "wsl")
        wsl.append(a)
        t_w.append(t)
    wgl, t_wgl = rY.alloc([16, 24], BF16, "wgl")
    vgain, t_vgain = rY.alloc([1024], F32, "vgain")
    bsbc, t_bsbc = rY.alloc([8, 128], F32, "bsbc")
    wmT, t_wmT = rY.alloc([8, 128], BF16, "wmT")
    wsT, t_wsT = rY.alloc([8, 128], F32, "wsT")
    gv, t_gv = rY.alloc([1024], F32, "gv")
    vn, t_vn = rY.alloc([1024], F32, "vn")
    rY.commit()
    vb16, t_vb16 = rZ.alloc([1024], BF16, "vb16")
    uTt, t_uTt = rZ.alloc([8, 128], BF16, "uTt")
    tS, t_tS = rZ.alloc([8, 128], F32, "tS")
    aT, t_aT = rZ.alloc([8, 128], BF16, "aT")
    raw, t_raw = [None, None], [None, None]
    for a_ in range(2):
        raw[a_], t_raw[a_] = rZ.alloc([512], F32, "raw")
    tmpf, t_tmpf = rZ.alloc([512], F32, "tmpf")
    rp, t_rp = [None] * 2, [None] * 2
    for a_ in range(2):
        rp[a_], t_rp[a_] = rZ.alloc([4, 64], F32, "rp")
    qn16, t_qn16 = rZ.alloc([512], BF16, "qn16")
    qr16, t_qr16 = rZ.alloc([512], BF16, "qr16")
    kb16, t_kb16 = rZ.alloc([2, 128], BF16, "kb16")
    va16, t_va16 = rZ.alloc([2, 130], BF16, "va16")
    trs, t_trs = [None, None], [None, None]
    for a_ in range(2):
        trs[a_], t_trs[a_] = rZ.alloc([4, 128], BF16, "trs")
    ub = [trs[a_][:, :, :].rearrange("p a b -> p (a b)") for a_ in range(2)]
    t_ub = t_trs
    cosT, t_cos = rZ.alloc([17, 64], F32, "cos")
    sinT, t_sin = rZ.alloc([17, 64], F32, "sin")
    gate, t_gate = rZ.alloc([17, 24], F32, "gate")
    st8, t_st8 = rZ.alloc([8], F32, "st8")
    qg, t_qg = rZ.alloc([128], F32, "qg")
    kg, t_kg = rZ.alloc([3, 128], F32, "kg")
    WF, t_WF = rZ.alloc([2, 16, 32], BF16, "WF")
    w2, t_w2 = rZ.alloc([2], F32, "w2")
    pl, t_pl = rZ.alloc([128], F32, "pl")
    KcT, t_KcT = rZ.alloc([2, 32], BF16, "KcT")
    Vca, t_Vca = rZ.alloc([2, 130], BF16, "Vca")
    moS, t_moS = rZ.alloc([16, SS], BF16, "moS")
    rZ.commit()
    tSflat = tS[:, :, :].rearrange("p a b -> p (a b)").bitcast(BF16)
    qn16_2 = [qn16, tSflat[:, 0:512]]
    qr16_2 = [qr16, tSflat[:, 512:1024]]
    kb16_2 = [kb16, tSflat[:, 1024:1280].rearrange("p (h d) -> p h d", h=2)]
    va16_2 = [va16, tSflat[:, 1280:1540].rearrange("p (h d) -> p h d", h=2)]
    t_qn16_2 = [t_qn16, Tok("qn16b")]
    t_qr16_2 = [t_qr16, Tok("qr16b")]
    t_kb16_2 = [t_kb16, Tok("kb16b")]
    t_va16_2 = [t_va16, Tok("va16b")]
    k.ev = dict(gate=gate, t_gate=t_gate, KcT=KcT, t_KcT=t_KcT, Vca=Vca, t_Vca=t_Vca, moS=moS, t_moS=t_moS, WF=WF, t_WF=t_WF)

    P.op("sp", lambda e: e.dma_start(out=vgain, in_=k.gmlp_v_norm[i:i + 1, :].to_broadcast([128, 1024])), W=[t_vgain], dma=True)
    P.op("sp", lambda e: e.dma_start(out=bsbc, in_=k.gmlp_bs[i:i + 1].to_broadcast([128, 8, 128])), W=[t_bsbc], dma=True)
    P.op("sp", lambda e: e.dma_start(out=wsT, in_=k.gmlp_wsT[i]), W=[t_wsT], dma=True)
    P.op("dve", lambda e: e.tensor_tensor(wmT, wsT, k.utri[:, :].unsqueeze(1).to_broadcast([128, 8, 128]), ALU.mult),
         R=[t_wsT, k.t_cst], W=[t_wmT])
    P.op("sp", lambda e: e.dma_start(out=cosT, in_=k.rope_cos), W=[t_cos], dma=True)
    P.op("sp", lambda e: e.dma_start(out=sinT, in_=k.rope_sin), W=[t_sin], dma=True)
    P.op("sp", lambda e: e.dma_start(out=qg, in_=k.q_norm[i:i + 1, :].to_broadcast([128, 128])), W=[t_qg], dma=True)
    P.op("sp", lambda e: e.dma_start(out=kg, in_=k.k_norm[i:i + 1].to_broadcast([128, 3, 128])), W=[t_kg], dma=True)
    P.op("sp", lambda e: e.dma_start(out=pl[0:2, 0:64], in_=k.cmp_pool[i]), W=[t_pl], dma=True)
    P.op("dve", lambda e: e.reduce_max(st8[0:2, 0:1], pl[0:2, 0:64], AX.X), R=[t_pl], W=[t_st8])
    P.op("act", lambda e: e.mul(st8[0:2, 0:1], st8[0:2, 0:1], -1.0), R=[t_st8], W=[t_st8])
    P.op("pool", lambda e: e.memset(st8[0:2, 1:2], 0.0), R=[t_st8], W=[t_st8])
    P.op("act", lambda e: e.activation(pl[0:2, 0:64], pl[0:2, 0:64], AF.Exp, bias=st8[0:2, 0:1], accum_out=st8[0:2, 1:2]), R=[t_pl, t_st8], W=[t_pl, t_st8])
    P.op("dve", lambda e: e.reciprocal(st8[0:2, 1:2], st8[0:2, 1:2]), R=[t_st8], W=[t_st8])
    P.op("dve", lambda e: e.tensor_scalar(pl[0:2, 0:64], pl[0:2, 0:64], st8[0:2, 1:2], None, ALU.mult), R=[t_pl, t_st8], W=[t_pl])
    P.op("dve", lambda e: e.tensor_copy(pl[0:2, 64:128], pl[0:2, 0:64]), R=[t_pl], W=[t_pl])
    P.op("pe", lambda e: e.matmul(pb[7][:, 0:2], pl[0:2, :], k.identf[0:2, 0:2], start=True, stop=True), R=[t_pl, k.t_cst], W=[tpb[7]])
    P.op("act", lambda e: e.copy(w2, pb[7][:, 0:2]), R=[tpb[7]], W=[t_w2])
    for h in range(2):
        P.op("dve", lambda e, h=h: e.tensor_scalar(WF[:, h, :, :], k.ecmp[:, :, :], w2[:, h:h + 1], None, ALU.mult), R=[t_w2, k.t_ecmp], W=[t_WF])

    win = k.w_in_even[i].rearrange("(kc p) n -> p kc n", p=128)
    blocks = [b * 512 for b in range(9)]
    t_uT_scr = [Tok("uT_scr%d" % t) for t in range(17)]
    t_qT_scr = [Tok("qT_scr%d" % t) for t in range(17)]
    t_qrT_scr = [Tok("qrT_scr%d" % t) for t in range(17)]
    t_ksT_scr = [Tok("ksT_scr%d" % t) for t in range(17)]
    t_kwT_scr = [Tok("kwT_scr%d" % t) for t in range(17)]
    t_vs_scr = [Tok("vs_scr%d" % t) for t in range(17)]
    t_vw_scr = [Tok("vw_scr%d" % t) for t in range(17)]
    t_mo_scr = [Tok("mo_scr%d" % t) for t in range(16)]
    k.ev.update(t_qT_scr=t_qT_scr, t_qrT_scr=t_qrT_scr, t_ksT_scr=t_ksT_scr, t_kwT_scr=t_kwT_scr, t_vs_scr=t_vs_scr,
                t_vw_scr=t_vw_scr, t_mo_scr=t_mo_scr, cosT=cosT, sinT=sinT)

    def load_w(bi):
        sl = bi % 4
        c0 = blocks[bi]
        P.op("pool", lambda e, sl=sl, c0=c0: e.dma_start(out=wsl[sl], in_=win[:, :, c0:c0 + 512]), W=[t_w[sl]], dma=True)

    load_w(0)
    load_w(1)
    P.op("pool", lambda e: e.dma_start(out=wgl, in_=win[:, :, 4608:4632]), W=[t_wgl], dma=True)
    P.op("pool", lambda e: e.memset(va16[:, :, 128:130], 1.0), W=[t_va16])
    cnt = [0]

    def proj_tm(sl, tt, bank):
        T = 128 if tt < 16 else SS
        tok0 = tt * 128
        for kc in range(16):
            P.op("pe", lambda e, kc=kc: e.matmul(pb[bank][0:T, :], k.xnT[:, kc, tok0:tok0 + T], wsl[sl][:, kc, :], start=(kc == 0), stop=(kc == 15)),
                 R=[t_w[sl], k.t_xnT], W=[tpb[bank]], sig=(kc == 15))

    def rms_heads(x3, T, nh, gain_ap, t_x, t_gain):
        sq = tmpf[0:T, 0:nh * 128].rearrange("p (h d) -> p h d", h=nh)
        P.op("dve", lambda e: e.tensor_tensor(sq, x3, x3, ALU.mult), R=[t_x], W=[t_tmpf])
        P.op("dve", lambda e: e.reduce_sum(st8[0:T, 0:nh], sq, AX.X), R=[t_tmpf], W=[t_st8])
        P.op("act", lambda e: e.activation(st8[0:T, 0:nh], st8[0:T, 0:nh], AF.Sqrt, scale=1.0 / 128, bias=EPS), R=[t_st8], W=[t_st8])
        P.op("dve", lambda e: e.reciprocal(st8[0:T, 0:nh], st8[0:T, 0:nh]), R=[t_st8], W=[t_st8])
        P.op("dve", lambda e: e.tensor_tensor(x3, x3, st8[0:T, 0:nh].unsqueeze(2).to_broadcast([T, nh, 128]), ALU.mult), R=[t_x, t_st8], W=[t_x])
        P.op("dve", lambda e: e.tensor_tensor(x3, x3, gain_ap.unsqueeze(1).to_broadcast([T, nh, 128]), ALU.mult), R=[t_x, t_gain], W=[t_x])

    def rope_ops(dst3, src3, T, nh, tt, t_dst, t_src):
        c_ = cosT[0:T, tt, :].unsqueeze(1).to_broadcast([T, nh, 64])
        s_ = sinT[0:T, tt, :].unsqueeze(1).to_broadcast([T, nh, 64])
        x1, x2 = src3[:, :, 0:64], src3[:, :, 64:128]
        a, b_ = [rp[j][0:T, 0:nh, :] for j in range(2)]
        P.op("dve", lambda e: e.tensor_tensor(a, x1, c_, ALU.mult), R=[t_src, t_cos], W=[t_rp[0]])
        P.op("pool", lambda e: e.tensor_tensor(b_, x2, s_, ALU.mult), R=[t_src, t_sin], W=[t_rp[1]])
        P.op("dve", lambda e: e.tensor_tensor(dst3[:, :, 0:64], a, b_, ALU.subtract), R=[t_rp[0], t_rp[1]], W=[t_dst])
        P.op("dve", lambda e: e.tensor_tensor(a, x2, c_, ALU.mult), R=[t_src, t_cos], W=[t_rp[0]])
        P.op("pool", lambda e: e.tensor_tensor(b_, x1, s_, ALU.mult), R=[t_src, t_sin], W=[t_rp[1]])
        P.op("dve", lambda e: e.tensor_tensor(dst3[:, :, 64:128], a, b_, ALU.add), R=[t_rp[0], t_rp[1]], W=[t_dst])

    def transp_out(src16, t_src, T, nh, dsts):
        tb_ = cnt[0] % 2
        cnt[0] += 1
        bank2 = 6 + tb_
        pt = pb[bank2][:, 0:256].bitcast(BF16).rearrange("p (a b) -> p a b", a=4)
        for h in range(nh):
            P.op("pe", lambda e, h=h: e.transpose(pt[:, h, 0:T], src16[0:T, h * 128:(h + 1) * 128], k.identb[0:T, 0:T]),
                 R=[t_src, k.t_ident], W=[tpb[bank2]], sig=(h == nh - 1))
        P.op("act", lambda e: e.copy(trs[tb_][:, 0:nh, 0:T], pt[:, 0:nh, 0:T]), R=[tpb[bank2]], W=[t_trs[tb_]])
        for h in range(nh):
            dap, toks = dsts[h]
            P.op("sp", lambda e, h=h, dap=dap: e.dma_start(out=dap, in_=trs[tb_][:, h, 0:T]), R=[t_trs[tb_]], W=toks, dma=True)

    for bi in range(9):
        if bi + 2 < 9:
            load_w(bi + 2)
        sl = bi % 4
        if bi < 2:
            for j in range(4):
                c = bi * 4 + j
                for tb in range(5):
                    N = 512 if tb < 4 else SS
                    c0 = tb * 512
                    bank = cnt[0] % 2
                    u_ = cnt[0] % 2
                    cnt[0] += 1
                    for kc in range(16):
                        P.op("pe", lambda e, bank=bank, j=j, kc=kc, c0=c0, N=N, sl=sl: e.matmul(
                            pb[bank][:, 0:N], wsl[sl][:, kc, j * 128:(j + 1) * 128], k.xnT[:, kc, c0:c0 + N], start=(kc == 0), stop=(kc == 15)),
                            R=[t_w[sl], k.t_xnT], W=[tpb[bank]], sig=(kc == 15))
                    gelu_ops(P, ub[u_][:, 0:N], pb[bank][:, 0:N], tmpf[:, 0:N], [tpb[bank]], [t_ub[u_]], t_tmpf, 128, N)
                    tts = list(range(tb * 4, tb * 4 + 4)) if tb < 4 else [16]
                    P.op("sp", lambda e, c=c, u_=u_, c0=c0, N=N: e.dma_start(out=k.uT_scr[c, :, c0:c0 + N], in_=ub[u_][:, 0:N]),
                         R=[t_ub[u_]], W=[t_uT_scr[t] for t in tts], dma=True)
        elif bi == 2:
            continue
        elif bi == 3:
            def v_tile(tt):
                T = 128 if tt < 16 else SS
                tok0 = tt * 128
                for vb in range(2):
                    proj_tm(2 + vb, tt, vb)
                    gelu_ops(P, gv[0:T, vb * 512:(vb + 1) * 512], pb[vb][0:T, :], tmpf[0:T, :], [tpb[vb]], [t_gv], t_tmpf, T, 512)
                g3 = gv[0:T, :].rearrange("p (g d) -> p g d", g=8)
                v3 = vn[0:T, :].rearrange("p (g d) -> p g d", g=8)
                P.op("dve", lambda e: e.tensor_tensor(v3, g3, g3, ALU.mult), R=[t_gv], W=[t_vn])
                P.op("dve", lambda e: e.reduce_sum(st8[0:T, 0:8], v3, AX.X), R=[t_vn], W=[t_st8])
                P.op("act", lambda e: e.activation(st8[0:T, 0:8], st8[0:T, 0:8], AF.Sqrt, scale=1.0 / 128, bias=EPS), R=[t_st8], W=[t_st8])
                P.op("dve", lambda e: e.reciprocal(st8[0:T, 0:8], st8[0:T, 0:8]), R=[t_st8], W=[t_st8])
                P.op("dve", lambda e: e.tensor_tensor(v3, g3, st8[0:T, 0:8].unsqueeze(2).to_broadcast([T, 8, 128]), ALU.mult), R=[t_gv, t_st8], W=[t_vn])
                P.op("dve", lambda e: e.tensor_tensor(vn[0:T, :], vn[0:T, :], vgain[0:T, :], ALU.mult), R=[t_vn, t_vgain], W=[t_vn])
                if tt == 16:
                    P.op("sp", lambda e: e.dma_start(out=k.o_sv[i], in_=vn[0:SS, :]), R=[t_vn], dma=True)
                P.op("act", lambda e: e.copy(vb16[0:T, :], vn[0:T, :]), R=[t_vn], W=[t_vb16])
                P.op("sp", lambda e: e.dma_start(out=uTt[:, :, 0:T], in_=k.uT_scr[:, :, tok0:tok0 + T].rearrange("g p t -> p g t")),
                     R=[t_uT_scr[tt]], W=[t_uTt], dma=True)
                PSv = [pb[2][:, :].rearrange("p (a b) -> p a b", a=4), pb[3][:, :].rearrange("p (a b) -> p a b", a=4)]
                for g in range(8):
                    P.op("pe", lambda e, g=g: e.matmul(PSv[g // 4][:, g % 4, 0:T], vb16[0:T, g * 128:(g + 1) * 128], wmT[0:T, g, 0:T], start=True, stop=True),
                         R=[t_vb16, t_wmT], W=[tpb[2 + g // 4]], sig=(g % 4 == 3))
                for hf in range(2):
                    P.op("dve", lambda e, hf=hf: e.tensor_tensor(tS[:, hf * 4:(hf + 1) * 4, 0:T], PSv[hf][:, :, 0:T], bsbc[:, hf * 4:(hf + 1) * 4, 0:T], ALU.add),
                         R=[tpb[2 + hf], t_bsbc], W=[t_tS])
                if tt < 16:
                    P.op("dve", lambda e: e.tensor_tensor(aT[:, :, :], tS[:, :, :], uTt[:, :, :], ALU.mult), R=[t_tS, t_uTt], W=[t_aT])
                    P.op("sp", lambda e: e.dma_start(out=k.mo_scr[tt, :, 0:8, :], in_=aT), R=[t_aT], W=[t_mo_scr[tt]], dma=True)
                else:
                    P.op("dve", lambda e: e.tensor_tensor(moS[:, 0:8, :], tS[:, :, 0:SS], uTt[:, :, 0:SS], ALU.mult), R=[t_tS, t_uTt], W=[t_moS])
            for tt in range(17):
                v_tile(tt)
        elif bi < 6:
            if bi == 4:
                P.inherit([t_qn16_2[1], t_qr16_2[1], t_kb16_2[1], t_va16_2[1]], [t_tS])
                P.op("pool", lambda e: e.memset(va16_2[1][:, :, 128:130], 1.0), W=[t_va16_2[1]])

            def q_tile(tt, sl, hb0):
                T = 128 if tt < 16 else SS
                tok0 = tt * 128
                bank = cnt[0] % 2
                r_ = cnt[0] % 2
                cnt[0] += 1
                qi = tt % 2
                qn, t_qn, qr, t_qr = qn16_2[qi], t_qn16_2[qi], qr16_2[qi], t_qr16_2[qi]
                proj_tm(sl, tt, bank)
                P.op("act", lambda e: e.copy(raw[r_][0:T, :], pb[bank][0:T, :]), R=[tpb[bank]], W=[t_raw[r_]])
                x3 = raw[r_][0:T, :].rearrange("p (h d) -> p h d", h=4)
                rms_heads(x3, T, 4, qg[0:T, :], t_raw[r_], t_qg)
                P.op("act", lambda e: e.copy(qn[0:T, :], raw[r_][0:T, :]), R=[t_raw[r_]], W=[t_qn])
                rope_ops(qr[0:T, :].rearrange("p (h d) -> p h d", h=4), x3, T, 4, tt, t_qr, t_raw[r_])

                def post():
                    transp_out(qn, t_qn, T, 4, [(k.qT_scr[hb0 + h, :, tok0:tok0 + T], [t_qT_scr[tt]]) for h in range(4)])
                    transp_out(qr, t_qr, T, 4, [(k.qrT_scr[hb0 + h, :, tok0:tok0 + T], [t_qrT_scr[tt]]) for h in range(4)])
                return post
            pend = None
            for tt in range(17):
                nxt = q_tile(tt, sl, (bi - 4) * 4)
                if pend is not None:
                    pend()
                pend = nxt
            pend()
        else:
            x = bi - 6

            def kv_tile(tt, sl, x):
                T = 128 if tt < 16 else SS
                tok0 = tt * 128
                bank = cnt[0] % 2
                r_ = cnt[0] % 2
                cnt[0] += 1
                proj_tm(sl, tt, bank)
                rw = raw[r_]
                P.op("act", lambda e: e.copy(rw[0:T, :], pb[bank][0:T, :]), R=[tpb[bank]], W=[t_raw[r_]])
                k3 = rw[0:T, 0:256].rearrange("p (h d) -> p h d", h=2)
                rms_heads(k3, T, 2, kg[0:T, x, :], t_raw[r_], t_kg)
                if x > 0:
                    kr = tmpf[0:T, 0:256].rearrange("p (h d) -> p h d", h=2)
                    rope_ops(kr, k3, T, 2, tt, t_tmpf, t_raw[r_])
                    P.op("dve", lambda e: e.tensor_copy(rw[0:T, 0:256], tmpf[0:T, 0:256]), R=[t_tmpf], W=[t_raw[r_]])
                if tt < 16:
                    dst = [k.o_pkc, k.o_pks, k.o_pkw][x]
                    if x < 2:
                        P.op("sp", lambda e, dst=dst: e.dma_start(out=dst[i, tok0:tok0 + T, :], in_=rw[0:T, :]), R=[t_raw[r_]], dma=True)
                    elif tt >= 12:
                        P.op("sp", lambda e, dst=dst: e.dma_start(out=dst[i, tok0 - 1536:tok0 - 1536 + T, :], in_=rw[0:T, :]), R=[t_raw[r_]], dma=True)
                else:
                    dst = [k.o_skc, k.o_sks, k.o_skw][x]
                    P.op("sp", lambda e, dst=dst: e.dma_start(out=dst[i], in_=rw[0:SS, :]), R=[t_raw[r_]], dma=True)
                ki = tt % 2
                kb16, t_kb16, va16, t_va16 = kb16_2[ki], t_kb16_2[ki], va16_2[ki], t_va16_2[ki]
                P.op("act", lambda e: e.copy(kb16[0:T, :, :], rw[0:T, 0:256].rearrange("p (h d) -> p h d", h=2)), R=[t_raw[r_]], W=[t_kb16])
                P.op("act", lambda e: e.copy(va16[0:T, :, 0:128], rw[0:T, 256:512].rearrange("p (h d) -> p h d", h=2)), R=[t_raw[r_]], W=[t_va16])
                def post():
                    if x == 0:
                        if tt < 16:
                            for h in range(2):
                                P.op("pe", lambda e, h=h: e.matmul(pb[2 + h][:, 0:32], kb16[0:T, h, :], WF[0:T, h, tt, :], start=(tt == 0), stop=(tt == 15)),
                                     R=[t_kb16, t_WF], W=[tpb[2 + h]], sig=False)
                                P.op("pe", lambda e, h=h: e.matmul(pb[4 + h][0:32, 0:128], WF[0:T, h, tt, :], va16[0:T, h, 0:128], start=(tt == 0), stop=(tt == 15)),
                                     R=[t_va16, t_WF], W=[tpb[4 + h]], sig=True)
                    else:
                        scrT = k.ksT_scr if x == 1 else k.kwT_scr
                        tscT = t_ksT_scr if x == 1 else t_kwT_scr
                        scrV = k.vs_scr if x == 1 else k.vw_scr
                        tscV = t_vs_scr if x == 1 else t_vw_scr
                        transp_out(kb16[:, :, :].rearrange("p h d -> p (h d)"), t_kb16, T, 2, [(scrT[h, :, tok0:tok0 + T], [tscT[tt]]) for h in range(2)])
                        P.op("sp", lambda e, scrV=scrV: e.dma_start(out=scrV[tok0:tok0 + T, :, :], in_=va16[0:T, :, :]), R=[t_va16], W=[tscV[tt]], dma=True)

                    return None
                return post
            pend = None
            for tt in range(17):
                nxt = kv_tile(tt, sl, x)
                if pend is not None:
                    pend()
                pend = nxt
            pend()
            if x == 0:
                P.op("pool", lambda e: e.memset(Vca[0:32, :, 128:130], 1.0), W=[t_Vca])
                for h in range(2):
                    P.op("act", lambda e, h=h: e.copy(KcT[:, h, :], pb[2 + h][:, 0:32]), R=[tpb[2 + h]], W=[t_KcT])
                    P.op("act", lambda e, h=h: e.copy(Vca[0:32, h, 0:128], pb[4 + h][0:32, 0:128]), R=[tpb[4 + h]], W=[t_Vca])
    for tt in range(17):
        T = 128 if tt < 16 else SS
        tok0 = tt * 128
        bank = cnt[0] % 2
        cnt[0] += 1
        for kc in range(16):
            P.op("pe", lambda e, kc=kc, bank=bank, tok0=tok0, T=T: e.matmul(pb[bank][0:T, 0:24], k.xnT[:, kc, tok0:tok0 + T], wgl[:, kc, :], start=(kc == 0), stop=(kc == 15)),
                 R=[t_wgl, k.t_xnT], W=[tpb[bank]], sig=(kc == 15))
        P.op("act", lambda e, bank=bank, T=T, tt=tt: e.activation(gate[0:T, tt, :], pb[bank][0:T, 0:24], AF.Sigmoid), R=[tpb[bank]], W=[t_gate])
    if k.cfg.get("even_stop", 99) <= 1:
        return
    even_attn_prompt(k, layer)
    if k.cfg.get("even_stop", 99) <= 2:
        return
    if k.cfg.get("even_sample", True):
        even_attn_sample(k, layer)
    out_proj(k, k.w_out_even[i], 16, k.mo_scr, t_mo_scr, moS, t_moS)


def obank(h):
    return 4 + h % 4, 0


def attn_combine(k, T, Oc, Os, Ow, t_O, gate_ap, t_gate, s3, t_s3, tmpb, t_tmpb, bout, t_bout):
    P = k.P
    for x, O in enumerate((Oc, Os, Ow)):
        P.op("dve", lambda e, x=x, O=O: e.tensor_copy(s3[0:T, :, x], O[0:T, :, 128]), R=t_O, W=[t_s3])
    P.op("dve", lambda e: e.tensor_scalar_max(s3[0:T, :, :], s3[0:T, :, :], 1e-30), R=[t_s3], W=[t_s3])
    P.op("dve", lambda e: e.reciprocal(s3[0:T, :, :], s3[0:T, :, :]), R=[t_s3], W=[t_s3])
    P.op("dve", lambda e: e.tensor_tensor(s3[0:T, :, :], s3[0:T, :, :], gate_ap.rearrange("p (h x) -> p h x", h=8), ALU.mult),
         R=[t_s3, t_gate], W=[t_s3])
    for h in range(8):
        eng = "dve"
        P.op(eng, lambda e, h=h: e.tensor_scalar(tmpb[0:T, h, :], Oc[0:T, h, 0:128], s3[0:T, h, 0:1], None, ALU.mult), R=t_O + [t_s3], W=[t_tmpb])
        P.op(eng, lambda e, h=h: e.scalar_tensor_tensor(tmpb[0:T, h, :], Os[0:T, h, 0:128], s3[0:T, h, 1:2], tmpb[0:T, h, :], ALU.mult, ALU.add),
             R=t_O + [t_s3, t_tmpb], W=[t_tmpb])
        P.op(eng, lambda e, h=h: e.scalar_tensor_tensor(bout[0:T, h * 128:(h + 1) * 128], Ow[0:T, h, 0:128], s3[0:T, h, 2:3], tmpb[0:T, h, :], ALU.mult, ALU.add),
             R=t_O + [t_s3, t_tmpb], W=[t_bout])


def even_attn_prompt(k, layer):
    P = k.P
    i = layer // 2
    pb, tpb = k.pb, k.tpb
    ev = k.ev
    gate, t_gate, KcT, t_KcT, Vca, t_Vca = ev["gate"], ev["t_gate"], ev["KcT"], ev["t_KcT"], ev["Vca"], ev["t_Vca"]
    rX, rY = k.rX, k.rY
    rX.reset()
    rY.reset()
    ksT, t_ksT = rX.alloc([2, 2048], BF16, "ksT")
    kwT, t_kwT = rX.alloc([2, 2048], BF16, "kwT")
    vs, t_vs = rX.alloc([16, 2, 130], BF16, "vs")
    vw, t_vw = rX.alloc([16, 2, 130], BF16, "vw")
    qTt, t_qTt, qrTt, t_qrTt = [None, None], [None, None], [None, None], [None, None]
    for a_ in range(2):
        qTt[a_], t_qTt[a_] = rX.alloc([8, 128], BF16, "qTt")
        qrTt[a_], t_qrTt[a_] = rX.alloc([8, 128], BF16, "qrTt")
    Oc, t_Oc = rX.alloc([8, 130], F32, "Oc")
    Os, t_Os = rX.alloc([8, 130], F32, "Os")
    Ow, t_Ow = rX.alloc([8, 130], F32, "Ow")
    tmpb, t_tmpb = rX.alloc([8, 128], F32, "tmpb")
    rX.commit()
    ebig, t_ebig = rY.alloc([2048], BF16, "ebig")
    negc, t_negc = rY.alloc([128], BF16, "negc")
    negu, t_negu = rY.alloc([128], BF16, "negu")
    validq, t_validq = rY.alloc([16, 32], F32, "validq")
    keepm, t_keepm = rY.alloc([16, 32], F32, "keepm")
    addm, t_addm = rY.alloc([16, 32], F32, "addm")
    validT, t_validT = rY.alloc([16, 128], BF16, "validT")
    ee, t_ee = rY.alloc([8, 32], F32, "ee")
    imp, t_imp = rY.alloc([2, 32], F32, "imp")
    cmpb, t_cmpb = rY.alloc([32, 32], F32, "cmpb")
    rank, t_rank = rY.alloc([2, 32], F32, "rank")
    negm, t_negm = rY.alloc([64], BF16, "negm")
    negmT, t_negmT = rY.alloc([128], BF16, "negmT")
    ecf, t_ecf = rY.alloc([512], F32, "ecf")
    eTb, t_eTb = rY.alloc([512], BF16, "eTb")
    PT, t_PT = [None, None], [None, None]
    for a_ in range(2):
        PT[a_], t_PT[a_] = rY.alloc([512], BF16, "PT")
    s8, t_s8 = rY.alloc([8], F32, "s8")
    s3, t_s3 = rY.alloc([8, 3], F32, "s3")
    bout, t_bout = rY.alloc([1024], BF16, "bout")
    boT, t_boT = [None, None], [None, None]
    for a_ in range(2):
        boT[a_], t_boT[a_] = rY.alloc([4, 128], BF16, "boT")
    rY.commit()
    ld = lambda dst, src, R, W: P.op("sp", lambda e: e.dma_start(out=dst, in_=src), R=R, W=W, dma=True)
    ld(ksT, k.ksT_scr[:, :, 0:2048].rearrange("h p t -> p h t"), ev["t_ksT_scr"], [t_ksT])
    ld(kwT, k.kwT_scr[:, :, 0:2048].rearrange("h p t -> p h t"), ev["t_kwT_scr"], [t_kwT])
    ld(vs, k.vs_scr[0:2048].rearrange("(t p) h d -> p t h d", p=128), ev["t_vs_scr"], [t_vs])
    ld(vw, k.vw_scr[0:2048].rearrange("(t p) h d -> p t h d", p=128), ev["t_vw_scr"], [t_vw])
    ld(ebig[0:64, :], k.c_ebig, [], [t_ebig])
    ld(negc, k.c_negb[0], [], [t_negc])
    ld(negu, k.c_negb[1], [], [t_negu])
    ld(validq, k.c_valid, [], [t_validq])
    ld(keepm, k.c_keep, [], [t_keepm])
    ld(addm, k.c_add, [], [t_addm])
    ld(validT[0:32, :, :], k.c_validT, [], [t_validT])
    cnt = [0]
    t_mo_scr = ev["t_mo_scr"]

    def evac(O, t_O, kvh):
        for g in range(4):
            P.op("act", lambda e, g=g: e.copy(O[:, kvh * 4 + g, :], pb[4 + g][:, 0:130]), R=[tpb[4 + g]], W=[t_O])

    def branch(qt, kts, KT, t_KT, V, t_V, qr, t_qr, blockmask, O, t_O):
        for kvh in range(2):
            branch1(qt, kts, KT, t_KT, V, t_V, qr, t_qr, blockmask, kvh)
            evac(O, t_O, kvh)

    def branch1(qt, kts, KT, t_KT, V, t_V, qr, t_qr, blockmask, kvh):
        def scores(kt):
            bs_ = 2 + cnt[0] % 2
            p_ = cnt[0] % 2
            cnt[0] += 1
            extra = []
            if blockmask:
                extra.append("blk")
            if kt == qt:
                extra.append("caus")
            if (not blockmask) and kt == qt - 4:
                extra.append("upper")
            P.op("pe", lambda e, last=(len(extra) == 0): e.matmul(
                pb[bs_][:, :], KT[:, kvh, kt * 128:(kt + 1) * 128], qr[:, kvh * 4:(kvh + 1) * 4, :], start=True, stop=last),
                R=[t_KT, t_qr], W=[tpb[bs_]], sig=(len(extra) == 0))
            for j, nm in enumerate(extra):
                last = (j == len(extra) - 1)
                if nm == "blk":
                    P.op("pe", lambda e, last=last: e.matmul(
                        pb[bs_][:, :], ebig[kvh * 32:(kvh + 1) * 32, kt * 128:(kt + 1) * 128],
                        negmT[kvh * 32:(kvh + 1) * 32, :].unsqueeze(1).to_broadcast([32, 4, 128]), start=False, stop=last),
                        R=[t_ebig, t_negmT], W=[tpb[bs_]], sig=last)
                else:
                    mk, tmk = (negc, t_negc) if nm == "caus" else (negu, t_negu)
                    P.op("pe", lambda e, last=last, mk=mk: e.matmul(
                        pb[bs_][:, :], k.identb[:, :], mk[:, :].unsqueeze(1).to_broadcast([128, 4, 128]), start=False, stop=last),
                        R=[k.t_ident, tmk], W=[tpb[bs_]], sig=last)
            P.op("act", lambda e: e.activation(PT[p_], pb[bs_][:, :], AF.Exp, scale=ATTN_SCALE), R=[tpb[bs_]], W=[t_PT[p_]])
            return p_

        def pv(kt, p_):
            for g in range(4):
                h = kvh * 4 + g
                bk, col = obank(h)
                P.op("pe", lambda e, g=g, bk=bk, col=col: e.matmul(
                    pb[bk][:, col:col + 130], PT[p_][:, g * 128:(g + 1) * 128], V[:, kt, kvh, :], start=(kt == kts[0]), stop=(kt == kts[-1])),
                    R=[t_PT[p_], t_V], W=[tpb[bk]], sig=True)

        pend = scores(kts[0])
        for idx_, kt in enumerate(kts):
            nxt = scores(kts[idx_ + 1]) if idx_ + 1 < len(kts) else None
            pv(kt, pend)
            pend = nxt

    def qtile(qt):
        b = qt % 2
        q_, tq_, qr_, tqr_ = qTt[b], t_qTt[b], qrTt[b], t_qrTt[b]
        ld(q_, k.qT_scr[:, :, qt * 128:(qt + 1) * 128].rearrange("h p t -> p h t"), [ev["t_qT_scr"][qt]], [tq_])
        ld(qr_, k.qrT_scr[:, :, qt * 128:(qt + 1) * 128].rearrange("h p t -> p h t"), [ev["t_qrT_scr"][qt]], [tqr_])
        for h in range(8):
            P.op("pe", lambda e, h=h: e.matmul(pb[0][:, h * 32:(h + 1) * 32], q_[:, h, :], KcT[:, h // 4, :], start=True, stop=True),
                 R=[tq_, t_KcT], W=[tpb[0]], sig=(h == 7))
        P.op("act", lambda e: e.activation(ee, pb[0][:, 0:256].rearrange("p (h n) -> p h n", h=8), AF.Exp, scale=ATTN_SCALE), R=[tpb[0]], W=[t_ee])
        P.op("dve", lambda e: e.tensor_tensor(ee, ee, validq[:, qt, :].unsqueeze(1).to_broadcast([128, 8, 32]), ALU.mult), R=[t_ee, t_validq], W=[t_ee])
        P.op("dve", lambda e: e.reduce_sum(s8, ee, AX.X), R=[t_ee], W=[t_s8])
        P.op("dve", lambda e: e.tensor_scalar_max(s8, s8, 1e-30), R=[t_s8], W=[t_s8])
        P.op("dve", lambda e: e.reciprocal(s8, s8), R=[t_s8], W=[t_s8])
        P.op("dve", lambda e: e.tensor_tensor(ee, ee, s8[:, :].unsqueeze(2).to_broadcast([128, 8, 32]), ALU.mult), R=[t_ee, t_s8], W=[t_ee])
        P.op("dve", lambda e: e.reduce_sum(imp, ee[:, :, :].rearrange("p (k g) n -> p k n g", k=2), AX.X), R=[t_ee], W=[t_imp])
        P.op("dve", lambda e: e.tensor_tensor(imp, imp, keepm[:, qt, :].unsqueeze(1).to_broadcast([128, 2, 32]), ALU.mult), R=[t_imp, t_keepm], W=[t_imp])
        P.op("dve", lambda e: e.tensor_tensor(imp, imp, addm[:, qt, :].unsqueeze(1).to_broadcast([128, 2, 32]), ALU.add), R=[t_imp, t_addm], W=[t_imp])
        for kvh in range(2):
            P.op("dve", lambda e, kvh=kvh: e.tensor_tensor(cmpb, imp[:, kvh, :].unsqueeze(1).to_broadcast([128, 32, 32]),
                                                           imp[:, kvh, :].unsqueeze(2).to_broadcast([128, 32, 32]), ALU.is_gt), R=[t_imp], W=[t_cmpb])
            P.op("dve", lambda e, kvh=kvh: e.reduce_sum(rank[:, kvh, :], cmpb, AX.X), R=[t_cmpb], W=[t_rank])
        P.op("dve", lambda e: e.tensor_scalar(negm, rank[:, :, :].rearrange("p k n -> p (k n)"), 15.5, NEGM, ALU.is_gt, ALU.mult), R=[t_rank], W=[t_negm])
        ptm = pb[0][:, 0:64].bitcast(BF16)
        P.op("pe", lambda e: e.transpose(ptm[0:64, :], negm, k.identb[:, :]), R=[t_negm, k.t_ident], W=[tpb[0]])
        P.op("act", lambda e: e.copy(negmT[0:64, :], ptm[0:64, :]), R=[tpb[0]], W=[t_negmT])
        for kvh in range(2):
            P.op("pe", lambda e, kvh=kvh: e.matmul(pb[1][0:32, :], KcT[:, kvh, :], q_[:, kvh * 4:(kvh + 1) * 4, :], start=True, stop=True),
                 R=[tq_, t_KcT], W=[tpb[1]])
            P.op("act", lambda e: e.activation(ecf[0:32, :], pb[1][0:32, :], AF.Exp, scale=ATTN_SCALE), R=[tpb[1]], W=[t_ecf])
            P.op("dve", lambda e: e.tensor_tensor(eTb[0:32, :].rearrange("p (g q) -> p g q", g=4), ecf[0:32, :].rearrange("p (g q) -> p g q", g=4),
                                                  validT[0:32, qt, :].unsqueeze(1).to_broadcast([32, 4, 128]), ALU.mult), R=[t_ecf, t_validT], W=[t_eTb])
            for g in range(4):
                h = kvh * 4 + g
                bk, col = obank(h)
                P.op("pe", lambda e, g=g, bk=bk, col=col, kvh=kvh: e.matmul(pb[bk][:, col:col + 130], eTb[0:32, g * 128:(g + 1) * 128], Vca[0:32, kvh, :],
                                                                           start=True, stop=True), R=[t_eTb, t_Vca], W=[tpb[bk]], sig=True)
            evac(Oc, t_Oc, kvh)
        branch(qt, list(range(0, qt + 1)), ksT, t_ksT, vs, t_vs, qr_, tqr_, True, Os, t_Os)
        branch(qt, list(range(max(0, qt - 4), qt + 1)), kwT, t_kwT, vw, t_vw, qr_, tqr_, False, Ow, t_Ow)
        attn_combine(k, 128, Oc, Os, Ow, [t_Oc, t_Os, t_Ow], gate[:, qt, :], t_gate, s3, t_s3, tmpb, t_tmpb, bout, t_bout)
        for half in range(2):
            pt = pb[0][:, 0:256].bitcast(BF16).rearrange("p (a b) -> p a b", a=4)
            for j in range(4):
                h = half * 4 + j
                P.op("pe", lambda e, j=j, h=h: e.transpose(pt[:, j, :], bout[:, h * 128:(h + 1) * 128], k.identb[:, :]), R=[t_bout, k.t_ident], W=[tpb[0]], sig=(j == 3))
            P.op("act", lambda e, half=half: e.copy(boT[half], pt), R=[tpb[0]], W=[t_boT[half]])
            P.op("sp", lambda e, half=half: e.dma_start(out=k.mo_scr[qt, :, 8 + half * 4:12 + half * 4, :], in_=boT[half]), R=[t_boT[half]], W=[t_mo_scr[qt]], dma=True)

    for qt in range(k.cfg.get("nqt", 16)):
        qtile(qt)


def even_attn_sample(k, layer):
    P = k.P
    i = layer // 2
    pb, tpb = k.pb, k.tpb
    ev = k.ev
    gate, t_gate, moS, t_moS = ev["gate"], ev["t_gate"], ev["moS"], ev["t_moS"]
    WF, t_WF = ev["WF"], ev["t_WF"]
    rX, rY = k.rX, k.rY
    rX.reset()
    rY.reset()
    G = 16
    ptb, t_ptb = rX.alloc([128], I32, "ptb")
    ptf, t_ptf = rX.alloc([128], F32, "ptf")
    idx, t_idx = rX.alloc([128], I32, "idx")
    iot, t_iot = rX.alloc([8], F32, "iot")
    pgf, t_pgf = [None] * 4, [None] * 4
    for a_ in range(4):
        pgf[a_], t_pgf[a_] = rX.alloc([512], F32, "pgf")
    pgk, t_pgk = [None] * 4, [None] * 4
    vau, t_vau = [None] * 4, [None] * 4
    for a_ in range(4):
        pgk[a_], t_pgk[a_] = rX.alloc([256], BF16, "pgk")
        vau[a_], t_vau[a_] = rX.alloc([2, 130], BF16, "vau")
    KT, t_KT = rX.alloc([2, 512], BF16, "KT")
    PTs, t_PTs = rX.alloc([8, 16], BF16, "PTs")
    KcTs, t_KcTs = rX.alloc([2, 256], BF16, "KcTs")
    VcTs, t_VcTs = rX.alloc([2, 256], BF16, "VcTs")
    Vcs, t_Vcs = rX.alloc([2, 2, 130], BF16, "Vcs")
    qTs, t_qTs = rX.alloc([8, SS], BF16, "qTs")
    qrTs, t_qrTs = rX.alloc([8, SS], BF16, "qrTs")
    knT, t_knT = rX.alloc([2, 2, SS], BF16, "knT")
    vnew, t_vnew = rX.alloc([2, 2, 130], BF16, "vnew")
    rX.commit()
    ef, t_ef = rY.alloc([2, 256], F32, "ef")
    pf, t_pf = rY.alloc([2, 256], F32, "pf")
    eb, t_eb = rY.alloc([2, 256], BF16, "eb")
    impS, t_impS = rY.alloc([2, 256], F32, "impS")
    tt_, t_tt = rY.alloc([2, 255], F32, "tt")
    msk, t_msk = rY.alloc([2, 255], F32, "msk")
    mx, t_mx = rY.alloc([2], F32, "mx")
    sm2, t_sm2 = rY.alloc([2], F32, "sm2")
    negS, t_negS = rY.alloc([2, 256], F32, "negS")
    neg16, t_neg16 = rY.alloc([2, 256], F32, "neg16")
    Sm, t_Sm = rY.alloc([512], F32, "Sm")
    Pb, t_Pb = rY.alloc([512], BF16, "Pb")
    sum16, t_sum16 = rY.alloc([4], F32, "sum16")
    sum16T, t_sum16T = rY.alloc([16], F32, "sum16T")
    negnew, t_negnew = rY.alloc([4], F32, "negnew")
    negwin, t_negwin = rY.alloc([512], F32, "negwin")
    O16, t_O16 = rY.alloc([2, 130], F32, "O16")
    O4 = []
    t_O4 = []
    for a_ in range(3):
        a, t = rY.alloc([8, 130], F32, "O4")
        O4.append(a)
        t_O4.append(t)
    s3, t_s3 = rY.alloc([8, 3], F32, "s3")
    tmpb, t_tmpb = rY.alloc([8, 128], F32, "tmpb")
    bout, t_bout = rY.alloc([1024], BF16, "bout")
    rY.commit()
    ld = lambda dst, src, R, W: P.op("sp", lambda e: e.dma_start(out=dst, in_=src), R=R, W=W, dma=True)
    ld(ptb, k.ptab.to_broadcast([128, 128]), [], [t_ptb])
    ld(iot[:, 0:1], k.c_iota, [], [t_iot])
    ld(sum16[0:16, :], k.c_sum16, [], [t_sum16])
    ld(sum16T[0:4, :], k.c_sum16T, [], [t_sum16T])
    ld(negnew[0:16, :], k.c_negnew, [], [t_negnew])
    ld(negwin[0:16, :], k.c_negwin, [], [t_negwin])
    ld(qTs, k.qT_scr[:, :, 2048:2052].rearrange("h p t -> p h t"), [ev["t_qT_scr"][16]], [t_qTs])
    ld(qrTs, k.qrT_scr[:, :, 2048:2052].rearrange("h p t -> p h t"), [ev["t_qrT_scr"][16]], [t_qrTs])
    ld(knT[:, 0, :, :], k.ksT_scr[:, :, 2048:2052].rearrange("h p t -> p h t"), [ev["t_ksT_scr"][16]], [t_knT])
    ld(knT[:, 1, :, :], k.kwT_scr[:, :, 2048:2052].rearrange("h p t -> p h t"), [ev["t_kwT_scr"][16]], [t_knT])
    ld(vnew[0:SS, 0, :, :], k.vs_scr[2048:2052], [ev["t_vs_scr"][16]], [t_vnew])
    ld(vnew[0:SS, 1, :, :], k.vw_scr[2048:2052], [ev["t_vw_scr"][16]], [t_vnew])
    P.op("dve", lambda e: e.tensor_copy(ptf, ptb), R=[t_ptb], W=[t_ptf])
    P.op("dve", lambda e: e.tensor_scalar(ptf, ptf, 128.0, iot[:, 0:1], ALU.mult, ALU.add), R=[t_ptf, t_iot], W=[t_ptf])
    if i > 0:
        P.op("dve", lambda e: e.tensor_scalar_add(ptf, ptf, float(i * 1280 * 128)), R=[t_ptf], W=[t_ptf])
    P.op("dve", lambda e: e.tensor_copy(idx, ptf), R=[t_ptf], W=[t_idx])
    for a_ in range(4):
        P.op("pool", lambda e, a_=a_: e.memset(vau[a_][:, :, 128:130], 1.0), W=[t_vau[a_]])
    P.op("pool", lambda e: e.memset(Vcs[:, :, :, 128:130], 1.0), W=[t_Vcs])
    cnt = [0]

    def fetch(cache_rows, pg, slot, gather=True):
        if gather:
            P.op("pool", lambda e: e.indirect_dma_start(out=pgf[slot], out_offset=None, in_=cache_rows,
                                                        in_offset=bass.IndirectOffsetOnAxis(ap=idx[:, pg:pg + 1], axis=0)),
                 R=[t_idx], W=[t_pgf[slot]], dma=True)
        else:
            P.op("sp", lambda e: e.dma_start(out=pgf[slot], in_=cache_rows[pg * 128:(pg + 1) * 128, :]), W=[t_pgf[slot]], dma=True)
        P.op("act", lambda e: e.copy(pgk[slot], pgf[slot][:, 0:256]), R=[t_pgf[slot]], W=[t_pgk[slot]])
        P.op("dve", lambda e: e.tensor_copy(vau[slot][:, :, 0:128], pgf[slot][:, 256:512].rearrange("p (h d) -> p h d", h=2)),
             R=[t_pgf[slot]], W=[t_vau[slot]])

    cmp_rows = k.cache_cmp.rearrange("l r c -> (l r) c")
    for pg in range(128):
        sl_ = pg % 4

        def one(pg=pg, sl_=sl_):
            fetch(cmp_rows, pg, sl_)
            for kvh in range(2):
                P.op("pe", lambda e, kvh=kvh: e.matmul(pb[0][:, kvh * 256 + 2 * pg:kvh * 256 + 2 * pg + 2], pgk[sl_][:, kvh * 128:(kvh + 1) * 128],
                                                       WF[:, kvh, 0, 0:2], start=True, stop=True), R=[t_pgk[sl_], t_WF], W=[tpb[0]])
                P.op("pe", lambda e, kvh=kvh: e.matmul(pb[1][:, kvh * 256 + 2 * pg:kvh * 256 + 2 * pg + 2], vau[sl_][:, kvh, 0:128],
                                                       WF[:, kvh, 0, 0:2], start=True, stop=True), R=[t_vau[sl_], t_WF], W=[tpb[1]])
        one()
    P.op("act", lambda e: e.copy(KcTs, pb[0][:, :].rearrange("p (h n) -> p h n", h=2)), R=[tpb[0]], W=[t_KcTs])
    P.op("act", lambda e: e.copy(VcTs, pb[1][:, :].rearrange("p (h n) -> p h n", h=2)), R=[tpb[1]], W=[t_VcTs])
    ptv = pb[2][:, 0:256].bitcast(BF16).rearrange("p (a b) -> p a b", a=4)
    for kvh in range(2):
        for t in range(2):
            P.op("pe", lambda e, kvh=kvh, t=t: e.transpose(ptv[:, kvh * 2 + t, :], VcTs[:, kvh, t * 128:(t + 1) * 128], k.identb[:, :]),
                 R=[t_VcTs, k.t_ident], W=[tpb[2]])
    for kvh in range(2):
        for t in range(2):
            P.op("act", lambda e, kvh=kvh, t=t: e.copy(Vcs[:, t, kvh, 0:128], ptv[:, kvh * 2 + t, :]), R=[tpb[2]], W=[t_Vcs])
    for kvh in range(2):
        P.op("pe", lambda e, kvh=kvh: e.matmul(pb[3][0:G, kvh * 256:(kvh + 1) * 256], qTs[:, kvh * 4:(kvh + 1) * 4, :], KcTs[:, kvh, :], start=True, stop=True),
             R=[t_qTs, t_KcTs], W=[tpb[3]])
    P.op("pool", lambda e: e.memset(sm2[0:G, :], 0.0), W=[t_sm2])
    for kvh in range(2):
        P.op("act", lambda e, kvh=kvh: e.activation(ef[0:G, kvh, :], pb[3][0:G, kvh * 256:(kvh + 1) * 256], AF.Exp, scale=ATTN_SCALE,
                                                    accum_out=sm2[0:G, kvh:kvh + 1]), R=[tpb[3], t_sm2], W=[t_ef, t_sm2])
    P.op("act", lambda e: e.copy(eb[0:G, :, :], ef[0:G, :, :]), R=[t_ef], W=[t_eb])
    P.op("dve", lambda e: e.reciprocal(sm2[0:G, :], sm2[0:G, :]), R=[t_sm2], W=[t_sm2])
    P.op("dve", lambda e: e.tensor_tensor(pf[0:G, :, :], ef[0:G, :, :], sm2[0:G, :].unsqueeze(2).to_broadcast([G, 2, 256]), ALU.mult), R=[t_ef, t_sm2], W=[t_pf])
    P.op("pe", lambda e: e.matmul(pb[3][0:SS, :], sum16[0:G, :], pf[0:G, :, :].rearrange("p h n -> p (h n)"), start=True, stop=True),
         R=[t_sum16, t_pf, t_ef], W=[tpb[3]])
    P.op("act", lambda e: e.copy(impS[0:SS, :, :], pb[3][0:SS, :].rearrange("p (h n) -> p h n", h=2)), R=[tpb[3]], W=[t_impS])
    P.op("dve", lambda e: e.tensor_copy(tt_[0:SS, :, :], impS[0:SS, :, 1:256]), R=[t_impS], W=[t_tt])
    for it in range(14):
        P.op("dve", lambda e: e.reduce_max(mx[0:SS, :], tt_[0:SS, :, :], AX.X), R=[t_tt], W=[t_mx])
        if it < 13:
            P.op("dve", lambda e: e.tensor_tensor(msk[0:SS, :, :], tt_[0:SS, :, :], mx[0:SS, :].unsqueeze(2).to_broadcast([SS, 2, 255]), ALU.is_ge),
                 R=[t_tt, t_mx], W=[t_msk])
            P.op("dve", lambda e: e.scalar_tensor_tensor(tt_[0:SS, :, :], msk[0:SS, :, :], -1e30, tt_[0:SS, :, :], ALU.mult, ALU.add),
                 R=[t_msk, t_tt], W=[t_tt])
    P.op("dve", lambda e: e.tensor_tensor(negS[0:SS, :, :], impS[0:SS, :, :], mx[0:SS, :].unsqueeze(2).to_broadcast([SS, 2, 256]), ALU.is_lt),
         R=[t_impS, t_mx], W=[t_negS])
    P.op("dve", lambda e: e.tensor_scalar(negS[0:SS, :, :], negS[0:SS, :, :], NEGM, None, ALU.mult), R=[t_negS], W=[t_negS])
    P.op("pool", lambda e: e.memset(negS[0:SS, :, 0:1], 0.0), R=[t_negS], W=[t_negS])
    P.op("pe", lambda e: e.matmul(pb[3][0:G, :], sum16T[0:SS, :], negS[0:SS, :, :].rearrange("p h n -> p (h n)"), start=True, stop=True),
         R=[t_sum16T, t_negS, t_impS], W=[tpb[3]])
    P.op("act", lambda e: e.copy(neg16[0:G, :, :], pb[3][0:G, :].rearrange("p (h n) -> p h n", h=2)), R=[tpb[3]], W=[t_neg16])

    state = {"first": [True, True]}

    def pv(kvh, lhsT_ap, t_l, rhs_ap, t_r, last):
        first = state["first"][kvh]
        state["first"][kvh] = False
        P.op("pe", lambda e: e.matmul(pb[6 + kvh][0:G, 0:130], lhsT_ap, rhs_ap, start=first, stop=last), R=[t_l, t_r], W=[tpb[6 + kvh]])

    def finish(x):
        for kvh in range(2):
            P.op("act", lambda e, kvh=kvh: e.copy(O16[0:G, kvh, :], pb[6 + kvh][0:G, 0:130]), R=[tpb[6 + kvh]], W=[t_O16])
        for h in range(8):
            kvh, g = h // 4, h % 4
            bk, col = h // 3, (h % 3) * 130
            P.op("pe", lambda e, kvh=kvh, g=g, bk=bk, col=col: e.matmul(pb[bk][0:SS, col:col + 130], k.identf[0:G, g * 4:(g + 1) * 4], O16[0:G, kvh, :],
                                                                      start=True, stop=True), R=[t_O16, k.t_cst], W=[tpb[bk]])
        for bk, (h0, h1) in enumerate(((0, 3), (3, 6), (6, 8))):
            nh = h1 - h0
            P.op("act", lambda e, bk=bk, h0=h0, h1=h1, nh=nh: e.copy(O4[x][0:SS, h0:h1, :], pb[bk][0:SS, 0:nh * 130].rearrange("p (h d) -> p h d", h=nh)),
                 R=[tpb[bk]], W=[t_O4[x]])
        state["first"] = [True, True]

    def key_group(slots, mask_fn, has_more):
        ktv = pb[2][:, :].bitcast(BF16).rearrange("p (h a b) -> p h a b", h=2, a=4)
        for kvh in range(2):
            for j, sl_ in enumerate(slots):
                P.op("pe", lambda e, kvh=kvh, j=j, sl_=sl_: e.transpose(ktv[:, kvh, j, :], pgk[sl_][:, kvh * 128:(kvh + 1) * 128], k.identb[:, :]),
                     R=[t_pgk[sl_], k.t_ident], W=[tpb[2]])
        P.op("act", lambda e: e.copy(KT, pb[2][:, :].bitcast(BF16).rearrange("p (h n) -> p h n", h=2)), R=[tpb[2]], W=[t_KT])
        ptp = pb[5][:, 0:64].bitcast(BF16).rearrange("p (a b) -> p a b", a=8)
        for kvh in range(2):
            P.op("pe", lambda e, kvh=kvh: e.matmul(pb[3 + kvh][0:G, :], qrTs[:, kvh * 4:(kvh + 1) * 4, :], KT[:, kvh, :], start=True, stop=True),
                 R=[t_qrTs, t_KT], W=[tpb[3 + kvh]])
            mask_fn(kvh)
            P.op("act", lambda e: e.activation(Pb[0:G, :], Sm[0:G, :], AF.Exp, scale=ATTN_SCALE), R=[t_Sm], W=[t_Pb])
            for j in range(4):
                P.op("pe", lambda e, kvh=kvh, j=j: e.transpose(ptp[:, kvh * 4 + j, :], Pb[0:G, j * 128:(j + 1) * 128], k.identb[0:G, 0:G]),
                     R=[t_Pb, k.t_ident], W=[tpb[5]])
        P.op("act", lambda e: e.copy(PTs, ptp), R=[tpb[5]], W=[t_PTs])
        for kvh in range(2):
            for j, sl_ in enumerate(slots):
                pv(kvh, PTs[:, kvh * 4 + j, :], t_PTs, vau[sl_][:, kvh, :], t_vau[sl_], False)

    def new_rows(x):
        ptn = pb[5][:, 0:16].bitcast(BF16).rearrange("p (a b) -> p a b", a=2)
        for kvh in range(2):
            P.op("pe", lambda e, kvh=kvh: e.matmul(pb[3 + kvh][0:G, 0:SS], qrTs[:, kvh * 4:(kvh + 1) * 4, :], knT[:, x, kvh, :], start=True, stop=True),
                 R=[t_qrTs, t_knT], W=[tpb[3 + kvh]])
            P.op("dve", lambda e, kvh=kvh: e.tensor_tensor(Sm[0:G, 0:SS], pb[3 + kvh][0:G, 0:SS], negnew[0:G, :], ALU.add), R=[tpb[3 + kvh], t_negnew], W=[t_Sm])
            P.op("act", lambda e: e.activation(Pb[0:G, 0:SS], Sm[0:G, 0:SS], AF.Exp, scale=ATTN_SCALE), R=[t_Sm], W=[t_Pb])
            P.op("pe", lambda e, kvh=kvh: e.transpose(ptn[0:SS, kvh, :], Pb[0:G, 0:SS], k.identb[0:G, 0:G]), R=[t_Pb, k.t_ident], W=[tpb[5]])
        P.op("act", lambda e: e.copy(PTs[0:SS, 0:2, :], ptn[0:SS, :, :]), R=[tpb[5]], W=[t_PTs])
        for kvh in range(2):
            pv(kvh, PTs[0:SS, kvh, :], t_PTs, vnew[0:SS, x, kvh, :], t_vnew, True)

    ptc = pb[5][:, 0:32].bitcast(BF16).rearrange("p (a b) -> p a b", a=4)
    for kvh in range(2):
        for t in range(2):
            P.op("pe", lambda e, kvh=kvh, t=t: e.transpose(ptc[:, kvh * 2 + t, :], eb[0:G, kvh, t * 128:(t + 1) * 128], k.identb[0:G, 0:G]),
                 R=[t_eb, k.t_ident], W=[tpb[5]])
    P.op("act", lambda e: e.copy(PTs[:, 0:4, :], ptc), R=[tpb[5]], W=[t_PTs])
    for kvh in range(2):
        for t in range(2):
            pv(kvh, PTs[:, kvh * 2 + t, :], t_PTs, Vcs[:, t, kvh, :], t_Vcs, t == 1)
    finish(0)

    sel_rows = k.cache_sel.rearrange("l r c -> (l r) c")
    for grp in range(32):
        def sel_grp(grp=grp):
            for j in range(4):
                fetch(sel_rows, grp * 4 + j, j)

            def mask_fn(kvh):
                P.op("dve", lambda e: e.tensor_tensor(Sm[0:G, :].rearrange("p (b c) -> p b c", b=8), pb[3 + kvh][0:G, :].rearrange("p (b c) -> p b c", b=8),
                                                      neg16[0:G, kvh, grp * 8:(grp + 1) * 8].unsqueeze(2).to_broadcast([G, 8, 64]), ALU.add),
                     R=[tpb[3 + kvh], t_neg16], W=[t_Sm])
            key_group([0, 1, 2, 3], mask_fn, True)
        sel_grp()
    new_rows(0)
    finish(1)

    win_rows = k.cache_win[i]
    for j in range(4):
        fetch(win_rows, j, j, gather=False)

    def mask_win(kvh):
        P.op("dve", lambda e: e.tensor_tensor(Sm[0:G, :], pb[3 + kvh][0:G, :], negwin[0:G, :], ALU.add), R=[tpb[3 + kvh], t_negwin], W=[t_Sm])
    key_group([0, 1, 2, 3], mask_win, True)
    new_rows(1)
    finish(2)

    attn_combine(k, SS, O4[0], O4[1], O4[2], t_O4, gate[0:SS, 16, :], t_gate, s3, t_s3, tmpb, t_tmpb, bout, t_bout)
    ptb_ = pb[5][:, 0:16].bitcast(BF16).rearrange("p (a b) -> p a b", a=8)
    for h in range(8):
        P.op("pe", lambda e, h=h: e.transpose(ptb_[:, h, :], bout[0:SS, h * 128:(h + 1) * 128], k.identb[0:SS, 0:SS]), R=[t_bout, k.t_ident], W=[tpb[5]])
    P.op("act", lambda e: e.copy(moS[:, 8:16, :], ptb_), R=[tpb[5]], W=[t_moS])


def _relayout_ffn(a, last):
    d = a.shape[0]
    return np.ascontiguousarray(a.reshape(d, last, 88, 128).transpose(0, 3, 2, 1))


def _relayout_ch(a, nchunk):
    sh = a.shape
    T = sh[-2]
    b = a.reshape(sh[:-2] + (T, nchunk, 128))
    nd = b.ndim
    perm = tuple(range(nd - 3)) + (nd - 1, nd - 2, nd - 3)
    return np.ascontiguousarray(b.transpose(perm))


def make_core_inputs(inp, c):
    ident = np.eye(128, dtype=np.float32)
    jj, ii = np.meshgrid(np.arange(128), np.arange(128), indexing="ij")
    utri = (jj <= ii).astype(np.float32)
    negtriT = np.where(jj <= ii, 0.0, -30000.0).astype(np.float32)
    cst4 = np.stack([ident, utri, negtriT, np.ones((128, 128), np.float32)])
    d = {
        "x_p": np.ascontiguousarray(inp["x_prompt"][c % 4]),
        "x_s": np.ascontiguousarray(inp["x_sample"][c]),
        "norm_mix": inp["norm_mix"], "norm_ffn": inp["norm_ffn"],
        "ffn_w_up": inp["ffn_w_up"], "ffn_w_down": inp["ffn_w_down"],
        "ffn_cw": _relayout_ch(inp["ffn_conv_w"], 88),
        "ffn_cb": np.ascontiguousarray(_relayout_ch(inp["ffn_conv_b"][:, None, :], 88)[..., 0]),
        "ffn_st": _relayout_ch(inp["state_ffn_conv"][:, c], 88),
        "ident": ident.astype(ml_dtypes.bfloat16), "cst4": cst4,
        "w_in_odd": inp["w_in_odd"], "w_out_odd": inp["w_out_odd"],
        "ssm_cw": _relayout_ch(inp["ssm_conv_w"], 48),
        "ssm_cb": np.ascontiguousarray(_relayout_ch(inp["ssm_conv_b"][:, None, :], 48)[..., 0]),
        "ssm_cst": _relayout_ch(inp["state_ssm_conv"][:, c], 48),
        "ssm_st": np.ascontiguousarray(inp["state_ssm"][:, c].reshape(2, 4096, 128)),
        "ssm_dt_bias": inp["ssm_dt_bias"], "ssm_a_log": inp["ssm_a_log"], "ssm_d": inp["ssm_d"], "ssm_norm": inp["ssm_norm"],
        "w_in_even": inp["w_in_even"], "w_out_even": inp["w_out_even"], "gmlp_v_norm": inp["gmlp_v_norm"],
        "gmlp_wsT": np.ascontiguousarray(inp["gmlp_ws"].transpose(0, 3, 1, 2)), "gmlp_bs": inp["gmlp_bs"],
        "q_norm": inp["q_norm"], "k_norm": inp["k_norm"], "cmp_pool": inp["cmp_pool"],
        "cache_cmp": inp["cache_kv_cmp"].reshape(2, 1280 * 128, 512), "cache_sel": inp["cache_kv_sel"].reshape(2, 1280 * 128, 512),
        "cache_win": np.ascontiguousarray(inp["cache_kv_win"][:, c].reshape(2, 512, 512)),
        "ptab": np.ascontiguousarray(inp["page_table"][c:c + 1]).astype(np.int32),
    }
    d.update(_CONSTS())
    return d


_CC = {}


def _CONSTS():
    if _CC:
        return _CC
    bf = ml_dtypes.bfloat16
    pos = np.concatenate([np.arange(2048), 16384 + np.arange(4), np.zeros(124)]).astype(np.float32)
    inv = (10000.0 ** (-np.arange(64, dtype=np.float32) / 64)).astype(np.float32)
    ang = pos[:, None] * inv[None, :]
    _CC["rope_cos"] = np.ascontiguousarray(np.cos(ang).astype(np.float32).reshape(17, 128, 64).transpose(1, 0, 2))
    _CC["rope_sin"] = np.ascontiguousarray(np.sin(ang).astype(np.float32).reshape(17, 128, 64).transpose(1, 0, 2))
    p = np.arange(128)
    ecmp = np.zeros((128, 16, 32), np.float32)
    for tt in range(16):
        ecmp[p, tt, 2 * tt + p // 64] = 1.0
    _CC["c_ecmp"] = ecmp.astype(bf)
    ebig = np.zeros((64, 2048), np.float32)
    kk = np.arange(2048)
    for h in range(2):
        ebig[h * 32 + kk // 64, kk] = 1.0
    _CC["c_ebig"] = ebig.astype(bf)
    jj, ii = np.meshgrid(np.arange(128), np.arange(128), indexing="ij")
    negc = np.where(jj > ii, -30000.0, 0.0)
    negu = np.where(jj < ii, -30000.0, 0.0)
    _CC["c_negb"] = np.stack([negc, negu]).astype(bf)
    q = np.arange(2048)
    n = np.arange(32)
    valid = (64 * n[None, :] + 63 <= q[:, None]).astype(np.float32)
    cur = q // 64
    forced = (n[None, :] == 0) | (n[None, :] == cur[:, None])
    future = n[None, :] > cur[:, None]
    keep = (~(forced | future)).astype(np.float32)
    add = np.where(forced, 1e4, np.where(future, -1e30, 0.0)).astype(np.float32)
    t3 = lambda a: np.ascontiguousarray(a.reshape(16, 128, 32).transpose(1, 0, 2))
    _CC["c_valid"], _CC["c_keep"], _CC["c_add"] = t3(valid), t3(keep), t3(add)
    _CC["c_validT"] = np.ascontiguousarray(valid.reshape(16, 128, 32).transpose(2, 0, 1)).astype(bf)
    _CC["c_iota"] = np.arange(128, dtype=np.float32)[:, None]
    r = np.arange(16)
    _CC["c_sum16"] = (r[:, None] % 4 == np.arange(4)[None, :]).astype(np.float32)
    _CC["c_sum16T"] = np.ascontiguousarray(_CC["c_sum16"].T)
    _CC["c_negnew"] = np.where(np.arange(4)[None, :] > (r % 4)[:, None], -30000.0, 0.0).astype(np.float32)
    _CC["c_negwin"] = np.where(np.arange(512)[None, :] < (r % 4)[:, None], -30000.0, 0.0).astype(np.float32)
    return _CC


_NC_CACHE = {}


def kernel(**inputs):
    inp = {n: np.asarray(v) for n, v in inputs.items()}
    if "nc" not in _NC_CACHE:
        _NC_CACHE["nc"] = build({"layers": DEPTH})
    nc = _NC_CACHE["nc"]
    in_maps = [make_core_inputs(inp, c) for c in range(8)]
    res = run_bass_kernel_spmd(nc, in_maps, core_ids=list(range(8)))
    r = res.results
    f32 = np.float32
    P4, S8 = range(4), range(8)
    st = lambda key, rng, fn=(lambda a: a): np.stack([fn(np.asarray(r[c][key])) for c in rng], axis=1).astype(f32)
    y_p = np.stack([r[b]["y_p"] for b in P4]).astype(f32)
    y_s = np.stack([r[c]["y_s"] for c in S8]).astype(f32)
    unch = lambda a: a.transpose(0, 3, 2, 1).reshape(a.shape[0], a.shape[3], a.shape[2] * 128)
    rows = lambda a: a.reshape(a.shape[0], a.shape[1], 2, 2, 128)
    sst = lambda a: a.reshape(2, 64, 64, 128)
    return (y_p, y_s,
            st("o_pkc", P4, rows), st("o_pks", P4, rows), st("o_pkw", P4, rows),
            st("o_psc", P4, unch), st("o_pst", P4, sst), st("o_pfc", P4, unch),
            st("o_skc", S8, rows), st("o_sks", S8, rows), st("o_skw", S8, rows), st("o_sv", S8),
            st("o_ssc", S8, unch), st("o_sst", S8, sst), st("o_sfc", S8, unch))
```
